# Optimizing a Trainium2 kernel written in Bass

```python
import jax, jax.numpy as jnp
from jax import lax
import numpy as np

D_MODEL = 1024
BATCH = 8
SEQ = 4096
DEPTH = 4

CHUNK = 64
HEAD_DIM = 64
NORM_EPS = 1e-6
D_RWKV = D_MODEL // 2
N_RWKV_HEADS = D_RWKV // HEAD_DIM
D_CONV = D_MODEL // 2
CONV_WIDTH = 31
LORA_DECAY = 64
LORA_ICLR = 64
LORA_GATE = 128
GN_EPS = 1e-5 * HEAD_DIM
CONV_LN_EPS = 1e-5
COLS_RWKV = 3 * D_RWKV + LORA_DECAY + LORA_ICLR + LORA_GATE
COLS_A = COLS_RWKV + 2 * D_CONV
N_FOX_HEADS = D_MODEL // HEAD_DIM
D_FOX = N_FOX_HEADS * HEAD_DIM
COLS_C = 4 * D_FOX + N_FOX_HEADS
Q_BLOCK = 128
D_FF = 2816
N_A_LAYERS = (DEPTH + 1) // 2
N_C_LAYERS = DEPTH // 2

kernel_name = "hybrid_rwkv7_conformer_fox_macaron"


def rms_norm(x, g):
    xf = x.astype(jnp.float32)
    y = xf * lax.rsqrt(jnp.mean(xf * xf, axis=-1, keepdims=True) + NORM_EPS)
    return (y * g.astype(jnp.float32)).astype(x.dtype)


def layer_norm(x, g, b, eps):
    xf = x.astype(jnp.float32)
    mu = jnp.mean(xf, axis=-1, keepdims=True)
    xc = xf - mu
    y = xc * lax.rsqrt(jnp.mean(xc * xc, axis=-1, keepdims=True) + eps)
    return (y * g.astype(jnp.float32) + b.astype(jnp.float32)).astype(x.dtype)


def swiglu(x, w_gate, w_up, w_down):
    return (jax.nn.silu(x @ w_gate) * (x @ w_up)) @ w_down


def split_cols(t, sizes):
    out, start = [], 0
    for s in sizes:
        out.append(t[..., start:start + s])
        start += s
    return out


def rwkv7_scan(r, w, k, v, kk, a):
    b, _, h, n = r.shape

    def step(state, inp):
        r_t, w_t, k_t, v_t, kk_t, a_t = inp
        s_kk = jnp.einsum('bhvk,bhk->bhv', state, kk_t)
        state = (state * w_t[:, :, None, :]
                 - s_kk[..., None] * (kk_t * a_t)[:, :, None, :]
                 + v_t[..., None] * k_t[:, :, None, :])
        y_t = jnp.einsum('bhvk,bhk->bhv', state, r_t)
        return state, y_t

    xs = tuple(jnp.moveaxis(t, 1, 0) for t in (r, w, k, v, kk, a))
    s0 = jnp.zeros((b, h, n, n), jnp.float32)
    _, ys = lax.scan(step, s0, xs)
    return jnp.moveaxis(ys, 0, 1)


def mixer_ab(h, w_in, shift_mu, decay_up, decay_base, iclr_up, iclr_base, gate_up,
             k_k, k_a, r_k, lnx_g, lnx_b, conv_w, conv_b, conv_ln_g, conv_ln_b, w_out):
    b, t, _ = h.shape
    f32 = jnp.float32
    proj = h @ w_in
    p = proj[..., :COLS_RWKV]
    p_prev = jnp.pad(p, ((0, 0), (1, 0), (0, 0)))[:, :-1]
    p = p + shift_mu * (p_prev - p)
    r, k, v, d_w, d_a, d_g = split_cols(p, [D_RWKV, D_RWKV, D_RWKV, LORA_DECAY, LORA_ICLR, LORA_GATE])
    w_log = -jax.nn.softplus(-(decay_base + jnp.tanh(d_w) @ decay_up)) - 0.5
    decay = jnp.exp(-jnp.exp(w_log.astype(f32)))
    a = jax.nn.sigmoid(iclr_base + d_a @ iclr_up).astype(f32)
    g = jax.nn.sigmoid(d_g) @ gate_up
    r = r.astype(f32)
    k = k.astype(f32)
    v = v.astype(f32)
    kk = (k * k_k.astype(f32)).reshape(b, t, N_RWKV_HEADS, HEAD_DIM)
    kk = kk * lax.rsqrt(jnp.maximum(jnp.sum(kk * kk, axis=-1, keepdims=True), 1e-24))
    k = k * (1.0 + (a - 1.0) * k_a.astype(f32))
    hs = lambda z: z.reshape(b, t, N_RWKV_HEADS, HEAD_DIM)
    rh, kh, vh = hs(r), hs(k), hs(v)
    y = rwkv7_scan(rh, hs(decay), kh, vh, kk, hs(a))
    mu = jnp.mean(y, axis=-1, keepdims=True)
    yc = y - mu
    y = yc * lax.rsqrt(jnp.mean(yc * yc, axis=-1, keepdims=True) + GN_EPS)
    y = y.reshape(b, t, D_RWKV) * lnx_g.astype(f32) + lnx_b.astype(f32)
    bonus = jnp.sum(rh * kh * r_k.astype(f32), axis=-1, keepdims=True) * vh
    y_a = ((y + bonus.reshape(b, t, D_RWKV)) * g.astype(f32)).astype(h.dtype)
    c_val, c_gate = split_cols(proj[..., COLS_RWKV:], [D_CONV, D_CONV])
    glu = c_val * jax.nn.sigmoid(c_gate)
    conv = lax.conv_general_dilated(
        glu, conv_w[:, None, :], window_strides=(1,), padding=[(CONV_WIDTH - 1, 0)],
        dimension_numbers=('NWC', 'WIO', 'NWC'), feature_group_count=D_CONV) + conv_b
    y_b = jax.nn.silu(layer_norm(conv, conv_ln_g, conv_ln_b, CONV_LN_EPS))
    return jnp.concatenate([y_a, y_b], axis=-1) @ w_out


def mixer_c(h, w_in, forget_bias, q_norm_g, k_norm_g, w_out):
    b, t, _ = h.shape
    f32 = jnp.float32
    proj = h @ w_in
    q, k, v, gate, f_logit = split_cols(proj, [D_FOX, D_FOX, D_FOX, D_FOX, N_FOX_HEADS])
    q = rms_norm(q.reshape(b, t, N_FOX_HEADS, HEAD_DIM), q_norm_g) * (HEAD_DIM ** -0.5)
    k = rms_norm(k.reshape(b, t, N_FOX_HEADS, HEAD_DIM), k_norm_g)
    v = v.reshape(b, t, N_FOX_HEADS, HEAD_DIM)
    log_f = jax.nn.log_sigmoid((f_logit + forget_bias).astype(f32))
    cum = jnp.transpose(jnp.cumsum(log_f, axis=1), (0, 2, 1))
    outs = []
    for blk in range(t // Q_BLOCK):
        q0 = blk * Q_BLOCK
        k_end = q0 + Q_BLOCK
        s = jnp.einsum('bqhd,bkhd->bhqk', q[:, q0:k_end], k[:, :k_end],
                       preferred_element_type=f32)
        s = s + cum[:, :, q0:k_end, None] - cum[:, :, None, :k_end]
        q_pos = q0 + jnp.arange(Q_BLOCK)
        mask = q_pos[:, None] >= jnp.arange(k_end)[None, :]
        prob = jax.nn.softmax(jnp.where(mask, s, -jnp.inf), axis=-1)
        outs.append(jnp.einsum('bhqk,bkhd->bqhd', prob.astype(v.dtype), v[:, :k_end]))
    o = jnp.concatenate(outs, axis=1).reshape(b, t, D_FOX)
    o = o * jax.nn.sigmoid(gate)
    return o @ w_out


def setup_inputs(seed: int = 0) -> dict:
    key = jax.random.key(seed)
    ks = list(jax.random.split(key, 40))

    def nrm(i, shape, scale):
        return scale * jax.random.normal(ks[i], shape, jnp.float32)

    def uni(i, shape, lo, hi):
        return jax.random.uniform(ks[i], shape, jnp.float32, minval=lo, maxval=hi)

    na, nc = N_A_LAYERS, N_C_LAYERS
    return {
        "x": nrm(0, (BATCH, SEQ, D_MODEL), 1.0),
        "norm_gains": 1.0 + nrm(1, (DEPTH, 6, D_MODEL), 0.02),
        "ffn_w_gate": nrm(2, (DEPTH, 2, D_MODEL, D_FF), D_MODEL ** -0.5),
        "ffn_w_up": nrm(3, (DEPTH, 2, D_MODEL, D_FF), D_MODEL ** -0.5),
        "ffn_w_down": nrm(4, (DEPTH, 2, D_FF, D_MODEL), D_FF ** -0.5),
        "a_w_in": nrm(5, (na, D_MODEL, COLS_A), D_MODEL ** -0.5),
        "a_shift_mu": uni(6, (na, COLS_RWKV), 0.0, 1.0),
        "a_decay_up": nrm(7, (na, LORA_DECAY, D_RWKV), 0.1),
        "a_decay_base": uni(8, (na, D_RWKV), -6.0, 1.0),
        "a_iclr_up": nrm(9, (na, LORA_ICLR, D_RWKV), 0.5 * LORA_ICLR ** -0.5),
        "a_iclr_base": nrm(10, (na, D_RWKV), 0.5),
        "a_gate_up": nrm(11, (na, LORA_GATE, D_RWKV), LORA_GATE ** -0.5),
        "a_k_k": 0.85 + nrm(12, (na, D_RWKV), 0.05),
        "a_k_a": 1.0 + nrm(13, (na, D_RWKV), 0.05),
        "a_r_k": nrm(14, (na, N_RWKV_HEADS, HEAD_DIM), 0.1),
        "a_lnx_g": 1.0 + nrm(15, (na, D_RWKV), 0.02),
        "a_lnx_b": nrm(16, (na, D_RWKV), 0.02),
        "b_conv_w": nrm(17, (na, CONV_WIDTH, D_CONV), CONV_WIDTH ** -0.5),
        "b_conv_b": nrm(18, (na, D_CONV), 0.02),
        "b_ln_g": 1.0 + nrm(19, (na, D_CONV), 0.02),
        "b_ln_b": nrm(20, (na, D_CONV), 0.02),
        "ab_w_out": nrm(21, (na, D_RWKV + D_CONV, D_MODEL), (D_RWKV + D_CONV) ** -0.5),
        "c_w_in": nrm(22, (nc, D_MODEL, COLS_C), D_MODEL ** -0.5),
        "c_forget_bias": 2.0 + nrm(23, (nc, N_FOX_HEADS), 0.5),
        "c_q_norm_g": 1.0 + nrm(24, (nc, HEAD_DIM), 0.02),
        "c_k_norm_g": 1.0 + nrm(25, (nc, HEAD_DIM), 0.02),
        "c_w_out": nrm(26, (nc, D_FOX, D_MODEL), D_FOX ** -0.5),
    }


def reference(x, norm_gains, ffn_w_gate, ffn_w_up, ffn_w_down,
              a_w_in, a_shift_mu, a_decay_up, a_decay_base, a_iclr_up, a_iclr_base, a_gate_up,
              a_k_k, a_k_a, a_r_k, a_lnx_g, a_lnx_b,
              b_conv_w, b_conv_b, b_ln_g, b_ln_b, ab_w_out,
              c_w_in, c_forget_bias, c_q_norm_g, c_k_norm_g, c_w_out):
    for layer in range(DEPTH):
        g = norm_gains[layer]
        f = swiglu(rms_norm(x, g[0]), ffn_w_gate[layer, 0], ffn_w_up[layer, 0], ffn_w_down[layer, 0])
        x = x + 0.5 * rms_norm(f, g[1])
        h = rms_norm(x, g[2])
        i = layer // 2
        if layer % 2 == 0:
            m = mixer_ab(h, a_w_in[i], a_shift_mu[i], a_decay_up[i], a_decay_base[i],
                         a_iclr_up[i], a_iclr_base[i], a_gate_up[i], a_k_k[i], a_k_a[i],
                         a_r_k[i], a_lnx_g[i], a_lnx_b[i], b_conv_w[i], b_conv_b[i],
                         b_ln_g[i], b_ln_b[i], ab_w_out[i])
        else:
            m = mixer_c(h, c_w_in[i], c_forget_bias[i], c_q_norm_g[i], c_k_norm_g[i], c_w_out[i])
        x = x + rms_norm(m, g[3])
        f = swiglu(rms_norm(x, g[4]), ffn_w_gate[layer, 1], ffn_w_up[layer, 1], ffn_w_down[layer, 1])
        x = x + 0.5 * rms_norm(f, g[5])
    return x
```

```python
import contextlib
import numpy as np
import concourse.bass as bass
import concourse.mybir as mybir
from concourse.bass_utils import run_bass_kernel_spmd

F32 = mybir.dt.float32
BF16 = mybir.dt.bfloat16
AF = mybir.ActivationFunctionType
ALU = mybir.AluOpType
AX = mybir.AxisListType

D = 1024
DFF = 2816
NCH = D // 128
NFF = DFF // 128
EPS = 1e-6


class Buf:
    def __init__(self, prog, t, name, dma=False):
        self.t = t
        self.name = name
        self.lw = None
        self.rd = {}
        self.dsem = None
        self.psum = False
        self.dkind = None
        if dma:
            self.dkind = "sw" if dma == "sw" else "hw"
            self.dsem = prog.get_dsem(self.dkind)

    def __getitem__(self, k):
        return self.t[k]


class Prog:
    ENG = ("pe", "act", "dve", "pool", "sp")

    def __init__(self, nc):
        self.nc = nc
        self.e = {"pe": nc.tensor, "act": nc.scalar, "dve": nc.vector, "pool": nc.gpsimd, "sp": nc.sync}
        self.stack = contextlib.ExitStack()
        self.sems = {}
        self.val = {}
        self.seen = {e: {} for e in self.ENG}
        for e in self.ENG:
            self.sems[e] = self.stack.enter_context(nc.semaphore("c_" + e))
            self.val[e] = 0
        self.dpool = {"hw": [], "sw": []}
        self.ndsem = 0
        self.live_dsems = set()
        self.n_ins = 0
        self.n_wait = 0

    def get_dsem(self, kind):
        if self.dpool[kind]:
            k = self.dpool[kind].pop()
        else:
            k = "d%s%d" % (kind, self.ndsem)
            self.ndsem += 1
            self.sems[k] = self.stack.enter_context(self.nc.semaphore(k))
            self.val[k] = 0
        self.live_dsems.add(k)
        return k

    def release(self, bufs):
        for b in bufs:
            if b.dsem is not None:
                self.live_dsems.discard(b.dsem)
                self.dpool[b.dkind].append(b.dsem)
                b.dsem = None

    def clear_all(self):
        for k, h in self.sems.items():
            self.nc.gpsimd.sem_clear(h)

    def _need(self, e, waits, tok):
        if tok is None:
            return
        k, v = tok[0], tok[1]
        if self.seen[e].get(k, 0) >= v:
            return
        if waits.get(k, 0) < v:
            waits[k] = v

    def _deps(self, e, rd, wr):
        waits = {}
        for b in rd:
            self._need(e, waits, b.lw)
            if b.psum:
                for k, (v, re_) in b.rd.items():
                    if re_ != e:
                        self._need(e, waits, (k, v))
        for b in wr:
            if b.lw is not None and not (e == "pe" and b.lw[2] == "pe"):
                self._need(e, waits, b.lw)
            for k, (v, re_) in b.rd.items():
                self._need(e, waits, (k, v))
        for k, v in waits.items():
            self.e[e].wait_ge(self.sems[k], v)
            self.seen[e][k] = v
            self.n_wait += 1

    def op(self, e, fn, rd=(), wr=()):
        self._deps(e, rd, wr)
        ins = fn()
        self.val[e] += 1
        ins.then_inc(self.sems[e], 1)
        self.n_ins += 1
        v = self.val[e]
        for b in rd:
            b.rd[e] = (v, e)
        for b in wr:
            b.lw = (e, v, e)
            b.rd = {}
        return ins

    def dma(self, e, dst, src, out_ap, in_ap, **kw):
        assert dst.dsem is not None, dst.name
        assert (dst.dkind == "sw") == (e == "pool"), (dst.name, e)
        self._deps(e, (src,), (dst,))
        ins = self.e[e].dma_start(out=out_ap, in_=in_ap, **kw)
        k = dst.dsem
        self.val[k] += 16
        ins.then_inc(self.sems[k], 16)
        self.n_ins += 1
        v = self.val[k]
        src.rd[k] = (v, "dma")
        dst.lw = (k, v, "dma")
        dst.rd = {}
        return ins

    def barrier(self):
        for k in list(self.live_dsems):
            v = self.val[k]
            if v > 0 and self.seen["sp"].get(k, 0) < v:
                self.e["sp"].wait_ge(self.sems[k], v)
                self.seen["sp"][k] = v
        self.nc.all_engine_barrier()
        for e in self.ENG:
            for k in self.sems:
                self.seen[e][k] = self.val[k]

    def wait_all_dma(self, e):
        for k in list(self.live_dsems):
            v = self.val[k]
            if v > 0 and self.seen[e].get(k, 0) < v:
                self.e[e].wait_ge(self.sems[k], v)
                self.seen[e][k] = v


class Ctx:
    def __init__(self, P):
        self.P = P
        self.nc = P.nc
        self.stack = contextlib.ExitStack()
        self.bufs = []

    def sb(self, name, shape, dt, dma=False):
        t = self.stack.enter_context(self.nc.sbuf_tensor(name, list(shape), dt))
        b = Buf(self.P, t, name, dma=dma)
        self.bufs.append(b)
        return b

    def ps(self, name, shape=(128, 512), dt=F32):
        t = self.stack.enter_context(self.nc.psum_tensor(name, list(shape), dt))
        b = Buf(self.P, t, name)
        b.psum = True
        self.bufs.append(b)
        return b

    def close(self):
        self.P.barrier()
        self.P.release(self.bufs)
        self.stack.close()


_uid = [0]


def uname(s):
    _uid[0] += 1
    return "%s_%d" % (s, _uid[0])


def make_consts(P, C):
    nc = P.nc
    idf = C.sb(uname("ident_f"), (128, 128), F32)
    idb = C.sb(uname("ident_b"), (128, 128), BF16)
    P.op("pool", lambda: nc.gpsimd.memset(idf[:], 0.0), wr=(idf,))
    P.op("pool", lambda: nc.gpsimd.affine_select(out=idf[:], in_=idf[:], compare_op=ALU.not_equal, fill=1.0,
                                                  base=0, pattern=[[-1, 128]], channel_multiplier=1),
         rd=(idf,), wr=(idf,))
    P.op("pool", lambda: nc.gpsimd.tensor_copy(out=idb[:], in_=idf[:]), rd=(idf,), wr=(idb,))
    return idf, idb


def load_gain_bc(P, C, name, src_ap, scale):
    nc = P.nc
    g = C.sb(name, (128, D), F32, dma=True)
    dsrc = Buf(P, None, name + "_src")
    P.dma("sp", g, dsrc, g[:], src_ap.partition_broadcast(128))
    if scale != 1.0:
        P.op("pool", lambda: nc.gpsimd.tensor_scalar(out=g[:], in0=g[:], scalar1=float(scale), scalar2=None,
                                                      op0=ALU.mult), rd=(g,), wr=(g,))
    return g


def prenorm_tile(P, nc, xs, gpre, hT, col0, idb, sq_junk, stat, hb, tp_ps):
    P.op("act", lambda: nc.scalar.activation(out=sq_junk[:], in_=xs[:], func=AF.Square, accum_out=stat[:, 0:1]),
         rd=(xs,), wr=(sq_junk, stat))
    P.op("act", lambda: nc.scalar.activation(out=stat[:, 6:7], in_=stat[:, 0:1], func=AF.Sqrt, bias=float(EPS),
                                             scale=1.0 / D), rd=(stat,), wr=(stat,))
    P.op("dve", lambda: nc.vector.reciprocal(out=stat[:, 1:2], in_=stat[:, 6:7]), rd=(stat,), wr=(stat,))
    P.op("dve", lambda: nc.vector.scalar_tensor_tensor(out=hb[:], in0=xs[:], scalar=stat[:, 1:2], in1=gpre[:],
                                                       op0=ALU.mult, op1=ALU.mult), rd=(xs, stat, gpre), wr=(hb,))
    for c in range(NCH):
        P.op("pe", lambda c=c: nc.tensor.transpose(out=tp_ps[:, c * 128:(c + 1) * 128],
                                                   in_=hb[:, c * 128:(c + 1) * 128], identity=idb[:]),
             rd=(hb, idb), wr=(tp_ps,))
    P.op("act", lambda: nc.scalar.copy(out=hT[:, :, col0:col0 + 128],
                                       in_=tp_ps[:].rearrange("p (c t) -> p c t", c=NCH)),
         rd=(tp_ps,), wr=(hT,))


def postnorm_tile(P, nc, f_ps, xs, gpost, xo, sq_junk, stat, tmp):
    P.op("act", lambda: nc.scalar.activation(out=sq_junk[:, 0:512], in_=f_ps[0][:], func=AF.Square,
                                             accum_out=stat[:, 2:3]), rd=(f_ps[0],), wr=(sq_junk, stat))
    P.op("act", lambda: nc.scalar.activation(out=sq_junk[:, 512:1024], in_=f_ps[1][:], func=AF.Square,
                                             accum_out=stat[:, 3:4]), rd=(f_ps[1],), wr=(sq_junk, stat))
    P.op("dve", lambda: nc.vector.tensor_tensor(out=stat[:, 4:5], in0=stat[:, 2:3], in1=stat[:, 3:4], op=ALU.add),
         rd=(stat,), wr=(stat,))
    P.op("act", lambda: nc.scalar.activation(out=stat[:, 7:8], in_=stat[:, 4:5], func=AF.Sqrt, bias=float(EPS),
                                             scale=1.0 / D), rd=(stat,), wr=(stat,))
    P.op("dve", lambda: nc.vector.reciprocal(out=stat[:, 5:6], in_=stat[:, 7:8]), rd=(stat,), wr=(stat,))
    for h in range(2):
        P.op("dve", lambda h=h: nc.vector.scalar_tensor_tensor(out=tmp[:, h * 512:(h + 1) * 512], in0=f_ps[h][:],
                                                               scalar=stat[:, 5:6],
                                                               in1=gpost[:, h * 512:(h + 1) * 512],
                                                               op0=ALU.mult, op1=ALU.mult),
             rd=(f_ps[h], stat, gpost), wr=(tmp,))
    P.op("pool", lambda: nc.gpsimd.tensor_tensor(out=xo[:], in0=tmp[:], in1=xs[:], op=ALU.add),
         rd=(tmp, xs), wr=(xo,))


def load_w(P, wb, out_ap, src_ap):
    P.dma("pool", wb, Buf(P, None, "wsrc"), out_ap, src_ap)


def ffn_phase(P, T, xin, xout, xin_ap, xout_ap, wg_ap, wu_ap, wd_ap, gpre_ap, gpost_ap):
    nc = P.nc
    C = Ctx(P)
    idf, idb = make_consts(P, C)
    gpre = load_gain_bc(P, C, uname("gpre"), gpre_ap, 1.0)
    gpost = load_gain_bc(P, C, uname("gpost"), gpost_ap, 0.5)
    NB = 11
    CB = DFF // NB
    JB = NFF // NB
    wg = [C.sb(uname("wg"), (128, NCH, CB), BF16, dma="sw") for _ in range(NB)]
    wu = [C.sb(uname("wu"), (128, NCH, CB), BF16, dma="sw") for _ in range(NB)]
    wd = [C.sb(uname("wd"), (128, 2, D), BF16, dma="sw") for _ in range(NFF // 2)]
    wg_v = wg_ap.rearrange("(c p) n -> p c n", p=128)
    wu_v = wu_ap.rearrange("(c p) n -> p c n", p=128)
    wd_v = wd_ap.rearrange("(j p) n -> p j n", p=128)
    for b in range(NB):
        for (wl, wv) in ((wg, wg_v), (wu, wu_v)):
            load_w(P, wl[b], wl[b][:, :, :], wv[:, :, b * CB:(b + 1) * CB])
        if b >= 5:
            for j2 in (2 * (b - 5), 2 * (b - 5) + 1):
                if j2 < NFF // 2:
                    load_w(P, wd[j2], wd[j2][:, :, :], wd_v[:, 2 * j2:2 * j2 + 2, :])

    TT = 512 if T >= 512 else T
    NS = TT // 128
    xs = [C.sb(uname("xs"), (128, D), F32, dma=True) for _ in range(2)]
    xp = [C.sb(uname("xp"), (128, D), F32, dma=True) for _ in range(2)]
    hT = [C.sb(uname("hT"), (128, NCH, TT), BF16) for _ in range(2)]
    actT = C.sb(uname("actT"), (128, NFF, TT), BF16)
    sq_junk = C.sb(uname("sqj"), (128, D), BF16)
    hb = [C.sb(uname("hb"), (128, D), BF16) for _ in range(2)]
    tmp = C.sb(uname("tmp"), (128, D), F32)
    sg = [C.sb(uname("sg"), (128, TT), BF16) for _ in range(2)]
    stats = [C.sb(uname("stat"), (128, 8), F32) for _ in range(4)]
    tp_ps = C.ps(uname("tp"), (128, D), BF16)
    g_ps = [C.ps(uname("gps")) for _ in range(2)]
    u_ps = [C.ps(uname("ups")) for _ in range(2)]
    f3 = [C.ps(uname("fps")) for _ in range(3)]
    xin_v = xin_ap.rearrange("(n p) d -> n p d", p=128)
    xout_v = xout_ap.rearrange("(n p) d -> n p d", p=128)
    nt = T // TT

    def pre(ti, s):
        xst = xs[s % 2]
        P.dma("sp", xst, xin, xst[:], xin_v[ti * NS + s])
        prenorm_tile(P, nc, xst, gpre, hT[ti % 2], s * 128, idb, sq_junk, stats[s % 4], hb[s % 2], tp_ps)

    for s in range(NS):
        pre(0, s)
    it = 0
    for ti in range(nt):
        hTt = hT[ti % 2]
        for j in range(NFF):
            gp, up = g_ps[j % 2], u_ps[j % 2]
            bb, a0 = j // JB, (j % JB) * 128
            for (wl, pp) in ((wg, gp), (wu, up)):
                for c in range(NCH):
                    P.op("pe", lambda: nc.tensor.matmul(pp[:, 0:TT], lhsT=wl[bb][:, c, a0:a0 + 128],
                                                        rhs=hTt[:, c, :], start=(c == 0), stop=(c == NCH - 1)),
                         rd=(wl[bb], hTt), wr=(pp,))
            sgj = sg[j % 2]
            P.op("act", lambda: nc.scalar.activation(out=sgj[:], in_=gp[:, 0:TT], func=AF.Silu),
                 rd=(gp,), wr=(sgj,))
            P.op("dve", lambda: nc.vector.tensor_tensor(out=actT[:, j, :], in0=up[:, 0:TT], in1=sgj[:],
                                                        op=ALU.mult), rd=(up, sgj), wr=(actT,))
            if ti + 1 < nt and j in (4, 8, 12, 16) and (j // 4 - 1) < NS:
                pre(ti + 1, j // 4 - 1)
        for s in range(NS):
            xst = xp[it % 2]
            f_ps = [f3[(2 * it) % 3], f3[(2 * it + 1) % 3]]
            P.dma("sp", xst, xin, xst[:], xin_v[ti * NS + s])
            for h in range(2):
                for j in range(NFF):
                    P.op("pe", lambda: nc.tensor.matmul(
                        f_ps[h][:, :], lhsT=actT[:, j, s * 128:(s + 1) * 128],
                        rhs=wd[j // 2][:, j % 2, h * 512:(h + 1) * 512],
                        start=(j == 0), stop=(j == NFF - 1)), rd=(actT, wd[j // 2]), wr=(f_ps[h],))
            it += 1
            postnorm_tile(P, nc, f_ps, xst, gpost, xst, sq_junk, stats[s % 4], tmp)
            P.dma("sp", xout, xst, xout_v[ti * NS + s], xst[:])
    C.close()


def fox_phase(P, T, xin, xout, xin_ap, xout_ap, win_ap, fb_ap, qg_ap, kg_ap, wout_ap, gpre_ap, gpost_ap, scr):
    nc = P.nc
    H = 16
    NT = T // 512 if T >= 512 else 1
    TT = 512 if T >= 512 else T
    NSUB = T // 128
    hTd_ap, hTd = scr["hT"]
    oTd_ap, oTd = scr["oT"]
    win_v = win_ap.rearrange("(c p) n -> p c n", p=128)
    xin_v = xin_ap.rearrange("(n p) d -> n p d", p=128)
    xout_v = xout_ap.rearrange("(n p) d -> n p d", p=128)
    dsrc = Buf(P, None, "dsrc")

    C = Ctx(P)
    idf, idb = make_consts(P, C)
    gpre = load_gain_bc(P, C, uname("gpre"), gpre_ap, 1.0)
    wf = C.sb(uname("wf"), (128, NCH, H), BF16, dma="sw")
    load_w(P, wf, wf[:], win_v[:, :, 4 * D:4 * D + H])
    nfb = C.sb(uname("nfb"), (H, 1), F32, dma=True)
    P.dma("sp", nfb, dsrc, nfb[:], fb_ap.rearrange("(h o) -> h o", o=1))
    P.op("dve", lambda: nc.vector.tensor_scalar(out=nfb[:], in0=nfb[:], scalar1=-1.0, scalar2=None, op0=ALU.mult),
         rd=(nfb,), wr=(nfb,))
    cum = C.sb(uname("cum"), (H, T), F32)
    ones16 = C.sb(uname("ones16"), (H, TT), F32)
    P.op("pool", lambda: nc.gpsimd.memset(ones16[:], 1.0), wr=(ones16,))
    lfe = [C.sb(uname("lfe"), (H, TT), F32) for _ in range(2)]
    xs = [C.sb(uname("xs"), (128, D), F32, dma=True) for _ in range(2)]
    hTt = [C.sb(uname("hTt"), (128, NCH, TT), BF16) for _ in range(2)]
    sq_junk = C.sb(uname("sqj"), (128, D), BF16)
    hb = [C.sb(uname("hb"), (128, D), BF16) for _ in range(2)]
    stats = [C.sb(uname("stat"), (128, 8), F32) for _ in range(4)]
    tp_ps = C.ps(uname("tp"), (128, D), BF16)
    f_ps = [C.ps(uname("fl")) for _ in range(2)]
    NS = TT // 128
    for ti in range(NT):
        ht = hTt[ti % 2]
        for s_ in range(NS):
            xst = xs[s_ % 2]
            P.dma("sp", xst, xin, xst[:], xin_v[ti * NS + s_])
            prenorm_tile(P, nc, xst, gpre, ht, s_ * 128, idb, sq_junk, stats[s_ % 4], hb[s_ % 2], tp_ps)
        P.dma("sp", hTd, ht, hTd_ap[:, :, ti * TT:(ti + 1) * TT], ht[:])
        fp = f_ps[ti % 2]
        for c in range(NCH):
            P.op("pe", lambda: nc.tensor.matmul(fp[0:H, 0:TT], lhsT=wf[:, c, :], rhs=ht[:, c, :],
                                                start=(c == 0), stop=(c == NCH - 1)), rd=(wf, ht), wr=(fp,))
        le = lfe[ti % 2]
        P.op("act", lambda: nc.scalar.activation(out=le[:], in_=fp[0:H, 0:TT], func=AF.Exp, bias=nfb[:, 0:1],
                                                 scale=-1.0), rd=(fp, nfb), wr=(le,))
        P.op("act", lambda: nc.scalar.activation(out=le[:], in_=le[:], func=AF.Ln, bias=1.0, scale=1.0),
             rd=(le,), wr=(le,))
        init = 0.0 if ti == 0 else cum[:, ti * TT - 1:ti * TT]
        P.op("dve", lambda: nc.vector.tensor_tensor_scan(out=cum[:, ti * TT:(ti + 1) * TT], data0=ones16[:],
                                                         data1=le[:], initial=init, op0=ALU.mult,
                                                         op1=ALU.subtract), rd=(ones16, le, cum), wr=(cum,))
    cumd_ap, cumd = scr["cum"]
    P.dma("sp", cumd, cum, cumd_ap[:, 0:T], cum[:])
    C.close()

    C = Ctx(P)
    idf, idb = make_consts(P, C)
    cum = C.sb(uname("cum"), (H, T), F32, dma=True)
    P.dma("sp", cum, cumd, cum[:], cumd_ap[:, 0:T])
    sel = C.sb(uname("sel"), (H, H, 128), F32)
    P.op("pool", lambda: nc.gpsimd.memset(sel[:], 0.0), wr=(sel,))
    P.op("pool", lambda: nc.gpsimd.affine_select(out=sel[:], in_=sel[:], compare_op=ALU.not_equal, fill=1.0, base=0,
                                                  pattern=[[-1, H], [0, 128]], channel_multiplier=1),
         rd=(sel,), wr=(sel,))
    swp = C.sb(uname("swp"), (128, 128), F32)
    P.op("pool", lambda: nc.gpsimd.memset(swp[:], 0.0), wr=(swp,))
    P.op("pool", lambda: nc.gpsimd.affine_select(out=swp[:, 0:64], in_=swp[:, 0:64], compare_op=ALU.not_equal,
                                                  fill=1.0, base=-64, pattern=[[-1, 64]], channel_multiplier=1),
         rd=(swp,), wr=(swp,))
    P.op("pool", lambda: nc.gpsimd.affine_select(out=swp[:, 64:128], in_=swp[:, 64:128], compare_op=ALU.not_equal,
                                                  fill=1.0, base=0, pattern=[[-1, 64]], channel_multiplier=1),
         rd=(swp,), wr=(swp,))
    bones = C.sb(uname("bones"), (128, 128), F32)
    P.op("pool", lambda: nc.gpsimd.memset(bones[:], 0.0), wr=(bones,))
    P.op("pool", lambda: nc.gpsimd.memset(bones[0:64, 0:64], 1.0), wr=(bones,))
    P.op("pool", lambda: nc.gpsimd.memset(bones[64:128, 64:128], 1.0), wr=(bones,))
    tri = C.sb(uname("tri"), (128, 128), F32)
    P.op("pool", lambda: nc.gpsimd.memset(tri[:], 0.0), wr=(tri,))
    P.op("pool", lambda: nc.gpsimd.affine_select(out=tri[:], in_=tri[:], compare_op=ALU.is_ge, fill=-30000.0, base=0,
                                                  pattern=[[1, 128]], channel_multiplier=-1), rd=(tri,), wr=(tri,))
    ncumT = C.sb(uname("ncumT"), (128, NSUB, H), F32)
    ct_ps = C.ps(uname("ctps"))
    for g0 in range(0, NSUB, 32):
        gn = min(32, NSUB - g0)
        for b in range(gn):
            P.op("pe", lambda: nc.tensor.transpose(out=ct_ps[:, b * H:(b + 1) * H],
                                                   in_=cum[:, (g0 + b) * 128:(g0 + b + 1) * 128],
                                                   identity=idf[0:H, 0:H]), rd=(cum, idf), wr=(ct_ps,))
        P.op("dve", lambda: nc.vector.tensor_scalar(out=ncumT[:, g0:g0 + gn, :],
                                                    in0=ct_ps[:, 0:gn * H].rearrange("p (b h) -> p b h", h=H),
                                                    scalar1=-1.0, scalar2=None, op0=ALU.mult),
             rd=(ct_ps,), wr=(ncumT,))
    gq2 = C.sb(uname("gq2"), (128, 1), F32, dma=True)
    gk2 = C.sb(uname("gk2"), (128, 1), F32, dma=True)
    for hh in range(2):
        P.dma("sp", gq2, dsrc, gq2[hh * 64:(hh + 1) * 64, :], qg_ap.rearrange("(d o) -> d o", o=1))
        P.dma("sp", gk2, dsrc, gk2[hh * 64:(hh + 1) * 64, :], kg_ap.rearrange("(d o) -> d o", o=1))
    P.op("dve", lambda: nc.vector.tensor_scalar(out=gq2[:], in0=gq2[:], scalar1=0.125, scalar2=None, op0=ALU.mult),
         rd=(gq2,), wr=(gq2,))
    wq = [C.sb(uname("wq"), (128, NCH, 4, 128), BF16, dma="sw") for _ in range(2)]
    hTt = [C.sb(uname("hTt"), (128, NCH, TT), BF16, dma=True) for _ in range(2)]
    qT = C.sb(uname("qT"), (128, T), BF16)
    kT = "kT"
    kTz = [C.sb(uname("kTz"), (128, T), BF16) for _ in range(2)]
    for hh in range(2):
        P.op("pool", lambda: nc.gpsimd.memset(kTz[hh][:], 0.0), wr=(kTz[hh],))
    sgT = C.sb(uname("sgT"), (128, T), BF16)
    oT = C.sb(uname("oT"), (128, T), BF16)
    Va = [C.sb(uname("Va"), (128, NSUB, 128), BF16) for _ in range(2)]
    P.op("pool", lambda: nc.gpsimd.memset(Va[0][:, :, 64:128], 1.0), wr=(Va[0],))
    P.op("pool", lambda: nc.gpsimd.memset(Va[1][:, :, 0:64], 1.0), wr=(Va[1],))
    cbc = C.sb(uname("cbc"), (128, T), F32)
    sqf = [C.sb(uname("sqf"), (128, TT), F32) for _ in range(2)]
    rsf = [C.sb(uname("rsf"), (128, TT), F32) for _ in range(2)]
    tmpb = [C.sb(uname("tmpb"), (128, TT), F32) for _ in range(3)]
    osb = C.sb(uname("osb"), (128, TT), F32)
    rden = C.sb(uname("rden"), (128, TT), F32)
    onum = C.sb(uname("onum"), (128, TT), F32)
    st2 = [C.ps(uname("st2"), (128, 2 * TT), F32) for _ in range(2)]
    acc = [C.ps(uname("acc")) for _ in range(3)] + [ct_ps]
    bank = [st2[0], st2[1]] + acc
    pT2 = [C.sb(uname("pT2"), (128, 2 * TT), BF16) for _ in range(3)]
    ecr = [C.sb(uname("ecr"), (128, TT), F32) for _ in range(2)]
    dsw = C.sb(uname("dsw"), (128, TT), F32, dma=True)
    cbcm = C.sb(uname("cbcm"), (128, T), F32)
    tri4 = C.sb(uname("tri4"), (128, TT), F32)
    for r_ in range(TT // 128):
        P.op("pool", lambda: nc.gpsimd.tensor_copy(out=tri4[:, r_ * 128:(r_ + 1) * 128], in_=tri[:]), rd=(tri,),
             wr=(tri4,))
    ncb0 = C.sb(uname("ncb0"), (128, NT), F32)
    biasall = C.sb(uname("biasall"), (128, NT, NSUB), F32)
    for hp in range(H // 2):
        w = wq[hp % 2]
        for qi in range(4):
            load_w(P, w, w[:, :, qi, :], win_v[:, :, qi * D + hp * 128: qi * D + (hp + 1) * 128])
        for ti in range(NT):
            ht = hTt[ti % 2]
            P.dma("sp", ht, hTd, ht[:], hTd_ap[:, :, ti * TT:(ti + 1) * TT])
            tsl = slice(ti * TT, (ti + 1) * TT)
            pset = ti % 2
            qk_b = st2[pset]
            q_ps, k_ps = qk_b[:, 0:TT], qk_b[:, TT:2 * TT]
            g_ps, v_ps = acc[2 * pset], acc[2 * pset + 1]
            ss_ps = v_ps
            for (qi, pp, pb_) in ((0, q_ps, qk_b), (1, k_ps, qk_b), (3, g_ps[:, 0:TT], g_ps)):
                for c in range(NCH):
                    P.op("pe", lambda: nc.tensor.matmul(pp, lhsT=w[:, c, qi, :], rhs=ht[:, c, :],
                                                        start=(c == 0), stop=(c == NCH - 1)), rd=(w, ht), wr=(pb_,))
            for s_ in range(NS):
                for c in range(NCH):
                    P.op("pe", lambda: nc.tensor.matmul(v_ps[:, s_ * 128:(s_ + 1) * 128],
                                                        lhsT=ht[:, c, s_ * 128:(s_ + 1) * 128], rhs=w[:, c, 2, :],
                                                        start=(c == 0), stop=(c == NCH - 1)), rd=(w, ht), wr=(v_ps,))
            vv = v_ps[:, 0:TT].rearrange("p (s e) -> p s e", e=128)
            P.op("act", lambda: nc.scalar.copy(out=Va[0][:, ti * NS:(ti + 1) * NS, 0:64], in_=vv[:, :, 0:64]),
                 rd=(v_ps,), wr=(Va[0],))
            P.op("dve", lambda: nc.vector.tensor_copy(out=Va[1][:, ti * NS:(ti + 1) * NS, 64:128],
                                                      in_=vv[:, :, 64:128]), rd=(v_ps,), wr=(Va[1],))
            P.op("act", lambda: nc.scalar.activation(out=sgT[:, tsl], in_=g_ps[:, 0:TT], func=AF.Sigmoid),
                 rd=(g_ps,), wr=(sgT,))
            for n_, (pp, gg, dst) in enumerate(((q_ps, gq2, qT), (k_ps, gk2, kT))):
                sq, rs = sqf[n_], rsf[n_]
                P.op("act", lambda: nc.scalar.activation(out=sq[:], in_=pp, func=AF.Square),
                     rd=(qk_b,), wr=(sq,))
                P.op("pe", lambda: nc.tensor.matmul(ss_ps[:, 0:TT], lhsT=bones[:], rhs=sq[:], start=True, stop=True),
                     rd=(bones, sq), wr=(ss_ps,))
                P.op("act", lambda: nc.scalar.activation(out=rs[:], in_=ss_ps[:, 0:TT], func=AF.Ln,
                                                         bias=float(EPS), scale=1.0 / 64), rd=(ss_ps,), wr=(rs,))
                P.op("act", lambda: nc.scalar.activation(out=rs[:], in_=rs[:], func=AF.Exp, scale=-0.5),
                     rd=(rs,), wr=(rs,))
                if dst is kT:
                    for hh in range(2):
                        hs_ = slice(hh * 64, (hh + 1) * 64)
                        P.op("dve", lambda: nc.vector.scalar_tensor_tensor(out=kTz[hh][hs_, tsl], in0=pp[hs_, :],
                                                                           scalar=gg[hs_, 0:1], in1=rs[hs_, :],
                                                                           op0=ALU.mult, op1=ALU.mult),
                             rd=(qk_b, gg, rs), wr=(kTz[hh],))
                else:
                    P.op("dve", lambda: nc.vector.scalar_tensor_tensor(out=dst[:, tsl], in0=pp,
                                                                       scalar=gg[:, 0:1], in1=rs[:], op0=ALU.mult,
                                                                       op1=ALU.mult), rd=(qk_b, gg, rs), wr=(dst,))
        for par in range(2):
            h = 2 * hp + par
            pr = slice(par * 64, (par + 1) * 64)
            for ti in range(NT):
                cp = acc[ti % 4]
                P.op("pe", lambda: nc.tensor.matmul(cp[:, 0:TT], lhsT=sel[:, h, :], rhs=cum[:, ti * TT:(ti + 1) * TT],
                                                    start=True, stop=True), rd=(sel, cum), wr=(cp,))
                P.op("act", lambda: nc.scalar.copy(out=cbc[:, ti * TT:(ti + 1) * TT], in_=cp[:, 0:TT]),
                     rd=(cp,), wr=(cbc,))
                P.op("pool", lambda: nc.gpsimd.tensor_tensor(out=cbcm[:, ti * TT:(ti + 1) * TT],
                                                              in0=cbc[:, ti * TT:(ti + 1) * TT], in1=tri4[:],
                                                              op=ALU.add), rd=(cbc, tri4), wr=(cbcm,))
            pairs = [list(range(t0_, min(t0_ + 2, NT))) for t0_ in range(0, NT, 2)]
            c0v = cbc[:].rearrange("p (n t) -> p n t", t=TT * 2 if NT > 1 else TT)[:, :, 0]
            P.op("dve", lambda: nc.vector.tensor_scalar(out=ncb0[:, 0:len(pairs)], in0=c0v, scalar1=-1.0, scalar2=None,
                                                        op0=ALU.mult), rd=(cbc,), wr=(ncb0,))
            for m, tl in enumerate(pairs):
                if m == 0:
                    continue
                nb_ = tl[0] * NS
                P.op("dve", lambda: nc.vector.tensor_scalar(out=biasall[:, m, 0:nb_], in0=ncumT[:, 0:nb_, h],
                                                            scalar1=cbc[:, tl[0] * TT:tl[0] * TT + 1], scalar2=None,
                                                            op0=ALU.add), rd=(ncumT, cbc), wr=(biasall,))
            for m, tl in enumerate(pairs):
                ntl = len(tl)
                noff = tl[0] * NS
                a_off = [acc[0], acc[1]]
                a_dg = [acc[2], acc[3]]
                offs = [("off", kb) for kb in range(noff)]
                dvs = []
                for xi, ti in enumerate(tl):
                    for kb in range(noff, (ti + 1) * NS):
                        r = kb - ti * NS
                        dvs.append(("dv", xi, kb, (128 * r if r > 0 else 0), r >= 0))
                items = offs + dvs
                dv_first, dv_last, off_first, off_last = {}, {}, None, None
                for n_, it_ in enumerate(items):
                    if it_[0] == "off":
                        off_first = n_ if off_first is None else off_first
                        off_last = n_
                    else:
                        dv_first.setdefault(it_[1], n_)
                        dv_last[it_[1]] = n_
                LAG = 2
                for xi, ti in enumerate(tl):
                    if noff > 0:
                        P.op("act", lambda: nc.scalar.activation(out=ecr[xi][:], in_=cbc[:, ti * TT:(ti + 1) * TT],
                                                                 func=AF.Exp, bias=ncb0[:, m:m + 1], scale=1.0),
                             rd=(cbc, ncb0), wr=(ecr[xi],))

                def emit_s(n):
                    it_ = items[n]
                    sp_ = st2[n % 2]
                    pb = pT2[n % 3]
                    if it_[0] == "off":
                        kb = it_[1]
                        for xi, ti in enumerate(tl):
                            P.op("pe", lambda: nc.tensor.matmul(sp_[:, xi * TT:(xi + 1) * TT],
                                                                lhsT=kTz[par][:, kb * 128:(kb + 1) * 128],
                                                                rhs=qT[:, ti * TT:(ti + 1) * TT], start=True, stop=True),
                                 rd=(kTz[par], qT), wr=(sp_,))
                        P.op("act", lambda: nc.scalar.activation(out=pb[:, 0:ntl * TT], in_=sp_[:, 0:ntl * TT],
                                                                 func=AF.Exp, bias=biasall[:, m, kb:kb + 1], scale=1.0),
                             rd=(sp_, biasall), wr=(pb,))
                        return
                    _, xi, kb, c0, tri_ = it_
                    ti = tl[xi]
                    P.op("pe", lambda: nc.tensor.matmul(sp_[:, c0:TT], lhsT=kTz[par][:, kb * 128:(kb + 1) * 128],
                                                        rhs=qT[:, ti * TT + c0:(ti + 1) * TT], start=True, stop=True),
                         rd=(kTz[par], qT), wr=(sp_,))
                    tb = tmpb[n % 3]
                    if tri_:
                        P.op("dve", lambda: nc.vector.scalar_tensor_tensor(
                            out=tb[:, c0:c0 + 128], in0=sp_[:, c0:c0 + 128], scalar=ncumT[:, kb, h:h + 1],
                            in1=cbcm[:, ti * TT + c0:ti * TT + c0 + 128], op0=ALU.add, op1=ALU.add),
                            rd=(sp_, ncumT, cbcm), wr=(tb,))
                        c1 = c0 + 128
                    else:
                        c1 = c0
                    if c1 < TT:
                        P.op("dve", lambda: nc.vector.scalar_tensor_tensor(
                            out=tb[:, c1:TT], in0=sp_[:, c1:TT], scalar=ncumT[:, kb, h:h + 1],
                            in1=cbc[:, ti * TT + c1:(ti + 1) * TT], op0=ALU.add, op1=ALU.add),
                            rd=(sp_, ncumT, cbc), wr=(tb,))
                    P.op("act", lambda: nc.scalar.activation(out=pb[:, c0:TT], in_=tb[:, c0:TT], func=AF.Exp),
                         rd=(tb,), wr=(pb,))

                def emit_pv(n):
                    it_ = items[n]
                    pb = pT2[n % 3]
                    if it_[0] == "off":
                        kb = it_[1]
                        for xi, ti in enumerate(tl):
                            P.op("pe", lambda: nc.tensor.matmul(a_off[xi][:, 0:TT], lhsT=Va[par][:, kb, :],
                                                                rhs=pb[:, xi * TT:(xi + 1) * TT],
                                                                start=(n == off_first), stop=(n == off_last)),
                                 rd=(Va[par], pb), wr=(a_off[xi],))
                        return
                    _, xi, kb, c0, tri_ = it_
                    P.op("pe", lambda: nc.tensor.matmul(a_dg[xi][:, c0:TT], lhsT=Va[par][:, kb, :], rhs=pb[:, c0:TT],
                                                        start=(n == dv_first[xi]), stop=(n == dv_last[xi])),
                         rd=(Va[par], pb), wr=(a_dg[xi],))

                for n in range(len(items) + LAG):
                    if n < len(items):
                        emit_s(n)
                    if n - LAG >= 0:
                        emit_pv(n - LAG)
                for xi, ti in enumerate(tl):
                    if noff > 0:
                        P.op("dve", lambda: nc.vector.tensor_tensor(out=osb[:], in0=a_off[xi][:, 0:TT], in1=ecr[xi][:],
                                                                    op=ALU.mult), rd=(a_off[xi], ecr[xi]), wr=(osb,))
                        P.op("dve", lambda: nc.vector.tensor_tensor(out=osb[:], in0=a_dg[xi][:, 0:TT], in1=osb[:],
                                                                    op=ALU.add), rd=(a_dg[xi], osb), wr=(osb,))
                    else:
                        P.op("act", lambda: nc.scalar.copy(out=osb[:], in_=a_dg[xi][:, 0:TT]), rd=(a_dg[xi],), wr=(osb,))
                    opr = slice((1 - par) * 64, (2 - par) * 64)
                    P.dma("sp", dsw, osb, dsw[pr, :], osb[opr, :])
                    P.op("act", lambda: nc.scalar.activation(out=rden[pr, :], in_=dsw[pr, :], func=AF.Ln),
                         rd=(dsw,), wr=(rden,))
                    P.op("act", lambda: nc.scalar.activation(out=rden[pr, :], in_=rden[pr, :], func=AF.Exp, scale=-1.0),
                         rd=(rden,), wr=(rden,))
                    P.op("pool", lambda: nc.gpsimd.tensor_tensor(out=onum[pr, :], in0=osb[pr, :], in1=rden[pr, :],
                                                                  op=ALU.mult), rd=(osb, rden), wr=(onum,))
                    P.op("pool", lambda: nc.gpsimd.tensor_tensor(out=oT[pr, ti * TT:(ti + 1) * TT], in0=onum[pr, :],
                                                                  in1=sgT[pr, ti * TT:(ti + 1) * TT], op=ALU.mult),
                         rd=(onum, sgT), wr=(oT,))
        P.dma("sp", oTd, oT, oTd_ap[:, hp, 0:T], oT[:])
    C.close()

    outproj_stage(P, T, xin, xout, xin_ap, xout_ap, oTd_ap, oTd, wout_ap, gpost_ap)


def outproj_stage(P, T, xin, xout, xin_ap, xout_ap, oTd_ap, oTd, wout_ap, gpost_ap):
    nc = P.nc
    TT = 512 if T >= 512 else T
    NT = T // TT
    NS = TT // 128
    xin_v = xin_ap.rearrange("(n p) d -> n p d", p=128)
    xout_v = xout_ap.rearrange("(n p) d -> n p d", p=128)
    C = Ctx(P)
    gpost = load_gain_bc(P, C, uname("gpost"), gpost_ap, 1.0)
    wo = C.sb(uname("wo"), (128, NCH, D), BF16, dma="sw")
    wo_v = wout_ap.rearrange("(c p) n -> p c n", p=128)
    for q in range(4):
        load_w(P, wo, wo[:, 2 * q:2 * q + 2, :], wo_v[:, 2 * q:2 * q + 2, :])
    ot = [C.sb(uname("ot"), (128, NCH, TT), BF16, dma=True) for _ in range(2)]
    xp = [C.sb(uname("xp"), (128, D), F32, dma=True) for _ in range(2)]
    sq_junk = C.sb(uname("sqj"), (128, D), BF16)
    tmp = C.sb(uname("tmp"), (128, D), F32)
    stats = [C.sb(uname("stat"), (128, 8), F32) for _ in range(4)]
    f_ps = [[C.ps(uname("fps")) for _ in range(2)] for _ in range(2)]
    it = 0
    for ti in range(NT):
        o_ = ot[ti % 2]
        P.dma("sp", o_, oTd, o_[:], oTd_ap[:, :, ti * TT:(ti + 1) * TT])
        for s_ in range(NS):
            xst = xp[it % 2]
            fp = f_ps[it % 2]
            P.dma("sp", xst, xin, xst[:], xin_v[ti * NS + s_])
            for hh in range(2):
                for c in range(NCH):
                    P.op("pe", lambda: nc.tensor.matmul(fp[hh][:, :], lhsT=o_[:, c, s_ * 128:(s_ + 1) * 128],
                                                        rhs=wo[:, c, hh * 512:(hh + 1) * 512],
                                                        start=(c == 0), stop=(c == NCH - 1)), rd=(o_, wo), wr=(fp[hh],))
            postnorm_tile(P, nc, fp, xst, gpost, xst, sq_junk, stats[it % 4], tmp)
            P.dma("sp", xout, xst, xout_v[ti * NS + s_], xst[:])
            it += 1
    C.close()


CW = 31
CK = 64
DEC_C = 0.6065306597126334


def rwkv_prep_stage(P, T, xin, xin_ap, gpre_ap, A, scr):
    nc = P.nc
    TT = 512 if T >= 512 else T
    NT = T // TT
    NS = TT // 128
    NQ = 4
    CR = 1792
    rw_ap, rw = scr["rw"]
    dc_ap, dcb = scr["dc"]
    yab_ap, yab = scr["oT"]
    xin_v = xin_ap.rearrange("(n p) d -> n p d", p=128)
    win_v = A["w_in"].rearrange("(c p) n -> p c n", p=128)
    dsrc = Buf(P, None, "dsrc")
    C = Ctx(P)
    idf, idb = make_consts(P, C)
    gpre = load_gain_bc(P, C, uname("gpre"), gpre_ap, 1.0)
    W1 = C.sb(uname("W1"), (128, NCH, CR), BF16)
    W2 = C.sb(uname("W2"), (128, NCH, CR), BF16)
    Wc = C.sb(uname("Wc"), (128, NCH, 1024), BF16, dma="sw")
    for q in range(4):
        load_w(P, Wc, Wc[:, 2 * q:2 * q + 2, :], win_v[:, 2 * q:2 * q + 2, CR:CR + 1024])
    C0 = Ctx(P)
    mu = C0.sb(uname("mu"), (128, CR), F32, dma=True)
    omu = C0.sb(uname("omu"), (128, CR), F32)
    P.dma("sp", mu, dsrc, mu[:], A["shift_mu"].partition_broadcast(128))
    P.op("dve", lambda: nc.vector.tensor_scalar(out=omu[:], in0=mu[:], scalar1=-1.0, scalar2=1.0, op0=ALU.mult,
                                                op1=ALU.add), rd=(mu,), wr=(omu,))
    stg = [C0.sb(uname("stg"), (128, CR), F32, dma=True) for _ in range(2)]
    for c in range(NCH):
        st = stg[c % 2]
        P.dma("sp", st, dsrc, st[:], win_v[:, c, 0:CR])
        P.op("dve", lambda: nc.vector.tensor_tensor(out=W2[:, c, :], in0=st[:], in1=mu[:], op=ALU.mult),
             rd=(st, mu), wr=(W2,))
        P.op("pool", lambda: nc.gpsimd.tensor_tensor(out=W1[:, c, :], in0=st[:], in1=omu[:], op=ALU.mult),
             rd=(st, omu), wr=(W1,))
    C0.close()
    lup = C.sb(uname("lup"), (128, 512), BF16, dma="sw")
    load_w(P, lup, lup[0:64, :], A["decay_up"])
    load_w(P, lup, lup[64:128, :], A["iclr_up"])
    gup = C.sb(uname("gup"), (128, 512), BF16, dma="sw")
    load_w(P, gup, gup[:], A["gate_up"])
    pc = {}
    with nc.allow_non_contiguous_dma("tiny per-channel parameter vectors"):
        for nm in ("decay_base", "iclr_base", "k_k", "k_a", "r_k", "lnx_b", "conv_b", "ln_g", "ln_b"):
            tl = C.sb(uname("pc_" + nm), (128, NQ), F32, dma=True)
            src = A[nm]
            if nm == "r_k":
                src = src.rearrange("h n -> (h n)")
            P.dma("sp", tl, dsrc, tl[:], src.rearrange("(q p) -> p q", p=128))
            pc[nm] = tl
        cw = C.sb(uname("cw"), (128, NQ, CW), F32, dma=True)
        for q in range(NQ):
            P.dma("sp", cw, dsrc, cw[:, q, :], A["conv_w"][:, q * 128:(q + 1) * 128].rearrange("j p -> p j"))
    omka = C.sb(uname("omka"), (128, NQ), F32)
    P.op("dve", lambda: nc.vector.tensor_scalar(out=omka[:], in0=pc["k_a"][:], scalar1=-1.0, scalar2=1.0,
                                                op0=ALU.mult, op1=ALU.add), rd=(pc["k_a"],), wr=(omka,))
    bones = C.sb(uname("bones"), (128, 128), F32)
    P.op("pool", lambda: nc.gpsimd.memset(bones[:], 0.0), wr=(bones,))
    P.op("pool", lambda: nc.gpsimd.memset(bones[0:64, 0:64], 1.0), rd=(bones,), wr=(bones,))
    P.op("pool", lambda: nc.gpsimd.memset(bones[64:128, 64:128], 1.0), rd=(bones,), wr=(bones,))
    ones = C.sb(uname("ones"), (128, 128), F32)
    P.op("pool", lambda: nc.gpsimd.memset(ones[:], 1.0), wr=(ones,))
    rmask = C.sb(uname("rmask"), (128, TT), F32)
    P.op("pool", lambda: nc.gpsimd.memset(rmask[:], 1.0), wr=(rmask,))
    P.op("pool", lambda: nc.gpsimd.memset(rmask[:].rearrange("p (c j) -> p c j", j=CK)[:, :, 0:1], 0.0),
         rd=(rmask,), wr=(rmask,))
    xs = [C.sb(uname("xs"), (128, D), F32, dma=True) for _ in range(1)]
    hTh = [C.sb(uname("hTh"), (128, NCH, TT + 1), BF16) for _ in range(2)]
    P.op("pool", lambda: nc.gpsimd.memset(hTh[0][:, :, 0:1], 0.0), wr=(hTh[0],))
    sq_junk = C.sb(uname("sqj"), (128, D), BF16)
    hb = [C.sb(uname("hb"), (128, D), BF16) for _ in range(2)]
    stats = [C.sb(uname("stat"), (128, 8), F32) for _ in range(4)]
    tdw = C.sb(uname("tdw"), (128, TT), BF16)
    sdg = C.sb(uname("sdg"), (128, TT), BF16)
    FF = []
    for _ in range(2):
        Fd = {}
        for nm in ("rf", "kf", "sgw", "av", "gf", "kk", "rn", "t1", "kn", "rk", "Lc", "eL", "enL"):
            Fd[nm] = C.sb(uname(nm), (128, TT), F32)
        FF.append(Fd)
    pack = [C.sb(uname("pack"), (128, 7, TT), BF16) for _ in range(2)]
    dct = C.sb(uname("dct"), (128, NQ, T // CK), F32)
    glub = [C.sb(uname("glub"), (128, TT + CW - 1), F32) for _ in range(NQ)]
    for q in range(NQ):
        P.op("pool", lambda: nc.gpsimd.memset(glub[q][:, 0:CW - 1], 0.0), wr=(glub[q],))
    sgc = C.sb(uname("sgc"), (128, TT), F32)
    acc = [C.sb(uname("acc"), (128, TT), F32) for _ in range(NQ)]
    sqc = C.sb(uname("sqc"), (128, TT), F32)
    mean = C.sb(uname("mean"), (128, TT), F32)
    msq = sqc
    rstd = C.sb(uname("rstd"), (128, TT), F32)
    tcv = sgc
    ybt = [C.sb(uname("ybt"), (128, TT), BF16) for _ in range(2)]
    tp_ps = C.ps(uname("tp"), (128, D), BF16)
    bk = [C.ps(uname("bk")) for _ in range(7)]

    def proj(pp, col0, shifted, ht):
        n = 2 * NCH if shifted else NCH
        i = 0
        for c in range(NCH):
            if shifted:
                P.op("pe", lambda: nc.tensor.matmul(pp[:, 0:TT], lhsT=W1[:, c, col0:col0 + 128], rhs=ht[:, c, 1:TT + 1],
                                                    start=(i == 0), stop=(i == n - 1)), rd=(W1, ht), wr=(pp,))
                i += 1
                P.op("pe", lambda: nc.tensor.matmul(pp[:, 0:TT], lhsT=W2[:, c, col0:col0 + 128], rhs=ht[:, c, 0:TT],
                                                    start=False, stop=(i == n - 1)), rd=(W2, ht), wr=(pp,))
                i += 1
            else:
                P.op("pe", lambda: nc.tensor.matmul(pp[:, 0:TT], lhsT=Wc[:, c, col0:col0 + 128], rhs=ht[:, c, 1:TT + 1],
                                                    start=(i == 0), stop=(i == n - 1)), rd=(Wc, ht), wr=(pp,))
                i += 1

    pk_i = 0
    for ti in range(NT):
        ht = hTh[ti % 2]
        tsl = slice(ti * TT, (ti + 1) * TT)
        for s_ in range(NS):
            xst = xs[0]
            P.dma("sp", xst, xin, xst[:], xin_v[ti * NS + s_])
            prenorm_tile(P, nc, xst, gpre, ht, 1 + s_ * 128, idb, sq_junk, stats[s_ % 4], hb[s_ % 2], tp_ps)
        if ti + 1 < NT:
            P.op("pool", lambda: nc.gpsimd.tensor_copy(out=hTh[(ti + 1) % 2][:, :, 0:1], in_=ht[:, :, TT:TT + 1]),
                 rd=(ht,), wr=(hTh[(ti + 1) % 2],))
        proj(bk[6], 1536, True, ht)
        P.op("act", lambda: nc.scalar.activation(out=tdw[0:64, :], in_=bk[6][0:64, 0:TT], func=AF.Tanh),
             rd=(bk[6],), wr=(tdw,))
        P.op("act", lambda: nc.scalar.copy(out=tdw[64:128, :], in_=bk[6][64:128, 0:TT]), rd=(bk[6],), wr=(tdw,))
        proj(bk[5], 1664, True, ht)
        P.op("act", lambda: nc.scalar.activation(out=sdg[:], in_=bk[5][:, 0:TT], func=AF.Sigmoid),
             rd=(bk[5],), wr=(sdg,))
        bset = [(bk[0], bk[1], bk[2]), (bk[3], bk[4], bk[5])]

        def head(q):
            F = FF[q % 2]
            pk = pack[q % 2]
            qs = slice(q * 128, (q + 1) * 128)
            col = lambda nm: pc[nm][:, q:q + 1]
            r_ps, k_ps, v_ps = bset[q % 2]
            proj(r_ps, q * 128, True, ht)
            proj(k_ps, 512 + q * 128, True, ht)
            proj(v_ps, 1024 + q * 128, True, ht)
            P.op("act", lambda: nc.scalar.copy(out=F["rf"][:], in_=r_ps[:, 0:TT]), rd=(r_ps,), wr=(F["rf"],))
            P.op("act", lambda: nc.scalar.copy(out=F["kf"][:], in_=k_ps[:, 0:TT]), rd=(k_ps,), wr=(F["kf"],))
            P.op("act", lambda: nc.scalar.copy(out=pk[:, 4, :], in_=v_ps[:, 0:TT]), rd=(v_ps,), wr=(pk,))
            zw_ps, za_ps, g_ps = r_ps, k_ps, v_ps
            P.op("pe", lambda: nc.tensor.matmul(zw_ps[:, 0:TT], lhsT=lup[0:64, qs], rhs=tdw[0:64, :], start=True,
                                                stop=True), rd=(lup, tdw), wr=(zw_ps,))
            P.op("pe", lambda: nc.tensor.matmul(za_ps[:, 0:TT], lhsT=lup[64:128, qs], rhs=tdw[64:128, :], start=True,
                                                stop=True), rd=(lup, tdw), wr=(za_ps,))
            P.op("pe", lambda: nc.tensor.matmul(g_ps[:, 0:TT], lhsT=gup[:, qs], rhs=sdg[:], start=True, stop=True),
                 rd=(gup, sdg), wr=(g_ps,))
            P.op("act", lambda: nc.scalar.activation(out=F["sgw"][:], in_=zw_ps[:, 0:TT], func=AF.Sigmoid,
                                                     bias=col("decay_base")), rd=(zw_ps, pc["decay_base"]),
                 wr=(F["sgw"],))
            P.op("act", lambda: nc.scalar.activation(out=F["av"][:], in_=za_ps[:, 0:TT], func=AF.Sigmoid,
                                                     bias=col("iclr_base")), rd=(za_ps, pc["iclr_base"]),
                 wr=(F["av"],))
            P.op("act", lambda: nc.scalar.copy(out=F["gf"][:], in_=g_ps[:, 0:TT]), rd=(g_ps,), wr=(F["gf"],))

        def tail(q):
            F = FF[q % 2]
            pk = pack[q % 2]
            qs = slice(q * 128, (q + 1) * 128)
            col = lambda nm: pc[nm][:, q:q + 1]
            s_ps = bk[6]
            P.op("dve", lambda: nc.vector.tensor_scalar(out=F["kk"][:], in0=F["kf"][:], scalar1=col("k_k"),
                                                        scalar2=None, op0=ALU.mult), rd=(F["kf"], pc["k_k"]),
                 wr=(F["kk"],))
            P.op("pool", lambda: nc.gpsimd.tensor_tensor(out=F["rk"][:], in0=F["kk"][:], in1=F["kk"][:],
                                                          op=ALU.mult), rd=(F["kk"],), wr=(F["rk"],))
            P.op("pe", lambda: nc.tensor.matmul(s_ps[:, 0:TT], lhsT=bones[:], rhs=F["rk"][:], start=True, stop=True),
                 rd=(bones, F["rk"]), wr=(s_ps,))
            P.op("act", lambda: nc.scalar.activation(out=F["rn"][:], in_=s_ps[:, 0:TT], func=AF.Ln, bias=1e-24,
                                                     scale=1.0), rd=(s_ps,), wr=(F["rn"],))
            P.op("act", lambda: nc.scalar.activation(out=F["rn"][:], in_=F["rn"][:], func=AF.Exp, scale=-0.5),
                 rd=(F["rn"],), wr=(F["rn"],))
            P.op("pool", lambda: nc.gpsimd.tensor_tensor(out=F["kk"][:], in0=F["kk"][:], in1=F["rn"][:],
                                                          op=ALU.mult), rd=(F["kk"], F["rn"]), wr=(F["kk"],))
            P.op("dve", lambda: nc.vector.tensor_scalar(out=F["t1"][:], in0=F["av"][:], scalar1=col("k_a"),
                                                        scalar2=omka[:, q:q + 1], op0=ALU.mult, op1=ALU.add),
                 rd=(F["av"], pc["k_a"], omka), wr=(F["t1"],))
            P.op("pool", lambda: nc.gpsimd.tensor_tensor(out=F["kn"][:], in0=F["kf"][:], in1=F["t1"][:],
                                                          op=ALU.mult), rd=(F["kf"], F["t1"]), wr=(F["kn"],))
            P.op("pool", lambda: nc.gpsimd.tensor_tensor(out=F["av"][:], in0=F["kk"][:], in1=F["av"][:],
                                                          op=ALU.mult), rd=(F["kk"], F["av"]), wr=(F["av"],))
            P.op("dve", lambda: nc.vector.scalar_tensor_tensor(out=F["rk"][:], in0=F["rf"][:], scalar=col("r_k"),
                                                               in1=F["kn"][:], op0=ALU.mult, op1=ALU.mult),
                 rd=(F["rf"], pc["r_k"], F["kn"]), wr=(F["rk"],))
            P.op("pe", lambda: nc.tensor.matmul(s_ps[:, 0:TT], lhsT=bones[:], rhs=F["rk"][:], start=True, stop=True),
                 rd=(bones, F["rk"]), wr=(s_ps,))
            P.op("dve", lambda: nc.vector.tensor_tensor(out=F["rk"][:], in0=s_ps[:, 0:TT], in1=pk[:, 4, :],
                                                        op=ALU.mult), rd=(s_ps, pk), wr=(F["rk"],))
            P.op("dve", lambda: nc.vector.scalar_tensor_tensor(out=pk[:, 6, :], in0=F["rk"][:], scalar=col("lnx_b"),
                                                               in1=F["gf"][:], op0=ALU.add, op1=ALU.mult),
                 rd=(F["rk"], pc["lnx_b"], F["gf"]), wr=(pk,))
            P.op("pool", lambda: nc.gpsimd.tensor_copy(out=pk[:, 5, :], in_=F["gf"][:]), rd=(F["gf"],), wr=(pk,))
            P.op("dve", lambda: nc.vector.tensor_tensor_scan(out=F["Lc"][:], data0=rmask[:], data1=F["sgw"][:],
                                                             initial=0.0, op0=ALU.mult, op1=ALU.add),
                 rd=(rmask, F["sgw"]), wr=(F["Lc"],))
            P.op("pool", lambda: nc.gpsimd.tensor_tensor(out=F["sgw"][:], in0=F["Lc"][:], in1=F["sgw"][:],
                                                          op=ALU.subtract), rd=(F["Lc"], F["sgw"]), wr=(F["sgw"],))
            P.op("act", lambda: nc.scalar.activation(out=F["eL"][:], in_=F["Lc"][:], func=AF.Exp, scale=-DEC_C),
                 rd=(F["Lc"],), wr=(F["eL"],))
            P.op("act", lambda: nc.scalar.activation(out=F["enL"][:], in_=F["Lc"][:], func=AF.Exp, scale=DEC_C),
                 rd=(F["Lc"],), wr=(F["enL"],))
            P.op("act", lambda: nc.scalar.activation(out=F["sgw"][:], in_=F["sgw"][:], func=AF.Exp, scale=-DEC_C),
                 rd=(F["sgw"],), wr=(F["sgw"],))
            nck = TT // CK
            P.op("pool", lambda: nc.gpsimd.tensor_copy(
                out=dct[:, q, ti * nck:(ti + 1) * nck],
                in_=F["eL"][:].rearrange("p (c j) -> p c j", j=CK)[:, :, CK - 1]), rd=(F["eL"],), wr=(dct,))
            P.op("pool", lambda: nc.gpsimd.tensor_tensor(out=pk[:, 0, :], in0=F["rf"][:], in1=F["eL"][:], op=ALU.mult),
                 rd=(F["rf"], F["eL"]), wr=(pk,))
            P.op("dve", lambda: nc.vector.scalar_tensor_tensor(out=pk[:, 1, :], in0=F["kk"][:], scalar=-1.0,
                                                               in1=F["sgw"][:], op0=ALU.mult, op1=ALU.mult),
                 rd=(F["kk"], F["sgw"]), wr=(pk,))
            P.op("pool", lambda: nc.gpsimd.tensor_tensor(out=pk[:, 2, :], in0=F["kn"][:], in1=F["enL"][:], op=ALU.mult),
                 rd=(F["kn"], F["enL"]), wr=(pk,))
            P.op("dve", lambda: nc.vector.tensor_tensor(out=pk[:, 3, :], in0=F["av"][:], in1=F["enL"][:], op=ALU.mult),
                 rd=(F["av"], F["enL"]), wr=(pk,))
            P.dma("sp", rw, pk, rw_ap[qs, :, tsl], pk[:])

        head(0)
        for q in range(NQ):
            if q + 1 < NQ:
                head(q + 1)
            tail(q)
        for q in range(NQ):
            val_ps, gt_ps = bk[0 + 2 * (q % 2)], bk[1 + 2 * (q % 2)]
            proj(val_ps, q * 128, False, ht)
            proj(gt_ps, 512 + q * 128, False, ht)
            gb = glub[q]
            P.op("act", lambda: nc.scalar.activation(out=sgc[:], in_=gt_ps[:, 0:TT], func=AF.Sigmoid),
                 rd=(gt_ps,), wr=(sgc,))
            P.op("dve", lambda: nc.vector.tensor_tensor(out=gb[:, CW - 1:CW - 1 + TT], in0=val_ps[:, 0:TT], in1=sgc[:],
                                                        op=ALU.mult), rd=(val_ps, sgc), wr=(gb,))
            ac = acc[q]
            P.op("dve", lambda: nc.vector.tensor_scalar(out=ac[:], in0=gb[:, 0:TT], scalar1=cw[:, q, 0:1],
                                                        scalar2=pc["conv_b"][:, q:q + 1], op0=ALU.mult, op1=ALU.add),
                 rd=(gb, cw, pc["conv_b"]), wr=(ac,))
            for j in range(1, CW):
                P.op("dve", lambda: nc.vector.scalar_tensor_tensor(out=ac[:], in0=gb[:, j:j + TT],
                                                                   scalar=cw[:, q, j:j + 1], in1=ac[:], op0=ALU.mult,
                                                                   op1=ALU.add), rd=(gb, cw, ac), wr=(ac,))
            P.op("pool", lambda: nc.gpsimd.tensor_copy(out=gb[:, 0:CW - 1], in_=gb[:, TT:TT + CW - 1]),
                 rd=(gb,), wr=(gb,))
        sum_ps, ssq_ps = bk[4], bk[5]
        for q in range(NQ):
            P.op("pe", lambda: nc.tensor.matmul(sum_ps[:, 0:TT], lhsT=ones[:], rhs=acc[q][:], start=(q == 0),
                                                stop=(q == NQ - 1)), rd=(ones, acc[q]), wr=(sum_ps,))
        for q in range(NQ):
            P.op("act", lambda: nc.scalar.activation(out=sqc[:], in_=acc[q][:], func=AF.Square), rd=(acc[q],),
                 wr=(sqc,))
            P.op("pe", lambda: nc.tensor.matmul(ssq_ps[:, 0:TT], lhsT=ones[:], rhs=sqc[:], start=(q == 0),
                                                stop=(q == NQ - 1)), rd=(ones, sqc), wr=(ssq_ps,))
        P.op("act", lambda: nc.scalar.activation(out=mean[:], in_=sum_ps[:, 0:TT], func=AF.Copy, scale=1.0 / 512),
             rd=(sum_ps,), wr=(mean,))
        P.op("pool", lambda: nc.gpsimd.tensor_tensor(out=msq[:], in0=mean[:], in1=mean[:], op=ALU.mult),
             rd=(mean,), wr=(msq,))
        P.op("dve", lambda: nc.vector.scalar_tensor_tensor(out=rstd[:], in0=ssq_ps[:, 0:TT], scalar=1.0 / 512,
                                                           in1=msq[:], op0=ALU.mult, op1=ALU.subtract),
             rd=(ssq_ps, msq), wr=(rstd,))
        P.op("act", lambda: nc.scalar.activation(out=rstd[:], in_=rstd[:], func=AF.Sqrt, bias=1e-5, scale=1.0),
             rd=(rstd,), wr=(rstd,))
        P.op("dve", lambda: nc.vector.reciprocal(out=rstd[:], in_=rstd[:]), rd=(rstd,), wr=(rstd,))
        for q in range(NQ):
            yb_ = ybt[q % 2]
            P.op("pool", lambda: nc.gpsimd.tensor_tensor(out=tcv[:], in0=acc[q][:], in1=mean[:], op=ALU.subtract),
                 rd=(acc[q], mean), wr=(tcv,))
            P.op("pool", lambda: nc.gpsimd.tensor_tensor(out=tcv[:], in0=tcv[:], in1=rstd[:], op=ALU.mult),
                 rd=(tcv, rstd), wr=(tcv,))
            P.op("act", lambda: nc.scalar.activation(out=yb_[:], in_=tcv[:], func=AF.Silu,
                                                     bias=pc["ln_b"][:, q:q + 1], scale=pc["ln_g"][:, q:q + 1]),
                 rd=(tcv, pc["ln_b"], pc["ln_g"]), wr=(yb_,))
            P.dma("sp", yab, yb_, yab_ap[:, 4 + q, tsl], yb_[:])
    for q in range(NQ):
        P.dma("sp", dcb, dct, dc_ap[q * 128:(q + 1) * 128, :], dct[:, q, :])
    C.close()


def rwkv_scan_stage(P, T, A, scr):
    nc = P.nc
    NH = 8
    NHC = 4
    TT = 512 if T >= 512 else T
    NG = T // TT
    NCG = TT // CK
    NC = T // CK
    rw_ap, rw = scr["rw"]
    dc_ap, dcb = scr["dc"]
    yab_ap, yab = scr["oT"]
    dsrc = Buf(P, None, "dsrc")
    C = Ctx(P)
    idf, idb = make_consts(P, C)
    ones64 = C.sb(uname("ones64"), (128, 64), F32)
    P.op("pool", lambda: nc.gpsimd.memset(ones64[:], 1.0), wr=(ones64,))

    def mk_mask(name, base, cm, step):
        m = C.sb(uname(name), (64, NCG, CK), F32)
        P.op("pool", lambda: nc.gpsimd.memset(m[:], 1.0), wr=(m,))
        P.op("pool", lambda: nc.gpsimd.affine_select(out=m[:], in_=m[:], compare_op=ALU.is_ge, fill=0.0, base=base,
                                                      pattern=[[0, NCG], [step, CK]], channel_multiplier=cm),
             rd=(m,), wr=(m,))
        return m
    m_su = mk_mask("m_su", -1, -1, 1)
    m_sl = mk_mask("m_sl", -1, 1, -1)
    m_iu = mk_mask("m_iu", 0, -1, 1)
    I8 = C.sb(uname("I8"), (64, NCG, CK), F32)
    P.op("pool", lambda: nc.gpsimd.memset(I8[:], 0.0), wr=(I8,))
    P.op("pool", lambda: nc.gpsimd.affine_select(out=I8[:], in_=I8[:], compare_op=ALU.not_equal, fill=1.0, base=0,
                                                  pattern=[[0, NCG], [-1, CK]], channel_multiplier=1),
         rd=(I8,), wr=(I8,))
    lnxg = C.sb(uname("lnxg"), (64, NH), F32, dma=True)
    with nc.allow_non_contiguous_dma("tiny per-channel parameter vectors"):
        P.dma("sp", lnxg, dsrc, lnxg[:], A["lnx_g"].rearrange("(h p) -> p h", p=64))
    flat = lambda m: m[:].rearrange("p c i -> p (c i)")

    class Slot:
        pass
    slots = []
    for i in range(NHC):
        S = Slot()
        S.ops = [C.sb(uname("ops"), (128, 7, TT), BF16, dma=True) for _ in range(2)]
        for o_ in S.ops:
            P.op("pool", lambda: nc.gpsimd.memset(o_[64:128, :, :], 0.0), wr=(o_,))
        S.dC = C.sb(uname("dC"), (64, NC), F32, dma=True)
        for nm in ("Akt", "Arbt", "Arkt", "Tt", "Btok", "Ktok", "Vtok", "U", "Mb0", "Mb1", "Mt0", "Mt1"):
            setattr(S, nm, C.sb(uname(nm), (128, TT), BF16))
            P.op("pool", lambda: nc.gpsimd.memset(getattr(S, nm)[64:128, :], 0.0), wr=(getattr(S, nm),))
        S.Hall = C.sb(uname("Hall"), (128, NCG + 1, CK), BF16)
        P.op("pool", lambda: nc.gpsimd.memset(S.Hall[64:128, :, :], 0.0), wr=(S.Hall,))
        S.Hf = C.sb(uname("Hf"), (64, CK), F32)
        S.tmpH = C.sb(uname("tmpH"), (64, CK), F32)
        S.Wsb = C.sb(uname("Wsb"), (128, CK), BF16)
        P.op("pool", lambda: nc.gpsimd.memset(S.Wsb[64:128, :], 0.0), wr=(S.Wsb,))
        S.yT = C.sb(uname("yT"), (128, TT), F32)
        S.sqy = C.sb(uname("sqy"), (128, TT), F32)
        P.op("pool", lambda: nc.gpsimd.memset(S.yT[64:128, :], 0.0), wr=(S.yT,))
        P.op("pool", lambda: nc.gpsimd.memset(S.sqy[64:128, :], 0.0), wr=(S.sqy,))
        S.mean = C.sb(uname("mean"), (64, TT), F32)
        S.rstd = C.sb(uname("rstd"), (64, TT), F32)
        S.yo = C.sb(uname("yo"), (64, TT), BF16)
        S.bk = [C.ps(uname("bk")) for _ in range(2)]
        S.bi = 0
        slots.append(S)

    def head_prog(S, h):
        def nb():
            S.bi += 1
            return S.bk[S.bi % 2]

        def macro(lt, lsl, rt, rsl, ps):
            for c in range(NCG):
                cs = slice(c * CK, (c + 1) * CK)
                P.op("pe", lambda: nc.tensor.matmul(ps[0:64, cs], lhsT=lsl(c), rhs=rsl(c), start=True, stop=True),
                     rd=(lt, rt), wr=(ps,))
        P.dma("sp", S.dC, dcb, S.dC[:], dc_ap[h * 64:(h + 1) * 64, :])
        P.op("pool", lambda: nc.gpsimd.memset(S.Hf[:], 0.0), wr=(S.Hf,))
        P.op("pool", lambda: nc.gpsimd.memset(S.Hall[0:64, 0, :], 0.0), wr=(S.Hall,))
        Mb, Mtb = [S.Mb0, S.Mb1], [S.Mt0, S.Mt1]
        for g in range(NG):
            g0 = g * TT
            ops = S.ops[g % 2]
            P.dma("sp", ops, rw, ops[0:64, :, :], rw_ap[h * 64:(h + 1) * 64, :, g0:g0 + TT])
            osl = lambda kind: (lambda c: ops[:, kind, c * CK:(c + 1) * CK])
            loc = lambda t_: (lambda c: t_[:, c * CK:(c + 1) * CK])
            Rs, As, Ks, Bs, Vs = osl(0), osl(1), osl(2), osl(3), osl(4)
            idl = lambda c: idb[:, 0:64]
            if g > 0:
                P.op("pool", lambda: nc.gpsimd.tensor_copy(out=S.Hall[0:64, 0, :], in_=S.Hall[0:64, NCG, :]), rd=(S.Hall,),
                     wr=(S.Hall,))
            yield
            ps = nb()
            macro(ops, Bs, ops, As, ps)
            P.op("dve", lambda: nc.vector.tensor_tensor(out=Mtb[0][0:64, :], in0=ps[0:64, 0:TT], in1=flat(m_su), op=ALU.mult),
                 rd=(ps, m_su), wr=(Mtb[0],))
            P.op("pool", lambda: nc.gpsimd.tensor_tensor(out=S.Tt[0:64, :], in0=Mtb[0][0:64, :], in1=flat(I8), op=ALU.add),
                 rd=(Mtb[0], I8), wr=(S.Tt,))
            yield
            ps = nb()
            macro(ops, As, ops, Bs, ps)
            P.op("dve", lambda: nc.vector.tensor_tensor(out=Mb[0][0:64, :], in0=ps[0:64, 0:TT], in1=flat(m_sl), op=ALU.mult),
                 rd=(ps, m_sl), wr=(Mb[0],))
            yield
            for (ls, rs_, dst, mk) in ((Ks, As, S.Akt, m_su), (Bs, Rs, S.Arbt, m_iu), (Ks, Rs, S.Arkt, m_iu)):
                ps = nb()
                macro(ops, ls, ops, rs_, ps)
                P.op("dve", lambda: nc.vector.tensor_tensor(out=dst[0:64, :], in0=ps[0:64, 0:TT], in1=flat(mk), op=ALU.mult),
                     rd=(ps, mk), wr=(dst,))
                yield
            for (src, dst) in ((Bs, S.Btok), (Ks, S.Ktok), (Vs, S.Vtok)):
                ps = nb()
                macro(ops, src, idb, idl, ps)
                P.op("act", lambda: nc.scalar.copy(out=dst[0:64, :], in_=ps[0:64, 0:TT]), rd=(ps,), wr=(dst,))
                yield
            cur = 0
            for p in range(1, 6):
                nxt = 1 - cur
                ps = nb()
                macro(Mtb[cur], loc(Mtb[cur]), Mb[cur], loc(Mb[cur]), ps)
                P.op("act", lambda: nc.scalar.copy(out=Mb[nxt][0:64, :], in_=ps[0:64, 0:TT]), rd=(ps,), wr=(Mb[nxt],))
                yield
                if p < 5:
                    ps2 = nb()
                    macro(Mb[cur], loc(Mb[cur]), Mtb[cur], loc(Mtb[cur]), ps2)
                    P.op("act", lambda: nc.scalar.copy(out=Mtb[nxt][0:64, :], in_=ps2[0:64, 0:TT]), rd=(ps2,),
                         wr=(Mtb[nxt],))
                    yield
                ps3 = nb()
                macro(Mb[nxt], loc(Mb[nxt]), S.Tt, loc(S.Tt), ps3)
                P.op("dve", lambda: nc.vector.tensor_tensor(out=S.Tt[0:64, :], in0=ps3[0:64, 0:TT], in1=S.Tt[0:64, :], op=ALU.add),
                     rd=(ps3, S.Tt), wr=(S.Tt,))
                yield
                cur = nxt
            for c in range(NCG):
                cg = g * NCG + c
                cs = slice(c * CK, (c + 1) * CK)
                w_ps, u_ps = S.bk[0], S.bk[1]
                P.op("pe", lambda: nc.tensor.matmul(w_ps[0:64, 0:CK], lhsT=As(c), rhs=S.Hall[:, c, :], start=True,
                                                    stop=False), rd=(ops, S.Hall), wr=(w_ps,))
                P.op("pe", lambda: nc.tensor.matmul(w_ps[0:64, 0:CK], lhsT=S.Akt[:, cs], rhs=S.Vtok[:, cs], start=False,
                                                    stop=True), rd=(S.Akt, S.Vtok), wr=(w_ps,))
                P.op("act", lambda: nc.scalar.copy(out=S.Wsb[0:64, :], in_=w_ps[0:64, 0:CK]), rd=(w_ps,), wr=(S.Wsb,))
                P.op("act", lambda: nc.scalar.activation(out=S.tmpH[:], in_=S.Hf[:], func=AF.Copy,
                                                         scale=S.dC[:, cg:cg + 1]), rd=(S.Hf, S.dC), wr=(S.tmpH,))
                yield
                P.op("pe", lambda: nc.tensor.matmul(u_ps[0:64, 0:CK], lhsT=S.Tt[:, cs], rhs=S.Wsb[:], start=True,
                                                    stop=True), rd=(S.Tt, S.Wsb), wr=(u_ps,))
                P.op("dve", lambda: nc.vector.tensor_copy(out=S.U[0:64, cs], in_=u_ps[0:64, 0:CK]), rd=(u_ps,), wr=(S.U,))
                yield
                h_ps = w_ps
                P.op("pe", lambda: nc.tensor.matmul(h_ps[0:64, 64:64 + CK], lhsT=S.Btok[:, cs], rhs=S.U[:, cs],
                                                    start=True, stop=False), rd=(S.Btok, S.U), wr=(h_ps,))
                P.op("pe", lambda: nc.tensor.matmul(h_ps[0:64, 64:64 + CK], lhsT=S.Ktok[:, cs], rhs=S.Vtok[:, cs],
                                                    start=False, stop=True), rd=(S.Ktok, S.Vtok), wr=(h_ps,))
                P.op("dve", lambda: nc.vector.scalar_tensor_tensor(out=S.Hf[:], in0=h_ps[0:64, 64:64 + CK],
                                                                   scalar=S.dC[:, cg:cg + 1], in1=S.tmpH[:],
                                                                   op0=ALU.mult, op1=ALU.add),
                     rd=(h_ps, S.dC, S.tmpH), wr=(S.Hf,))
                P.op("act", lambda: nc.scalar.copy(out=S.Hall[0:64, c + 1, :], in_=S.Hf[:]), rd=(S.Hf,), wr=(S.Hall,))
                yield
            y_ps = nb()
            for c in range(NCG):
                cs = slice(c * CK, (c + 1) * CK)
                P.op("pe", lambda: nc.tensor.matmul(y_ps[0:64, cs], lhsT=S.Hall[:, c, :], rhs=Rs(c), start=True,
                                                    stop=False), rd=(S.Hall, ops), wr=(y_ps,))
                P.op("pe", lambda: nc.tensor.matmul(y_ps[0:64, cs], lhsT=S.U[:, cs], rhs=S.Arbt[:, cs], start=False,
                                                    stop=False), rd=(S.U, S.Arbt), wr=(y_ps,))
                P.op("pe", lambda: nc.tensor.matmul(y_ps[0:64, cs], lhsT=S.Vtok[:, cs], rhs=S.Arkt[:, cs], start=False,
                                                    stop=True), rd=(S.Vtok, S.Arkt), wr=(y_ps,))
            P.op("act", lambda: nc.scalar.copy(out=S.yT[0:64, :], in_=y_ps[0:64, 0:TT]), rd=(y_ps,), wr=(S.yT,))
            P.op("act", lambda: nc.scalar.activation(out=S.sqy[0:64, :], in_=S.yT[0:64, :], func=AF.Square), rd=(S.yT,),
                 wr=(S.sqy,))
            yield
            s_ps, q_ps = nb(), nb()
            P.op("pe", lambda: nc.tensor.matmul(s_ps[0:64, 0:TT], lhsT=ones64[:], rhs=S.yT[:], start=True, stop=True),
                 rd=(ones64, S.yT), wr=(s_ps,))
            P.op("pe", lambda: nc.tensor.matmul(q_ps[0:64, 0:TT], lhsT=ones64[:], rhs=S.sqy[:], start=True, stop=True),
                 rd=(ones64, S.sqy), wr=(q_ps,))
            P.op("act", lambda: nc.scalar.activation(out=S.mean[:], in_=s_ps[0:64, 0:TT], func=AF.Copy,
                                                     scale=1.0 / 64), rd=(s_ps,), wr=(S.mean,))
            P.op("pool", lambda: nc.gpsimd.tensor_tensor(out=S.sqy[0:64, :], in0=S.mean[:], in1=S.mean[:], op=ALU.mult),
                 rd=(S.mean,), wr=(S.sqy,))
            P.op("dve", lambda: nc.vector.scalar_tensor_tensor(out=S.rstd[:], in0=q_ps[0:64, 0:TT], scalar=1.0 / 64,
                                                               in1=S.sqy[0:64, :], op0=ALU.mult, op1=ALU.subtract),
                 rd=(q_ps, S.sqy), wr=(S.rstd,))
            yield
            P.op("act", lambda: nc.scalar.activation(out=S.rstd[:], in_=S.rstd[:], func=AF.Ln, bias=64e-5, scale=1.0),
                 rd=(S.rstd,), wr=(S.rstd,))
            P.op("act", lambda: nc.scalar.activation(out=S.rstd[:], in_=S.rstd[:], func=AF.Exp, scale=-0.5),
                 rd=(S.rstd,), wr=(S.rstd,))
            P.op("pool", lambda: nc.gpsimd.tensor_tensor(out=S.yT[0:64, :], in0=S.yT[0:64, :], in1=S.mean[:], op=ALU.subtract),
                 rd=(S.yT, S.mean), wr=(S.yT,))
            yield
            P.op("dve", lambda: nc.vector.scalar_tensor_tensor(out=S.yT[0:64, :], in0=S.yT[0:64, :], scalar=lnxg[:, h:h + 1],
                                                               in1=S.rstd[:], op0=ALU.mult, op1=ALU.mult),
                 rd=(S.yT, lnxg, S.rstd), wr=(S.yT,))
            P.op("dve", lambda: nc.vector.tensor_tensor(out=S.yT[0:64, :], in0=S.yT[0:64, :], in1=ops[0:64, 5, :], op=ALU.mult),
                 rd=(S.yT, ops), wr=(S.yT,))
            P.op("pool", lambda: nc.gpsimd.tensor_tensor(out=S.yo[:], in0=S.yT[0:64, :], in1=ops[0:64, 6, :], op=ALU.add),
                 rd=(S.yT, ops), wr=(S.yo,))
            P.dma("sp", yab, S.yo, yab_ap[(h % 2) * 64:(h % 2) * 64 + 64, h // 2, g0:g0 + TT], S.yo[:])
            yield

    for h0 in range(0, NH, NHC):
        gens = [head_prog(slots[i], h0 + i) for i in range(NHC)]
        while gens:
            for gn in list(gens):
                try:
                    next(gn)
                except StopIteration:
                    gens.remove(gn)
    C.close()


def mixer_ab_phase(P, T, xin, xout, xin_ap, xout_ap, A, gpre_ap, gpost_ap, scr):
    rwkv_prep_stage(P, T, xin, xin_ap, gpre_ap, A, scr)
    rwkv_scan_stage(P, T, A, scr)
    outproj_stage(P, T, xin, xout, xin_ap, xout_ap, scr["oT"][0], scr["oT"][1], A["w_out"], gpost_ap)


def build(T, phases):
    nc = bass.Bass("TRN2", target_bir_lowering=False)
    t = {}
    t["x"] = nc.dram_tensor("x", [T, D], F32, kind="ExternalInput").ap()
    t["norm_gains"] = nc.dram_tensor("norm_gains", [4, 6, D], F32, kind="ExternalInput").ap()
    t["ffn_w_gate"] = nc.dram_tensor("ffn_w_gate", [4, 2, D, DFF], F32, kind="ExternalInput").ap()
    t["ffn_w_up"] = nc.dram_tensor("ffn_w_up", [4, 2, D, DFF], F32, kind="ExternalInput").ap()
    t["ffn_w_down"] = nc.dram_tensor("ffn_w_down", [4, 2, DFF, D], F32, kind="ExternalInput").ap()
    t["c_w_in"] = nc.dram_tensor("c_w_in", [2, D, 4 * D + 16], F32, kind="ExternalInput").ap()
    t["c_forget_bias"] = nc.dram_tensor("c_forget_bias", [2, 16], F32, kind="ExternalInput").ap()
    t["c_q_norm_g"] = nc.dram_tensor("c_q_norm_g", [2, 64], F32, kind="ExternalInput").ap()
    t["c_k_norm_g"] = nc.dram_tensor("c_k_norm_g", [2, 64], F32, kind="ExternalInput").ap()
    t["c_w_out"] = nc.dram_tensor("c_w_out", [2, D, D], F32, kind="ExternalInput").ap()
    for nm, shp in (("a_w_in", [2, D, 2816]), ("a_shift_mu", [2, 1792]), ("a_decay_up", [2, 64, 512]),
                    ("a_decay_base", [2, 512]), ("a_iclr_up", [2, 64, 512]), ("a_iclr_base", [2, 512]),
                    ("a_gate_up", [2, 128, 512]), ("a_k_k", [2, 512]), ("a_k_a", [2, 512]), ("a_r_k", [2, 8, 64]),
                    ("a_lnx_g", [2, 512]), ("a_lnx_b", [2, 512]), ("b_conv_w", [2, 31, 512]), ("b_conv_b", [2, 512]),
                    ("b_ln_g", [2, 512]), ("b_ln_b", [2, 512]), ("ab_w_out", [2, D, D])):
        t[nm] = nc.dram_tensor(nm, shp, F32, kind="ExternalInput").ap()
    y = nc.dram_tensor("y", [T, D], F32, kind="ExternalOutput").ap()
    rwd = nc.dram_tensor("rwd", [512, 7, T], BF16).ap()
    dcd = nc.dram_tensor("dcd", [512, max(T // 64, 1)], F32).ap()
    hTd = nc.dram_tensor("hTd", [128, NCH, T], BF16).ap()
    oTd = nc.dram_tensor("oTd", [128, NCH, T], BF16).ap()
    cumd = nc.dram_tensor("cumd", [16, T], F32).ap()
    xa = nc.dram_tensor("xa", [T, D], F32).ap()
    xb = nc.dram_tensor("xb", [T, D], F32).ap()
    P = Prog(nc)
    with nc.allow_low_precision("bf16 matmul operands, fp32 accumulation"):
        P.clear_all()
        nc.all_engine_barrier()
        bx = Buf(P, None, "x_in")
        cur_ap, cur = t["x"], bx
        pp = [(xa, Buf(P, None, "xa", dma=True)), (xb, Buf(P, None, "xb", dma=True))]
        yb = Buf(P, None, "y", dma=True)
        scr = {"hT": (hTd, Buf(P, None, "hTd", dma=True)), "oT": (oTd, Buf(P, None, "oTd", dma=True)),
               "cum": (cumd, Buf(P, None, "cumd", dma=True)), "rw": (rwd, Buf(P, None, "rwd", dma=True)),
               "dc": (dcd, Buf(P, None, "dcd", dma=True))}
        for i, ph in enumerate(phases):
            last = (i == len(phases) - 1)
            dst_ap, dst = (y, yb) if last else pp[i % 2]
            if ph[0] == "ffn":
                l, s = ph[1], ph[2]
                ffn_phase(P, T, cur, dst, cur_ap, dst_ap, t["ffn_w_gate"][l, s], t["ffn_w_up"][l, s],
                          t["ffn_w_down"][l, s], t["norm_gains"][l, 3 * s if s == 0 else 4],
                          t["norm_gains"][l, 1 if s == 0 else 5])
            elif ph[0] == "ab":
                l = ph[1]
                i2 = l // 2
                A = {"w_in": t["a_w_in"][i2], "shift_mu": t["a_shift_mu"][i2], "decay_up": t["a_decay_up"][i2],
                     "decay_base": t["a_decay_base"][i2], "iclr_up": t["a_iclr_up"][i2],
                     "iclr_base": t["a_iclr_base"][i2], "gate_up": t["a_gate_up"][i2], "k_k": t["a_k_k"][i2],
                     "k_a": t["a_k_a"][i2], "r_k": t["a_r_k"][i2], "lnx_g": t["a_lnx_g"][i2],
                     "lnx_b": t["a_lnx_b"][i2], "conv_w": t["b_conv_w"][i2], "conv_b": t["b_conv_b"][i2],
                     "ln_g": t["b_ln_g"][i2], "ln_b": t["b_ln_b"][i2], "w_out": t["ab_w_out"][i2]}
                mixer_ab_phase(P, T, cur, dst, cur_ap, dst_ap, A, t["norm_gains"][l, 2], t["norm_gains"][l, 3], scr)
            elif ph[0] in ("ab1", "ab2", "ab3"):
                i2 = 0
                A = {"w_in": t["a_w_in"][i2], "shift_mu": t["a_shift_mu"][i2], "decay_up": t["a_decay_up"][i2],
                     "decay_base": t["a_decay_base"][i2], "iclr_up": t["a_iclr_up"][i2],
                     "iclr_base": t["a_iclr_base"][i2], "gate_up": t["a_gate_up"][i2], "k_k": t["a_k_k"][i2],
                     "k_a": t["a_k_a"][i2], "r_k": t["a_r_k"][i2], "lnx_g": t["a_lnx_g"][i2],
                     "lnx_b": t["a_lnx_b"][i2], "conv_w": t["b_conv_w"][i2], "conv_b": t["b_conv_b"][i2],
                     "ln_g": t["b_ln_g"][i2], "ln_b": t["b_ln_b"][i2], "w_out": t["ab_w_out"][i2]}
                if ph[0] == "ab1":
                    rwkv_prep_stage(P, T, cur, cur_ap, t["norm_gains"][0, 2], A, scr)
                elif ph[0] == "ab2":
                    rwkv_scan_stage(P, T, A, scr)
                else:
                    outproj_stage(P, T, cur, dst, cur_ap, dst_ap, scr["oT"][0], scr["oT"][1], A["w_out"],
                                  t["norm_gains"][0, 3])
            elif ph[0] == "fox":
                l = ph[1]
                i2 = l // 2
                fox_phase(P, T, cur, dst, cur_ap, dst_ap, t["c_w_in"][i2], t["c_forget_bias"][i2],
                          t["c_q_norm_g"][i2], t["c_k_norm_g"][i2], t["c_w_out"][i2], t["norm_gains"][l, 2],
                          t["norm_gains"][l, 3], scr)
            cur_ap, cur = dst_ap, dst
        P.wait_all_dma("sp")
        nc.all_engine_barrier()
        P.clear_all()
    P.stack.close()
    print("instructions:", P.n_ins, "waits:", P.n_wait, "dsems:", P.ndsem)
    return nc


SEQ = 4096
N_CORES = 8
IN_NAMES = ("norm_gains", "ffn_w_gate", "ffn_w_up", "ffn_w_down", "a_w_in", "a_shift_mu", "a_decay_up",
            "a_decay_base", "a_iclr_up", "a_iclr_base", "a_gate_up", "a_k_k", "a_k_a", "a_r_k", "a_lnx_g", "a_lnx_b",
            "b_conv_w", "b_conv_b", "b_ln_g", "b_ln_b", "ab_w_out", "c_w_in", "c_forget_bias", "c_q_norm_g",
            "c_k_norm_g", "c_w_out")


def all_phases(depth=4):
    ph = []
    for l in range(depth):
        ph.append(("ffn", l, 0))
        ph.append(("ab", l) if l % 2 == 0 else ("fox", l))
        ph.append(("ffn", l, 1))
    return ph


_NC_CACHE = {}


def kernel(**inputs):
    x = np.ascontiguousarray(np.asarray(inputs["x"], dtype=np.float32))
    B, T, _ = x.shape
    key = (T,)
    if key not in _NC_CACHE:
        _NC_CACHE[key] = build(T, all_phases())
    nc = _NC_CACHE[key]
    shared = {k: np.ascontiguousarray(np.asarray(inputs[k], dtype=np.float32)) for k in IN_NAMES}
    in_maps = []
    for b in range(B):
        m = dict(shared)
        m["x"] = x[b]
        in_maps.append(m)
    res = run_bass_kernel_spmd(nc, in_maps, core_ids=list(range(B)))
    return np.stack([np.asarray(r["y"], dtype=np.float32) for r in res.results], axis=0)
```

```python
import contextlib
import numpy as np
import concourse.bass as bass
import concourse.mybir as mybir
from concourse.bass_utils import run_bass_kernel_spmd

F32 = mybir.dt.float32
BF16 = mybir.dt.bfloat16
AF = mybir.ActivationFunctionType
ALU = mybir.AluOpType
AX = mybir.AxisListType

D = 1024
DFF = 2816
NCH = D // 128
NFF = DFF // 128
EPS = 1e-6


class Buf:
    def __init__(self, prog, t, name, dma=False):
        self.t = t
        self.name = name
        self.lw = None
        self.rd = {}
        self.dsem = None
        self.psum = False
        self.dkind = None
        if dma:
            self.dkind = "sw" if dma == "sw" else "hw"
            self.dsem = prog.get_dsem(self.dkind)

    def __getitem__(self, k):
        return self.t[k]


class Prog:
    ENG = ("pe", "act", "dve", "pool", "sp")

    def __init__(self, nc):
        self.nc = nc
        self.e = {"pe": nc.tensor, "act": nc.scalar, "dve": nc.vector, "pool": nc.gpsimd, "sp": nc.sync}
        self.stack = contextlib.ExitStack()
        self.sems = {}
        self.val = {}
        self.seen = {e: {} for e in self.ENG}
        for e in self.ENG:
            self.sems[e] = self.stack.enter_context(nc.semaphore("c_" + e))
            self.val[e] = 0
        self.dpool = {"hw": [], "sw": []}
        self.ndsem = 0
        self.live_dsems = set()
        self.n_ins = 0
        self.n_wait = 0

    def get_dsem(self, kind):
        if self.dpool[kind]:
            k = self.dpool[kind].pop()
        else:
            k = "d%s%d" % (kind, self.ndsem)
            self.ndsem += 1
            self.sems[k] = self.stack.enter_context(self.nc.semaphore(k))
            self.val[k] = 0
        self.live_dsems.add(k)
        return k

    def release(self, bufs):
        for b in bufs:
            if b.dsem is not None:
                self.live_dsems.discard(b.dsem)
                self.dpool[b.dkind].append(b.dsem)
                b.dsem = None

    def clear_all(self):
        for k, h in self.sems.items():
            self.nc.gpsimd.sem_clear(h)

    def _need(self, e, waits, tok):
        if tok is None:
            return
        k, v = tok[0], tok[1]
        if self.seen[e].get(k, 0) >= v:
            return
        if waits.get(k, 0) < v:
            waits[k] = v

    def _deps(self, e, rd, wr):
        waits = {}
        for b in rd:
            self._need(e, waits, b.lw)
            if b.psum:
                for k, (v, re_) in b.rd.items():
                    if re_ != e:
                        self._need(e, waits, (k, v))
        for b in wr:
            if b.lw is not None and not (e == "pe" and b.lw[2] == "pe"):
                self._need(e, waits, b.lw)
            for k, (v, re_) in b.rd.items():
                self._need(e, waits, (k, v))
        for k, v in waits.items():
            self.e[e].wait_ge(self.sems[k], v)
            self.seen[e][k] = v
            self.n_wait += 1

    def op(self, e, fn, rd=(), wr=()):
        self._deps(e, rd, wr)
        ins = fn()
        self.val[e] += 1
        ins.then_inc(self.sems[e], 1)
        self.n_ins += 1
        v = self.val[e]
        for b in rd:
            b.rd[e] = (v, e)
        for b in wr:
            b.lw = (e, v, e)
            b.rd = {}
        return ins

    def dma(self, e, dst, src, out_ap, in_ap, **kw):
        assert dst.dsem is not None, dst.name
        assert (dst.dkind == "sw") == (e == "pool"), (dst.name, e)
        self._deps(e, (src,), (dst,))
        ins = self.e[e].dma_start(out=out_ap, in_=in_ap, **kw)
        k = dst.dsem
        self.val[k] += 16
        ins.then_inc(self.sems[k], 16)
        self.n_ins += 1
        v = self.val[k]
        src.rd[k] = (v, "dma")
        dst.lw = (k, v, "dma")
        dst.rd = {}
        return ins

    def barrier(self):
        for k in list(self.live_dsems):
            v = self.val[k]
            if v > 0 and self.seen["sp"].get(k, 0) < v:
                self.e["sp"].wait_ge(self.sems[k], v)
                self.seen["sp"][k] = v
        self.nc.all_engine_barrier()
        for e in self.ENG:
            for k in self.sems:
                self.seen[e][k] = self.val[k]

    def wait_all_dma(self, e):
        for k in list(self.live_dsems):
            v = self.val[k]
            if v > 0 and self.seen[e].get(k, 0) < v:
                self.e[e].wait_ge(self.sems[k], v)
                self.seen[e][k] = v


class Ctx:
    def __init__(self, P):
        self.P = P
        self.nc = P.nc
        self.stack = contextlib.ExitStack()
        self.bufs = []

    def sb(self, name, shape, dt, dma=False):
        t = self.stack.enter_context(self.nc.sbuf_tensor(name, list(shape), dt))
        b = Buf(self.P, t, name, dma=dma)
        self.bufs.append(b)
        return b

    def ps(self, name, shape=(128, 512), dt=F32):
        t = self.stack.enter_context(self.nc.psum_tensor(name, list(shape), dt))
        b = Buf(self.P, t, name)
        b.psum = True
        self.bufs.append(b)
        return b

    def close(self):
        self.P.barrier()
        self.P.release(self.bufs)
        self.stack.close()


_uid = [0]


def uname(s):
    _uid[0] += 1
    return "%s_%d" % (s, _uid[0])


def make_consts(P, C):
    nc = P.nc
    idf = C.sb(uname("ident_f"), (128, 128), F32)
    idb = C.sb(uname("ident_b"), (128, 128), BF16)
    P.op("pool", lambda: nc.gpsimd.memset(idf[:], 0.0), wr=(idf,))
    P.op("pool", lambda: nc.gpsimd.affine_select(out=idf[:], in_=idf[:], compare_op=ALU.not_equal, fill=1.0,
                                                  base=0, pattern=[[-1, 128]], channel_multiplier=1),
         rd=(idf,), wr=(idf,))
    P.op("pool", lambda: nc.gpsimd.tensor_copy(out=idb[:], in_=idf[:]), rd=(idf,), wr=(idb,))
    return idf, idb


def load_gain_bc(P, C, name, src_ap, scale):
    nc = P.nc
    g = C.sb(name, (128, D), F32, dma=True)
    dsrc = Buf(P, None, name + "_src")
    P.dma("sp", g, dsrc, g[:], src_ap.partition_broadcast(128))
    if scale != 1.0:
        P.op("pool", lambda: nc.gpsimd.tensor_scalar(out=g[:], in0=g[:], scalar1=float(scale), scalar2=None,
                                                      op0=ALU.mult), rd=(g,), wr=(g,))
    return g


def prenorm_tile(P, nc, xs, gpre, hT, col0, idb, sq_junk, stat, hb, tp_ps):
    P.op("act", lambda: nc.scalar.activation(out=sq_junk[:], in_=xs[:], func=AF.Square, accum_out=stat[:, 0:1]),
         rd=(xs,), wr=(sq_junk, stat))
    P.op("act", lambda: nc.scalar.activation(out=stat[:, 6:7], in_=stat[:, 0:1], func=AF.Sqrt, bias=float(EPS),
                                             scale=1.0 / D), rd=(stat,), wr=(stat,))
    P.op("dve", lambda: nc.vector.reciprocal(out=stat[:, 1:2], in_=stat[:, 6:7]), rd=(stat,), wr=(stat,))
    P.op("dve", lambda: nc.vector.scalar_tensor_tensor(out=hb[:], in0=xs[:], scalar=stat[:, 1:2], in1=gpre[:],
                                                       op0=ALU.mult, op1=ALU.mult), rd=(xs, stat, gpre), wr=(hb,))
    for c in range(NCH):
        P.op("pe", lambda c=c: nc.tensor.transpose(out=tp_ps[:, c * 128:(c + 1) * 128],
                                                   in_=hb[:, c * 128:(c + 1) * 128], identity=idb[:]),
             rd=(hb, idb), wr=(tp_ps,))
    P.op("act", lambda: nc.scalar.copy(out=hT[:, :, col0:col0 + 128],
                                       in_=tp_ps[:].rearrange("p (c t) -> p c t", c=NCH)),
         rd=(tp_ps,), wr=(hT,))


def postnorm_tile(P, nc, f_ps, xs, gpost, xo, sq_junk, stat, tmp):
    P.op("act", lambda: nc.scalar.activation(out=sq_junk[:, 0:512], in_=f_ps[0][:], func=AF.Square,
                                             accum_out=stat[:, 2:3]), rd=(f_ps[0],), wr=(sq_junk, stat))
    P.op("act", lambda: nc.scalar.activation(out=sq_junk[:, 512:1024], in_=f_ps[1][:], func=AF.Square,
                                             accum_out=stat[:, 3:4]), rd=(f_ps[1],), wr=(sq_junk, stat))
    P.op("dve", lambda: nc.vector.tensor_tensor(out=stat[:, 4:5], in0=stat[:, 2:3], in1=stat[:, 3:4], op=ALU.add),
         rd=(stat,), wr=(stat,))
    P.op("act", lambda: nc.scalar.activation(out=stat[:, 7:8], in_=stat[:, 4:5], func=AF.Sqrt, bias=float(EPS),
                                             scale=1.0 / D), rd=(stat,), wr=(stat,))
    P.op("dve", lambda: nc.vector.reciprocal(out=stat[:, 5:6], in_=stat[:, 7:8]), rd=(stat,), wr=(stat,))
    for h in range(2):
        P.op("dve", lambda h=h: nc.vector.scalar_tensor_tensor(out=tmp[:, h * 512:(h + 1) * 512], in0=f_ps[h][:],
                                                               scalar=stat[:, 5:6],
                                                               in1=gpost[:, h * 512:(h + 1) * 512],
                                                               op0=ALU.mult, op1=ALU.mult),
             rd=(f_ps[h], stat, gpost), wr=(tmp,))
    P.op("pool", lambda: nc.gpsimd.tensor_tensor(out=xo[:], in0=tmp[:], in1=xs[:], op=ALU.add),
         rd=(tmp, xs), wr=(xo,))


def load_w(P, wb, out_ap, src_ap):
    P.dma("pool", wb, Buf(P, None, "wsrc"), out_ap, src_ap)


def ffn_phase(P, T, xin, xout, xin_ap, xout_ap, wg_ap, wu_ap, wd_ap, gpre_ap, gpost_ap):
    nc = P.nc
    C = Ctx(P)
    idf, idb = make_consts(P, C)
    gpre = load_gain_bc(P, C, uname("gpre"), gpre_ap, 1.0)
    gpost = load_gain_bc(P, C, uname("gpost"), gpost_ap, 0.5)
    NB = 11
    CB = DFF // NB
    JB = NFF // NB
    wg = [C.sb(uname("wg"), (128, NCH, CB), BF16, dma="sw") for _ in range(NB)]
    wu = [C.sb(uname("wu"), (128, NCH, CB), BF16, dma="sw") for _ in range(NB)]
    wd = [C.sb(uname("wd"), (128, 2, D), BF16, dma="sw") for _ in range(NFF // 2)]
    wg_v = wg_ap.rearrange("(c p) n -> p c n", p=128)
    wu_v = wu_ap.rearrange("(c p) n -> p c n", p=128)
    wd_v = wd_ap.rearrange("(j p) n -> p j n", p=128)
    for b in range(NB):
        for (wl, wv) in ((wg, wg_v), (wu, wu_v)):
            load_w(P, wl[b], wl[b][:, :, :], wv[:, :, b * CB:(b + 1) * CB])
        if b >= 5:
            for j2 in (2 * (b - 5), 2 * (b - 5) + 1):
                if j2 < NFF // 2:
                    load_w(P, wd[j2], wd[j2][:, :, :], wd_v[:, 2 * j2:2 * j2 + 2, :])

    TT = 512 if T >= 512 else T
    NS = TT // 128
    xs = [C.sb(uname("xs"), (128, D), F32, dma=True) for _ in range(2)]
    xp = [C.sb(uname("xp"), (128, D), F32, dma=True) for _ in range(2)]
    hT = [C.sb(uname("hT"), (128, NCH, TT), BF16) for _ in range(2)]
    actT = C.sb(uname("actT"), (128, NFF, TT), BF16)
    sq_junk = C.sb(uname("sqj"), (128, D), BF16)
    hb = [C.sb(uname("hb"), (128, D), BF16) for _ in range(2)]
    tmp = C.sb(uname("tmp"), (128, D), F32)
    sg = [C.sb(uname("sg"), (128, TT), BF16) for _ in range(2)]
    stats = [C.sb(uname("stat"), (128, 8), F32) for _ in range(4)]
    tp_ps = C.ps(uname("tp"), (128, D), BF16)
    g_ps = [C.ps(uname("gps")) for _ in range(2)]
    u_ps = [C.ps(uname("ups")) for _ in range(2)]
    f3 = [C.ps(uname("fps")) for _ in range(3)]
    xin_v = xin_ap.rearrange("(n p) d -> n p d", p=128)
    xout_v = xout_ap.rearrange("(n p) d -> n p d", p=128)
    nt = T // TT

    def pre(ti, s):
        xst = xs[s % 2]
        P.dma("sp", xst, xin, xst[:], xin_v[ti * NS + s])
        prenorm_tile(P, nc, xst, gpre, hT[ti % 2], s * 128, idb, sq_junk, stats[s % 4], hb[s % 2], tp_ps)

    for s in range(NS):
        pre(0, s)
    it = 0
    for ti in range(nt):
        hTt = hT[ti % 2]
        for j in range(NFF):
            gp, up = g_ps[j % 2], u_ps[j % 2]
            bb, a0 = j // JB, (j % JB) * 128
            for (wl, pp) in ((wg, gp), (wu, up)):
                for c in range(NCH):
                    P.op("pe", lambda: nc.tensor.matmul(pp[:, 0:TT], lhsT=wl[bb][:, c, a0:a0 + 128],
                                                        rhs=hTt[:, c, :], start=(c == 0), stop=(c == NCH - 1)),
                         rd=(wl[bb], hTt), wr=(pp,))
            sgj = sg[j % 2]
            P.op("act", lambda: nc.scalar.activation(out=sgj[:], in_=gp[:, 0:TT], func=AF.Silu),
                 rd=(gp,), wr=(sgj,))
            P.op("dve", lambda: nc.vector.tensor_tensor(out=actT[:, j, :], in0=up[:, 0:TT], in1=sgj[:],
                                                        op=ALU.mult), rd=(up, sgj), wr=(actT,))
            if ti + 1 < nt and j in (4, 8, 12, 16) and (j // 4 - 1) < NS:
                pre(ti + 1, j // 4 - 1)
        for s in range(NS):
            xst = xp[it % 2]
            f_ps = [f3[(2 * it) % 3], f3[(2 * it + 1) % 3]]
            P.dma("sp", xst, xin, xst[:], xin_v[ti * NS + s])
            for h in range(2):
                for j in range(NFF):
                    P.op("pe", lambda: nc.tensor.matmul(
                        f_ps[h][:, :], lhsT=actT[:, j, s * 128:(s + 1) * 128],
                        rhs=wd[j // 2][:, j % 2, h * 512:(h + 1) * 512],
                        start=(j == 0), stop=(j == NFF - 1)), rd=(actT, wd[j // 2]), wr=(f_ps[h],))
            it += 1
            postnorm_tile(P, nc, f_ps, xst, gpost, xst, sq_junk, stats[s % 4], tmp)
            P.dma("sp", xout, xst, xout_v[ti * NS + s], xst[:])
    C.close()


def fox_phase(P, T, xin, xout, xin_ap, xout_ap, win_ap, fb_ap, qg_ap, kg_ap, wout_ap, gpre_ap, gpost_ap, scr):
    nc = P.nc
    H = 16
    NT = T // 512 if T >= 512 else 1
    TT = 512 if T >= 512 else T
    NSUB = T // 128
    hTd_ap, hTd = scr["hT"]
    oTd_ap, oTd = scr["oT"]
    win_v = win_ap.rearrange("(c p) n -> p c n", p=128)
    xin_v = xin_ap.rearrange("(n p) d -> n p d", p=128)
    xout_v = xout_ap.rearrange("(n p) d -> n p d", p=128)
    dsrc = Buf(P, None, "dsrc")

    C = Ctx(P)
    idf, idb = make_consts(P, C)
    gpre = load_gain_bc(P, C, uname("gpre"), gpre_ap, 1.0)
    wf = C.sb(uname("wf"), (128, NCH, H), BF16, dma="sw")
    load_w(P, wf, wf[:], win_v[:, :, 4 * D:4 * D + H])
    nfb = C.sb(uname("nfb"), (H, 1), F32, dma=True)
    P.dma("sp", nfb, dsrc, nfb[:], fb_ap.rearrange("(h o) -> h o", o=1))
    P.op("dve", lambda: nc.vector.tensor_scalar(out=nfb[:], in0=nfb[:], scalar1=-1.0, scalar2=None, op0=ALU.mult),
         rd=(nfb,), wr=(nfb,))
    cum = C.sb(uname("cum"), (H, T), F32)
    ones16 = C.sb(uname("ones16"), (H, TT), F32)
    P.op("pool", lambda: nc.gpsimd.memset(ones16[:], 1.0), wr=(ones16,))
    lfe = [C.sb(uname("lfe"), (H, TT), F32) for _ in range(2)]
    xs = [C.sb(uname("xs"), (128, D), F32, dma=True) for _ in range(2)]
    hTt = [C.sb(uname("hTt"), (128, NCH, TT), BF16) for _ in range(2)]
    sq_junk = C.sb(uname("sqj"), (128, D), BF16)
    hb = [C.sb(uname("hb"), (128, D), BF16) for _ in range(2)]
    stats = [C.sb(uname("stat"), (128, 8), F32) for _ in range(4)]
    tp_ps = C.ps(uname("tp"), (128, D), BF16)
    f_ps = [C.ps(uname("fl")) for _ in range(2)]
    NS = TT // 128
    for ti in range(NT):
        ht = hTt[ti % 2]
        for s_ in range(NS):
            xst = xs[s_ % 2]
            P.dma("sp", xst, xin, xst[:], xin_v[ti * NS + s_])
            prenorm_tile(P, nc, xst, gpre, ht, s_ * 128, idb, sq_junk, stats[s_ % 4], hb[s_ % 2], tp_ps)
        P.dma("sp", hTd, ht, hTd_ap[:, :, ti * TT:(ti + 1) * TT], ht[:])
        fp = f_ps[ti % 2]
        for c in range(NCH):
            P.op("pe", lambda: nc.tensor.matmul(fp[0:H, 0:TT], lhsT=wf[:, c, :], rhs=ht[:, c, :],
                                                start=(c == 0), stop=(c == NCH - 1)), rd=(wf, ht), wr=(fp,))
        le = lfe[ti % 2]
        P.op("act", lambda: nc.scalar.activation(out=le[:], in_=fp[0:H, 0:TT], func=AF.Exp, bias=nfb[:, 0:1],
                                                 scale=-1.0), rd=(fp, nfb), wr=(le,))
        P.op("act", lambda: nc.scalar.activation(out=le[:], in_=le[:], func=AF.Ln, bias=1.0, scale=1.0),
             rd=(le,), wr=(le,))
        init = 0.0 if ti == 0 else cum[:, ti * TT - 1:ti * TT]
        P.op("dve", lambda: nc.vector.tensor_tensor_scan(out=cum[:, ti * TT:(ti + 1) * TT], data0=ones16[:],
                                                         data1=le[:], initial=init, op0=ALU.mult,
                                                         op1=ALU.subtract), rd=(ones16, le, cum), wr=(cum,))
    cumd_ap, cumd = scr["cum"]
    P.dma("sp", cumd, cum, cumd_ap[:, 0:T], cum[:])
    C.close()

    C = Ctx(P)
    idf, idb = make_consts(P, C)
    cum = C.sb(uname("cum"), (H, T), F32, dma=True)
    P.dma("sp", cum, cumd, cum[:], cumd_ap[:, 0:T])
    sel = C.sb(uname("sel"), (H, H, 128), F32)
    P.op("pool", lambda: nc.gpsimd.memset(sel[:], 0.0), wr=(sel,))
    P.op("pool", lambda: nc.gpsimd.affine_select(out=sel[:], in_=sel[:], compare_op=ALU.not_equal, fill=1.0, base=0,
                                                  pattern=[[-1, H], [0, 128]], channel_multiplier=1),
         rd=(sel,), wr=(sel,))
    swp = C.sb(uname("swp"), (128, 128), F32)
    P.op("pool", lambda: nc.gpsimd.memset(swp[:], 0.0), wr=(swp,))
    P.op("pool", lambda: nc.gpsimd.affine_select(out=swp[:, 0:64], in_=swp[:, 0:64], compare_op=ALU.not_equal,
                                                  fill=1.0, base=-64, pattern=[[-1, 64]], channel_multiplier=1),
         rd=(swp,), wr=(swp,))
    P.op("pool", lambda: nc.gpsimd.affine_select(out=swp[:, 64:128], in_=swp[:, 64:128], compare_op=ALU.not_equal,
                                                  fill=1.0, base=0, pattern=[[-1, 64]], channel_multiplier=1),
         rd=(swp,), wr=(swp,))
    bones = C.sb(uname("bones"), (128, 128), F32)
    P.op("pool", lambda: nc.gpsimd.memset(bones[:], 0.0), wr=(bones,))
    P.op("pool", lambda: nc.gpsimd.memset(bones[0:64, 0:64], 1.0), wr=(bones,))
    P.op("pool", lambda: nc.gpsimd.memset(bones[64:128, 64:128], 1.0), wr=(bones,))
    tri = C.sb(uname("tri"), (128, 128), F32)
    P.op("pool", lambda: nc.gpsimd.memset(tri[:], 0.0), wr=(tri,))
    P.op("pool", lambda: nc.gpsimd.affine_select(out=tri[:], in_=tri[:], compare_op=ALU.is_ge, fill=-30000.0, base=0,
                                                  pattern=[[1, 128]], channel_multiplier=-1), rd=(tri,), wr=(tri,))
    ncumT = C.sb(uname("ncumT"), (128, NSUB, H), F32)
    ct_ps = C.ps(uname("ctps"))
    for g0 in range(0, NSUB, 32):
        gn = min(32, NSUB - g0)
        for b in range(gn):
            P.op("pe", lambda: nc.tensor.transpose(out=ct_ps[:, b * H:(b + 1) * H],
                                                   in_=cum[:, (g0 + b) * 128:(g0 + b + 1) * 128],
                                                   identity=idf[0:H, 0:H]), rd=(cum, idf), wr=(ct_ps,))
        P.op("dve", lambda: nc.vector.tensor_scalar(out=ncumT[:, g0:g0 + gn, :],
                                                    in0=ct_ps[:, 0:gn * H].rearrange("p (b h) -> p b h", h=H),
                                                    scalar1=-1.0, scalar2=None, op0=ALU.mult),
             rd=(ct_ps,), wr=(ncumT,))
    gq2 = C.sb(uname("gq2"), (128, 1), F32, dma=True)
    gk2 = C.sb(uname("gk2"), (128, 1), F32, dma=True)
    for hh in range(2):
        P.dma("sp", gq2, dsrc, gq2[hh * 64:(hh + 1) * 64, :], qg_ap.rearrange("(d o) -> d o", o=1))
        P.dma("sp", gk2, dsrc, gk2[hh * 64:(hh + 1) * 64, :], kg_ap.rearrange("(d o) -> d o", o=1))
    P.op("dve", lambda: nc.vector.tensor_scalar(out=gq2[:], in0=gq2[:], scalar1=0.125, scalar2=None, op0=ALU.mult),
         rd=(gq2,), wr=(gq2,))
    wq = [C.sb(uname("wq"), (128, NCH, 4, 128), BF16, dma="sw") for _ in range(2)]
    hTt = [C.sb(uname("hTt"), (128, NCH, TT), BF16, dma=True) for _ in range(2)]
    qT = C.sb(uname("qT"), (128, T), BF16)
    kT = "kT"
    kTz = [C.sb(uname("kTz"), (128, T), BF16) for _ in range(2)]
    for hh in range(2):
        P.op("pool", lambda: nc.gpsimd.memset(kTz[hh][:], 0.0), wr=(kTz[hh],))
    sgT = C.sb(uname("sgT"), (128, T), BF16)
    oT = C.sb(uname("oT"), (128, T), BF16)
    Va = [C.sb(uname("Va"), (128, NSUB, 128), BF16) for _ in range(2)]
    P.op("pool", lambda: nc.gpsimd.memset(Va[0][:, :, 64:128], 1.0), wr=(Va[0],))
    P.op("pool", lambda: nc.gpsimd.memset(Va[1][:, :, 0:64], 1.0), wr=(Va[1],))
    cbc = C.sb(uname("cbc"), (128, T), F32)
    sqf = [C.sb(uname("sqf"), (128, TT), F32) for _ in range(2)]
    rsf = [C.sb(uname("rsf"), (128, TT), F32) for _ in range(2)]
    tmpb = [C.sb(uname("tmpb"), (128, TT), F32) for _ in range(3)]
    osb = C.sb(uname("osb"), (128, TT), F32)
    rden = C.sb(uname("rden"), (128, TT), F32)
    onum = C.sb(uname("onum"), (128, TT), F32)
    st2 = [C.ps(uname("st2"), (128, 2 * TT), F32) for _ in range(2)]
    acc = [C.ps(uname("acc")) for _ in range(3)] + [ct_ps]
    bank = [st2[0], st2[1]] + acc
    pT2 = [C.sb(uname("pT2"), (128, 2 * TT), BF16) for _ in range(3)]
    ecr = [C.sb(uname("ecr"), (128, TT), F32) for _ in range(2)]
    dsw = C.sb(uname("dsw"), (128, TT), F32, dma=True)
    cbcm = C.sb(uname("cbcm"), (128, T), F32)
    tri4 = C.sb(uname("tri4"), (128, TT), F32)
    for r_ in range(TT // 128):
        P.op("pool", lambda: nc.gpsimd.tensor_copy(out=tri4[:, r_ * 128:(r_ + 1) * 128], in_=tri[:]), rd=(tri,),
             wr=(tri4,))
    ncb0 = C.sb(uname("ncb0"), (128, NT), F32)
    biasall = C.sb(uname("biasall"), (128, NT, NSUB), F32)
    for hp in range(H // 2):
        w = wq[hp % 2]
        for qi in range(4):
            load_w(P, w, w[:, :, qi, :], win_v[:, :, qi * D + hp * 128: qi * D + (hp + 1) * 128])
        for ti in range(NT):
            ht = hTt[ti % 2]
            P.dma("sp", ht, hTd, ht[:], hTd_ap[:, :, ti * TT:(ti + 1) * TT])
            tsl = slice(ti * TT, (ti + 1) * TT)
            pset = ti % 2
            qk_b = st2[pset]
            q_ps, k_ps = qk_b[:, 0:TT], qk_b[:, TT:2 * TT]
            g_ps, v_ps = acc[2 * pset], acc[2 * pset + 1]
            ss_ps = v_ps
            for (qi, pp, pb_) in ((0, q_ps, qk_b), (1, k_ps, qk_b), (3, g_ps[:, 0:TT], g_ps)):
                for c in range(NCH):
                    P.op("pe", lambda: nc.tensor.matmul(pp, lhsT=w[:, c, qi, :], rhs=ht[:, c, :],
                                                        start=(c == 0), stop=(c == NCH - 1)), rd=(w, ht), wr=(pb_,))
            for s_ in range(NS):
                for c in range(NCH):
                    P.op("pe", lambda: nc.tensor.matmul(v_ps[:, s_ * 128:(s_ + 1) * 128],
                                                        lhsT=ht[:, c, s_ * 128:(s_ + 1) * 128], rhs=w[:, c, 2, :],
                                                        start=(c == 0), stop=(c == NCH - 1)), rd=(w, ht), wr=(v_ps,))
            vv = v_ps[:, 0:TT].rearrange("p (s e) -> p s e", e=128)
            P.op("act", lambda: nc.scalar.copy(out=Va[0][:, ti * NS:(ti + 1) * NS, 0:64], in_=vv[:, :, 0:64]),
                 rd=(v_ps,), wr=(Va[0],))
            P.op("dve", lambda: nc.vector.tensor_copy(out=Va[1][:, ti * NS:(ti + 1) * NS, 64:128],
                                                      in_=vv[:, :, 64:128]), rd=(v_ps,), wr=(Va[1],))
            P.op("act", lambda: nc.scalar.activation(out=sgT[:, tsl], in_=g_ps[:, 0:TT], func=AF.Sigmoid),
                 rd=(g_ps,), wr=(sgT,))
            for n_, (pp, gg, dst) in enumerate(((q_ps, gq2, qT), (k_ps, gk2, kT))):
                sq, rs = sqf[n_], rsf[n_]
                P.op("act", lambda: nc.scalar.activation(out=sq[:], in_=pp, func=AF.Square),
                     rd=(qk_b,), wr=(sq,))
                P.op("pe", lambda: nc.tensor.matmul(ss_ps[:, 0:TT], lhsT=bones[:], rhs=sq[:], start=True, stop=True),
                     rd=(bones, sq), wr=(ss_ps,))
                P.op("act", lambda: nc.scalar.activation(out=rs[:], in_=ss_ps[:, 0:TT], func=AF.Ln,
                                                         bias=float(EPS), scale=1.0 / 64), rd=(ss_ps,), wr=(rs,))
                P.op("act", lambda: nc.scalar.activation(out=rs[:], in_=rs[:], func=AF.Exp, scale=-0.5),
                     rd=(rs,), wr=(rs,))
                if dst is kT:
                    for hh in range(2):
                        hs_ = slice(hh * 64, (hh + 1) * 64)
                        P.op("dve", lambda: nc.vector.scalar_tensor_tensor(out=kTz[hh][hs_, tsl], in0=pp[hs_, :],
                                                                           scalar=gg[hs_, 0:1], in1=rs[hs_, :],
                                                                           op0=ALU.mult, op1=ALU.mult),
                             rd=(qk_b, gg, rs), wr=(kTz[hh],))
                else:
                    P.op("dve", lambda: nc.vector.scalar_tensor_tensor(out=dst[:, tsl], in0=pp,
                                                                       scalar=gg[:, 0:1], in1=rs[:], op0=ALU.mult,
                                                                       op1=ALU.mult), rd=(qk_b, gg, rs), wr=(dst,))
        for par in range(2):
            h = 2 * hp + par
            pr = slice(par * 64, (par + 1) * 64)
            for ti in range(NT):
                cp = acc[ti % 4]
                P.op("pe", lambda: nc.tensor.matmul(cp[:, 0:TT], lhsT=sel[:, h, :], rhs=cum[:, ti * TT:(ti + 1) * TT],
                                                    start=True, stop=True), rd=(sel, cum), wr=(cp,))
                P.op("act", lambda: nc.scalar.copy(out=cbc[:, ti * TT:(ti + 1) * TT], in_=cp[:, 0:TT]),
                     rd=(cp,), wr=(cbc,))
                P.op("pool", lambda: nc.gpsimd.tensor_tensor(out=cbcm[:, ti * TT:(ti + 1) * TT],
                                                              in0=cbc[:, ti * TT:(ti + 1) * TT], in1=tri4[:],
                                                              op=ALU.add), rd=(cbc, tri4), wr=(cbcm,))
            pairs = [list(range(t0_, min(t0_ + 2, NT))) for t0_ in range(0, NT, 2)]
            c0v = cbc[:].rearrange("p (n t) -> p n t", t=TT * 2 if NT > 1 else TT)[:, :, 0]
            P.op("dve", lambda: nc.vector.tensor_scalar(out=ncb0[:, 0:len(pairs)], in0=c0v, scalar1=-1.0, scalar2=None,
                                                        op0=ALU.mult), rd=(cbc,), wr=(ncb0,))
            for m, tl in enumerate(pairs):
                if m == 0:
                    continue
                nb_ = tl[0] * NS
                P.op("dve", lambda: nc.vector.tensor_scalar(out=biasall[:, m, 0:nb_], in0=ncumT[:, 0:nb_, h],
                                                            scalar1=cbc[:, tl[0] * TT:tl[0] * TT + 1], scalar2=None,
                                                            op0=ALU.add), rd=(ncumT, cbc), wr=(biasall,))
            for m, tl in enumerate(pairs):
                ntl = len(tl)
                noff = tl[0] * NS
                a_off = [acc[0], acc[1]]
                a_dg = [acc[2], acc[3]]
                offs = [("off", kb) for kb in range(noff)]
                dvs = []
                for xi, ti in enumerate(tl):
                    for kb in range(noff, (ti + 1) * NS):
                        r = kb - ti * NS
                        dvs.append(("dv", xi, kb, (128 * r if r > 0 else 0), r >= 0))
                items = offs + dvs
                dv_first, dv_last, off_first, off_last = {}, {}, None, None
                for n_, it_ in enumerate(items):
                    if it_[0] == "off":
                        off_first = n_ if off_first is None else off_first
                        off_last = n_
                    else:
                        dv_first.setdefault(it_[1], n_)
                        dv_last[it_[1]] = n_
                LAG = 2
                for xi, ti in enumerate(tl):
                    if noff > 0:
                        P.op("act", lambda: nc.scalar.activation(out=ecr[xi][:], in_=cbc[:, ti * TT:(ti + 1) * TT],
                                                                 func=AF.Exp, bias=ncb0[:, m:m + 1], scale=1.0),
                             rd=(cbc, ncb0), wr=(ecr[xi],))

                def emit_s(n):
                    it_ = items[n]
                    sp_ = st2[n % 2]
                    pb = pT2[n % 3]
                    if it_[0] == "off":
                        kb = it_[1]
                        for xi, ti in enumerate(tl):
                            P.op("pe", lambda: nc.tensor.matmul(sp_[:, xi * TT:(xi + 1) * TT],
                                                                lhsT=kTz[par][:, kb * 128:(kb + 1) * 128],
                                                                rhs=qT[:, ti * TT:(ti + 1) * TT], start=True, stop=True),
                                 rd=(kTz[par], qT), wr=(sp_,))
                        P.op("act", lambda: nc.scalar.activation(out=pb[:, 0:ntl * TT], in_=sp_[:, 0:ntl * TT],
                                                                 func=AF.Exp, bias=biasall[:, m, kb:kb + 1], scale=1.0),
                             rd=(sp_, biasall), wr=(pb,))
                        return
                    _, xi, kb, c0, tri_ = it_
                    ti = tl[xi]
                    P.op("pe", lambda: nc.tensor.matmul(sp_[:, c0:TT], lhsT=kTz[par][:, kb * 128:(kb + 1) * 128],
                                                        rhs=qT[:, ti * TT + c0:(ti + 1) * TT], start=True, stop=True),
                         rd=(kTz[par], qT), wr=(sp_,))
                    tb = tmpb[n % 3]
                    if tri_:
                        P.op("dve", lambda: nc.vector.scalar_tensor_tensor(
                            out=tb[:, c0:c0 + 128], in0=sp_[:, c0:c0 + 128], scalar=ncumT[:, kb, h:h + 1],
                            in1=cbcm[:, ti * TT + c0:ti * TT + c0 + 128], op0=ALU.add, op1=ALU.add),
                            rd=(sp_, ncumT, cbcm), wr=(tb,))
                        c1 = c0 + 128
                    else:
                        c1 = c0
                    if c1 < TT:
                        P.op("dve", lambda: nc.vector.scalar_tensor_tensor(
                            out=tb[:, c1:TT], in0=sp_[:, c1:TT], scalar=ncumT[:, kb, h:h + 1],
                            in1=cbc[:, ti * TT + c1:(ti + 1) * TT], op0=ALU.add, op1=ALU.add),
                            rd=(sp_, ncumT, cbc), wr=(tb,))
                    P.op("act", lambda: nc.scalar.activation(out=pb[:, c0:TT], in_=tb[:, c0:TT], func=AF.Exp),
                         rd=(tb,), wr=(pb,))

                def emit_pv(n):
                    it_ = items[n]
                    pb = pT2[n % 3]
                    if it_[0] == "off":
                        kb = it_[1]
                        for xi, ti in enumerate(tl):
                            P.op("pe", lambda: nc.tensor.matmul(a_off[xi][:, 0:TT], lhsT=Va[par][:, kb, :],
                                                                rhs=pb[:, xi * TT:(xi + 1) * TT],
                                                                start=(n == off_first), stop=(n == off_last)),
                                 rd=(Va[par], pb), wr=(a_off[xi],))
                        return
                    _, xi, kb, c0, tri_ = it_
                    P.op("pe", lambda: nc.tensor.matmul(a_dg[xi][:, c0:TT], lhsT=Va[par][:, kb, :], rhs=pb[:, c0:TT],
                                                        start=(n == dv_first[xi]), stop=(n == dv_last[xi])),
                         rd=(Va[par], pb), wr=(a_dg[xi],))

                for n in range(len(items) + LAG):
                    if n < len(items):
                        emit_s(n)
                    if n - LAG >= 0:
                        emit_pv(n - LAG)
                for xi, ti in enumerate(tl):
                    if noff > 0:
                        P.op("dve", lambda: nc.vector.tensor_tensor(out=osb[:], in0=a_off[xi][:, 0:TT], in1=ecr[xi][:],
                                                                    op=ALU.mult), rd=(a_off[xi], ecr[xi]), wr=(osb,))
                        P.op("dve", lambda: nc.vector.tensor_tensor(out=osb[:], in0=a_dg[xi][:, 0:TT], in1=osb[:],
                                                                    op=ALU.add), rd=(a_dg[xi], osb), wr=(osb,))
                    else:
                        P.op("act", lambda: nc.scalar.copy(out=osb[:], in_=a_dg[xi][:, 0:TT]), rd=(a_dg[xi],), wr=(osb,))
                    opr = slice((1 - par) * 64, (2 - par) * 64)
                    P.dma("sp", dsw, osb, dsw[pr, :], osb[opr, :])
                    P.op("act", lambda: nc.scalar.activation(out=rden[pr, :], in_=dsw[pr, :], func=AF.Ln),
                         rd=(dsw,), wr=(rden,))
                    P.op("act", lambda: nc.scalar.activation(out=rden[pr, :], in_=rden[pr, :], func=AF.Exp, scale=-1.0),
                         rd=(rden,), wr=(rden,))
                    P.op("pool", lambda: nc.gpsimd.tensor_tensor(out=onum[pr, :], in0=osb[pr, :], in1=rden[pr, :],
                                                                  op=ALU.mult), rd=(osb, rden), wr=(onum,))
                    P.op("pool", lambda: nc.gpsimd.tensor_tensor(out=oT[pr, ti * TT:(ti + 1) * TT], in0=onum[pr, :],
                                                                  in1=sgT[pr, ti * TT:(ti + 1) * TT], op=ALU.mult),
                         rd=(onum, sgT), wr=(oT,))
        P.dma("sp", oTd, oT, oTd_ap[:, hp, 0:T], oT[:])
    C.close()

    outproj_stage(P, T, xin, xout, xin_ap, xout_ap, oTd_ap, oTd, wout_ap, gpost_ap)


def outproj_stage(P, T, xin, xout, xin_ap, xout_ap, oTd_ap, oTd, wout_ap, gpost_ap):
    nc = P.nc
    TT = 512 if T >= 512 else T
    NT = T // TT
    NS = TT // 128
    xin_v = xin_ap.rearrange("(n p) d -> n p d", p=128)
    xout_v = xout_ap.rearrange("(n p) d -> n p d", p=128)
    C = Ctx(P)
    gpost = load_gain_bc(P, C, uname("gpost"), gpost_ap, 1.0)
    wo = C.sb(uname("wo"), (128, NCH, D), BF16, dma="sw")
    wo_v = wout_ap.rearrange("(c p) n -> p c n", p=128)
    for q in range(4):
        load_w(P, wo, wo[:, 2 * q:2 * q + 2, :], wo_v[:, 2 * q:2 * q + 2, :])
    ot = [C.sb(uname("ot"), (128, NCH, TT), BF16, dma=True) for _ in range(2)]
    xp = [C.sb(uname("xp"), (128, D), F32, dma=True) for _ in range(2)]
    sq_junk = C.sb(uname("sqj"), (128, D), BF16)
    tmp = C.sb(uname("tmp"), (128, D), F32)
    stats = [C.sb(uname("stat"), (128, 8), F32) for _ in range(4)]
    f_ps = [[C.ps(uname("fps")) for _ in range(2)] for _ in range(2)]
    it = 0
    for ti in range(NT):
        o_ = ot[ti % 2]
        P.dma("sp", o_, oTd, o_[:], oTd_ap[:, :, ti * TT:(ti + 1) * TT])
        for s_ in range(NS):
            xst = xp[it % 2]
            fp = f_ps[it % 2]
            P.dma("sp", xst, xin, xst[:], xin_v[ti * NS + s_])
            for hh in range(2):
                for c in range(NCH):
                    P.op("pe", lambda: nc.tensor.matmul(fp[hh][:, :], lhsT=o_[:, c, s_ * 128:(s_ + 1) * 128],
                                                        rhs=wo[:, c, hh * 512:(hh + 1) * 512],
                                                        start=(c == 0), stop=(c == NCH - 1)), rd=(o_, wo), wr=(fp[hh],))
            postnorm_tile(P, nc, fp, xst, gpost, xst, sq_junk, stats[it % 4], tmp)
            P.dma("sp", xout, xst, xout_v[ti * NS + s_], xst[:])
            it += 1
    C.close()


CW = 31
CK = 64
DEC_C = 0.6065306597126334


def rwkv_prep_stage(P, T, xin, xin_ap, gpre_ap, A, scr):
    nc = P.nc
    TT = 512 if T >= 512 else T
    NT = T // TT
    NS = TT // 128
    NQ = 4
    CR = 1792
    rw_ap, rw = scr["rw"]
    dc_ap, dcb = scr["dc"]
    yab_ap, yab = scr["oT"]
    xin_v = xin_ap.rearrange("(n p) d -> n p d", p=128)
    win_v = A["w_in"].rearrange("(c p) n -> p c n", p=128)
    dsrc = Buf(P, None, "dsrc")
    C = Ctx(P)
    idf, idb = make_consts(P, C)
    gpre = load_gain_bc(P, C, uname("gpre"), gpre_ap, 1.0)
    W1 = C.sb(uname("W1"), (128, NCH, CR), BF16)
    W2 = C.sb(uname("W2"), (128, NCH, CR), BF16)
    Wc = C.sb(uname("Wc"), (128, NCH, 1024), BF16, dma="sw")
    for q in range(4):
        load_w(P, Wc, Wc[:, 2 * q:2 * q + 2, :], win_v[:, 2 * q:2 * q + 2, CR:CR + 1024])
    C0 = Ctx(P)
    mu = C0.sb(uname("mu"), (128, CR), F32, dma=True)
    omu = C0.sb(uname("omu"), (128, CR), F32)
    P.dma("sp", mu, dsrc, mu[:], A["shift_mu"].partition_broadcast(128))
    P.op("dve", lambda: nc.vector.tensor_scalar(out=omu[:], in0=mu[:], scalar1=-1.0, scalar2=1.0, op0=ALU.mult,
                                                op1=ALU.add), rd=(mu,), wr=(omu,))
    stg = [C0.sb(uname("stg"), (128, CR), F32, dma=True) for _ in range(2)]
    for c in range(NCH):
        st = stg[c % 2]
        P.dma("sp", st, dsrc, st[:], win_v[:, c, 0:CR])
        P.op("dve", lambda: nc.vector.tensor_tensor(out=W2[:, c, :], in0=st[:], in1=mu[:], op=ALU.mult),
             rd=(st, mu), wr=(W2,))
        P.op("pool", lambda: nc.gpsimd.tensor_tensor(out=W1[:, c, :], in0=st[:], in1=omu[:], op=ALU.mult),
             rd=(st, omu), wr=(W1,))
    C0.close()
    lup = C.sb(uname("lup"), (128, 512), BF16, dma="sw")
    load_w(P, lup, lup[0:64, :], A["decay_up"])
    load_w(P, lup, lup[64:128, :], A["iclr_up"])
    gup = C.sb(uname("gup"), (128, 512), BF16, dma="sw")
    load_w(P, gup, gup[:], A["gate_up"])
    pnames = ("decay_base", "iclr_base", "k_k", "k_a", "r_k", "lnx_b", "conv_b", "ln_g", "ln_b")
    NPR = len(pnames) + CW
    prow = C.sb(uname("prow"), (NPR, 512), F32, dma=True)
    for i_, nm in enumerate(pnames):
        src = A[nm]
        if nm == "r_k":
            src = src.rearrange("h n -> (h n)")
        P.dma("sp", prow, dsrc, prow[i_:i_ + 1, :], src.rearrange("(o n) -> o n", o=1))
    P.dma("sp", prow, dsrc, prow[len(pnames):NPR, :], A["conv_w"])
    pc = {nm: C.sb(uname("pc_" + nm), (128, NQ), F32) for nm in pnames}
    cw = C.sb(uname("cw"), (128, NQ, CW), F32)
    ptp = C.ps(uname("ptp"))
    for q in range(NQ):
        P.op("pe", lambda: nc.tensor.transpose(out=ptp[:, q * 64:q * 64 + NPR], in_=prow[0:NPR, q * 128:(q + 1) * 128],
                                               identity=idf[0:NPR, 0:NPR]), rd=(prow, idf), wr=(ptp,))
    for q in range(NQ):
        for i_, nm in enumerate(pnames):
            P.op("dve", lambda: nc.vector.tensor_copy(out=pc[nm][:, q:q + 1], in_=ptp[:, q * 64 + i_:q * 64 + i_ + 1]),
                 rd=(ptp,), wr=(pc[nm],))
        P.op("act", lambda: nc.scalar.copy(out=cw[:, q, :], in_=ptp[:, q * 64 + len(pnames):q * 64 + NPR]),
             rd=(ptp,), wr=(cw,))
    omka = C.sb(uname("omka"), (128, NQ), F32)
    P.op("dve", lambda: nc.vector.tensor_scalar(out=omka[:], in0=pc["k_a"][:], scalar1=-1.0, scalar2=1.0,
                                                op0=ALU.mult, op1=ALU.add), rd=(pc["k_a"],), wr=(omka,))
    bones = C.sb(uname("bones"), (128, 128), F32)
    P.op("pool", lambda: nc.gpsimd.memset(bones[:], 0.0), wr=(bones,))
    P.op("pool", lambda: nc.gpsimd.memset(bones[0:64, 0:64], 1.0), rd=(bones,), wr=(bones,))
    P.op("pool", lambda: nc.gpsimd.memset(bones[64:128, 64:128], 1.0), rd=(bones,), wr=(bones,))
    ones = C.sb(uname("ones"), (128, 128), F32)
    P.op("pool", lambda: nc.gpsimd.memset(ones[:], 1.0), wr=(ones,))
    rmask = C.sb(uname("rmask"), (128, TT), F32)
    P.op("pool", lambda: nc.gpsimd.memset(rmask[:], 1.0), wr=(rmask,))
    P.op("pool", lambda: nc.gpsimd.memset(rmask[:].rearrange("p (c j) -> p c j", j=CK)[:, :, 0:1], 0.0),
         rd=(rmask,), wr=(rmask,))
    xs = [C.sb(uname("xs"), (128, D), F32, dma=True) for _ in range(2)]
    hTh = [C.sb(uname("hTh"), (128, NCH, TT + 1), BF16) for _ in range(2)]
    P.op("pool", lambda: nc.gpsimd.memset(hTh[0][:, :, 0:1], 0.0), wr=(hTh[0],))
    sq_junk = C.sb(uname("sqj"), (128, D), BF16)
    hb = [C.sb(uname("hb"), (128, D), BF16) for _ in range(2)]
    stats = [C.sb(uname("stat"), (128, 8), F32) for _ in range(4)]
    tdw = C.sb(uname("tdw"), (128, TT), BF16)
    sdg = C.sb(uname("sdg"), (128, TT), BF16)
    F = {}
    for nm in ("rf", "kf", "sgw", "av", "gf", "kk", "kk2", "rn", "kkn", "t1", "kn", "bb", "rk", "Lc", "eL", "enL",
               "Lx", "eLx", "bon"):
        F[nm] = C.sb(uname(nm), (128, TT), F32)
    pack = [C.sb(uname("pack"), (128, 7, TT), BF16) for _ in range(2)]
    dct = C.sb(uname("dct"), (128, NQ, T // CK), F32)
    glub = [C.sb(uname("glub"), (128, TT + CW - 1), F32) for _ in range(NQ)]
    for q in range(NQ):
        P.op("pool", lambda: nc.gpsimd.memset(glub[q][:, 0:CW - 1], 0.0), wr=(glub[q],))
    sgc = C.sb(uname("sgc"), (128, TT), F32)
    acc = [C.sb(uname("acc"), (128, TT), F32) for _ in range(NQ)]
    sqc = C.sb(uname("sqc"), (128, TT), F32)
    mean = C.sb(uname("mean"), (128, TT), F32)
    msq = C.sb(uname("msq"), (128, TT), F32)
    rstd = C.sb(uname("rstd"), (128, TT), F32)
    tcv = C.sb(uname("tcv"), (128, TT), F32)
    ybt = [C.sb(uname("ybt"), (128, TT), BF16) for _ in range(2)]
    tp_ps = C.ps(uname("tp"), (128, D), BF16)
    bk = [C.ps(uname("bk")) for _ in range(6)] + [ptp]

    def proj(pp, col0, shifted, ht):
        n = 2 * NCH if shifted else NCH
        i = 0
        for c in range(NCH):
            if shifted:
                P.op("pe", lambda: nc.tensor.matmul(pp[:, 0:TT], lhsT=W1[:, c, col0:col0 + 128], rhs=ht[:, c, 1:TT + 1],
                                                    start=(i == 0), stop=(i == n - 1)), rd=(W1, ht), wr=(pp,))
                i += 1
                P.op("pe", lambda: nc.tensor.matmul(pp[:, 0:TT], lhsT=W2[:, c, col0:col0 + 128], rhs=ht[:, c, 0:TT],
                                                    start=False, stop=(i == n - 1)), rd=(W2, ht), wr=(pp,))
                i += 1
            else:
                P.op("pe", lambda: nc.tensor.matmul(pp[:, 0:TT], lhsT=Wc[:, c, col0:col0 + 128], rhs=ht[:, c, 1:TT + 1],
                                                    start=(i == 0), stop=(i == n - 1)), rd=(Wc, ht), wr=(pp,))
                i += 1

    pk_i = 0
    for ti in range(NT):
        ht = hTh[ti % 2]
        tsl = slice(ti * TT, (ti + 1) * TT)
        for s_ in range(NS):
            xst = xs[s_ % 2]
            P.dma("sp", xst, xin, xst[:], xin_v[ti * NS + s_])
            prenorm_tile(P, nc, xst, gpre, ht, 1 + s_ * 128, idb, sq_junk, stats[s_ % 4], hb[s_ % 2], tp_ps)
        if ti + 1 < NT:
            P.op("pool", lambda: nc.gpsimd.tensor_copy(out=hTh[(ti + 1) % 2][:, :, 0:1], in_=ht[:, :, TT:TT + 1]),
                 rd=(ht,), wr=(hTh[(ti + 1) % 2],))
        proj(bk[6], 1536, True, ht)
        P.op("act", lambda: nc.scalar.activation(out=tdw[0:64, :], in_=bk[6][0:64, 0:TT], func=AF.Tanh),
             rd=(bk[6],), wr=(tdw,))
        P.op("act", lambda: nc.scalar.copy(out=tdw[64:128, :], in_=bk[6][64:128, 0:TT]), rd=(bk[6],), wr=(tdw,))
        proj(bk[5], 1664, True, ht)
        P.op("act", lambda: nc.scalar.activation(out=sdg[:], in_=bk[5][:, 0:TT], func=AF.Sigmoid),
             rd=(bk[5],), wr=(sdg,))
        for q in range(NQ):
            pk = pack[pk_i % 2]
            pk_i += 1
            qs = slice(q * 128, (q + 1) * 128)
            r_ps, k_ps, v_ps, zw_ps, za_ps, g_ps, s_ps = bk[0], bk[1], bk[2], bk[3], bk[4], bk[5], bk[6]
            proj(r_ps, q * 128, True, ht)
            proj(k_ps, 512 + q * 128, True, ht)
            proj(v_ps, 1024 + q * 128, True, ht)
            P.op("pe", lambda: nc.tensor.matmul(zw_ps[:, 0:TT], lhsT=lup[0:64, qs], rhs=tdw[0:64, :], start=True,
                                                stop=True), rd=(lup, tdw), wr=(zw_ps,))
            P.op("pe", lambda: nc.tensor.matmul(za_ps[:, 0:TT], lhsT=lup[64:128, qs], rhs=tdw[64:128, :], start=True,
                                                stop=True), rd=(lup, tdw), wr=(za_ps,))
            P.op("pe", lambda: nc.tensor.matmul(g_ps[:, 0:TT], lhsT=gup[:, qs], rhs=sdg[:], start=True, stop=True),
                 rd=(gup, sdg), wr=(g_ps,))
            col = lambda nm: pc[nm][:, q:q + 1]
            P.op("act", lambda: nc.scalar.copy(out=F["rf"][:], in_=r_ps[:, 0:TT]), rd=(r_ps,), wr=(F["rf"],))
            P.op("act", lambda: nc.scalar.copy(out=F["kf"][:], in_=k_ps[:, 0:TT]), rd=(k_ps,), wr=(F["kf"],))
            P.op("act", lambda: nc.scalar.copy(out=pk[:, 4, :], in_=v_ps[:, 0:TT]), rd=(v_ps,), wr=(pk,))
            P.op("act", lambda: nc.scalar.activation(out=F["sgw"][:], in_=zw_ps[:, 0:TT], func=AF.Sigmoid,
                                                     bias=col("decay_base")), rd=(zw_ps, pc["decay_base"]),
                 wr=(F["sgw"],))
            P.op("act", lambda: nc.scalar.activation(out=F["av"][:], in_=za_ps[:, 0:TT], func=AF.Sigmoid,
                                                     bias=col("iclr_base")), rd=(za_ps, pc["iclr_base"]),
                 wr=(F["av"],))
            P.op("act", lambda: nc.scalar.copy(out=F["gf"][:], in_=g_ps[:, 0:TT]), rd=(g_ps,), wr=(F["gf"],))
            P.op("dve", lambda: nc.vector.tensor_scalar(out=F["kk"][:], in0=F["kf"][:], scalar1=col("k_k"),
                                                        scalar2=None, op0=ALU.mult), rd=(F["kf"], pc["k_k"]),
                 wr=(F["kk"],))
            P.op("pool", lambda: nc.gpsimd.tensor_tensor(out=F["kk2"][:], in0=F["kk"][:], in1=F["kk"][:],
                                                          op=ALU.mult), rd=(F["kk"],), wr=(F["kk2"],))
            P.op("pe", lambda: nc.tensor.matmul(s_ps[:, 0:TT], lhsT=bones[:], rhs=F["kk2"][:], start=True, stop=True),
                 rd=(bones, F["kk2"]), wr=(s_ps,))
            P.op("act", lambda: nc.scalar.activation(out=F["rn"][:], in_=s_ps[:, 0:TT], func=AF.Ln, bias=1e-24,
                                                     scale=1.0), rd=(s_ps,), wr=(F["rn"],))
            P.op("act", lambda: nc.scalar.activation(out=F["rn"][:], in_=F["rn"][:], func=AF.Exp, scale=-0.5),
                 rd=(F["rn"],), wr=(F["rn"],))
            P.op("pool", lambda: nc.gpsimd.tensor_tensor(out=F["kkn"][:], in0=F["kk"][:], in1=F["rn"][:],
                                                          op=ALU.mult), rd=(F["kk"], F["rn"]), wr=(F["kkn"],))
            P.op("dve", lambda: nc.vector.tensor_scalar(out=F["t1"][:], in0=F["av"][:], scalar1=col("k_a"),
                                                        scalar2=omka[:, q:q + 1], op0=ALU.mult, op1=ALU.add),
                 rd=(F["av"], pc["k_a"], omka), wr=(F["t1"],))
            P.op("pool", lambda: nc.gpsimd.tensor_tensor(out=F["kn"][:], in0=F["kf"][:], in1=F["t1"][:],
                                                          op=ALU.mult), rd=(F["kf"], F["t1"]), wr=(F["kn"],))
            P.op("pool", lambda: nc.gpsimd.tensor_tensor(out=F["bb"][:], in0=F["kkn"][:], in1=F["av"][:],
                                                          op=ALU.mult), rd=(F["kkn"], F["av"]), wr=(F["bb"],))
            P.op("dve", lambda: nc.vector.scalar_tensor_tensor(out=F["rk"][:], in0=F["rf"][:], scalar=col("r_k"),
                                                               in1=F["kn"][:], op0=ALU.mult, op1=ALU.mult),
                 rd=(F["rf"], pc["r_k"], F["kn"]), wr=(F["rk"],))
            P.op("pe", lambda: nc.tensor.matmul(s_ps[:, 0:TT], lhsT=bones[:], rhs=F["rk"][:], start=True, stop=True),
                 rd=(bones, F["rk"]), wr=(s_ps,))
            P.op("dve", lambda: nc.vector.tensor_tensor(out=F["bon"][:], in0=s_ps[:, 0:TT], in1=pk[:, 4, :],
                                                        op=ALU.mult), rd=(s_ps, pk), wr=(F["bon"],))
            P.op("dve", lambda: nc.vector.scalar_tensor_tensor(out=pk[:, 6, :], in0=F["bon"][:], scalar=col("lnx_b"),
                                                               in1=F["gf"][:], op0=ALU.add, op1=ALU.mult),
                 rd=(F["bon"], pc["lnx_b"], F["gf"]), wr=(pk,))
            P.op("pool", lambda: nc.gpsimd.tensor_copy(out=pk[:, 5, :], in_=F["gf"][:]), rd=(F["gf"],), wr=(pk,))
            P.op("dve", lambda: nc.vector.tensor_tensor_scan(out=F["Lc"][:], data0=rmask[:], data1=F["sgw"][:],
                                                             initial=0.0, op0=ALU.mult, op1=ALU.add),
                 rd=(rmask, F["sgw"]), wr=(F["Lc"],))
            P.op("pool", lambda: nc.gpsimd.tensor_tensor(out=F["Lx"][:], in0=F["Lc"][:], in1=F["sgw"][:],
                                                          op=ALU.subtract), rd=(F["Lc"], F["sgw"]), wr=(F["Lx"],))
            P.op("act", lambda: nc.scalar.activation(out=F["eL"][:], in_=F["Lc"][:], func=AF.Exp, scale=-DEC_C),
                 rd=(F["Lc"],), wr=(F["eL"],))
            P.op("act", lambda: nc.scalar.activation(out=F["enL"][:], in_=F["Lc"][:], func=AF.Exp, scale=DEC_C),
                 rd=(F["Lc"],), wr=(F["enL"],))
            P.op("act", lambda: nc.scalar.activation(out=F["eLx"][:], in_=F["Lx"][:], func=AF.Exp, scale=-DEC_C),
                 rd=(F["Lx"],), wr=(F["eLx"],))
            nck = TT // CK
            P.op("pool", lambda: nc.gpsimd.tensor_copy(
                out=dct[:, q, ti * nck:(ti + 1) * nck],
                in_=F["eL"][:].rearrange("p (c j) -> p c j", j=CK)[:, :, CK - 1]), rd=(F["eL"],), wr=(dct,))
            P.op("pool", lambda: nc.gpsimd.tensor_tensor(out=pk[:, 0, :], in0=F["rf"][:], in1=F["eL"][:], op=ALU.mult),
                 rd=(F["rf"], F["eL"]), wr=(pk,))
            P.op("dve", lambda: nc.vector.scalar_tensor_tensor(out=pk[:, 1, :], in0=F["kkn"][:], scalar=-1.0,
                                                               in1=F["eLx"][:], op0=ALU.mult, op1=ALU.mult),
                 rd=(F["kkn"], F["eLx"]), wr=(pk,))
            P.op("pool", lambda: nc.gpsimd.tensor_tensor(out=pk[:, 2, :], in0=F["kn"][:], in1=F["enL"][:], op=ALU.mult),
                 rd=(F["kn"], F["enL"]), wr=(pk,))
            P.op("dve", lambda: nc.vector.tensor_tensor(out=pk[:, 3, :], in0=F["bb"][:], in1=F["enL"][:], op=ALU.mult),
                 rd=(F["bb"], F["enL"]), wr=(pk,))
            P.dma("sp", rw, pk, rw_ap[qs, :, tsl], pk[:])
        for q in range(NQ):
            val_ps, gt_ps = bk[0 + 2 * (q % 2)], bk[1 + 2 * (q % 2)]
            proj(val_ps, q * 128, False, ht)
            proj(gt_ps, 512 + q * 128, False, ht)
            gb = glub[q]
            P.op("act", lambda: nc.scalar.activation(out=sgc[:], in_=gt_ps[:, 0:TT], func=AF.Sigmoid),
                 rd=(gt_ps,), wr=(sgc,))
            P.op("dve", lambda: nc.vector.tensor_tensor(out=gb[:, CW - 1:CW - 1 + TT], in0=val_ps[:, 0:TT], in1=sgc[:],
                                                        op=ALU.mult), rd=(val_ps, sgc), wr=(gb,))
            ac = acc[q]
            P.op("dve", lambda: nc.vector.tensor_scalar(out=ac[:], in0=gb[:, 0:TT], scalar1=cw[:, q, 0:1],
                                                        scalar2=pc["conv_b"][:, q:q + 1], op0=ALU.mult, op1=ALU.add),
                 rd=(gb, cw, pc["conv_b"]), wr=(ac,))
            for j in range(1, CW):
                P.op("dve", lambda: nc.vector.scalar_tensor_tensor(out=ac[:], in0=gb[:, j:j + TT],
                                                                   scalar=cw[:, q, j:j + 1], in1=ac[:], op0=ALU.mult,
                                                                   op1=ALU.add), rd=(gb, cw, ac), wr=(ac,))
            P.op("pool", lambda: nc.gpsimd.tensor_copy(out=gb[:, 0:CW - 1], in_=gb[:, TT:TT + CW - 1]),
                 rd=(gb,), wr=(gb,))
        sum_ps, ssq_ps = bk[4], bk[5]
        for q in range(NQ):
            P.op("pe", lambda: nc.tensor.matmul(sum_ps[:, 0:TT], lhsT=ones[:], rhs=acc[q][:], start=(q == 0),
                                                stop=(q == NQ - 1)), rd=(ones, acc[q]), wr=(sum_ps,))
        for q in range(NQ):
            P.op("act", lambda: nc.scalar.activation(out=sqc[:], in_=acc[q][:], func=AF.Square), rd=(acc[q],),
                 wr=(sqc,))
            P.op("pe", lambda: nc.tensor.matmul(ssq_ps[:, 0:TT], lhsT=ones[:], rhs=sqc[:], start=(q == 0),
                                                stop=(q == NQ - 1)), rd=(ones, sqc), wr=(ssq_ps,))
        P.op("act", lambda: nc.scalar.activation(out=mean[:], in_=sum_ps[:, 0:TT], func=AF.Copy, scale=1.0 / 512),
             rd=(sum_ps,), wr=(mean,))
        P.op("pool", lambda: nc.gpsimd.tensor_tensor(out=msq[:], in0=mean[:], in1=mean[:], op=ALU.mult),
             rd=(mean,), wr=(msq,))
        P.op("dve", lambda: nc.vector.scalar_tensor_tensor(out=rstd[:], in0=ssq_ps[:, 0:TT], scalar=1.0 / 512,
                                                           in1=msq[:], op0=ALU.mult, op1=ALU.subtract),
             rd=(ssq_ps, msq), wr=(rstd,))
        P.op("act", lambda: nc.scalar.activation(out=rstd[:], in_=rstd[:], func=AF.Ln, bias=1e-5, scale=1.0),
             rd=(rstd,), wr=(rstd,))
        P.op("act", lambda: nc.scalar.activation(out=rstd[:], in_=rstd[:], func=AF.Exp, scale=-0.5),
             rd=(rstd,), wr=(rstd,))
        for q in range(NQ):
            yb_ = ybt[q % 2]
            P.op("pool", lambda: nc.gpsimd.tensor_tensor(out=tcv[:], in0=acc[q][:], in1=mean[:], op=ALU.subtract),
                 rd=(acc[q], mean), wr=(tcv,))
            P.op("pool", lambda: nc.gpsimd.tensor_tensor(out=tcv[:], in0=tcv[:], in1=rstd[:], op=ALU.mult),
                 rd=(tcv, rstd), wr=(tcv,))
            P.op("act", lambda: nc.scalar.activation(out=yb_[:], in_=tcv[:], func=AF.Silu,
                                                     bias=pc["ln_b"][:, q:q + 1], scale=pc["ln_g"][:, q:q + 1]),
                 rd=(tcv, pc["ln_b"], pc["ln_g"]), wr=(yb_,))
            P.dma("sp", yab, yb_, yab_ap[:, 4 + q, tsl], yb_[:])
    for q in range(NQ):
        P.dma("sp", dcb, dct, dc_ap[q * 128:(q + 1) * 128, :], dct[:, q, :])
    C.close()


def rwkv_scan_stage(P, T, A, scr):
    nc = P.nc
    NH = 8
    NHC = 4
    TT = 512 if T >= 512 else T
    NG = T // TT
    NCG = TT // CK
    NC = T // CK
    rw_ap, rw = scr["rw"]
    dc_ap, dcb = scr["dc"]
    yab_ap, yab = scr["oT"]
    dsrc = Buf(P, None, "dsrc")
    C = Ctx(P)
    idf, idb = make_consts(P, C)
    ones64 = C.sb(uname("ones64"), (128, 64), F32)
    P.op("pool", lambda: nc.gpsimd.memset(ones64[:], 1.0), wr=(ones64,))

    def mk_mask(name, base, cm, step):
        m = C.sb(uname(name), (64, NCG, CK), F32)
        P.op("pool", lambda: nc.gpsimd.memset(m[:], 1.0), wr=(m,))
        P.op("pool", lambda: nc.gpsimd.affine_select(out=m[:], in_=m[:], compare_op=ALU.is_ge, fill=0.0, base=base,
                                                      pattern=[[0, NCG], [step, CK]], channel_multiplier=cm),
             rd=(m,), wr=(m,))
        return m
    m_su = mk_mask("m_su", -1, -1, 1)
    m_sl = mk_mask("m_sl", -1, 1, -1)
    m_iu = mk_mask("m_iu", 0, -1, 1)
    I8 = C.sb(uname("I8"), (64, NCG, CK), F32)
    P.op("pool", lambda: nc.gpsimd.memset(I8[:], 0.0), wr=(I8,))
    P.op("pool", lambda: nc.gpsimd.affine_select(out=I8[:], in_=I8[:], compare_op=ALU.not_equal, fill=1.0, base=0,
                                                  pattern=[[0, NCG], [-1, CK]], channel_multiplier=1),
         rd=(I8,), wr=(I8,))
    lnrow = C.sb(uname("lnrow"), (1, 512), F32, dma=True)
    P.dma("sp", lnrow, dsrc, lnrow[:], A["lnx_g"].rearrange("(o n) -> o n", o=1))
    lnxg = C.sb(uname("lnxg"), (64, NH), F32)
    flat = lambda m: m[:].rearrange("p c i -> p (c i)")

    class Slot:
        pass
    slots = []
    for i in range(NHC):
        S = Slot()
        S.ops = [C.sb(uname("ops"), (128, 7, TT), BF16, dma=True) for _ in range(2)]
        for o_ in S.ops:
            P.op("pool", lambda: nc.gpsimd.memset(o_[64:128, :, :], 0.0), wr=(o_,))
        S.dC = C.sb(uname("dC"), (64, NC), F32, dma=True)
        for nm in ("Akt", "Arbt", "Arkt", "Tt", "Btok", "Ktok", "Vtok", "U", "Mb0", "Mb1", "Mt0", "Mt1"):
            setattr(S, nm, C.sb(uname(nm), (128, TT), BF16))
            P.op("pool", lambda: nc.gpsimd.memset(getattr(S, nm)[64:128, :], 0.0), wr=(getattr(S, nm),))
        S.Hall = C.sb(uname("Hall"), (128, NCG + 1, CK), BF16)
        P.op("pool", lambda: nc.gpsimd.memset(S.Hall[64:128, :, :], 0.0), wr=(S.Hall,))
        S.Hf = C.sb(uname("Hf"), (64, CK), F32)
        S.tmpH = C.sb(uname("tmpH"), (64, CK), F32)
        S.Wsb = C.sb(uname("Wsb"), (128, CK), BF16)
        P.op("pool", lambda: nc.gpsimd.memset(S.Wsb[64:128, :], 0.0), wr=(S.Wsb,))
        S.yT = C.sb(uname("yT"), (128, TT), F32)
        S.sqy = C.sb(uname("sqy"), (128, TT), F32)
        P.op("pool", lambda: nc.gpsimd.memset(S.yT[64:128, :], 0.0), wr=(S.yT,))
        P.op("pool", lambda: nc.gpsimd.memset(S.sqy[64:128, :], 0.0), wr=(S.sqy,))
        S.mean = C.sb(uname("mean"), (64, TT), F32)
        S.rstd = C.sb(uname("rstd"), (64, TT), F32)
        S.yo = C.sb(uname("yo"), (64, TT), BF16)
        S.bk = [C.ps(uname("bk")) for _ in range(2)]
        S.bi = 0
        slots.append(S)
    lps = slots[0].bk[0]
    for h_ in range(NH):
        P.op("pe", lambda: nc.tensor.transpose(out=lps[0:64, h_:h_ + 1], in_=lnrow[0:1, h_ * 64:(h_ + 1) * 64],
                                               identity=idf[0:1, 0:1]), rd=(lnrow, idf), wr=(lps,))
    P.op("dve", lambda: nc.vector.tensor_copy(out=lnxg[:], in_=lps[0:64, 0:NH]), rd=(lps,), wr=(lnxg,))

    def head_prog(S, h):
        def nb():
            S.bi += 1
            return S.bk[S.bi % 2]

        def macro(lt, lsl, rt, rsl, ps):
            for c in range(NCG):
                cs = slice(c * CK, (c + 1) * CK)
                P.op("pe", lambda: nc.tensor.matmul(ps[0:64, cs], lhsT=lsl(c), rhs=rsl(c), start=True, stop=True),
                     rd=(lt, rt), wr=(ps,))
        P.dma("sp", S.dC, dcb, S.dC[:], dc_ap[h * 64:(h + 1) * 64, :])
        P.op("pool", lambda: nc.gpsimd.memset(S.Hf[:], 0.0), wr=(S.Hf,))
        P.op("pool", lambda: nc.gpsimd.memset(S.Hall[0:64, 0, :], 0.0), wr=(S.Hall,))
        Mb, Mtb = [S.Mb0, S.Mb1], [S.Mt0, S.Mt1]
        for g in range(NG):
            g0 = g * TT
            ops = S.ops[g % 2]
            P.dma("sp", ops, rw, ops[0:64, :, :], rw_ap[h * 64:(h + 1) * 64, :, g0:g0 + TT])
            osl = lambda kind: (lambda c: ops[:, kind, c * CK:(c + 1) * CK])
            loc = lambda t_: (lambda c: t_[:, c * CK:(c + 1) * CK])
            Rs, As, Ks, Bs, Vs = osl(0), osl(1), osl(2), osl(3), osl(4)
            idl = lambda c: idb[:, 0:64]
            if g > 0:
                P.op("pool", lambda: nc.gpsimd.tensor_copy(out=S.Hall[0:64, 0, :], in_=S.Hall[0:64, NCG, :]), rd=(S.Hall,),
                     wr=(S.Hall,))
            yield
            ps = nb()
            macro(ops, Bs, ops, As, ps)
            P.op("dve", lambda: nc.vector.tensor_tensor(out=Mtb[0][0:64, :], in0=ps[0:64, 0:TT], in1=flat(m_su), op=ALU.mult),
                 rd=(ps, m_su), wr=(Mtb[0],))
            P.op("pool", lambda: nc.gpsimd.tensor_tensor(out=S.Tt[0:64, :], in0=Mtb[0][0:64, :], in1=flat(I8), op=ALU.add),
                 rd=(Mtb[0], I8), wr=(S.Tt,))
            yield
            ps = nb()
            macro(ops, As, ops, Bs, ps)
            P.op("dve", lambda: nc.vector.tensor_tensor(out=Mb[0][0:64, :], in0=ps[0:64, 0:TT], in1=flat(m_sl), op=ALU.mult),
                 rd=(ps, m_sl), wr=(Mb[0],))
            yield
            for (ls, rs_, dst, mk) in ((Ks, As, S.Akt, m_su), (Bs, Rs, S.Arbt, m_iu), (Ks, Rs, S.Arkt, m_iu)):
                ps = nb()
                macro(ops, ls, ops, rs_, ps)
                P.op("dve", lambda: nc.vector.tensor_tensor(out=dst[0:64, :], in0=ps[0:64, 0:TT], in1=flat(mk), op=ALU.mult),
                     rd=(ps, mk), wr=(dst,))
                yield
            for (src, dst) in ((Bs, S.Btok), (Ks, S.Ktok), (Vs, S.Vtok)):
                ps = nb()
                macro(ops, src, idb, idl, ps)
                P.op("act", lambda: nc.scalar.copy(out=dst[0:64, :], in_=ps[0:64, 0:TT]), rd=(ps,), wr=(dst,))
                yield
            cur = 0
            for p in range(1, 6):
                nxt = 1 - cur
                ps = nb()
                macro(Mtb[cur], loc(Mtb[cur]), Mb[cur], loc(Mb[cur]), ps)
                P.op("act", lambda: nc.scalar.copy(out=Mb[nxt][0:64, :], in_=ps[0:64, 0:TT]), rd=(ps,), wr=(Mb[nxt],))
                yield
                if p < 5:
                    ps2 = nb()
                    macro(Mb[cur], loc(Mb[cur]), Mtb[cur], loc(Mtb[cur]), ps2)
                    P.op("act", lambda: nc.scalar.copy(out=Mtb[nxt][0:64, :], in_=ps2[0:64, 0:TT]), rd=(ps2,),
                         wr=(Mtb[nxt],))
                    yield
                ps3 = nb()
                macro(Mb[nxt], loc(Mb[nxt]), S.Tt, loc(S.Tt), ps3)
                P.op("dve", lambda: nc.vector.tensor_tensor(out=S.Tt[0:64, :], in0=ps3[0:64, 0:TT], in1=S.Tt[0:64, :], op=ALU.add),
                     rd=(ps3, S.Tt), wr=(S.Tt,))
                yield
                cur = nxt
            for c in range(NCG):
                cg = g * NCG + c
                cs = slice(c * CK, (c + 1) * CK)
                w_ps, u_ps = S.bk[0], S.bk[1]
                P.op("pe", lambda: nc.tensor.matmul(w_ps[0:64, 0:CK], lhsT=As(c), rhs=S.Hall[:, c, :], start=True,
                                                    stop=False), rd=(ops, S.Hall), wr=(w_ps,))
                P.op("pe", lambda: nc.tensor.matmul(w_ps[0:64, 0:CK], lhsT=S.Akt[:, cs], rhs=S.Vtok[:, cs], start=False,
                                                    stop=True), rd=(S.Akt, S.Vtok), wr=(w_ps,))
                P.op("act", lambda: nc.scalar.copy(out=S.Wsb[0:64, :], in_=w_ps[0:64, 0:CK]), rd=(w_ps,), wr=(S.Wsb,))
                P.op("act", lambda: nc.scalar.activation(out=S.tmpH[:], in_=S.Hf[:], func=AF.Copy,
                                                         scale=S.dC[:, cg:cg + 1]), rd=(S.Hf, S.dC), wr=(S.tmpH,))
                yield
                P.op("pe", lambda: nc.tensor.matmul(u_ps[0:64, 0:CK], lhsT=S.Tt[:, cs], rhs=S.Wsb[:], start=True,
                                                    stop=True), rd=(S.Tt, S.Wsb), wr=(u_ps,))
                P.op("dve", lambda: nc.vector.tensor_copy(out=S.U[0:64, cs], in_=u_ps[0:64, 0:CK]), rd=(u_ps,), wr=(S.U,))
                yield
                h_ps = w_ps
                P.op("pe", lambda: nc.tensor.matmul(h_ps[0:64, 64:64 + CK], lhsT=S.Btok[:, cs], rhs=S.U[:, cs],
                                                    start=True, stop=False), rd=(S.Btok, S.U), wr=(h_ps,))
                P.op("pe", lambda: nc.tensor.matmul(h_ps[0:64, 64:64 + CK], lhsT=S.Ktok[:, cs], rhs=S.Vtok[:, cs],
                                                    start=False, stop=True), rd=(S.Ktok, S.Vtok), wr=(h_ps,))
                P.op("dve", lambda: nc.vector.scalar_tensor_tensor(out=S.Hf[:], in0=h_ps[0:64, 64:64 + CK],
                                                                   scalar=S.dC[:, cg:cg + 1], in1=S.tmpH[:],
                                                                   op0=ALU.mult, op1=ALU.add),
                     rd=(h_ps, S.dC, S.tmpH), wr=(S.Hf,))
                P.op("act", lambda: nc.scalar.copy(out=S.Hall[0:64, c + 1, :], in_=S.Hf[:]), rd=(S.Hf,), wr=(S.Hall,))
                yield
            y_ps = nb()
            for c in range(NCG):
                cs = slice(c * CK, (c + 1) * CK)
                P.op("pe", lambda: nc.tensor.matmul(y_ps[0:64, cs], lhsT=S.Hall[:, c, :], rhs=Rs(c), start=True,
                                                    stop=False), rd=(S.Hall, ops), wr=(y_ps,))
                P.op("pe", lambda: nc.tensor.matmul(y_ps[0:64, cs], lhsT=S.U[:, cs], rhs=S.Arbt[:, cs], start=False,
                                                    stop=False), rd=(S.U, S.Arbt), wr=(y_ps,))
                P.op("pe", lambda: nc.tensor.matmul(y_ps[0:64, cs], lhsT=S.Vtok[:, cs], rhs=S.Arkt[:, cs], start=False,
                                                    stop=True), rd=(S.Vtok, S.Arkt), wr=(y_ps,))
            P.op("act", lambda: nc.scalar.copy(out=S.yT[0:64, :], in_=y_ps[0:64, 0:TT]), rd=(y_ps,), wr=(S.yT,))
            P.op("act", lambda: nc.scalar.activation(out=S.sqy[0:64, :], in_=S.yT[0:64, :], func=AF.Square), rd=(S.yT,),
                 wr=(S.sqy,))
            yield
            s_ps, q_ps = nb(), nb()
            P.op("pe", lambda: nc.tensor.matmul(s_ps[0:64, 0:TT], lhsT=ones64[:], rhs=S.yT[:], start=True, stop=True),
                 rd=(ones64, S.yT), wr=(s_ps,))
            P.op("pe", lambda: nc.tensor.matmul(q_ps[0:64, 0:TT], lhsT=ones64[:], rhs=S.sqy[:], start=True, stop=True),
                 rd=(ones64, S.sqy), wr=(q_ps,))
            P.op("act", lambda: nc.scalar.activation(out=S.mean[:], in_=s_ps[0:64, 0:TT], func=AF.Copy,
                                                     scale=1.0 / 64), rd=(s_ps,), wr=(S.mean,))
            P.op("pool", lambda: nc.gpsimd.tensor_tensor(out=S.sqy[0:64, :], in0=S.mean[:], in1=S.mean[:], op=ALU.mult),
                 rd=(S.mean,), wr=(S.sqy,))
            P.op("dve", lambda: nc.vector.scalar_tensor_tensor(out=S.rstd[:], in0=q_ps[0:64, 0:TT], scalar=1.0 / 64,
                                                               in1=S.sqy[0:64, :], op0=ALU.mult, op1=ALU.subtract),
                 rd=(q_ps, S.sqy), wr=(S.rstd,))
            yield
            P.op("act", lambda: nc.scalar.activation(out=S.rstd[:], in_=S.rstd[:], func=AF.Ln, bias=64e-5, scale=1.0),
                 rd=(S.rstd,), wr=(S.rstd,))
            P.op("act", lambda: nc.scalar.activation(out=S.rstd[:], in_=S.rstd[:], func=AF.Exp, scale=-0.5),
                 rd=(S.rstd,), wr=(S.rstd,))
            P.op("pool", lambda: nc.gpsimd.tensor_tensor(out=S.yT[0:64, :], in0=S.yT[0:64, :], in1=S.mean[:], op=ALU.subtract),
                 rd=(S.yT, S.mean), wr=(S.yT,))
            yield
            P.op("dve", lambda: nc.vector.scalar_tensor_tensor(out=S.yT[0:64, :], in0=S.yT[0:64, :], scalar=lnxg[:, h:h + 1],
                                                               in1=S.rstd[:], op0=ALU.mult, op1=ALU.mult),
                 rd=(S.yT, lnxg, S.rstd), wr=(S.yT,))
            P.op("dve", lambda: nc.vector.tensor_tensor(out=S.yT[0:64, :], in0=S.yT[0:64, :], in1=ops[0:64, 5, :], op=ALU.mult),
                 rd=(S.yT, ops), wr=(S.yT,))
            P.op("pool", lambda: nc.gpsimd.tensor_tensor(out=S.yo[:], in0=S.yT[0:64, :], in1=ops[0:64, 6, :], op=ALU.add),
                 rd=(S.yT, ops), wr=(S.yo,))
            P.dma("sp", yab, S.yo, yab_ap[(h % 2) * 64:(h % 2) * 64 + 64, h // 2, g0:g0 + TT], S.yo[:])
            yield

    for h0 in range(0, NH, NHC):
        gens = [head_prog(slots[i], h0 + i) for i in range(NHC)]
        while gens:
            for gn in list(gens):
                try:
                    next(gn)
                except StopIteration:
                    gens.remove(gn)
    C.close()


def mixer_ab_phase(P, T, xin, xout, xin_ap, xout_ap, A, gpre_ap, gpost_ap, scr):
    rwkv_prep_stage(P, T, xin, xin_ap, gpre_ap, A, scr)
    rwkv_scan_stage(P, T, A, scr)
    outproj_stage(P, T, xin, xout, xin_ap, xout_ap, scr["oT"][0], scr["oT"][1], A["w_out"], gpost_ap)


def build(T, phases):
    nc = bass.Bass("TRN2", target_bir_lowering=False)
    t = {}
    t["x"] = nc.dram_tensor("x", [T, D], F32, kind="ExternalInput").ap()
    t["norm_gains"] = nc.dram_tensor("norm_gains", [4, 6, D], F32, kind="ExternalInput").ap()
    t["ffn_w_gate"] = nc.dram_tensor("ffn_w_gate", [4, 2, D, DFF], F32, kind="ExternalInput").ap()
    t["ffn_w_up"] = nc.dram_tensor("ffn_w_up", [4, 2, D, DFF], F32, kind="ExternalInput").ap()
    t["ffn_w_down"] = nc.dram_tensor("ffn_w_down", [4, 2, DFF, D], F32, kind="ExternalInput").ap()
    t["c_w_in"] = nc.dram_tensor("c_w_in", [2, D, 4 * D + 16], F32, kind="ExternalInput").ap()
    t["c_forget_bias"] = nc.dram_tensor("c_forget_bias", [2, 16], F32, kind="ExternalInput").ap()
    t["c_q_norm_g"] = nc.dram_tensor("c_q_norm_g", [2, 64], F32, kind="ExternalInput").ap()
    t["c_k_norm_g"] = nc.dram_tensor("c_k_norm_g", [2, 64], F32, kind="ExternalInput").ap()
    t["c_w_out"] = nc.dram_tensor("c_w_out", [2, D, D], F32, kind="ExternalInput").ap()
    for nm, shp in (("a_w_in", [2, D, 2816]), ("a_shift_mu", [2, 1792]), ("a_decay_up", [2, 64, 512]),
                    ("a_decay_base", [2, 512]), ("a_iclr_up", [2, 64, 512]), ("a_iclr_base", [2, 512]),
                    ("a_gate_up", [2, 128, 512]), ("a_k_k", [2, 512]), ("a_k_a", [2, 512]), ("a_r_k", [2, 8, 64]),
                    ("a_lnx_g", [2, 512]), ("a_lnx_b", [2, 512]), ("b_conv_w", [2, 31, 512]), ("b_conv_b", [2, 512]),
                    ("b_ln_g", [2, 512]), ("b_ln_b", [2, 512]), ("ab_w_out", [2, D, D])):
        t[nm] = nc.dram_tensor(nm, shp, F32, kind="ExternalInput").ap()
    y = nc.dram_tensor("y", [T, D], F32, kind="ExternalOutput").ap()
    rwd = nc.dram_tensor("rwd", [512, 7, T], BF16).ap()
    dcd = nc.dram_tensor("dcd", [512, max(T // 64, 1)], F32).ap()
    hTd = nc.dram_tensor("hTd", [128, NCH, T], BF16).ap()
    oTd = nc.dram_tensor("oTd", [128, NCH, T], BF16).ap()
    cumd = nc.dram_tensor("cumd", [16, T], F32).ap()
    xa = nc.dram_tensor("xa", [T, D], F32).ap()
    xb = nc.dram_tensor("xb", [T, D], F32).ap()
    P = Prog(nc)
    with nc.allow_low_precision("bf16 matmul operands, fp32 accumulation"):
        P.clear_all()
        nc.all_engine_barrier()
        bx = Buf(P, None, "x_in")
        cur_ap, cur = t["x"], bx
        pp = [(xa, Buf(P, None, "xa", dma=True)), (xb, Buf(P, None, "xb", dma=True))]
        yb = Buf(P, None, "y", dma=True)
        scr = {"hT": (hTd, Buf(P, None, "hTd", dma=True)), "oT": (oTd, Buf(P, None, "oTd", dma=True)),
               "cum": (cumd, Buf(P, None, "cumd", dma=True)), "rw": (rwd, Buf(P, None, "rwd", dma=True)),
               "dc": (dcd, Buf(P, None, "dcd", dma=True))}
        for i, ph in enumerate(phases):
            last = (i == len(phases) - 1)
            dst_ap, dst = (y, yb) if last else pp[i % 2]
            if ph[0] == "ffn":
                l, s = ph[1], ph[2]
                ffn_phase(P, T, cur, dst, cur_ap, dst_ap, t["ffn_w_gate"][l, s], t["ffn_w_up"][l, s],
                          t["ffn_w_down"][l, s], t["norm_gains"][l, 3 * s if s == 0 else 4],
                          t["norm_gains"][l, 1 if s == 0 else 5])
            elif ph[0] == "ab":
                l = ph[1]
                i2 = l // 2
                A = {"w_in": t["a_w_in"][i2], "shift_mu": t["a_shift_mu"][i2], "decay_up": t["a_decay_up"][i2],
                     "decay_base": t["a_decay_base"][i2], "iclr_up": t["a_iclr_up"][i2],
                     "iclr_base": t["a_iclr_base"][i2], "gate_up": t["a_gate_up"][i2], "k_k": t["a_k_k"][i2],
                     "k_a": t["a_k_a"][i2], "r_k": t["a_r_k"][i2], "lnx_g": t["a_lnx_g"][i2],
                     "lnx_b": t["a_lnx_b"][i2], "conv_w": t["b_conv_w"][i2], "conv_b": t["b_conv_b"][i2],
                     "ln_g": t["b_ln_g"][i2], "ln_b": t["b_ln_b"][i2], "w_out": t["ab_w_out"][i2]}
                mixer_ab_phase(P, T, cur, dst, cur_ap, dst_ap, A, t["norm_gains"][l, 2], t["norm_gains"][l, 3], scr)
            elif ph[0] in ("ab1", "ab2", "ab3"):
                i2 = 0
                A = {"w_in": t["a_w_in"][i2], "shift_mu": t["a_shift_mu"][i2], "decay_up": t["a_decay_up"][i2],
                     "decay_base": t["a_decay_base"][i2], "iclr_up": t["a_iclr_up"][i2],
                     "iclr_base": t["a_iclr_base"][i2], "gate_up": t["a_gate_up"][i2], "k_k": t["a_k_k"][i2],
                     "k_a": t["a_k_a"][i2], "r_k": t["a_r_k"][i2], "lnx_g": t["a_lnx_g"][i2],
                     "lnx_b": t["a_lnx_b"][i2], "conv_w": t["b_conv_w"][i2], "conv_b": t["b_conv_b"][i2],
                     "ln_g": t["b_ln_g"][i2], "ln_b": t["b_ln_b"][i2], "w_out": t["ab_w_out"][i2]}
                if ph[0] == "ab1":
                    rwkv_prep_stage(P, T, cur, cur_ap, t["norm_gains"][0, 2], A, scr)
                elif ph[0] == "ab2":
                    rwkv_scan_stage(P, T, A, scr)
                else:
                    outproj_stage(P, T, cur, dst, cur_ap, dst_ap, scr["oT"][0], scr["oT"][1], A["w_out"],
                                  t["norm_gains"][0, 3])
            elif ph[0] == "fox":
                l = ph[1]
                i2 = l // 2
                fox_phase(P, T, cur, dst, cur_ap, dst_ap, t["c_w_in"][i2], t["c_forget_bias"][i2],
                          t["c_q_norm_g"][i2], t["c_k_norm_g"][i2], t["c_w_out"][i2], t["norm_gains"][l, 2],
                          t["norm_gains"][l, 3], scr)
            cur_ap, cur = dst_ap, dst
        P.wait_all_dma("sp")
        nc.all_engine_barrier()
        P.clear_all()
    P.stack.close()
    print("instructions:", P.n_ins, "waits:", P.n_wait, "dsems:", P.ndsem)
    return nc


SEQ = 4096
N_CORES = 8
IN_NAMES = ("norm_gains", "ffn_w_gate", "ffn_w_up", "ffn_w_down", "a_w_in", "a_shift_mu", "a_decay_up",
            "a_decay_base", "a_iclr_up", "a_iclr_base", "a_gate_up", "a_k_k", "a_k_a", "a_r_k", "a_lnx_g", "a_lnx_b",
            "b_conv_w", "b_conv_b", "b_ln_g", "b_ln_b", "ab_w_out", "c_w_in", "c_forget_bias", "c_q_norm_g",
            "c_k_norm_g", "c_w_out")


def all_phases(depth=4):
    ph = []
    for l in range(depth):
        ph.append(("ffn", l, 0))
        ph.append(("ab", l) if l % 2 == 0 else ("fox", l))
        ph.append(("ffn", l, 1))
    return ph


_NC_CACHE = {}


def kernel(**inputs):
    x = np.ascontiguousarray(np.asarray(inputs["x"], dtype=np.float32))
    B, T, _ = x.shape
    key = (T,)
    if key not in _NC_CACHE:
        _NC_CACHE[key] = build(T, all_phases())
    nc = _NC_CACHE[key]
    shared = {k: np.ascontiguousarray(np.asarray(inputs[k], dtype=np.float32)) for k in IN_NAMES}
    in_maps = []
    for b in range(B):
        m = dict(shared)
        m["x"] = x[b]
        in_maps.append(m)
    res = run_bass_kernel_spmd(nc, in_maps, core_ids=list(range(B)))
    return np.stack([np.asarray(r["y"], dtype=np.float32) for r in res.results], axis=0)
```

```python
import contextlib
import numpy as np
import concourse.bass as bass
import concourse.mybir as mybir
from concourse.bass_utils import run_bass_kernel_spmd

F32 = mybir.dt.float32
BF16 = mybir.dt.bfloat16
AF = mybir.ActivationFunctionType
ALU = mybir.AluOpType
AX = mybir.AxisListType

D = 1024
DFF = 2816
NCH = D // 128
NFF = DFF // 128
EPS = 1e-6


class Buf:
    def __init__(self, prog, t, name, dma=False):
        self.t = t
        self.name = name
        self.lw = None
        self.rd = {}
        self.dsem = None
        self.psum = False
        self.dkind = None
        if dma:
            self.dkind = "sw" if dma == "sw" else "hw"
            self.dsem = prog.get_dsem(self.dkind)

    def __getitem__(self, k):
        return self.t[k]


class Prog:
    ENG = ("pe", "act", "dve", "pool", "sp")

    def __init__(self, nc):
        self.nc = nc
        self.e = {"pe": nc.tensor, "act": nc.scalar, "dve": nc.vector, "pool": nc.gpsimd, "sp": nc.sync}
        self.stack = contextlib.ExitStack()
        self.sems = {}
        self.val = {}
        self.seen = {e: {} for e in self.ENG}
        for e in self.ENG:
            self.sems[e] = self.stack.enter_context(nc.semaphore("c_" + e))
            self.val[e] = 0
        self.dpool = {"hw": [], "sw": []}
        self.ndsem = 0
        self.live_dsems = set()
        self.n_ins = 0
        self.n_wait = 0

    def get_dsem(self, kind):
        if self.dpool[kind]:
            k = self.dpool[kind].pop()
        else:
            k = "d%s%d" % (kind, self.ndsem)
            self.ndsem += 1
            self.sems[k] = self.stack.enter_context(self.nc.semaphore(k))
            self.val[k] = 0
        self.live_dsems.add(k)
        return k

    def release(self, bufs):
        for b in bufs:
            if b.dsem is not None:
                self.live_dsems.discard(b.dsem)
                self.dpool[b.dkind].append(b.dsem)
                b.dsem = None

    def clear_all(self):
        for k, h in self.sems.items():
            self.nc.gpsimd.sem_clear(h)

    def _need(self, e, waits, tok):
        if tok is None:
            return
        k, v = tok[0], tok[1]
        if self.seen[e].get(k, 0) >= v:
            return
        if waits.get(k, 0) < v:
            waits[k] = v

    def _deps(self, e, rd, wr):
        waits = {}
        for b in rd:
            self._need(e, waits, b.lw)
            if b.psum:
                for k, (v, re_) in b.rd.items():
                    if re_ != e:
                        self._need(e, waits, (k, v))
        for b in wr:
            if b.lw is not None and not (e == "pe" and b.lw[2] == "pe"):
                self._need(e, waits, b.lw)
            for k, (v, re_) in b.rd.items():
                self._need(e, waits, (k, v))
        for k, v in waits.items():
            self.e[e].wait_ge(self.sems[k], v)
            self.seen[e][k] = v
            self.n_wait += 1

    def op(self, e, fn, rd=(), wr=()):
        self._deps(e, rd, wr)
        ins = fn()
        self.val[e] += 1
        ins.then_inc(self.sems[e], 1)
        self.n_ins += 1
        v = self.val[e]
        for b in rd:
            b.rd[e] = (v, e)
        for b in wr:
            b.lw = (e, v, e)
            b.rd = {}
        return ins

    def dma(self, e, dst, src, out_ap, in_ap, **kw):
        assert dst.dsem is not None, dst.name
        assert (dst.dkind == "sw") == (e == "pool"), (dst.name, e)
        self._deps(e, (src,), (dst,))
        ins = self.e[e].dma_start(out=out_ap, in_=in_ap, **kw)
        k = dst.dsem
        self.val[k] += 16
        ins.then_inc(self.sems[k], 16)
        self.n_ins += 1
        v = self.val[k]
        src.rd[k] = (v, "dma")
        dst.lw = (k, v, "dma")
        dst.rd = {}
        return ins

    def barrier(self):
        for k in list(self.live_dsems):
            v = self.val[k]
            if v > 0 and self.seen["sp"].get(k, 0) < v:
                self.e["sp"].wait_ge(self.sems[k], v)
                self.seen["sp"][k] = v
        self.nc.all_engine_barrier()
        for e in self.ENG:
            for k in self.sems:
                self.seen[e][k] = self.val[k]

    def wait_all_dma(self, e):
        for k in list(self.live_dsems):
            v = self.val[k]
            if v > 0 and self.seen[e].get(k, 0) < v:
                self.e[e].wait_ge(self.sems[k], v)
                self.seen[e][k] = v


class Ctx:
    def __init__(self, P):
        self.P = P
        self.nc = P.nc
        self.stack = contextlib.ExitStack()
        self.bufs = []

    def sb(self, name, shape, dt, dma=False):
        t = self.stack.enter_context(self.nc.sbuf_tensor(name, list(shape), dt))
        b = Buf(self.P, t, name, dma=dma)
        self.bufs.append(b)
        return b

    def ps(self, name, shape=(128, 512), dt=F32):
        t = self.stack.enter_context(self.nc.psum_tensor(name, list(shape), dt))
        b = Buf(self.P, t, name)
        b.psum = True
        self.bufs.append(b)
        return b

    def close(self):
        self.P.barrier()
        self.P.release(self.bufs)
        self.stack.close()


_uid = [0]


def uname(s):
    _uid[0] += 1
    return "%s_%d" % (s, _uid[0])


def make_consts(P, C):
    nc = P.nc
    idf = C.sb(uname("ident_f"), (128, 128), F32)
    idb = C.sb(uname("ident_b"), (128, 128), BF16)
    P.op("pool", lambda: nc.gpsimd.memset(idf[:], 0.0), wr=(idf,))
    P.op("pool", lambda: nc.gpsimd.affine_select(out=idf[:], in_=idf[:], compare_op=ALU.not_equal, fill=1.0,
                                                  base=0, pattern=[[-1, 128]], channel_multiplier=1),
         rd=(idf,), wr=(idf,))
    P.op("pool", lambda: nc.gpsimd.tensor_copy(out=idb[:], in_=idf[:]), rd=(idf,), wr=(idb,))
    return idf, idb


def load_gain_bc(P, C, name, src_ap, scale):
    nc = P.nc
    g = C.sb(name, (128, D), F32, dma=True)
    dsrc = Buf(P, None, name + "_src")
    P.dma("sp", g, dsrc, g[:], src_ap.partition_broadcast(128))
    if scale != 1.0:
        P.op("pool", lambda: nc.gpsimd.tensor_scalar(out=g[:], in0=g[:], scalar1=float(scale), scalar2=None,
                                                      op0=ALU.mult), rd=(g,), wr=(g,))
    return g


def prenorm_tile(P, nc, xs, gpre, hT, col0, idb, sq_junk, stat, hb, tp_ps):
    P.op("act", lambda: nc.scalar.activation(out=sq_junk[:], in_=xs[:], func=AF.Square, accum_out=stat[:, 0:1]),
         rd=(xs,), wr=(sq_junk, stat))
    P.op("act", lambda: nc.scalar.activation(out=stat[:, 6:7], in_=stat[:, 0:1], func=AF.Sqrt, bias=float(EPS),
                                             scale=1.0 / D), rd=(stat,), wr=(stat,))
    P.op("dve", lambda: nc.vector.reciprocal(out=stat[:, 1:2], in_=stat[:, 6:7]), rd=(stat,), wr=(stat,))
    P.op("dve", lambda: nc.vector.scalar_tensor_tensor(out=hb[:], in0=xs[:], scalar=stat[:, 1:2], in1=gpre[:],
                                                       op0=ALU.mult, op1=ALU.mult), rd=(xs, stat, gpre), wr=(hb,))
    for c in range(NCH):
        P.op("pe", lambda c=c: nc.tensor.transpose(out=tp_ps[:, c * 128:(c + 1) * 128],
                                                   in_=hb[:, c * 128:(c + 1) * 128], identity=idb[:]),
             rd=(hb, idb), wr=(tp_ps,))
    P.op("act", lambda: nc.scalar.copy(out=hT[:, :, col0:col0 + 128],
                                       in_=tp_ps[:].rearrange("p (c t) -> p c t", c=NCH)),
         rd=(tp_ps,), wr=(hT,))


def postnorm_tile(P, nc, f_ps, xs, gpost, xo, sq_junk, stat, tmp):
    P.op("act", lambda: nc.scalar.activation(out=sq_junk[:, 0:512], in_=f_ps[0][:], func=AF.Square,
                                             accum_out=stat[:, 2:3]), rd=(f_ps[0],), wr=(sq_junk, stat))
    P.op("act", lambda: nc.scalar.activation(out=sq_junk[:, 512:1024], in_=f_ps[1][:], func=AF.Square,
                                             accum_out=stat[:, 3:4]), rd=(f_ps[1],), wr=(sq_junk, stat))
    P.op("dve", lambda: nc.vector.tensor_tensor(out=stat[:, 4:5], in0=stat[:, 2:3], in1=stat[:, 3:4], op=ALU.add),
         rd=(stat,), wr=(stat,))
    P.op("act", lambda: nc.scalar.activation(out=stat[:, 7:8], in_=stat[:, 4:5], func=AF.Sqrt, bias=float(EPS),
                                             scale=1.0 / D), rd=(stat,), wr=(stat,))
    P.op("dve", lambda: nc.vector.reciprocal(out=stat[:, 5:6], in_=stat[:, 7:8]), rd=(stat,), wr=(stat,))
    for h in range(2):
        P.op("dve", lambda h=h: nc.vector.scalar_tensor_tensor(out=tmp[:, h * 512:(h + 1) * 512], in0=f_ps[h][:],
                                                               scalar=stat[:, 5:6],
                                                               in1=gpost[:, h * 512:(h + 1) * 512],
                                                               op0=ALU.mult, op1=ALU.mult),
             rd=(f_ps[h], stat, gpost), wr=(tmp,))
    P.op("pool", lambda: nc.gpsimd.tensor_tensor(out=xo[:], in0=tmp[:], in1=xs[:], op=ALU.add),
         rd=(tmp, xs), wr=(xo,))


def load_w(P, wb, out_ap, src_ap):
    P.dma("pool", wb, Buf(P, None, "wsrc"), out_ap, src_ap)


def ffn_phase(P, T, xin, xout, xin_ap, xout_ap, wg_ap, wu_ap, wd_ap, gpre_ap, gpost_ap):
    nc = P.nc
    C = Ctx(P)
    idf, idb = make_consts(P, C)
    gpre = load_gain_bc(P, C, uname("gpre"), gpre_ap, 1.0)
    gpost = load_gain_bc(P, C, uname("gpost"), gpost_ap, 0.5)
    NB = 11
    CB = DFF // NB
    JB = NFF // NB
    wg = [C.sb(uname("wg"), (128, NCH, CB), BF16, dma="sw") for _ in range(NB)]
    wu = [C.sb(uname("wu"), (128, NCH, CB), BF16, dma="sw") for _ in range(NB)]
    wd = [C.sb(uname("wd"), (128, 2, D), BF16, dma="sw") for _ in range(NFF // 2)]
    wg_v = wg_ap.rearrange("(c p) n -> p c n", p=128)
    wu_v = wu_ap.rearrange("(c p) n -> p c n", p=128)
    wd_v = wd_ap.rearrange("(j p) n -> p j n", p=128)
    for b in range(NB):
        for (wl, wv) in ((wg, wg_v), (wu, wu_v)):
            load_w(P, wl[b], wl[b][:, :, :], wv[:, :, b * CB:(b + 1) * CB])
        if b >= 5:
            for j2 in (2 * (b - 5), 2 * (b - 5) + 1):
                if j2 < NFF // 2:
                    load_w(P, wd[j2], wd[j2][:, :, :], wd_v[:, 2 * j2:2 * j2 + 2, :])

    TT = 512 if T >= 512 else T
    NS = TT // 128
    xs = [C.sb(uname("xs"), (128, D), F32, dma=True) for _ in range(2)]
    xp = [C.sb(uname("xp"), (128, D), F32, dma=True) for _ in range(2)]
    hT = [C.sb(uname("hT"), (128, NCH, TT), BF16) for _ in range(2)]
    actT = C.sb(uname("actT"), (128, NFF, TT), BF16)
    sq_junk = C.sb(uname("sqj"), (128, D), BF16)
    hb = [C.sb(uname("hb"), (128, D), BF16) for _ in range(2)]
    tmp = C.sb(uname("tmp"), (128, D), F32)
    sg = [C.sb(uname("sg"), (128, TT), BF16) for _ in range(2)]
    stats = [C.sb(uname("stat"), (128, 8), F32) for _ in range(4)]
    tp_ps = C.ps(uname("tp"), (128, D), BF16)
    g_ps = [C.ps(uname("gps")) for _ in range(2)]
    u_ps = [C.ps(uname("ups")) for _ in range(2)]
    f3 = [C.ps(uname("fps")) for _ in range(3)]
    xin_v = xin_ap.rearrange("(n p) d -> n p d", p=128)
    xout_v = xout_ap.rearrange("(n p) d -> n p d", p=128)
    nt = T // TT

    def pre(ti, s):
        xst = xs[s % 2]
        P.dma("sp", xst, xin, xst[:], xin_v[ti * NS + s])
        prenorm_tile(P, nc, xst, gpre, hT[ti % 2], s * 128, idb, sq_junk, stats[s % 4], hb[s % 2], tp_ps)

    for s in range(NS):
        pre(0, s)
    it = 0
    for ti in range(nt):
        hTt = hT[ti % 2]
        for j in range(NFF):
            gp, up = g_ps[j % 2], u_ps[j % 2]
            bb, a0 = j // JB, (j % JB) * 128
            for (wl, pp) in ((wg, gp), (wu, up)):
                for c in range(NCH):
                    P.op("pe", lambda: nc.tensor.matmul(pp[:, 0:TT], lhsT=wl[bb][:, c, a0:a0 + 128],
                                                        rhs=hTt[:, c, :], start=(c == 0), stop=(c == NCH - 1)),
                         rd=(wl[bb], hTt), wr=(pp,))
            sgj = sg[j % 2]
            P.op("act", lambda: nc.scalar.activation(out=sgj[:], in_=gp[:, 0:TT], func=AF.Silu),
                 rd=(gp,), wr=(sgj,))
            P.op("dve", lambda: nc.vector.tensor_tensor(out=actT[:, j, :], in0=up[:, 0:TT], in1=sgj[:],
                                                        op=ALU.mult), rd=(up, sgj), wr=(actT,))
            if ti + 1 < nt and j in (4, 8, 12, 16) and (j // 4 - 1) < NS:
                pre(ti + 1, j // 4 - 1)
        for s in range(NS):
            xst = xp[it % 2]
            f_ps = [f3[(2 * it) % 3], f3[(2 * it + 1) % 3]]
            P.dma("sp", xst, xin, xst[:], xin_v[ti * NS + s])
            for h in range(2):
                for j in range(NFF):
                    P.op("pe", lambda: nc.tensor.matmul(
                        f_ps[h][:, :], lhsT=actT[:, j, s * 128:(s + 1) * 128],
                        rhs=wd[j // 2][:, j % 2, h * 512:(h + 1) * 512],
                        start=(j == 0), stop=(j == NFF - 1)), rd=(actT, wd[j // 2]), wr=(f_ps[h],))
            it += 1
            postnorm_tile(P, nc, f_ps, xst, gpost, xst, sq_junk, stats[s % 4], tmp)
            P.dma("sp", xout, xst, xout_v[ti * NS + s], xst[:])
    C.close()


def fox_phase(P, T, xin, xout, xin_ap, xout_ap, win_ap, fb_ap, qg_ap, kg_ap, wout_ap, gpre_ap, gpost_ap, scr):
    nc = P.nc
    H = 16
    NT = T // 512 if T >= 512 else 1
    TT = 512 if T >= 512 else T
    NSUB = T // 128
    hTd_ap, hTd = scr["hT"]
    oTd_ap, oTd = scr["oT"]
    win_v = win_ap.rearrange("(c p) n -> p c n", p=128)
    xin_v = xin_ap.rearrange("(n p) d -> n p d", p=128)
    xout_v = xout_ap.rearrange("(n p) d -> n p d", p=128)
    dsrc = Buf(P, None, "dsrc")

    C = Ctx(P)
    idf, idb = make_consts(P, C)
    gpre = load_gain_bc(P, C, uname("gpre"), gpre_ap, 1.0)
    wf = C.sb(uname("wf"), (128, NCH, H), BF16, dma="sw")
    load_w(P, wf, wf[:], win_v[:, :, 4 * D:4 * D + H])
    nfb = C.sb(uname("nfb"), (H, 1), F32, dma=True)
    P.dma("sp", nfb, dsrc, nfb[:], fb_ap.rearrange("(h o) -> h o", o=1))
    P.op("dve", lambda: nc.vector.tensor_scalar(out=nfb[:], in0=nfb[:], scalar1=-1.0, scalar2=None, op0=ALU.mult),
         rd=(nfb,), wr=(nfb,))
    cum = C.sb(uname("cum"), (H, T), F32)
    ones16 = C.sb(uname("ones16"), (H, TT), F32)
    P.op("pool", lambda: nc.gpsimd.memset(ones16[:], 1.0), wr=(ones16,))
    lfe = [C.sb(uname("lfe"), (H, TT), F32) for _ in range(2)]
    xs = [C.sb(uname("xs"), (128, D), F32, dma=True) for _ in range(2)]
    hTt = [C.sb(uname("hTt"), (128, NCH, TT), BF16) for _ in range(2)]
    sq_junk = C.sb(uname("sqj"), (128, D), BF16)
    hb = [C.sb(uname("hb"), (128, D), BF16) for _ in range(2)]
    stats = [C.sb(uname("stat"), (128, 8), F32) for _ in range(4)]
    tp_ps = C.ps(uname("tp"), (128, D), BF16)
    f_ps = [C.ps(uname("fl")) for _ in range(2)]
    NS = TT // 128
    for ti in range(NT):
        ht = hTt[ti % 2]
        for s_ in range(NS):
            xst = xs[s_ % 2]
            P.dma("sp", xst, xin, xst[:], xin_v[ti * NS + s_])
            prenorm_tile(P, nc, xst, gpre, ht, s_ * 128, idb, sq_junk, stats[s_ % 4], hb[s_ % 2], tp_ps)
        P.dma("sp", hTd, ht, hTd_ap[:, :, ti * TT:(ti + 1) * TT], ht[:])
        fp = f_ps[ti % 2]
        for c in range(NCH):
            P.op("pe", lambda: nc.tensor.matmul(fp[0:H, 0:TT], lhsT=wf[:, c, :], rhs=ht[:, c, :],
                                                start=(c == 0), stop=(c == NCH - 1)), rd=(wf, ht), wr=(fp,))
        le = lfe[ti % 2]
        P.op("act", lambda: nc.scalar.activation(out=le[:], in_=fp[0:H, 0:TT], func=AF.Exp, bias=nfb[:, 0:1],
                                                 scale=-1.0), rd=(fp, nfb), wr=(le,))
        P.op("act", lambda: nc.scalar.activation(out=le[:], in_=le[:], func=AF.Ln, bias=1.0, scale=1.0),
             rd=(le,), wr=(le,))
        init = 0.0 if ti == 0 else cum[:, ti * TT - 1:ti * TT]
        P.op("dve", lambda: nc.vector.tensor_tensor_scan(out=cum[:, ti * TT:(ti + 1) * TT], data0=ones16[:],
                                                         data1=le[:], initial=init, op0=ALU.mult,
                                                         op1=ALU.subtract), rd=(ones16, le, cum), wr=(cum,))
    cumd_ap, cumd = scr["cum"]
    P.dma("sp", cumd, cum, cumd_ap[:, 0:T], cum[:])
    C.close()

    C = Ctx(P)
    idf, idb = make_consts(P, C)
    cum = C.sb(uname("cum"), (H, T), F32, dma=True)
    P.dma("sp", cum, cumd, cum[:], cumd_ap[:, 0:T])
    sel = C.sb(uname("sel"), (H, H, 128), F32)
    P.op("pool", lambda: nc.gpsimd.memset(sel[:], 0.0), wr=(sel,))
    P.op("pool", lambda: nc.gpsimd.affine_select(out=sel[:], in_=sel[:], compare_op=ALU.not_equal, fill=1.0, base=0,
                                                  pattern=[[-1, H], [0, 128]], channel_multiplier=1),
         rd=(sel,), wr=(sel,))
    swp = C.sb(uname("swp"), (128, 128), F32)
    P.op("pool", lambda: nc.gpsimd.memset(swp[:], 0.0), wr=(swp,))
    P.op("pool", lambda: nc.gpsimd.affine_select(out=swp[:, 0:64], in_=swp[:, 0:64], compare_op=ALU.not_equal,
                                                  fill=1.0, base=-64, pattern=[[-1, 64]], channel_multiplier=1),
         rd=(swp,), wr=(swp,))
    P.op("pool", lambda: nc.gpsimd.affine_select(out=swp[:, 64:128], in_=swp[:, 64:128], compare_op=ALU.not_equal,
                                                  fill=1.0, base=0, pattern=[[-1, 64]], channel_multiplier=1),
         rd=(swp,), wr=(swp,))
    bones = C.sb(uname("bones"), (128, 128), F32)
    P.op("pool", lambda: nc.gpsimd.memset(bones[:], 0.0), wr=(bones,))
    P.op("pool", lambda: nc.gpsimd.memset(bones[0:64, 0:64], 1.0), wr=(bones,))
    P.op("pool", lambda: nc.gpsimd.memset(bones[64:128, 64:128], 1.0), wr=(bones,))
    tri = C.sb(uname("tri"), (128, 128), F32)
    P.op("pool", lambda: nc.gpsimd.memset(tri[:], 0.0), wr=(tri,))
    P.op("pool", lambda: nc.gpsimd.affine_select(out=tri[:], in_=tri[:], compare_op=ALU.is_ge, fill=-30000.0, base=0,
                                                  pattern=[[1, 128]], channel_multiplier=-1), rd=(tri,), wr=(tri,))
    ncumT = C.sb(uname("ncumT"), (128, NSUB, H), F32)
    ct_ps = C.ps(uname("ctps"))
    for g0 in range(0, NSUB, 32):
        gn = min(32, NSUB - g0)
        for b in range(gn):
            P.op("pe", lambda: nc.tensor.transpose(out=ct_ps[:, b * H:(b + 1) * H],
                                                   in_=cum[:, (g0 + b) * 128:(g0 + b + 1) * 128],
                                                   identity=idf[0:H, 0:H]), rd=(cum, idf), wr=(ct_ps,))
        P.op("dve", lambda: nc.vector.tensor_scalar(out=ncumT[:, g0:g0 + gn, :],
                                                    in0=ct_ps[:, 0:gn * H].rearrange("p (b h) -> p b h", h=H),
                                                    scalar1=-1.0, scalar2=None, op0=ALU.mult),
             rd=(ct_ps,), wr=(ncumT,))
    gq2 = C.sb(uname("gq2"), (128, 1), F32, dma=True)
    gk2 = C.sb(uname("gk2"), (128, 1), F32, dma=True)
    for hh in range(2):
        P.dma("sp", gq2, dsrc, gq2[hh * 64:(hh + 1) * 64, :], qg_ap.rearrange("(d o) -> d o", o=1))
        P.dma("sp", gk2, dsrc, gk2[hh * 64:(hh + 1) * 64, :], kg_ap.rearrange("(d o) -> d o", o=1))
    P.op("dve", lambda: nc.vector.tensor_scalar(out=gq2[:], in0=gq2[:], scalar1=0.125, scalar2=None, op0=ALU.mult),
         rd=(gq2,), wr=(gq2,))
    wq = [C.sb(uname("wq"), (128, NCH, 4, 128), BF16, dma="sw") for _ in range(2)]
    hTt = [C.sb(uname("hTt"), (128, NCH, TT), BF16, dma=True) for _ in range(2)]
    qT = C.sb(uname("qT"), (128, T), BF16)
    kT = "kT"
    kTz = [C.sb(uname("kTz"), (128, T), BF16) for _ in range(2)]
    for hh in range(2):
        P.op("pool", lambda: nc.gpsimd.memset(kTz[hh][:], 0.0), wr=(kTz[hh],))
    sgT = C.sb(uname("sgT"), (128, T), BF16)
    oT = C.sb(uname("oT"), (128, T), BF16)
    Va = [C.sb(uname("Va"), (128, NSUB, 128), BF16) for _ in range(2)]
    P.op("pool", lambda: nc.gpsimd.memset(Va[0][:, :, 64:128], 1.0), wr=(Va[0],))
    P.op("pool", lambda: nc.gpsimd.memset(Va[1][:, :, 0:64], 1.0), wr=(Va[1],))
    cbc = C.sb(uname("cbc"), (128, T), F32)
    sqf = [C.sb(uname("sqf"), (128, TT), F32) for _ in range(2)]
    rsf = [C.sb(uname("rsf"), (128, TT), F32) for _ in range(2)]
    tmpb = [C.sb(uname("tmpb"), (128, TT), F32) for _ in range(4)]
    osb = C.sb(uname("osb"), (128, TT), F32)
    rden = C.sb(uname("rden"), (128, TT), F32)
    onum = C.sb(uname("onum"), (128, TT), F32)
    st2 = [C.ps(uname("st2"), (128, 2 * TT), F32) for _ in range(2)]

    class HalfView:
        def __init__(self, t, off):
            self.t, self.off = t, off

        def __getitem__(self, k):
            p_, c_ = k
            return self.t[p_, slice(self.off + (c_.start or 0), self.off + c_.stop)]
    sth = []
    for k_ in range(2):
        for hf_ in range(2):
            b_ = Buf(P, HalfView(st2[k_].t, hf_ * TT), uname("sth"))
            b_.psum = True
            C.bufs.append(b_)
            sth.append(b_)
    acc = [C.ps(uname("acc")) for _ in range(3)] + [ct_ps]
    bank = [st2[0], st2[1]] + acc
    pT2 = [C.sb(uname("pT2"), (128, 2 * TT), BF16) for _ in range(4)]
    ecr = [C.sb(uname("ecr"), (128, TT), F32) for _ in range(2)]
    dsw = C.sb(uname("dsw"), (128, TT), F32, dma=True)
    cbcm = C.sb(uname("cbcm"), (128, T), F32)
    tri4 = C.sb(uname("tri4"), (128, TT), F32)
    for r_ in range(TT // 128):
        P.op("pool", lambda: nc.gpsimd.tensor_copy(out=tri4[:, r_ * 128:(r_ + 1) * 128], in_=tri[:]), rd=(tri,),
             wr=(tri4,))
    ncb0 = C.sb(uname("ncb0"), (128, NT), F32)
    biasall = C.sb(uname("biasall"), (128, NT, NSUB), F32)
    def load_pair_w(hp_):
        w_ = wq[hp_ % 2]
        for qi in range(4):
            load_w(P, w_, w_[:, :, qi, :], win_v[:, :, qi * D + hp_ * 128: qi * D + (hp_ + 1) * 128])

    load_pair_w(0)
    for hp in range(H // 2):
        w = wq[hp % 2]
        if hp + 1 < H // 2:
            load_pair_w(hp + 1)
        for ti in range(NT):
            ht = hTt[ti % 2]
            P.dma("sp", ht, hTd, ht[:], hTd_ap[:, :, ti * TT:(ti + 1) * TT])
            tsl = slice(ti * TT, (ti + 1) * TT)
            pset = ti % 2
            qb_, kb_ = sth[2 * pset], sth[2 * pset + 1]
            q_ps, k_ps = qb_[:, 0:TT], kb_[:, 0:TT]
            g_ps, v_ps = acc[2 * pset], acc[2 * pset + 1]
            ss_ps = v_ps
            for (qi, pp, pb_) in ((0, q_ps, qb_), (1, k_ps, kb_), (3, g_ps[:, 0:TT], g_ps)):
                for c in range(NCH):
                    P.op("pe", lambda: nc.tensor.matmul(pp, lhsT=w[:, c, qi, :], rhs=ht[:, c, :],
                                                        start=(c == 0), stop=(c == NCH - 1)), rd=(w, ht), wr=(pb_,))
            for s_ in range(NS):
                for c in range(NCH):
                    P.op("pe", lambda: nc.tensor.matmul(v_ps[:, s_ * 128:(s_ + 1) * 128],
                                                        lhsT=ht[:, c, s_ * 128:(s_ + 1) * 128], rhs=w[:, c, 2, :],
                                                        start=(c == 0), stop=(c == NCH - 1)), rd=(w, ht), wr=(v_ps,))
            vv = v_ps[:, 0:TT].rearrange("p (s e) -> p s e", e=128)
            P.op("act", lambda: nc.scalar.copy(out=Va[0][:, ti * NS:(ti + 1) * NS, 0:64], in_=vv[:, :, 0:64]),
                 rd=(v_ps,), wr=(Va[0],))
            P.op("dve", lambda: nc.vector.tensor_copy(out=Va[1][:, ti * NS:(ti + 1) * NS, 64:128],
                                                      in_=vv[:, :, 64:128]), rd=(v_ps,), wr=(Va[1],))
            P.op("act", lambda: nc.scalar.activation(out=sgT[:, tsl], in_=g_ps[:, 0:TT], func=AF.Sigmoid),
                 rd=(g_ps,), wr=(sgT,))
            for n_, (pp, gg, dst, qk_b) in enumerate(((q_ps, gq2, qT, qb_), (k_ps, gk2, kT, kb_))):
                sq, rs = sqf[n_], rsf[n_]
                P.op("act", lambda: nc.scalar.activation(out=sq[:], in_=pp, func=AF.Square),
                     rd=(qk_b,), wr=(sq,))
                P.op("pe", lambda: nc.tensor.matmul(ss_ps[:, 0:TT], lhsT=bones[:], rhs=sq[:], start=True, stop=True),
                     rd=(bones, sq), wr=(ss_ps,))
                P.op("act", lambda: nc.scalar.activation(out=rs[:], in_=ss_ps[:, 0:TT], func=AF.Ln,
                                                         bias=float(EPS), scale=1.0 / 64), rd=(ss_ps,), wr=(rs,))
                P.op("act", lambda: nc.scalar.activation(out=rs[:], in_=rs[:], func=AF.Exp, scale=-0.5),
                     rd=(rs,), wr=(rs,))
                if dst is kT:
                    for hh in range(2):
                        hs_ = slice(hh * 64, (hh + 1) * 64)
                        P.op("dve", lambda: nc.vector.scalar_tensor_tensor(out=kTz[hh][hs_, tsl], in0=pp[hs_, :],
                                                                           scalar=gg[hs_, 0:1], in1=rs[hs_, :],
                                                                           op0=ALU.mult, op1=ALU.mult),
                             rd=(qk_b, gg, rs), wr=(kTz[hh],))
                else:
                    P.op("dve", lambda: nc.vector.scalar_tensor_tensor(out=dst[:, tsl], in0=pp,
                                                                       scalar=gg[:, 0:1], in1=rs[:], op0=ALU.mult,
                                                                       op1=ALU.mult), rd=(qk_b, gg, rs), wr=(dst,))
        for par in range(2):
            h = 2 * hp + par
            pr = slice(par * 64, (par + 1) * 64)
            for ti in range(NT):
                cp = acc[ti % 4]
                P.op("pe", lambda: nc.tensor.matmul(cp[:, 0:TT], lhsT=sel[:, h, :], rhs=cum[:, ti * TT:(ti + 1) * TT],
                                                    start=True, stop=True), rd=(sel, cum), wr=(cp,))
                P.op("act", lambda: nc.scalar.copy(out=cbc[:, ti * TT:(ti + 1) * TT], in_=cp[:, 0:TT]),
                     rd=(cp,), wr=(cbc,))
                P.op("pool", lambda: nc.gpsimd.tensor_tensor(out=cbcm[:, ti * TT:(ti + 1) * TT],
                                                              in0=cbc[:, ti * TT:(ti + 1) * TT], in1=tri4[:],
                                                              op=ALU.add), rd=(cbc, tri4), wr=(cbcm,))
            pairs = [list(range(t0_, min(t0_ + 2, NT))) for t0_ in range(0, NT, 2)]
            c0v = cbc[:].rearrange("p (n t) -> p n t", t=TT * 2 if NT > 1 else TT)[:, :, 0]
            P.op("dve", lambda: nc.vector.tensor_scalar(out=ncb0[:, 0:len(pairs)], in0=c0v, scalar1=-1.0, scalar2=None,
                                                        op0=ALU.mult), rd=(cbc,), wr=(ncb0,))
            for m, tl in enumerate(pairs):
                if m == 0:
                    continue
                nb_ = tl[0] * NS
                P.op("dve", lambda: nc.vector.tensor_scalar(out=biasall[:, m, 0:nb_], in0=ncumT[:, 0:nb_, h],
                                                            scalar1=cbc[:, tl[0] * TT:tl[0] * TT + 1], scalar2=None,
                                                            op0=ALU.add), rd=(ncumT, cbc), wr=(biasall,))
            for m, tl in enumerate(pairs):
                ntl = len(tl)
                noff = tl[0] * NS
                a_off = [acc[0], acc[1]]
                a_dg = [acc[2], acc[3]]
                offs = [("off", kb) for kb in range(noff)]
                dvs = []
                for xi, ti in enumerate(tl):
                    for kb in range(noff, (ti + 1) * NS):
                        r = kb - ti * NS
                        dvs.append(("dv", xi, kb, (128 * r if r > 0 else 0), r >= 0))
                items = offs + dvs
                dv_first, dv_last, off_first, off_last = {}, {}, None, None
                for n_, it_ in enumerate(items):
                    if it_[0] == "off":
                        off_first = n_ if off_first is None else off_first
                        off_last = n_
                    else:
                        dv_first.setdefault(it_[1], n_)
                        dv_last[it_[1]] = n_
                LAG = 3
                for xi, ti in enumerate(tl):
                    if noff > 0:
                        P.op("act", lambda: nc.scalar.activation(out=ecr[xi][:], in_=cbc[:, ti * TT:(ti + 1) * TT],
                                                                 func=AF.Exp, bias=ncb0[:, m:m + 1], scale=1.0),
                             rd=(cbc, ncb0), wr=(ecr[xi],))

                def emit_s(n):
                    it_ = items[n]
                    pb = pT2[n % 4]
                    if it_[0] == "off":
                        kb = it_[1]
                        hb_ = [sth[2 * (n % 2) + xi] for xi in range(ntl)]
                        for xi, ti in enumerate(tl):
                            P.op("pe", lambda: nc.tensor.matmul(hb_[xi][:, 0:TT],
                                                                lhsT=kTz[par][:, kb * 128:(kb + 1) * 128],
                                                                rhs=qT[:, ti * TT:(ti + 1) * TT], start=True, stop=True),
                                 rd=(kTz[par], qT), wr=(hb_[xi],))
                        P.op("act", lambda: nc.scalar.activation(out=pb[:, 0:ntl * TT], in_=st2[n % 2].t[:, 0:ntl * TT],
                                                                 func=AF.Exp, bias=biasall[:, m, kb:kb + 1], scale=1.0),
                             rd=tuple(hb_) + (biasall,), wr=(pb,))
                        return
                    _, xi, kb, c0, tri_ = it_
                    ti = tl[xi]
                    dvn = n - len(offs)
                    sp_ = sth[dvn % 4]
                    P.op("pe", lambda: nc.tensor.matmul(sp_[:, c0:TT], lhsT=kTz[par][:, kb * 128:(kb + 1) * 128],
                                                        rhs=qT[:, ti * TT + c0:(ti + 1) * TT], start=True, stop=True),
                         rd=(kTz[par], qT), wr=(sp_,))
                    tb = tmpb[dvn % 4]
                    if tri_:
                        P.op("dve", lambda: nc.vector.scalar_tensor_tensor(
                            out=tb[:, c0:c0 + 128], in0=sp_[:, c0:c0 + 128], scalar=ncumT[:, kb, h:h + 1],
                            in1=cbcm[:, ti * TT + c0:ti * TT + c0 + 128], op0=ALU.add, op1=ALU.add),
                            rd=(sp_, ncumT, cbcm), wr=(tb,))
                        c1 = c0 + 128
                    else:
                        c1 = c0
                    if c1 < TT:
                        P.op("dve", lambda: nc.vector.scalar_tensor_tensor(
                            out=tb[:, c1:TT], in0=sp_[:, c1:TT], scalar=ncumT[:, kb, h:h + 1],
                            in1=cbc[:, ti * TT + c1:(ti + 1) * TT], op0=ALU.add, op1=ALU.add),
                            rd=(sp_, ncumT, cbc), wr=(tb,))
                    P.op("act", lambda: nc.scalar.activation(out=pb[:, c0:TT], in_=tb[:, c0:TT], func=AF.Exp),
                         rd=(tb,), wr=(pb,))

                def emit_pv(n):
                    it_ = items[n]
                    pb = pT2[n % 4]
                    if it_[0] == "off":
                        kb = it_[1]
                        for xi, ti in enumerate(tl):
                            P.op("pe", lambda: nc.tensor.matmul(a_off[xi][:, 0:TT], lhsT=Va[par][:, kb, :],
                                                                rhs=pb[:, xi * TT:(xi + 1) * TT],
                                                                start=(n == off_first), stop=(n == off_last)),
                                 rd=(Va[par], pb), wr=(a_off[xi],))
                        return
                    _, xi, kb, c0, tri_ = it_
                    P.op("pe", lambda: nc.tensor.matmul(a_dg[xi][:, c0:TT], lhsT=Va[par][:, kb, :], rhs=pb[:, c0:TT],
                                                        start=(n == dv_first[xi]), stop=(n == dv_last[xi])),
                         rd=(Va[par], pb), wr=(a_dg[xi],))

                for n in range(len(items) + LAG):
                    if n < len(items):
                        emit_s(n)
                    if n - LAG >= 0:
                        emit_pv(n - LAG)
                for xi, ti in enumerate(tl):
                    if noff > 0:
                        P.op("dve", lambda: nc.vector.tensor_tensor(out=osb[:], in0=a_off[xi][:, 0:TT], in1=ecr[xi][:],
                                                                    op=ALU.mult), rd=(a_off[xi], ecr[xi]), wr=(osb,))
                        P.op("dve", lambda: nc.vector.tensor_tensor(out=osb[:], in0=a_dg[xi][:, 0:TT], in1=osb[:],
                                                                    op=ALU.add), rd=(a_dg[xi], osb), wr=(osb,))
                    else:
                        P.op("act", lambda: nc.scalar.copy(out=osb[:], in_=a_dg[xi][:, 0:TT]), rd=(a_dg[xi],), wr=(osb,))
                    opr = slice((1 - par) * 64, (2 - par) * 64)
                    P.dma("sp", dsw, osb, dsw[pr, :], osb[opr, :])
                    P.op("act", lambda: nc.scalar.activation(out=rden[pr, :], in_=dsw[pr, :], func=AF.Ln),
                         rd=(dsw,), wr=(rden,))
                    P.op("act", lambda: nc.scalar.activation(out=rden[pr, :], in_=rden[pr, :], func=AF.Exp, scale=-1.0),
                         rd=(rden,), wr=(rden,))
                    P.op("pool", lambda: nc.gpsimd.tensor_tensor(out=onum[pr, :], in0=osb[pr, :], in1=rden[pr, :],
                                                                  op=ALU.mult), rd=(osb, rden), wr=(onum,))
                    P.op("pool", lambda: nc.gpsimd.tensor_tensor(out=oT[pr, ti * TT:(ti + 1) * TT], in0=onum[pr, :],
                                                                  in1=sgT[pr, ti * TT:(ti + 1) * TT], op=ALU.mult),
                         rd=(onum, sgT), wr=(oT,))
        P.dma("sp", oTd, oT, oTd_ap[:, hp, 0:T], oT[:])
    C.close()

    outproj_stage(P, T, xin, xout, xin_ap, xout_ap, oTd_ap, oTd, wout_ap, gpost_ap)


def outproj_stage(P, T, xin, xout, xin_ap, xout_ap, oTd_ap, oTd, wout_ap, gpost_ap):
    nc = P.nc
    TT = 512 if T >= 512 else T
    NT = T // TT
    NS = TT // 128
    xin_v = xin_ap.rearrange("(n p) d -> n p d", p=128)
    xout_v = xout_ap.rearrange("(n p) d -> n p d", p=128)
    C = Ctx(P)
    gpost = load_gain_bc(P, C, uname("gpost"), gpost_ap, 1.0)
    wo = C.sb(uname("wo"), (128, NCH, D), BF16, dma="sw")
    wo_v = wout_ap.rearrange("(c p) n -> p c n", p=128)
    for q in range(4):
        load_w(P, wo, wo[:, 2 * q:2 * q + 2, :], wo_v[:, 2 * q:2 * q + 2, :])
    ot = [C.sb(uname("ot"), (128, NCH, TT), BF16, dma=True) for _ in range(2)]
    xp = [C.sb(uname("xp"), (128, D), F32, dma=True) for _ in range(2)]
    sq_junk = C.sb(uname("sqj"), (128, D), BF16)
    tmp = C.sb(uname("tmp"), (128, D), F32)
    stats = [C.sb(uname("stat"), (128, 8), F32) for _ in range(4)]
    f_ps = [[C.ps(uname("fps")) for _ in range(2)] for _ in range(2)]
    it = 0
    for ti in range(NT):
        o_ = ot[ti % 2]
        P.dma("sp", o_, oTd, o_[:], oTd_ap[:, :, ti * TT:(ti + 1) * TT])
        for s_ in range(NS):
            xst = xp[it % 2]
            fp = f_ps[it % 2]
            P.dma("sp", xst, xin, xst[:], xin_v[ti * NS + s_])
            for hh in range(2):
                for c in range(NCH):
                    P.op("pe", lambda: nc.tensor.matmul(fp[hh][:, :], lhsT=o_[:, c, s_ * 128:(s_ + 1) * 128],
                                                        rhs=wo[:, c, hh * 512:(hh + 1) * 512],
                                                        start=(c == 0), stop=(c == NCH - 1)), rd=(o_, wo), wr=(fp[hh],))
            postnorm_tile(P, nc, fp, xst, gpost, xst, sq_junk, stats[it % 4], tmp)
            P.dma("sp", xout, xst, xout_v[ti * NS + s_], xst[:])
            it += 1
    C.close()


CW = 31
CK = 64
DEC_C = 0.6065306597126334


def rwkv_prep_stage(P, T, xin, xin_ap, gpre_ap, A, scr):
    nc = P.nc
    TT = 512 if T >= 512 else T
    NT = T // TT
    NS = TT // 128
    NQ = 4
    CR = 1792
    rw_ap, rw = scr["rw"]
    dc_ap, dcb = scr["dc"]
    yab_ap, yab = scr["oT"]
    xin_v = xin_ap.rearrange("(n p) d -> n p d", p=128)
    win_v = A["w_in"].rearrange("(c p) n -> p c n", p=128)
    dsrc = Buf(P, None, "dsrc")
    C = Ctx(P)
    idf, idb = make_consts(P, C)
    gpre = load_gain_bc(P, C, uname("gpre"), gpre_ap, 1.0)
    W1 = C.sb(uname("W1"), (128, NCH, CR), BF16)
    W2 = C.sb(uname("W2"), (128, NCH, CR), BF16)
    Wc = C.sb(uname("Wc"), (128, NCH, 1024), BF16, dma="sw")
    for q in range(4):
        load_w(P, Wc, Wc[:, 2 * q:2 * q + 2, :], win_v[:, 2 * q:2 * q + 2, CR:CR + 1024])
    C0 = Ctx(P)
    mu = C0.sb(uname("mu"), (128, CR), F32, dma=True)
    omu = C0.sb(uname("omu"), (128, CR), F32)
    P.dma("sp", mu, dsrc, mu[:], A["shift_mu"].partition_broadcast(128))
    P.op("dve", lambda: nc.vector.tensor_scalar(out=omu[:], in0=mu[:], scalar1=-1.0, scalar2=1.0, op0=ALU.mult,
                                                op1=ALU.add), rd=(mu,), wr=(omu,))
    stg = [C0.sb(uname("stg"), (128, CR), F32, dma=True) for _ in range(2)]
    for c in range(NCH):
        st = stg[c % 2]
        P.dma("sp", st, dsrc, st[:], win_v[:, c, 0:CR])
        P.op("dve", lambda: nc.vector.tensor_tensor(out=W2[:, c, :], in0=st[:], in1=mu[:], op=ALU.mult),
             rd=(st, mu), wr=(W2,))
        P.op("pool", lambda: nc.gpsimd.tensor_tensor(out=W1[:, c, :], in0=st[:], in1=omu[:], op=ALU.mult),
             rd=(st, omu), wr=(W1,))
    C0.close()
    lup = C.sb(uname("lup"), (128, 512), BF16, dma="sw")
    load_w(P, lup, lup[0:64, :], A["decay_up"])
    load_w(P, lup, lup[64:128, :], A["iclr_up"])
    gup = C.sb(uname("gup"), (128, 512), BF16, dma="sw")
    load_w(P, gup, gup[:], A["gate_up"])
    pnames = ("decay_base", "iclr_base", "k_k", "k_a", "r_k", "lnx_b", "conv_b", "ln_g", "ln_b")
    NPR = len(pnames) + CW
    prow = C.sb(uname("prow"), (NPR, 512), F32, dma=True)
    for i_, nm in enumerate(pnames):
        src = A[nm]
        if nm == "r_k":
            src = src.rearrange("h n -> (h n)")
        P.dma("sp", prow, dsrc, prow[i_:i_ + 1, :], src.rearrange("(o n) -> o n", o=1))
    P.dma("sp", prow, dsrc, prow[len(pnames):NPR, :], A["conv_w"])
    pc = {nm: C.sb(uname("pc_" + nm), (128, NQ), F32) for nm in pnames}
    cw = C.sb(uname("cw"), (128, NQ, CW), F32)
    ptp = C.ps(uname("ptp"))
    for q in range(NQ):
        P.op("pe", lambda: nc.tensor.transpose(out=ptp[:, q * 64:q * 64 + NPR], in_=prow[0:NPR, q * 128:(q + 1) * 128],
                                               identity=idf[0:NPR, 0:NPR]), rd=(prow, idf), wr=(ptp,))
    for q in range(NQ):
        for i_, nm in enumerate(pnames):
            P.op("dve", lambda: nc.vector.tensor_copy(out=pc[nm][:, q:q + 1], in_=ptp[:, q * 64 + i_:q * 64 + i_ + 1]),
                 rd=(ptp,), wr=(pc[nm],))
        P.op("act", lambda: nc.scalar.copy(out=cw[:, q, :], in_=ptp[:, q * 64 + len(pnames):q * 64 + NPR]),
             rd=(ptp,), wr=(cw,))
    omka = C.sb(uname("omka"), (128, NQ), F32)
    P.op("dve", lambda: nc.vector.tensor_scalar(out=omka[:], in0=pc["k_a"][:], scalar1=-1.0, scalar2=1.0,
                                                op0=ALU.mult, op1=ALU.add), rd=(pc["k_a"],), wr=(omka,))
    bones = C.sb(uname("bones"), (128, 128), F32)
    P.op("pool", lambda: nc.gpsimd.memset(bones[:], 0.0), wr=(bones,))
    P.op("pool", lambda: nc.gpsimd.memset(bones[0:64, 0:64], 1.0), rd=(bones,), wr=(bones,))
    P.op("pool", lambda: nc.gpsimd.memset(bones[64:128, 64:128], 1.0), rd=(bones,), wr=(bones,))
    ones = C.sb(uname("ones"), (128, 128), F32)
    P.op("pool", lambda: nc.gpsimd.memset(ones[:], 1.0), wr=(ones,))
    rmask = C.sb(uname("rmask"), (128, TT), F32)
    P.op("pool", lambda: nc.gpsimd.memset(rmask[:], 1.0), wr=(rmask,))
    P.op("pool", lambda: nc.gpsimd.memset(rmask[:].rearrange("p (c j) -> p c j", j=CK)[:, :, 0:1], 0.0),
         rd=(rmask,), wr=(rmask,))
    xs = [C.sb(uname("xs"), (128, D), F32, dma=True) for _ in range(2)]
    hTh = [C.sb(uname("hTh"), (128, NCH, TT + 1), BF16) for _ in range(2)]
    P.op("pool", lambda: nc.gpsimd.memset(hTh[0][:, :, 0:1], 0.0), wr=(hTh[0],))
    sq_junk = C.sb(uname("sqj"), (128, D), BF16)
    hb = [C.sb(uname("hb"), (128, D), BF16) for _ in range(2)]
    stats = [C.sb(uname("stat"), (128, 8), F32) for _ in range(4)]
    tdw = C.sb(uname("tdw"), (128, TT), BF16)
    sdg = C.sb(uname("sdg"), (128, TT), BF16)
    F = {}
    for nm in ("rf", "kf", "sgw", "av", "gf", "kk", "kk2", "rn", "kkn", "t1", "kn", "bb", "rk", "Lc", "eL", "enL",
               "Lx", "eLx", "bon"):
        F[nm] = C.sb(uname(nm), (128, TT), F32)
    pack = [C.sb(uname("pack"), (128, 7, TT), BF16) for _ in range(2)]
    dct = C.sb(uname("dct"), (128, NQ, T // CK), F32)
    glub = [C.sb(uname("glub"), (128, TT + CW - 1), F32) for _ in range(NQ)]
    for q in range(NQ):
        P.op("pool", lambda: nc.gpsimd.memset(glub[q][:, 0:CW - 1], 0.0), wr=(glub[q],))
    sgc = C.sb(uname("sgc"), (128, TT), F32)
    acc = [C.sb(uname("acc"), (128, TT), F32) for _ in range(NQ)]
    sqc = C.sb(uname("sqc"), (128, TT), F32)
    mean = C.sb(uname("mean"), (128, TT), F32)
    msq = C.sb(uname("msq"), (128, TT), F32)
    rstd = C.sb(uname("rstd"), (128, TT), F32)
    tcv = C.sb(uname("tcv"), (128, TT), F32)
    ybt = [C.sb(uname("ybt"), (128, TT), BF16) for _ in range(2)]
    tp_ps = C.ps(uname("tp"), (128, D), BF16)
    bk = [C.ps(uname("bk")) for _ in range(6)] + [ptp]

    def proj(pp, col0, shifted, ht):
        n = 2 * NCH if shifted else NCH
        i = 0
        for c in range(NCH):
            if shifted:
                P.op("pe", lambda: nc.tensor.matmul(pp[:, 0:TT], lhsT=W1[:, c, col0:col0 + 128], rhs=ht[:, c, 1:TT + 1],
                                                    start=(i == 0), stop=(i == n - 1)), rd=(W1, ht), wr=(pp,))
                i += 1
                P.op("pe", lambda: nc.tensor.matmul(pp[:, 0:TT], lhsT=W2[:, c, col0:col0 + 128], rhs=ht[:, c, 0:TT],
                                                    start=False, stop=(i == n - 1)), rd=(W2, ht), wr=(pp,))
                i += 1
            else:
                P.op("pe", lambda: nc.tensor.matmul(pp[:, 0:TT], lhsT=Wc[:, c, col0:col0 + 128], rhs=ht[:, c, 1:TT + 1],
                                                    start=(i == 0), stop=(i == n - 1)), rd=(Wc, ht), wr=(pp,))
                i += 1

    pk_i = 0
    for ti in range(NT):
        ht = hTh[ti % 2]
        tsl = slice(ti * TT, (ti + 1) * TT)
        for s_ in range(NS):
            xst = xs[s_ % 2]
            P.dma("sp", xst, xin, xst[:], xin_v[ti * NS + s_])
            prenorm_tile(P, nc, xst, gpre, ht, 1 + s_ * 128, idb, sq_junk, stats[s_ % 4], hb[s_ % 2], tp_ps)
        if ti + 1 < NT:
            P.op("pool", lambda: nc.gpsimd.tensor_copy(out=hTh[(ti + 1) % 2][:, :, 0:1], in_=ht[:, :, TT:TT + 1]),
                 rd=(ht,), wr=(hTh[(ti + 1) % 2],))
        proj(bk[6], 1536, True, ht)
        P.op("act", lambda: nc.scalar.activation(out=tdw[0:64, :], in_=bk[6][0:64, 0:TT], func=AF.Tanh),
             rd=(bk[6],), wr=(tdw,))
        P.op("act", lambda: nc.scalar.copy(out=tdw[64:128, :], in_=bk[6][64:128, 0:TT]), rd=(bk[6],), wr=(tdw,))
        proj(bk[5], 1664, True, ht)
        P.op("act", lambda: nc.scalar.activation(out=sdg[:], in_=bk[5][:, 0:TT], func=AF.Sigmoid),
             rd=(bk[5],), wr=(sdg,))
        for q in range(NQ):
            pk = pack[pk_i % 2]
            pk_i += 1
            qs = slice(q * 128, (q + 1) * 128)
            r_ps, k_ps, v_ps, zw_ps, za_ps, g_ps, s_ps = bk[0], bk[1], bk[2], bk[3], bk[4], bk[5], bk[6]
            proj(r_ps, q * 128, True, ht)
            proj(k_ps, 512 + q * 128, True, ht)
            proj(v_ps, 1024 + q * 128, True, ht)
            P.op("pe", lambda: nc.tensor.matmul(zw_ps[:, 0:TT], lhsT=lup[0:64, qs], rhs=tdw[0:64, :], start=True,
                                                stop=True), rd=(lup, tdw), wr=(zw_ps,))
            P.op("pe", lambda: nc.tensor.matmul(za_ps[:, 0:TT], lhsT=lup[64:128, qs], rhs=tdw[64:128, :], start=True,
                                                stop=True), rd=(lup, tdw), wr=(za_ps,))
            P.op("pe", lambda: nc.tensor.matmul(g_ps[:, 0:TT], lhsT=gup[:, qs], rhs=sdg[:], start=True, stop=True),
                 rd=(gup, sdg), wr=(g_ps,))
            col = lambda nm: pc[nm][:, q:q + 1]
            P.op("act", lambda: nc.scalar.copy(out=F["rf"][:], in_=r_ps[:, 0:TT]), rd=(r_ps,), wr=(F["rf"],))
            P.op("act", lambda: nc.scalar.copy(out=F["kf"][:], in_=k_ps[:, 0:TT]), rd=(k_ps,), wr=(F["kf"],))
            P.op("act", lambda: nc.scalar.copy(out=pk[:, 4, :], in_=v_ps[:, 0:TT]), rd=(v_ps,), wr=(pk,))
            P.op("act", lambda: nc.scalar.activation(out=F["sgw"][:], in_=zw_ps[:, 0:TT], func=AF.Sigmoid,
                                                     bias=col("decay_base")), rd=(zw_ps, pc["decay_base"]),
                 wr=(F["sgw"],))
            P.op("act", lambda: nc.scalar.activation(out=F["av"][:], in_=za_ps[:, 0:TT], func=AF.Sigmoid,
                                                     bias=col("iclr_base")), rd=(za_ps, pc["iclr_base"]),
                 wr=(F["av"],))
            P.op("act", lambda: nc.scalar.copy(out=F["gf"][:], in_=g_ps[:, 0:TT]), rd=(g_ps,), wr=(F["gf"],))
            P.op("dve", lambda: nc.vector.tensor_scalar(out=F["kk"][:], in0=F["kf"][:], scalar1=col("k_k"),
                                                        scalar2=None, op0=ALU.mult), rd=(F["kf"], pc["k_k"]),
                 wr=(F["kk"],))
            P.op("pool", lambda: nc.gpsimd.tensor_tensor(out=F["kk2"][:], in0=F["kk"][:], in1=F["kk"][:],
                                                          op=ALU.mult), rd=(F["kk"],), wr=(F["kk2"],))
            P.op("pe", lambda: nc.tensor.matmul(s_ps[:, 0:TT], lhsT=bones[:], rhs=F["kk2"][:], start=True, stop=True),
                 rd=(bones, F["kk2"]), wr=(s_ps,))
            P.op("act", lambda: nc.scalar.activation(out=F["rn"][:], in_=s_ps[:, 0:TT], func=AF.Ln, bias=1e-24,
                                                     scale=1.0), rd=(s_ps,), wr=(F["rn"],))
            P.op("act", lambda: nc.scalar.activation(out=F["rn"][:], in_=F["rn"][:], func=AF.Exp, scale=-0.5),
                 rd=(F["rn"],), wr=(F["rn"],))
            P.op("pool", lambda: nc.gpsimd.tensor_tensor(out=F["kkn"][:], in0=F["kk"][:], in1=F["rn"][:],
                                                          op=ALU.mult), rd=(F["kk"], F["rn"]), wr=(F["kkn"],))
            P.op("dve", lambda: nc.vector.tensor_scalar(out=F["t1"][:], in0=F["av"][:], scalar1=col("k_a"),
                                                        scalar2=omka[:, q:q + 1], op0=ALU.mult, op1=ALU.add),
                 rd=(F["av"], pc["k_a"], omka), wr=(F["t1"],))
            P.op("pool", lambda: nc.gpsimd.tensor_tensor(out=F["kn"][:], in0=F["kf"][:], in1=F["t1"][:],
                                                          op=ALU.mult), rd=(F["kf"], F["t1"]), wr=(F["kn"],))
            P.op("pool", lambda: nc.gpsimd.tensor_tensor(out=F["bb"][:], in0=F["kkn"][:], in1=F["av"][:],
                                                          op=ALU.mult), rd=(F["kkn"], F["av"]), wr=(F["bb"],))
            P.op("dve", lambda: nc.vector.scalar_tensor_tensor(out=F["rk"][:], in0=F["rf"][:], scalar=col("r_k"),
                                                               in1=F["kn"][:], op0=ALU.mult, op1=ALU.mult),
                 rd=(F["rf"], pc["r_k"], F["kn"]), wr=(F["rk"],))
            P.op("pe", lambda: nc.tensor.matmul(s_ps[:, 0:TT], lhsT=bones[:], rhs=F["rk"][:], start=True, stop=True),
                 rd=(bones, F["rk"]), wr=(s_ps,))
            P.op("dve", lambda: nc.vector.tensor_tensor(out=F["bon"][:], in0=s_ps[:, 0:TT], in1=pk[:, 4, :],
                                                        op=ALU.mult), rd=(s_ps, pk), wr=(F["bon"],))
            P.op("dve", lambda: nc.vector.scalar_tensor_tensor(out=pk[:, 6, :], in0=F["bon"][:], scalar=col("lnx_b"),
                                                               in1=F["gf"][:], op0=ALU.add, op1=ALU.mult),
                 rd=(F["bon"], pc["lnx_b"], F["gf"]), wr=(pk,))
            P.op("pool", lambda: nc.gpsimd.tensor_copy(out=pk[:, 5, :], in_=F["gf"][:]), rd=(F["gf"],), wr=(pk,))
            P.op("dve", lambda: nc.vector.tensor_tensor_scan(out=F["Lc"][:], data0=rmask[:], data1=F["sgw"][:],
                                                             initial=0.0, op0=ALU.mult, op1=ALU.add),
                 rd=(rmask, F["sgw"]), wr=(F["Lc"],))
            P.op("pool", lambda: nc.gpsimd.tensor_tensor(out=F["Lx"][:], in0=F["Lc"][:], in1=F["sgw"][:],
                                                          op=ALU.subtract), rd=(F["Lc"], F["sgw"]), wr=(F["Lx"],))
            P.op("act", lambda: nc.scalar.activation(out=F["eL"][:], in_=F["Lc"][:], func=AF.Exp, scale=-DEC_C),
                 rd=(F["Lc"],), wr=(F["eL"],))
            P.op("act", lambda: nc.scalar.activation(out=F["enL"][:], in_=F["Lc"][:], func=AF.Exp, scale=DEC_C),
                 rd=(F["Lc"],), wr=(F["enL"],))
            P.op("act", lambda: nc.scalar.activation(out=F["eLx"][:], in_=F["Lx"][:], func=AF.Exp, scale=-DEC_C),
                 rd=(F["Lx"],), wr=(F["eLx"],))
            nck = TT // CK
            P.op("pool", lambda: nc.gpsimd.tensor_copy(
                out=dct[:, q, ti * nck:(ti + 1) * nck],
                in_=F["eL"][:].rearrange("p (c j) -> p c j", j=CK)[:, :, CK - 1]), rd=(F["eL"],), wr=(dct,))
            P.op("pool", lambda: nc.gpsimd.tensor_tensor(out=pk[:, 0, :], in0=F["rf"][:], in1=F["eL"][:], op=ALU.mult),
                 rd=(F["rf"], F["eL"]), wr=(pk,))
            P.op("dve", lambda: nc.vector.scalar_tensor_tensor(out=pk[:, 1, :], in0=F["kkn"][:], scalar=-1.0,
                                                               in1=F["eLx"][:], op0=ALU.mult, op1=ALU.mult),
                 rd=(F["kkn"], F["eLx"]), wr=(pk,))
            P.op("pool", lambda: nc.gpsimd.tensor_tensor(out=pk[:, 2, :], in0=F["kn"][:], in1=F["enL"][:], op=ALU.mult),
                 rd=(F["kn"], F["enL"]), wr=(pk,))
            P.op("dve", lambda: nc.vector.tensor_tensor(out=pk[:, 3, :], in0=F["bb"][:], in1=F["enL"][:], op=ALU.mult),
                 rd=(F["bb"], F["enL"]), wr=(pk,))
            P.dma("sp", rw, pk, rw_ap[qs, :, tsl], pk[:])
        for q in range(NQ):
            val_ps, gt_ps = bk[0 + 2 * (q % 2)], bk[1 + 2 * (q % 2)]
            proj(val_ps, q * 128, False, ht)
            proj(gt_ps, 512 + q * 128, False, ht)
            gb = glub[q]
            P.op("act", lambda: nc.scalar.activation(out=sgc[:], in_=gt_ps[:, 0:TT], func=AF.Sigmoid),
                 rd=(gt_ps,), wr=(sgc,))
            P.op("dve", lambda: nc.vector.tensor_tensor(out=gb[:, CW - 1:CW - 1 + TT], in0=val_ps[:, 0:TT], in1=sgc[:],
                                                        op=ALU.mult), rd=(val_ps, sgc), wr=(gb,))
            ac = acc[q]
            P.op("dve", lambda: nc.vector.tensor_scalar(out=ac[:], in0=gb[:, 0:TT], scalar1=cw[:, q, 0:1],
                                                        scalar2=pc["conv_b"][:, q:q + 1], op0=ALU.mult, op1=ALU.add),
                 rd=(gb, cw, pc["conv_b"]), wr=(ac,))
            for j in range(1, CW):
                P.op("dve", lambda: nc.vector.scalar_tensor_tensor(out=ac[:], in0=gb[:, j:j + TT],
                                                                   scalar=cw[:, q, j:j + 1], in1=ac[:], op0=ALU.mult,
                                                                   op1=ALU.add), rd=(gb, cw, ac), wr=(ac,))
            P.op("pool", lambda: nc.gpsimd.tensor_copy(out=gb[:, 0:CW - 1], in_=gb[:, TT:TT + CW - 1]),
                 rd=(gb,), wr=(gb,))
        sum_ps, ssq_ps = bk[4], bk[5]
        for q in range(NQ):
            P.op("pe", lambda: nc.tensor.matmul(sum_ps[:, 0:TT], lhsT=ones[:], rhs=acc[q][:], start=(q == 0),
                                                stop=(q == NQ - 1)), rd=(ones, acc[q]), wr=(sum_ps,))
        for q in range(NQ):
            P.op("act", lambda: nc.scalar.activation(out=sqc[:], in_=acc[q][:], func=AF.Square), rd=(acc[q],),
                 wr=(sqc,))
            P.op("pe", lambda: nc.tensor.matmul(ssq_ps[:, 0:TT], lhsT=ones[:], rhs=sqc[:], start=(q == 0),
                                                stop=(q == NQ - 1)), rd=(ones, sqc), wr=(ssq_ps,))
        P.op("act", lambda: nc.scalar.activation(out=mean[:], in_=sum_ps[:, 0:TT], func=AF.Copy, scale=1.0 / 512),
             rd=(sum_ps,), wr=(mean,))
        P.op("pool", lambda: nc.gpsimd.tensor_tensor(out=msq[:], in0=mean[:], in1=mean[:], op=ALU.mult),
             rd=(mean,), wr=(msq,))
        P.op("dve", lambda: nc.vector.scalar_tensor_tensor(out=rstd[:], in0=ssq_ps[:, 0:TT], scalar=1.0 / 512,
                                                           in1=msq[:], op0=ALU.mult, op1=ALU.subtract),
             rd=(ssq_ps, msq), wr=(rstd,))
        P.op("act", lambda: nc.scalar.activation(out=rstd[:], in_=rstd[:], func=AF.Ln, bias=1e-5, scale=1.0),
             rd=(rstd,), wr=(rstd,))
        P.op("act", lambda: nc.scalar.activation(out=rstd[:], in_=rstd[:], func=AF.Exp, scale=-0.5),
             rd=(rstd,), wr=(rstd,))
        for q in range(NQ):
            yb_ = ybt[q % 2]
            P.op("pool", lambda: nc.gpsimd.tensor_tensor(out=tcv[:], in0=acc[q][:], in1=mean[:], op=ALU.subtract),
                 rd=(acc[q], mean), wr=(tcv,))
            P.op("pool", lambda: nc.gpsimd.tensor_tensor(out=tcv[:], in0=tcv[:], in1=rstd[:], op=ALU.mult),
                 rd=(tcv, rstd), wr=(tcv,))
            P.op("act", lambda: nc.scalar.activation(out=yb_[:], in_=tcv[:], func=AF.Silu,
                                                     bias=pc["ln_b"][:, q:q + 1], scale=pc["ln_g"][:, q:q + 1]),
                 rd=(tcv, pc["ln_b"], pc["ln_g"]), wr=(yb_,))
            P.dma("sp", yab, yb_, yab_ap[:, 4 + q, tsl], yb_[:])
    for q in range(NQ):
        P.dma("sp", dcb, dct, dc_ap[q * 128:(q + 1) * 128, :], dct[:, q, :])
    C.close()


def rwkv_scan_stage(P, T, A, scr):
    nc = P.nc
    NH = 8
    NHC = 4
    TT = 512 if T >= 512 else T
    NG = T // TT
    NCG = TT // CK
    NC = T // CK
    rw_ap, rw = scr["rw"]
    dc_ap, dcb = scr["dc"]
    yab_ap, yab = scr["oT"]
    dsrc = Buf(P, None, "dsrc")
    C = Ctx(P)
    idf, idb = make_consts(P, C)
    ones64 = C.sb(uname("ones64"), (128, 64), F32)
    P.op("pool", lambda: nc.gpsimd.memset(ones64[:], 1.0), wr=(ones64,))

    def mk_mask(name, base, cm, step):
        m = C.sb(uname(name), (64, NCG, CK), F32)
        P.op("pool", lambda: nc.gpsimd.memset(m[:], 1.0), wr=(m,))
        P.op("pool", lambda: nc.gpsimd.affine_select(out=m[:], in_=m[:], compare_op=ALU.is_ge, fill=0.0, base=base,
                                                      pattern=[[0, NCG], [step, CK]], channel_multiplier=cm),
             rd=(m,), wr=(m,))
        return m
    m_su = mk_mask("m_su", -1, -1, 1)
    m_sl = mk_mask("m_sl", -1, 1, -1)
    m_iu = mk_mask("m_iu", 0, -1, 1)
    I8 = C.sb(uname("I8"), (64, NCG, CK), F32)
    P.op("pool", lambda: nc.gpsimd.memset(I8[:], 0.0), wr=(I8,))
    P.op("pool", lambda: nc.gpsimd.affine_select(out=I8[:], in_=I8[:], compare_op=ALU.not_equal, fill=1.0, base=0,
                                                  pattern=[[0, NCG], [-1, CK]], channel_multiplier=1),
         rd=(I8,), wr=(I8,))
    lnrow = C.sb(uname("lnrow"), (1, 512), F32, dma=True)
    P.dma("sp", lnrow, dsrc, lnrow[:], A["lnx_g"].rearrange("(o n) -> o n", o=1))
    lnxg = C.sb(uname("lnxg"), (64, NH), F32)
    flat = lambda m: m[:].rearrange("p c i -> p (c i)")

    class Slot:
        pass
    slots = []
    for i in range(NHC):
        S = Slot()
        S.ops = [C.sb(uname("ops"), (128, 7, TT), BF16, dma=True) for _ in range(2)]
        for o_ in S.ops:
            P.op("pool", lambda: nc.gpsimd.memset(o_[64:128, :, :], 0.0), wr=(o_,))
        S.dC = C.sb(uname("dC"), (64, NC), F32, dma=True)
        for nm in ("Akt", "Arbt", "Arkt", "Tt", "Btok", "Ktok", "Vtok", "U", "Mb0", "Mb1", "Mt0", "Mt1"):
            setattr(S, nm, C.sb(uname(nm), (128, TT), BF16))
            P.op("pool", lambda: nc.gpsimd.memset(getattr(S, nm)[64:128, :], 0.0), wr=(getattr(S, nm),))
        S.Hall = C.sb(uname("Hall"), (128, NCG + 1, CK), BF16)
        P.op("pool", lambda: nc.gpsimd.memset(S.Hall[64:128, :, :], 0.0), wr=(S.Hall,))
        S.Hf = C.sb(uname("Hf"), (64, CK), F32)
        S.tmpH = C.sb(uname("tmpH"), (64, CK), F32)
        S.Wsb = C.sb(uname("Wsb"), (128, CK), BF16)
        P.op("pool", lambda: nc.gpsimd.memset(S.Wsb[64:128, :], 0.0), wr=(S.Wsb,))
        S.yT = C.sb(uname("yT"), (128, TT), F32)
        S.sqy = C.sb(uname("sqy"), (128, TT), F32)
        P.op("pool", lambda: nc.gpsimd.memset(S.yT[64:128, :], 0.0), wr=(S.yT,))
        P.op("pool", lambda: nc.gpsimd.memset(S.sqy[64:128, :], 0.0), wr=(S.sqy,))
        S.mean = C.sb(uname("mean"), (64, TT), F32)
        S.rstd = C.sb(uname("rstd"), (64, TT), F32)
        S.yo = C.sb(uname("yo"), (64, TT), BF16)
        S.bk = [C.ps(uname("bk")) for _ in range(2)]
        S.bi = 0
        slots.append(S)
    lps = slots[0].bk[0]
    for h_ in range(NH):
        P.op("pe", lambda: nc.tensor.transpose(out=lps[0:64, h_:h_ + 1], in_=lnrow[0:1, h_ * 64:(h_ + 1) * 64],
                                               identity=idf[0:1, 0:1]), rd=(lnrow, idf), wr=(lps,))
    P.op("dve", lambda: nc.vector.tensor_copy(out=lnxg[:], in_=lps[0:64, 0:NH]), rd=(lps,), wr=(lnxg,))

    def head_prog(S, h):
        def nb():
            S.bi += 1
            return S.bk[S.bi % 2]

        def macro(lt, lsl, rt, rsl, ps):
            for c in range(NCG):
                cs = slice(c * CK, (c + 1) * CK)
                P.op("pe", lambda: nc.tensor.matmul(ps[0:64, cs], lhsT=lsl(c), rhs=rsl(c), start=True, stop=True),
                     rd=(lt, rt), wr=(ps,))
        P.dma("sp", S.dC, dcb, S.dC[:], dc_ap[h * 64:(h + 1) * 64, :])
        P.op("pool", lambda: nc.gpsimd.memset(S.Hf[:], 0.0), wr=(S.Hf,))
        P.op("pool", lambda: nc.gpsimd.memset(S.Hall[0:64, 0, :], 0.0), wr=(S.Hall,))
        Mb, Mtb = [S.Mb0, S.Mb1], [S.Mt0, S.Mt1]
        for g in range(NG):
            g0 = g * TT
            ops = S.ops[g % 2]
            P.dma("sp", ops, rw, ops[0:64, :, :], rw_ap[h * 64:(h + 1) * 64, :, g0:g0 + TT])
            osl = lambda kind: (lambda c: ops[:, kind, c * CK:(c + 1) * CK])
            loc = lambda t_: (lambda c: t_[:, c * CK:(c + 1) * CK])
            Rs, As, Ks, Bs, Vs = osl(0), osl(1), osl(2), osl(3), osl(4)
            idl = lambda c: idb[:, 0:64]
            if g > 0:
                P.op("pool", lambda: nc.gpsimd.tensor_copy(out=S.Hall[0:64, 0, :], in_=S.Hall[0:64, NCG, :]), rd=(S.Hall,),
                     wr=(S.Hall,))
            yield
            ps = nb()
            macro(ops, Bs, ops, As, ps)
            P.op("dve", lambda: nc.vector.tensor_tensor(out=Mtb[0][0:64, :], in0=ps[0:64, 0:TT], in1=flat(m_su), op=ALU.mult),
                 rd=(ps, m_su), wr=(Mtb[0],))
            P.op("pool", lambda: nc.gpsimd.tensor_tensor(out=S.Tt[0:64, :], in0=Mtb[0][0:64, :], in1=flat(I8), op=ALU.add),
                 rd=(Mtb[0], I8), wr=(S.Tt,))
            yield
            ps = nb()
            macro(ops, As, ops, Bs, ps)
            P.op("dve", lambda: nc.vector.tensor_tensor(out=Mb[0][0:64, :], in0=ps[0:64, 0:TT], in1=flat(m_sl), op=ALU.mult),
                 rd=(ps, m_sl), wr=(Mb[0],))
            yield
            for (ls, rs_, dst, mk) in ((Ks, As, S.Akt, m_su), (Bs, Rs, S.Arbt, m_iu), (Ks, Rs, S.Arkt, m_iu)):
                ps = nb()
                macro(ops, ls, ops, rs_, ps)
                P.op("dve", lambda: nc.vector.tensor_tensor(out=dst[0:64, :], in0=ps[0:64, 0:TT], in1=flat(mk), op=ALU.mult),
                     rd=(ps, mk), wr=(dst,))
                yield
            for (src, dst) in ((Bs, S.Btok), (Ks, S.Ktok), (Vs, S.Vtok)):
                ps = nb()
                macro(ops, src, idb, idl, ps)
                P.op("act", lambda: nc.scalar.copy(out=dst[0:64, :], in_=ps[0:64, 0:TT]), rd=(ps,), wr=(dst,))
                yield
            cur = 0
            for p in range(1, 6):
                nxt = 1 - cur
                ps = nb()
                macro(Mtb[cur], loc(Mtb[cur]), Mb[cur], loc(Mb[cur]), ps)
                P.op("act", lambda: nc.scalar.copy(out=Mb[nxt][0:64, :], in_=ps[0:64, 0:TT]), rd=(ps,), wr=(Mb[nxt],))
                yield
                if p < 5:
                    ps2 = nb()
                    macro(Mb[cur], loc(Mb[cur]), Mtb[cur], loc(Mtb[cur]), ps2)
                    P.op("act", lambda: nc.scalar.copy(out=Mtb[nxt][0:64, :], in_=ps2[0:64, 0:TT]), rd=(ps2,),
                         wr=(Mtb[nxt],))
                    yield
                ps3 = nb()
                macro(Mb[nxt], loc(Mb[nxt]), S.Tt, loc(S.Tt), ps3)
                P.op("dve", lambda: nc.vector.tensor_tensor(out=S.Tt[0:64, :], in0=ps3[0:64, 0:TT], in1=S.Tt[0:64, :], op=ALU.add),
                     rd=(ps3, S.Tt), wr=(S.Tt,))
                yield
                cur = nxt
            for c in range(NCG):
                cg = g * NCG + c
                cs = slice(c * CK, (c + 1) * CK)
                w_ps, u_ps = S.bk[0], S.bk[1]
                P.op("pe", lambda: nc.tensor.matmul(w_ps[0:64, 0:CK], lhsT=As(c), rhs=S.Hall[:, c, :], start=True,
                                                    stop=False), rd=(ops, S.Hall), wr=(w_ps,))
                P.op("pe", lambda: nc.tensor.matmul(w_ps[0:64, 0:CK], lhsT=S.Akt[:, cs], rhs=S.Vtok[:, cs], start=False,
                                                    stop=True), rd=(S.Akt, S.Vtok), wr=(w_ps,))
                P.op("act", lambda: nc.scalar.copy(out=S.Wsb[0:64, :], in_=w_ps[0:64, 0:CK]), rd=(w_ps,), wr=(S.Wsb,))
                P.op("act", lambda: nc.scalar.activation(out=S.tmpH[:], in_=S.Hf[:], func=AF.Copy,
                                                         scale=S.dC[:, cg:cg + 1]), rd=(S.Hf, S.dC), wr=(S.tmpH,))
                yield
                P.op("pe", lambda: nc.tensor.matmul(u_ps[0:64, 0:CK], lhsT=S.Tt[:, cs], rhs=S.Wsb[:], start=True,
                                                    stop=True), rd=(S.Tt, S.Wsb), wr=(u_ps,))
                P.op("dve", lambda: nc.vector.tensor_copy(out=S.U[0:64, cs], in_=u_ps[0:64, 0:CK]), rd=(u_ps,), wr=(S.U,))
                yield
                h_ps = w_ps
                P.op("pe", lambda: nc.tensor.matmul(h_ps[0:64, 64:64 + CK], lhsT=S.Btok[:, cs], rhs=S.U[:, cs],
                                                    start=True, stop=False), rd=(S.Btok, S.U), wr=(h_ps,))
                P.op("pe", lambda: nc.tensor.matmul(h_ps[0:64, 64:64 + CK], lhsT=S.Ktok[:, cs], rhs=S.Vtok[:, cs],
                                                    start=False, stop=True), rd=(S.Ktok, S.Vtok), wr=(h_ps,))
                P.op("dve", lambda: nc.vector.scalar_tensor_tensor(out=S.Hf[:], in0=h_ps[0:64, 64:64 + CK],
                                                                   scalar=S.dC[:, cg:cg + 1], in1=S.tmpH[:],
                                                                   op0=ALU.mult, op1=ALU.add),
                     rd=(h_ps, S.dC, S.tmpH), wr=(S.Hf,))
                P.op("act", lambda: nc.scalar.copy(out=S.Hall[0:64, c + 1, :], in_=S.Hf[:]), rd=(S.Hf,), wr=(S.Hall,))
                yield
            y_ps = nb()
            for c in range(NCG):
                cs = slice(c * CK, (c + 1) * CK)
                P.op("pe", lambda: nc.tensor.matmul(y_ps[0:64, cs], lhsT=S.Hall[:, c, :], rhs=Rs(c), start=True,
                                                    stop=False), rd=(S.Hall, ops), wr=(y_ps,))
                P.op("pe", lambda: nc.tensor.matmul(y_ps[0:64, cs], lhsT=S.U[:, cs], rhs=S.Arbt[:, cs], start=False,
                                                    stop=False), rd=(S.U, S.Arbt), wr=(y_ps,))
                P.op("pe", lambda: nc.tensor.matmul(y_ps[0:64, cs], lhsT=S.Vtok[:, cs], rhs=S.Arkt[:, cs], start=False,
                                                    stop=True), rd=(S.Vtok, S.Arkt), wr=(y_ps,))
            P.op("act", lambda: nc.scalar.copy(out=S.yT[0:64, :], in_=y_ps[0:64, 0:TT]), rd=(y_ps,), wr=(S.yT,))
            P.op("act", lambda: nc.scalar.activation(out=S.sqy[0:64, :], in_=S.yT[0:64, :], func=AF.Square), rd=(S.yT,),
                 wr=(S.sqy,))
            yield
            s_ps, q_ps = nb(), nb()
            P.op("pe", lambda: nc.tensor.matmul(s_ps[0:64, 0:TT], lhsT=ones64[:], rhs=S.yT[:], start=True, stop=True),
                 rd=(ones64, S.yT), wr=(s_ps,))
            P.op("pe", lambda: nc.tensor.matmul(q_ps[0:64, 0:TT], lhsT=ones64[:], rhs=S.sqy[:], start=True, stop=True),
                 rd=(ones64, S.sqy), wr=(q_ps,))
            P.op("act", lambda: nc.scalar.activation(out=S.mean[:], in_=s_ps[0:64, 0:TT], func=AF.Copy,
                                                     scale=1.0 / 64), rd=(s_ps,), wr=(S.mean,))
            P.op("pool", lambda: nc.gpsimd.tensor_tensor(out=S.sqy[0:64, :], in0=S.mean[:], in1=S.mean[:], op=ALU.mult),
                 rd=(S.mean,), wr=(S.sqy,))
            P.op("dve", lambda: nc.vector.scalar_tensor_tensor(out=S.rstd[:], in0=q_ps[0:64, 0:TT], scalar=1.0 / 64,
                                                               in1=S.sqy[0:64, :], op0=ALU.mult, op1=ALU.subtract),
                 rd=(q_ps, S.sqy), wr=(S.rstd,))
            yield
            P.op("act", lambda: nc.scalar.activation(out=S.rstd[:], in_=S.rstd[:], func=AF.Ln, bias=64e-5, scale=1.0),
                 rd=(S.rstd,), wr=(S.rstd,))
            P.op("act", lambda: nc.scalar.activation(out=S.rstd[:], in_=S.rstd[:], func=AF.Exp, scale=-0.5),
                 rd=(S.rstd,), wr=(S.rstd,))
            P.op("pool", lambda: nc.gpsimd.tensor_tensor(out=S.yT[0:64, :], in0=S.yT[0:64, :], in1=S.mean[:], op=ALU.subtract),
                 rd=(S.yT, S.mean), wr=(S.yT,))
            yield
            P.op("dve", lambda: nc.vector.scalar_tensor_tensor(out=S.yT[0:64, :], in0=S.yT[0:64, :], scalar=lnxg[:, h:h + 1],
                                                               in1=S.rstd[:], op0=ALU.mult, op1=ALU.mult),
                 rd=(S.yT, lnxg, S.rstd), wr=(S.yT,))
            P.op("dve", lambda: nc.vector.tensor_tensor(out=S.yT[0:64, :], in0=S.yT[0:64, :], in1=ops[0:64, 5, :], op=ALU.mult),
                 rd=(S.yT, ops), wr=(S.yT,))
            P.op("pool", lambda: nc.gpsimd.tensor_tensor(out=S.yo[:], in0=S.yT[0:64, :], in1=ops[0:64, 6, :], op=ALU.add),
                 rd=(S.yT, ops), wr=(S.yo,))
            P.dma("sp", yab, S.yo, yab_ap[(h % 2) * 64:(h % 2) * 64 + 64, h // 2, g0:g0 + TT], S.yo[:])
            yield

    for h0 in range(0, NH, NHC):
        gens = [head_prog(slots[i], h0 + i) for i in range(NHC)]
        while gens:
            for gn in list(gens):
                try:
                    next(gn)
                except StopIteration:
                    gens.remove(gn)
    C.close()


def mixer_ab_phase(P, T, xin, xout, xin_ap, xout_ap, A, gpre_ap, gpost_ap, scr):
    rwkv_prep_stage(P, T, xin, xin_ap, gpre_ap, A, scr)
    rwkv_scan_stage(P, T, A, scr)
    outproj_stage(P, T, xin, xout, xin_ap, xout_ap, scr["oT"][0], scr["oT"][1], A["w_out"], gpost_ap)


def build(T, phases):
    nc = bass.Bass("TRN2", target_bir_lowering=False)
    t = {}
    t["x"] = nc.dram_tensor("x", [T, D], F32, kind="ExternalInput").ap()
    t["norm_gains"] = nc.dram_tensor("norm_gains", [4, 6, D], F32, kind="ExternalInput").ap()
    t["ffn_w_gate"] = nc.dram_tensor("ffn_w_gate", [4, 2, D, DFF], F32, kind="ExternalInput").ap()
    t["ffn_w_up"] = nc.dram_tensor("ffn_w_up", [4, 2, D, DFF], F32, kind="ExternalInput").ap()
    t["ffn_w_down"] = nc.dram_tensor("ffn_w_down", [4, 2, DFF, D], F32, kind="ExternalInput").ap()
    t["c_w_in"] = nc.dram_tensor("c_w_in", [2, D, 4 * D + 16], F32, kind="ExternalInput").ap()
    t["c_forget_bias"] = nc.dram_tensor("c_forget_bias", [2, 16], F32, kind="ExternalInput").ap()
    t["c_q_norm_g"] = nc.dram_tensor("c_q_norm_g", [2, 64], F32, kind="ExternalInput").ap()
    t["c_k_norm_g"] = nc.dram_tensor("c_k_norm_g", [2, 64], F32, kind="ExternalInput").ap()
    t["c_w_out"] = nc.dram_tensor("c_w_out", [2, D, D], F32, kind="ExternalInput").ap()
    for nm, shp in (("a_w_in", [2, D, 2816]), ("a_shift_mu", [2, 1792]), ("a_decay_up", [2, 64, 512]),
                    ("a_decay_base", [2, 512]), ("a_iclr_up", [2, 64, 512]), ("a_iclr_base", [2, 512]),
                    ("a_gate_up", [2, 128, 512]), ("a_k_k", [2, 512]), ("a_k_a", [2, 512]), ("a_r_k", [2, 8, 64]),
                    ("a_lnx_g", [2, 512]), ("a_lnx_b", [2, 512]), ("b_conv_w", [2, 31, 512]), ("b_conv_b", [2, 512]),
                    ("b_ln_g", [2, 512]), ("b_ln_b", [2, 512]), ("ab_w_out", [2, D, D])):
        t[nm] = nc.dram_tensor(nm, shp, F32, kind="ExternalInput").ap()
    y = nc.dram_tensor("y", [T, D], F32, kind="ExternalOutput").ap()
    rwd = nc.dram_tensor("rwd", [512, 7, T], BF16).ap()
    dcd = nc.dram_tensor("dcd", [512, max(T // 64, 1)], F32).ap()
    hTd = nc.dram_tensor("hTd", [128, NCH, T], BF16).ap()
    oTd = nc.dram_tensor("oTd", [128, NCH, T], BF16).ap()
    cumd = nc.dram_tensor("cumd", [16, T], F32).ap()
    xa = nc.dram_tensor("xa", [T, D], F32).ap()
    xb = nc.dram_tensor("xb", [T, D], F32).ap()
    P = Prog(nc)
    with nc.allow_low_precision("bf16 matmul operands, fp32 accumulation"):
        P.clear_all()
        nc.all_engine_barrier()
        bx = Buf(P, None, "x_in")
        cur_ap, cur = t["x"], bx
        pp = [(xa, Buf(P, None, "xa", dma=True)), (xb, Buf(P, None, "xb", dma=True))]
        yb = Buf(P, None, "y", dma=True)
        scr = {"hT": (hTd, Buf(P, None, "hTd", dma=True)), "oT": (oTd, Buf(P, None, "oTd", dma=True)),
               "cum": (cumd, Buf(P, None, "cumd", dma=True)), "rw": (rwd, Buf(P, None, "rwd", dma=True)),
               "dc": (dcd, Buf(P, None, "dcd", dma=True))}
        for i, ph in enumerate(phases):
            last = (i == len(phases) - 1)
            dst_ap, dst = (y, yb) if last else pp[i % 2]
            if ph[0] == "ffn":
                l, s = ph[1], ph[2]
                ffn_phase(P, T, cur, dst, cur_ap, dst_ap, t["ffn_w_gate"][l, s], t["ffn_w_up"][l, s],
                          t["ffn_w_down"][l, s], t["norm_gains"][l, 3 * s if s == 0 else 4],
                          t["norm_gains"][l, 1 if s == 0 else 5])
            elif ph[0] == "ab":
                l = ph[1]
                i2 = l // 2
                A = {"w_in": t["a_w_in"][i2], "shift_mu": t["a_shift_mu"][i2], "decay_up": t["a_decay_up"][i2],
                     "decay_base": t["a_decay_base"][i2], "iclr_up": t["a_iclr_up"][i2],
                     "iclr_base": t["a_iclr_base"][i2], "gate_up": t["a_gate_up"][i2], "k_k": t["a_k_k"][i2],
                     "k_a": t["a_k_a"][i2], "r_k": t["a_r_k"][i2], "lnx_g": t["a_lnx_g"][i2],
                     "lnx_b": t["a_lnx_b"][i2], "conv_w": t["b_conv_w"][i2], "conv_b": t["b_conv_b"][i2],
                     "ln_g": t["b_ln_g"][i2], "ln_b": t["b_ln_b"][i2], "w_out": t["ab_w_out"][i2]}
                mixer_ab_phase(P, T, cur, dst, cur_ap, dst_ap, A, t["norm_gains"][l, 2], t["norm_gains"][l, 3], scr)
            elif ph[0] in ("ab1", "ab2", "ab3"):
                i2 = 0
                A = {"w_in": t["a_w_in"][i2], "shift_mu": t["a_shift_mu"][i2], "decay_up": t["a_decay_up"][i2],
                     "decay_base": t["a_decay_base"][i2], "iclr_up": t["a_iclr_up"][i2],
                     "iclr_base": t["a_iclr_base"][i2], "gate_up": t["a_gate_up"][i2], "k_k": t["a_k_k"][i2],
                     "k_a": t["a_k_a"][i2], "r_k": t["a_r_k"][i2], "lnx_g": t["a_lnx_g"][i2],
                     "lnx_b": t["a_lnx_b"][i2], "conv_w": t["b_conv_w"][i2], "conv_b": t["b_conv_b"][i2],
                     "ln_g": t["b_ln_g"][i2], "ln_b": t["b_ln_b"][i2], "w_out": t["ab_w_out"][i2]}
                if ph[0] == "ab1":
                    rwkv_prep_stage(P, T, cur, cur_ap, t["norm_gains"][0, 2], A, scr)
                elif ph[0] == "ab2":
                    rwkv_scan_stage(P, T, A, scr)
                else:
                    outproj_stage(P, T, cur, dst, cur_ap, dst_ap, scr["oT"][0], scr["oT"][1], A["w_out"],
                                  t["norm_gains"][0, 3])
            elif ph[0] == "fox":
                l = ph[1]
                i2 = l // 2
                fox_phase(P, T, cur, dst, cur_ap, dst_ap, t["c_w_in"][i2], t["c_forget_bias"][i2],
                          t["c_q_norm_g"][i2], t["c_k_norm_g"][i2], t["c_w_out"][i2], t["norm_gains"][l, 2],
                          t["norm_gains"][l, 3], scr)
            cur_ap, cur = dst_ap, dst
        P.wait_all_dma("sp")
        nc.all_engine_barrier()
        P.clear_all()
    P.stack.close()
    print("instructions:", P.n_ins, "waits:", P.n_wait, "dsems:", P.ndsem)
    return nc


SEQ = 4096
N_CORES = 8
IN_NAMES = ("norm_gains", "ffn_w_gate", "ffn_w_up", "ffn_w_down", "a_w_in", "a_shift_mu", "a_decay_up",
            "a_decay_base", "a_iclr_up", "a_iclr_base", "a_gate_up", "a_k_k", "a_k_a", "a_r_k", "a_lnx_g", "a_lnx_b",
            "b_conv_w", "b_conv_b", "b_ln_g", "b_ln_b", "ab_w_out", "c_w_in", "c_forget_bias", "c_q_norm_g",
            "c_k_norm_g", "c_w_out")


def all_phases(depth=4):
    ph = []
    for l in range(depth):
        ph.append(("ffn", l, 0))
        ph.append(("ab", l) if l % 2 == 0 else ("fox", l))
        ph.append(("ffn", l, 1))
    return ph


_NC_CACHE = {}


def kernel(**inputs):
    x = np.ascontiguousarray(np.asarray(inputs["x"], dtype=np.float32))
    B, T, _ = x.shape
    key = (T,)
    if key not in _NC_CACHE:
        _NC_CACHE[key] = build(T, all_phases())
    nc = _NC_CACHE[key]
    shared = {k: np.ascontiguousarray(np.asarray(inputs[k], dtype=np.float32)) for k in IN_NAMES}
    in_maps = []
    for b in range(B):
        m = dict(shared)
        m["x"] = x[b]
        in_maps.append(m)
    res = run_bass_kernel_spmd(nc, in_maps, core_ids=list(range(B)))
    return np.stack([np.asarray(r["y"], dtype=np.float32) for r in res.results], axis=0)
```

```python
import contextlib
import numpy as np
import concourse.bass as bass
import concourse.mybir as mybir
from concourse.bass_utils import run_bass_kernel_spmd

F32 = mybir.dt.float32
BF16 = mybir.dt.bfloat16
AF = mybir.ActivationFunctionType
ALU = mybir.AluOpType
AX = mybir.AxisListType

D = 1024
DFF = 2816
NCH = D // 128
NFF = DFF // 128
EPS = 1e-6


class Buf:
    def __init__(self, prog, t, name, dma=False):
        self.t = t
        self.name = name
        self.lw = None
        self.rd = {}
        self.dsem = None
        self.psum = False
        self.dkind = None
        if dma:
            self.dkind = "sw" if dma == "sw" else "hw"
            self.dsem = prog.get_dsem(self.dkind)

    def __getitem__(self, k):
        return self.t[k]


class Prog:
    ENG = ("pe", "act", "dve", "pool", "sp")

    def __init__(self, nc):
        self.nc = nc
        self.e = {"pe": nc.tensor, "act": nc.scalar, "dve": nc.vector, "pool": nc.gpsimd, "sp": nc.sync}
        self.stack = contextlib.ExitStack()
        self.sems = {}
        self.val = {}
        self.seen = {e: {} for e in self.ENG}
        for e in self.ENG:
            self.sems[e] = self.stack.enter_context(nc.semaphore("c_" + e))
            self.val[e] = 0
        self.dpool = {"hw": [], "sw": []}
        self.ndsem = 0
        self.live_dsems = set()
        self.n_ins = 0
        self.n_wait = 0

    def get_dsem(self, kind):
        if self.dpool[kind]:
            k = self.dpool[kind].pop()
        else:
            k = "d%s%d" % (kind, self.ndsem)
            self.ndsem += 1
            self.sems[k] = self.stack.enter_context(self.nc.semaphore(k))
            self.val[k] = 0
        self.live_dsems.add(k)
        return k

    def release(self, bufs):
        for b in bufs:
            if b.dsem is not None:
                self.live_dsems.discard(b.dsem)
                self.dpool[b.dkind].append(b.dsem)
                b.dsem = None

    def clear_all(self):
        for k, h in self.sems.items():
            self.nc.gpsimd.sem_clear(h)

    def _need(self, e, waits, tok):
        if tok is None:
            return
        k, v = tok[0], tok[1]
        if self.seen[e].get(k, 0) >= v:
            return
        if waits.get(k, 0) < v:
            waits[k] = v

    def _deps(self, e, rd, wr):
        waits = {}
        for b in rd:
            self._need(e, waits, b.lw)
            if b.psum:
                for k, (v, re_) in b.rd.items():
                    if re_ != e:
                        self._need(e, waits, (k, v))
        for b in wr:
            if b.lw is not None and not (e == "pe" and b.lw[2] == "pe"):
                self._need(e, waits, b.lw)
            for k, (v, re_) in b.rd.items():
                self._need(e, waits, (k, v))
        for k, v in waits.items():
            self.e[e].wait_ge(self.sems[k], v)
            self.seen[e][k] = v
            self.n_wait += 1

    def op(self, e, fn, rd=(), wr=()):
        self._deps(e, rd, wr)
        ins = fn()
        self.val[e] += 1
        ins.then_inc(self.sems[e], 1)
        self.n_ins += 1
        v = self.val[e]
        for b in rd:
            b.rd[e] = (v, e)
        for b in wr:
            b.lw = (e, v, e)
            b.rd = {}
        return ins

    def dma(self, e, dst, src, out_ap, in_ap, **kw):
        assert dst.dsem is not None, dst.name
        assert (dst.dkind == "sw") == (e == "pool"), (dst.name, e)
        self._deps(e, (src,), (dst,))
        ins = self.e[e].dma_start(out=out_ap, in_=in_ap, **kw)
        k = dst.dsem
        self.val[k] += 16
        ins.then_inc(self.sems[k], 16)
        self.n_ins += 1
        v = self.val[k]
        src.rd[k] = (v, "dma")
        dst.lw = (k, v, "dma")
        dst.rd = {}
        return ins

    def barrier(self):
        for k in list(self.live_dsems):
            v = self.val[k]
            if v > 0 and self.seen["sp"].get(k, 0) < v:
                self.e["sp"].wait_ge(self.sems[k], v)
                self.seen["sp"][k] = v
        self.nc.all_engine_barrier()
        for e in self.ENG:
            for k in self.sems:
                self.seen[e][k] = self.val[k]

    def wait_all_dma(self, e):
        for k in list(self.live_dsems):
            v = self.val[k]
            if v > 0 and self.seen[e].get(k, 0) < v:
                self.e[e].wait_ge(self.sems[k], v)
                self.seen[e][k] = v


class Ctx:
    def __init__(self, P):
        self.P = P
        self.nc = P.nc
        self.stack = contextlib.ExitStack()
        self.bufs = []

    def sb(self, name, shape, dt, dma=False):
        t = self.stack.enter_context(self.nc.sbuf_tensor(name, list(shape), dt))
        b = Buf(self.P, t, name, dma=dma)
        self.bufs.append(b)
        return b

    def ps(self, name, shape=(128, 512), dt=F32):
        t = self.stack.enter_context(self.nc.psum_tensor(name, list(shape), dt))
        b = Buf(self.P, t, name)
        b.psum = True
        self.bufs.append(b)
        return b

    def close(self):
        self.P.barrier()
        self.P.release(self.bufs)
        self.stack.close()


_uid = [0]


def uname(s):
    _uid[0] += 1
    return "%s_%d" % (s, _uid[0])


def make_consts(P, C):
    nc = P.nc
    idf = C.sb(uname("ident_f"), (128, 128), F32)
    idb = C.sb(uname("ident_b"), (128, 128), BF16)
    P.op("pool", lambda: nc.gpsimd.memset(idf[:], 0.0), wr=(idf,))
    P.op("pool", lambda: nc.gpsimd.affine_select(out=idf[:], in_=idf[:], compare_op=ALU.not_equal, fill=1.0,
                                                  base=0, pattern=[[-1, 128]], channel_multiplier=1),
         rd=(idf,), wr=(idf,))
    P.op("pool", lambda: nc.gpsimd.tensor_copy(out=idb[:], in_=idf[:]), rd=(idf,), wr=(idb,))
    return idf, idb


def load_gain_bc(P, C, name, src_ap, scale):
    nc = P.nc
    g = C.sb(name, (128, D), F32, dma=True)
    dsrc = Buf(P, None, name + "_src")
    P.dma("sp", g, dsrc, g[:], src_ap.partition_broadcast(128))
    if scale != 1.0:
        P.op("pool", lambda: nc.gpsimd.tensor_scalar(out=g[:], in0=g[:], scalar1=float(scale), scalar2=None,
                                                      op0=ALU.mult), rd=(g,), wr=(g,))
    return g


def prenorm_tile(P, nc, xs, gpre, hT, col0, idb, sq_junk, stat, hb, tp_ps):
    P.op("act", lambda: nc.scalar.activation(out=sq_junk[:], in_=xs[:], func=AF.Square, accum_out=stat[:, 0:1]),
         rd=(xs,), wr=(sq_junk, stat))
    P.op("act", lambda: nc.scalar.activation(out=stat[:, 6:7], in_=stat[:, 0:1], func=AF.Sqrt, bias=float(EPS),
                                             scale=1.0 / D), rd=(stat,), wr=(stat,))
    P.op("dve", lambda: nc.vector.reciprocal(out=stat[:, 1:2], in_=stat[:, 6:7]), rd=(stat,), wr=(stat,))
    P.op("dve", lambda: nc.vector.scalar_tensor_tensor(out=hb[:], in0=xs[:], scalar=stat[:, 1:2], in1=gpre[:],
                                                       op0=ALU.mult, op1=ALU.mult), rd=(xs, stat, gpre), wr=(hb,))
    for c in range(NCH):
        P.op("pe", lambda c=c: nc.tensor.transpose(out=tp_ps[:, c * 128:(c + 1) * 128],
                                                   in_=hb[:, c * 128:(c + 1) * 128], identity=idb[:]),
             rd=(hb, idb), wr=(tp_ps,))
    P.op("act", lambda: nc.scalar.copy(out=hT[:, :, col0:col0 + 128],
                                       in_=tp_ps[:].rearrange("p (c t) -> p c t", c=NCH)),
         rd=(tp_ps,), wr=(hT,))


def postnorm_tile(P, nc, f_ps, xs, gpost, xo, sq_junk, stat, tmp):
    P.op("act", lambda: nc.scalar.activation(out=sq_junk[:, 0:512], in_=f_ps[0][:], func=AF.Square,
                                             accum_out=stat[:, 2:3]), rd=(f_ps[0],), wr=(sq_junk, stat))
    P.op("act", lambda: nc.scalar.activation(out=sq_junk[:, 512:1024], in_=f_ps[1][:], func=AF.Square,
                                             accum_out=stat[:, 3:4]), rd=(f_ps[1],), wr=(sq_junk, stat))
    P.op("dve", lambda: nc.vector.tensor_tensor(out=stat[:, 4:5], in0=stat[:, 2:3], in1=stat[:, 3:4], op=ALU.add),
         rd=(stat,), wr=(stat,))
    P.op("act", lambda: nc.scalar.activation(out=stat[:, 7:8], in_=stat[:, 4:5], func=AF.Sqrt, bias=float(EPS),
                                             scale=1.0 / D), rd=(stat,), wr=(stat,))
    P.op("dve", lambda: nc.vector.reciprocal(out=stat[:, 5:6], in_=stat[:, 7:8]), rd=(stat,), wr=(stat,))
    for h in range(2):
        P.op("dve", lambda h=h: nc.vector.scalar_tensor_tensor(out=tmp[:, h * 512:(h + 1) * 512], in0=f_ps[h][:],
                                                               scalar=stat[:, 5:6],
                                                               in1=gpost[:, h * 512:(h + 1) * 512],
                                                               op0=ALU.mult, op1=ALU.mult),
             rd=(f_ps[h], stat, gpost), wr=(tmp,))
    P.op("pool", lambda: nc.gpsimd.tensor_tensor(out=xo[:], in0=tmp[:], in1=xs[:], op=ALU.add),
         rd=(tmp, xs), wr=(xo,))


def load_w(P, wb, out_ap, src_ap):
    P.dma("pool", wb, Buf(P, None, "wsrc"), out_ap, src_ap)


def ffn_phase(P, T, xin, xout, xin_ap, xout_ap, wg_ap, wu_ap, wd_ap, gpre_ap, gpost_ap):
    nc = P.nc
    C = Ctx(P)
    idf, idb = make_consts(P, C)
    gpre = load_gain_bc(P, C, uname("gpre"), gpre_ap, 1.0)
    gpost = load_gain_bc(P, C, uname("gpost"), gpost_ap, 0.5)
    NB = 11
    CB = DFF // NB
    JB = NFF // NB
    wg = [C.sb(uname("wg"), (128, NCH, CB), BF16, dma="sw") for _ in range(NB)]
    wu = [C.sb(uname("wu"), (128, NCH, CB), BF16, dma="sw") for _ in range(NB)]
    wd = [C.sb(uname("wd"), (128, 2, D), BF16, dma="sw") for _ in range(NFF // 2)]
    wg_v = wg_ap.rearrange("(c p) n -> p c n", p=128)
    wu_v = wu_ap.rearrange("(c p) n -> p c n", p=128)
    wd_v = wd_ap.rearrange("(j p) n -> p j n", p=128)
    for b in range(NB):
        for (wl, wv) in ((wg, wg_v), (wu, wu_v)):
            load_w(P, wl[b], wl[b][:, :, :], wv[:, :, b * CB:(b + 1) * CB])
        if b >= 5:
            for j2 in (2 * (b - 5), 2 * (b - 5) + 1):
                if j2 < NFF // 2:
                    load_w(P, wd[j2], wd[j2][:, :, :], wd_v[:, 2 * j2:2 * j2 + 2, :])

    TT = 512 if T >= 512 else T
    NS = TT // 128
    xs = [C.sb(uname("xs"), (128, D), F32, dma=True) for _ in range(2)]
    xp = [C.sb(uname("xp"), (128, D), F32, dma=True) for _ in range(2)]
    hT = [C.sb(uname("hT"), (128, NCH, TT), BF16) for _ in range(2)]
    actT = C.sb(uname("actT"), (128, NFF, TT), BF16)
    sq_junk = C.sb(uname("sqj"), (128, D), BF16)
    hb = [C.sb(uname("hb"), (128, D), BF16) for _ in range(2)]
    tmp = C.sb(uname("tmp"), (128, D), F32)
    sg = [C.sb(uname("sg"), (128, TT), BF16) for _ in range(2)]
    stats = [C.sb(uname("stat"), (128, 8), F32) for _ in range(4)]
    tp_ps = C.ps(uname("tp"), (128, D), BF16)
    g_ps = [C.ps(uname("gps")) for _ in range(2)]
    u_ps = [C.ps(uname("ups")) for _ in range(2)]
    f3 = [C.ps(uname("fps")) for _ in range(3)]
    xin_v = xin_ap.rearrange("(n p) d -> n p d", p=128)
    xout_v = xout_ap.rearrange("(n p) d -> n p d", p=128)
    nt = T // TT

    def pre(ti, s):
        xst = xs[s % 2]
        P.dma("sp", xst, xin, xst[:], xin_v[ti * NS + s])
        prenorm_tile(P, nc, xst, gpre, hT[ti % 2], s * 128, idb, sq_junk, stats[s % 4], hb[s % 2], tp_ps)

    for s in range(NS):
        pre(0, s)
    it = 0
    for ti in range(nt):
        hTt = hT[ti % 2]
        for j in range(NFF):
            gp, up = g_ps[j % 2], u_ps[j % 2]
            bb, a0 = j // JB, (j % JB) * 128
            for (wl, pp) in ((wg, gp), (wu, up)):
                for c in range(NCH):
                    P.op("pe", lambda: nc.tensor.matmul(pp[:, 0:TT], lhsT=wl[bb][:, c, a0:a0 + 128],
                                                        rhs=hTt[:, c, :], start=(c == 0), stop=(c == NCH - 1)),
                         rd=(wl[bb], hTt), wr=(pp,))
            sgj = sg[j % 2]
            P.op("act", lambda: nc.scalar.activation(out=sgj[:], in_=gp[:, 0:TT], func=AF.Silu),
                 rd=(gp,), wr=(sgj,))
            P.op("dve", lambda: nc.vector.tensor_tensor(out=actT[:, j, :], in0=up[:, 0:TT], in1=sgj[:],
                                                        op=ALU.mult), rd=(up, sgj), wr=(actT,))
            if ti + 1 < nt and j in (4, 8, 12, 16) and (j // 4 - 1) < NS:
                pre(ti + 1, j // 4 - 1)
        for s in range(NS):
            xst = xp[it % 2]
            f_ps = [f3[(2 * it) % 3], f3[(2 * it + 1) % 3]]
            P.dma("sp", xst, xin, xst[:], xin_v[ti * NS + s])
            for h in range(2):
                for j in range(NFF):
                    P.op("pe", lambda: nc.tensor.matmul(
                        f_ps[h][:, :], lhsT=actT[:, j, s * 128:(s + 1) * 128],
                        rhs=wd[j // 2][:, j % 2, h * 512:(h + 1) * 512],
                        start=(j == 0), stop=(j == NFF - 1)), rd=(actT, wd[j // 2]), wr=(f_ps[h],))
            it += 1
            postnorm_tile(P, nc, f_ps, xst, gpost, xst, sq_junk, stats[s % 4], tmp)
            P.dma("sp", xout, xst, xout_v[ti * NS + s], xst[:])
    C.close()


def fox_phase(P, T, xin, xout, xin_ap, xout_ap, win_ap, fb_ap, qg_ap, kg_ap, wout_ap, gpre_ap, gpost_ap, scr):
    nc = P.nc
    H = 16
    NT = T // 512 if T >= 512 else 1
    TT = 512 if T >= 512 else T
    NSUB = T // 128
    hTd_ap, hTd = scr["hT"]
    oTd_ap, oTd = scr["oT"]
    win_v = win_ap.rearrange("(c p) n -> p c n", p=128)
    xin_v = xin_ap.rearrange("(n p) d -> n p d", p=128)
    xout_v = xout_ap.rearrange("(n p) d -> n p d", p=128)
    dsrc = Buf(P, None, "dsrc")

    C = Ctx(P)
    idf, idb = make_consts(P, C)
    gpre = load_gain_bc(P, C, uname("gpre"), gpre_ap, 1.0)
    wf = C.sb(uname("wf"), (128, NCH, H), BF16, dma="sw")
    load_w(P, wf, wf[:], win_v[:, :, 4 * D:4 * D + H])
    nfb = C.sb(uname("nfb"), (H, 1), F32, dma=True)
    P.dma("sp", nfb, dsrc, nfb[:], fb_ap.rearrange("(h o) -> h o", o=1))
    P.op("dve", lambda: nc.vector.tensor_scalar(out=nfb[:], in0=nfb[:], scalar1=-1.0, scalar2=None, op0=ALU.mult),
         rd=(nfb,), wr=(nfb,))
    cum = C.sb(uname("cum"), (H, T), F32)
    ones16 = C.sb(uname("ones16"), (H, TT), F32)
    P.op("pool", lambda: nc.gpsimd.memset(ones16[:], 1.0), wr=(ones16,))
    lfe = [C.sb(uname("lfe"), (H, TT), F32) for _ in range(2)]
    xs = [C.sb(uname("xs"), (128, D), F32, dma=True) for _ in range(2)]
    hTt = [C.sb(uname("hTt"), (128, NCH, TT), BF16) for _ in range(2)]
    sq_junk = C.sb(uname("sqj"), (128, D), BF16)
    hb = [C.sb(uname("hb"), (128, D), BF16) for _ in range(2)]
    stats = [C.sb(uname("stat"), (128, 8), F32) for _ in range(4)]
    tp_ps = C.ps(uname("tp"), (128, D), BF16)
    f_ps = [C.ps(uname("fl")) for _ in range(2)]
    NS = TT // 128
    for ti in range(NT):
        ht = hTt[ti % 2]
        for s_ in range(NS):
            xst = xs[s_ % 2]
            P.dma("sp", xst, xin, xst[:], xin_v[ti * NS + s_])
            prenorm_tile(P, nc, xst, gpre, ht, s_ * 128, idb, sq_junk, stats[s_ % 4], hb[s_ % 2], tp_ps)
        P.dma("sp", hTd, ht, hTd_ap[:, :, ti * TT:(ti + 1) * TT], ht[:])
        fp = f_ps[ti % 2]
        for c in range(NCH):
            P.op("pe", lambda: nc.tensor.matmul(fp[0:H, 0:TT], lhsT=wf[:, c, :], rhs=ht[:, c, :],
                                                start=(c == 0), stop=(c == NCH - 1)), rd=(wf, ht), wr=(fp,))
        le = lfe[ti % 2]
        P.op("act", lambda: nc.scalar.activation(out=le[:], in_=fp[0:H, 0:TT], func=AF.Exp, bias=nfb[:, 0:1],
                                                 scale=-1.0), rd=(fp, nfb), wr=(le,))
        P.op("act", lambda: nc.scalar.activation(out=le[:], in_=le[:], func=AF.Ln, bias=1.0, scale=1.0),
             rd=(le,), wr=(le,))
        init = 0.0 if ti == 0 else cum[:, ti * TT - 1:ti * TT]
        P.op("dve", lambda: nc.vector.tensor_tensor_scan(out=cum[:, ti * TT:(ti + 1) * TT], data0=ones16[:],
                                                         data1=le[:], initial=init, op0=ALU.mult,
                                                         op1=ALU.subtract), rd=(ones16, le, cum), wr=(cum,))
    cumd_ap, cumd = scr["cum"]
    P.dma("sp", cumd, cum, cumd_ap[:, 0:T], cum[:])
    C.close()

    C = Ctx(P)
    idf, idb = make_consts(P, C)
    cum = C.sb(uname("cum"), (H, T), F32, dma=True)
    P.dma("sp", cum, cumd, cum[:], cumd_ap[:, 0:T])
    sel = C.sb(uname("sel"), (H, H, 128), F32)
    P.op("pool", lambda: nc.gpsimd.memset(sel[:], 0.0), wr=(sel,))
    P.op("pool", lambda: nc.gpsimd.affine_select(out=sel[:], in_=sel[:], compare_op=ALU.not_equal, fill=1.0, base=0,
                                                  pattern=[[-1, H], [0, 128]], channel_multiplier=1),
         rd=(sel,), wr=(sel,))
    swp = C.sb(uname("swp"), (128, 128), F32)
    P.op("pool", lambda: nc.gpsimd.memset(swp[:], 0.0), wr=(swp,))
    P.op("pool", lambda: nc.gpsimd.affine_select(out=swp[:, 0:64], in_=swp[:, 0:64], compare_op=ALU.not_equal,
                                                  fill=1.0, base=-64, pattern=[[-1, 64]], channel_multiplier=1),
         rd=(swp,), wr=(swp,))
    P.op("pool", lambda: nc.gpsimd.affine_select(out=swp[:, 64:128], in_=swp[:, 64:128], compare_op=ALU.not_equal,
                                                  fill=1.0, base=0, pattern=[[-1, 64]], channel_multiplier=1),
         rd=(swp,), wr=(swp,))
    bones = C.sb(uname("bones"), (128, 128), F32)
    P.op("pool", lambda: nc.gpsimd.memset(bones[:], 0.0), wr=(bones,))
    P.op("pool", lambda: nc.gpsimd.memset(bones[0:64, 0:64], 1.0), wr=(bones,))
    P.op("pool", lambda: nc.gpsimd.memset(bones[64:128, 64:128], 1.0), wr=(bones,))
    tri = C.sb(uname("tri"), (128, 128), F32)
    P.op("pool", lambda: nc.gpsimd.memset(tri[:], 0.0), wr=(tri,))
    P.op("pool", lambda: nc.gpsimd.affine_select(out=tri[:], in_=tri[:], compare_op=ALU.is_ge, fill=-30000.0, base=0,
                                                  pattern=[[1, 128]], channel_multiplier=-1), rd=(tri,), wr=(tri,))
    ncumT = C.sb(uname("ncumT"), (128, NSUB, H), F32)
    ct_ps = C.ps(uname("ctps"))
    for g0 in range(0, NSUB, 32):
        gn = min(32, NSUB - g0)
        for b in range(gn):
            P.op("pe", lambda: nc.tensor.transpose(out=ct_ps[:, b * H:(b + 1) * H],
                                                   in_=cum[:, (g0 + b) * 128:(g0 + b + 1) * 128],
                                                   identity=idf[0:H, 0:H]), rd=(cum, idf), wr=(ct_ps,))
        P.op("dve", lambda: nc.vector.tensor_scalar(out=ncumT[:, g0:g0 + gn, :],
                                                    in0=ct_ps[:, 0:gn * H].rearrange("p (b h) -> p b h", h=H),
                                                    scalar1=-1.0, scalar2=None, op0=ALU.mult),
             rd=(ct_ps,), wr=(ncumT,))
    gq2 = C.sb(uname("gq2"), (128, 1), F32, dma=True)
    gk2 = C.sb(uname("gk2"), (128, 1), F32, dma=True)
    for hh in range(2):
        P.dma("sp", gq2, dsrc, gq2[hh * 64:(hh + 1) * 64, :], qg_ap.rearrange("(d o) -> d o", o=1))
        P.dma("sp", gk2, dsrc, gk2[hh * 64:(hh + 1) * 64, :], kg_ap.rearrange("(d o) -> d o", o=1))
    P.op("dve", lambda: nc.vector.tensor_scalar(out=gq2[:], in0=gq2[:], scalar1=0.125, scalar2=None, op0=ALU.mult),
         rd=(gq2,), wr=(gq2,))
    wq = [C.sb(uname("wq"), (128, NCH, 4, 128), BF16, dma="sw") for _ in range(2)]
    hTt = [C.sb(uname("hTt"), (128, NCH, TT), BF16, dma=True) for _ in range(2)]
    qT = C.sb(uname("qT"), (128, T), BF16)
    kT = "kT"
    kTz = [C.sb(uname("kTz"), (128, T), BF16) for _ in range(2)]
    for hh in range(2):
        P.op("pool", lambda: nc.gpsimd.memset(kTz[hh][:], 0.0), wr=(kTz[hh],))
    sgT = C.sb(uname("sgT"), (128, T), BF16)
    oT = C.sb(uname("oT"), (128, T), BF16)
    Va = [C.sb(uname("Va"), (128, NSUB, 128), BF16) for _ in range(2)]
    P.op("pool", lambda: nc.gpsimd.memset(Va[0][:, :, 64:128], 1.0), wr=(Va[0],))
    P.op("pool", lambda: nc.gpsimd.memset(Va[1][:, :, 0:64], 1.0), wr=(Va[1],))
    cbc = C.sb(uname("cbc"), (128, T), F32)
    sqf = [C.sb(uname("sqf"), (128, TT), F32) for _ in range(4)]
    rsf = [C.sb(uname("rsf"), (128, TT), F32) for _ in range(2)]
    tmpb = [C.sb(uname("tmpb"), (128, TT), F32) for _ in range(4)]
    osb = C.sb(uname("osb"), (128, TT), F32)
    rden = C.sb(uname("rden"), (128, TT), F32)
    onum = C.sb(uname("onum"), (128, TT), F32)
    st2 = [C.ps(uname("st2"), (128, 2 * TT), F32) for _ in range(2)]

    class HalfView:
        def __init__(self, t, off):
            self.t, self.off = t, off

        def __getitem__(self, k):
            p_, c_ = k
            return self.t[p_, slice(self.off + (c_.start or 0), self.off + c_.stop)]
    sth = []
    for k_ in range(2):
        for hf_ in range(2):
            b_ = Buf(P, HalfView(st2[k_].t, hf_ * TT), uname("sth"))
            b_.psum = True
            C.bufs.append(b_)
            sth.append(b_)
    acc = [C.ps(uname("acc")) for _ in range(3)] + [ct_ps]
    bank = [st2[0], st2[1]] + acc
    pT2 = [C.sb(uname("pT2"), (128, 2 * TT), BF16) for _ in range(4)]
    ecr = [C.sb(uname("ecr"), (128, TT), F32) for _ in range(2)]
    dsw = C.sb(uname("dsw"), (128, TT), F32, dma=True)
    cbcm = C.sb(uname("cbcm"), (128, T), F32)
    tri4 = C.sb(uname("tri4"), (128, TT), F32)
    for r_ in range(TT // 128):
        P.op("pool", lambda: nc.gpsimd.tensor_copy(out=tri4[:, r_ * 128:(r_ + 1) * 128], in_=tri[:]), rd=(tri,),
             wr=(tri4,))
    ncb0 = C.sb(uname("ncb0"), (128, NT), F32)
    biasall = C.sb(uname("biasall"), (128, NT, NSUB), F32)
    def load_pair_w(hp_):
        w_ = wq[hp_ % 2]
        for qi in range(4):
            load_w(P, w_, w_[:, :, qi, :], win_v[:, :, qi * D + hp_ * 128: qi * D + (hp_ + 1) * 128])

    load_pair_w(0)
    for hp in range(H // 2):
        w = wq[hp % 2]
        if hp + 1 < H // 2:
            load_pair_w(hp + 1)
        def psets(ti):
            pset = ti % 2
            return sth[2 * pset], sth[2 * pset + 1], acc[2 * pset], acc[2 * pset + 1]

        def p_mm(ti):
            ht = hTt[ti % 2]
            P.dma("sp", ht, hTd, ht[:], hTd_ap[:, :, ti * TT:(ti + 1) * TT])
            qb_, kb_, g_ps, v_ps = psets(ti)
            for (qi, pp, pb_) in ((0, qb_[:, 0:TT], qb_), (1, kb_[:, 0:TT], kb_), (3, g_ps[:, 0:TT], g_ps)):
                for c in range(NCH):
                    P.op("pe", lambda: nc.tensor.matmul(pp, lhsT=w[:, c, qi, :], rhs=ht[:, c, :],
                                                        start=(c == 0), stop=(c == NCH - 1)), rd=(w, ht), wr=(pb_,))
            for s_ in range(NS):
                for c in range(NCH):
                    P.op("pe", lambda: nc.tensor.matmul(v_ps[:, s_ * 128:(s_ + 1) * 128],
                                                        lhsT=ht[:, c, s_ * 128:(s_ + 1) * 128], rhs=w[:, c, 2, :],
                                                        start=(c == 0), stop=(c == NCH - 1)), rd=(w, ht), wr=(v_ps,))

        def p_part1(ti):
            tsl = slice(ti * TT, (ti + 1) * TT)
            qb_, kb_, g_ps, v_ps = psets(ti)
            vv = v_ps[:, 0:TT].rearrange("p (s e) -> p s e", e=128)
            P.op("act", lambda: nc.scalar.copy(out=Va[0][:, ti * NS:(ti + 1) * NS, 0:64], in_=vv[:, :, 0:64]),
                 rd=(v_ps,), wr=(Va[0],))
            P.op("dve", lambda: nc.vector.tensor_copy(out=Va[1][:, ti * NS:(ti + 1) * NS, 64:128],
                                                      in_=vv[:, :, 64:128]), rd=(v_ps,), wr=(Va[1],))
            P.op("act", lambda: nc.scalar.activation(out=sgT[:, tsl], in_=g_ps[:, 0:TT], func=AF.Sigmoid),
                 rd=(g_ps,), wr=(sgT,))
            for n_, qk_b in enumerate((qb_, kb_)):
                sq = sqf[2 * (ti % 2) + n_]
                P.op("act", lambda: nc.scalar.activation(out=sq[:], in_=qk_b[:, 0:TT], func=AF.Square),
                     rd=(qk_b,), wr=(sq,))

        def p_norm(ti):
            tsl = slice(ti * TT, (ti + 1) * TT)
            qb_, kb_, g_ps, v_ps = psets(ti)
            ss_ps = v_ps
            for n_, (gg, dst, qk_b) in enumerate(((gq2, qT, qb_), (gk2, kT, kb_))):
                sq, rs = sqf[2 * (ti % 2) + n_], rsf[n_]
                pp = qk_b[:, 0:TT]
                P.op("pe", lambda: nc.tensor.matmul(ss_ps[:, 0:TT], lhsT=bones[:], rhs=sq[:], start=True, stop=True),
                     rd=(bones, sq), wr=(ss_ps,))
                P.op("act", lambda: nc.scalar.activation(out=rs[:], in_=ss_ps[:, 0:TT], func=AF.Ln,
                                                         bias=float(EPS), scale=1.0 / 64), rd=(ss_ps,), wr=(rs,))
                P.op("act", lambda: nc.scalar.activation(out=rs[:], in_=rs[:], func=AF.Exp, scale=-0.5),
                     rd=(rs,), wr=(rs,))
                if dst is kT:
                    for hh in range(2):
                        hs_ = slice(hh * 64, (hh + 1) * 64)
                        P.op("dve", lambda: nc.vector.scalar_tensor_tensor(out=kTz[hh][hs_, tsl],
                                                                           in0=qk_b[hs_, 0:TT],
                                                                           scalar=gg[hs_, 0:1], in1=rs[hs_, :],
                                                                           op0=ALU.mult, op1=ALU.mult),
                             rd=(qk_b, gg, rs), wr=(kTz[hh],))
                else:
                    P.op("dve", lambda: nc.vector.scalar_tensor_tensor(out=dst[:, tsl], in0=pp,
                                                                       scalar=gg[:, 0:1], in1=rs[:], op0=ALU.mult,
                                                                       op1=ALU.mult), rd=(qk_b, gg, rs), wr=(dst,))

        p_mm(0)
        p_part1(0)
        for ti in range(1, NT):
            p_mm(ti)
            p_part1(ti)
            p_norm(ti - 1)
        p_norm(NT - 1)
        for par in range(2):
            h = 2 * hp + par
            pr = slice(par * 64, (par + 1) * 64)
            for ti in range(NT):
                cp = acc[ti % 4]
                P.op("pe", lambda: nc.tensor.matmul(cp[:, 0:TT], lhsT=sel[:, h, :], rhs=cum[:, ti * TT:(ti + 1) * TT],
                                                    start=True, stop=True), rd=(sel, cum), wr=(cp,))
                P.op("act", lambda: nc.scalar.copy(out=cbc[:, ti * TT:(ti + 1) * TT], in_=cp[:, 0:TT]),
                     rd=(cp,), wr=(cbc,))
                P.op("pool", lambda: nc.gpsimd.tensor_tensor(out=cbcm[:, ti * TT:(ti + 1) * TT],
                                                              in0=cbc[:, ti * TT:(ti + 1) * TT], in1=tri4[:],
                                                              op=ALU.add), rd=(cbc, tri4), wr=(cbcm,))
            pairs = [list(range(t0_, min(t0_ + 2, NT))) for t0_ in range(0, NT, 2)]
            c0v = cbc[:].rearrange("p (n t) -> p n t", t=TT * 2 if NT > 1 else TT)[:, :, 0]
            P.op("dve", lambda: nc.vector.tensor_scalar(out=ncb0[:, 0:len(pairs)], in0=c0v, scalar1=-1.0, scalar2=None,
                                                        op0=ALU.mult), rd=(cbc,), wr=(ncb0,))
            for m, tl in enumerate(pairs):
                if m == 0:
                    continue
                nb_ = tl[0] * NS
                P.op("dve", lambda: nc.vector.tensor_scalar(out=biasall[:, m, 0:nb_], in0=ncumT[:, 0:nb_, h],
                                                            scalar1=cbc[:, tl[0] * TT:tl[0] * TT + 1], scalar2=None,
                                                            op0=ALU.add), rd=(ncumT, cbc), wr=(biasall,))
            for m, tl in enumerate(pairs):
                ntl = len(tl)
                noff = tl[0] * NS
                a_off = [acc[0], acc[1]]
                a_dg = [acc[2], acc[3]]
                offs = [("off", kb) for kb in range(noff)]
                dvs = []
                for xi, ti in enumerate(tl):
                    for kb in range(noff, (ti + 1) * NS):
                        r = kb - ti * NS
                        dvs.append(("dv", xi, kb, (128 * r if r > 0 else 0), r >= 0))
                items = offs + dvs
                dv_first, dv_last, off_first, off_last = {}, {}, None, None
                for n_, it_ in enumerate(items):
                    if it_[0] == "off":
                        off_first = n_ if off_first is None else off_first
                        off_last = n_
                    else:
                        dv_first.setdefault(it_[1], n_)
                        dv_last[it_[1]] = n_
                LAG = 3
                for xi, ti in enumerate(tl):
                    if noff > 0:
                        P.op("act", lambda: nc.scalar.activation(out=ecr[xi][:], in_=cbc[:, ti * TT:(ti + 1) * TT],
                                                                 func=AF.Exp, bias=ncb0[:, m:m + 1], scale=1.0),
                             rd=(cbc, ncb0), wr=(ecr[xi],))

                def emit_s(n):
                    it_ = items[n]
                    pb = pT2[n % 4]
                    if it_[0] == "off":
                        kb = it_[1]
                        hb_ = [sth[2 * (n % 2) + xi] for xi in range(ntl)]
                        for xi, ti in enumerate(tl):
                            P.op("pe", lambda: nc.tensor.matmul(hb_[xi][:, 0:TT],
                                                                lhsT=kTz[par][:, kb * 128:(kb + 1) * 128],
                                                                rhs=qT[:, ti * TT:(ti + 1) * TT], start=True, stop=True),
                                 rd=(kTz[par], qT), wr=(hb_[xi],))
                        P.op("act", lambda: nc.scalar.activation(out=pb[:, 0:ntl * TT], in_=st2[n % 2].t[:, 0:ntl * TT],
                                                                 func=AF.Exp, bias=biasall[:, m, kb:kb + 1], scale=1.0),
                             rd=tuple(hb_) + (biasall,), wr=(pb,))
                        return
                    _, xi, kb, c0, tri_ = it_
                    ti = tl[xi]
                    dvn = n - len(offs)
                    sp_ = sth[dvn % 4]
                    P.op("pe", lambda: nc.tensor.matmul(sp_[:, c0:TT], lhsT=kTz[par][:, kb * 128:(kb + 1) * 128],
                                                        rhs=qT[:, ti * TT + c0:(ti + 1) * TT], start=True, stop=True),
                         rd=(kTz[par], qT), wr=(sp_,))
                    tb = tmpb[dvn % 4]
                    if tri_:
                        P.op("dve", lambda: nc.vector.scalar_tensor_tensor(
                            out=tb[:, c0:c0 + 128], in0=sp_[:, c0:c0 + 128], scalar=ncumT[:, kb, h:h + 1],
                            in1=cbcm[:, ti * TT + c0:ti * TT + c0 + 128], op0=ALU.add, op1=ALU.add),
                            rd=(sp_, ncumT, cbcm), wr=(tb,))
                        c1 = c0 + 128
                    else:
                        c1 = c0
                    if c1 < TT:
                        P.op("dve", lambda: nc.vector.scalar_tensor_tensor(
                            out=tb[:, c1:TT], in0=sp_[:, c1:TT], scalar=ncumT[:, kb, h:h + 1],
                            in1=cbc[:, ti * TT + c1:(ti + 1) * TT], op0=ALU.add, op1=ALU.add),
                            rd=(sp_, ncumT, cbc), wr=(tb,))
                    P.op("act", lambda: nc.scalar.activation(out=pb[:, c0:TT], in_=tb[:, c0:TT], func=AF.Exp),
                         rd=(tb,), wr=(pb,))

                def emit_pv(n):
                    it_ = items[n]
                    pb = pT2[n % 4]
                    if it_[0] == "off":
                        kb = it_[1]
                        for xi, ti in enumerate(tl):
                            P.op("pe", lambda: nc.tensor.matmul(a_off[xi][:, 0:TT], lhsT=Va[par][:, kb, :],
                                                                rhs=pb[:, xi * TT:(xi + 1) * TT],
                                                                start=(n == off_first), stop=(n == off_last)),
                                 rd=(Va[par], pb), wr=(a_off[xi],))
                        return
                    _, xi, kb, c0, tri_ = it_
                    P.op("pe", lambda: nc.tensor.matmul(a_dg[xi][:, c0:TT], lhsT=Va[par][:, kb, :], rhs=pb[:, c0:TT],
                                                        start=(n == dv_first[xi]), stop=(n == dv_last[xi])),
                         rd=(Va[par], pb), wr=(a_dg[xi],))

                for n in range(len(items) + LAG):
                    if n < len(items):
                        emit_s(n)
                    if n - LAG >= 0:
                        emit_pv(n - LAG)
                for xi, ti in enumerate(tl):
                    if noff > 0:
                        P.op("dve", lambda: nc.vector.tensor_tensor(out=osb[:], in0=a_off[xi][:, 0:TT], in1=ecr[xi][:],
                                                                    op=ALU.mult), rd=(a_off[xi], ecr[xi]), wr=(osb,))
                        P.op("dve", lambda: nc.vector.tensor_tensor(out=osb[:], in0=a_dg[xi][:, 0:TT], in1=osb[:],
                                                                    op=ALU.add), rd=(a_dg[xi], osb), wr=(osb,))
                    else:
                        P.op("act", lambda: nc.scalar.copy(out=osb[:], in_=a_dg[xi][:, 0:TT]), rd=(a_dg[xi],), wr=(osb,))
                    opr = slice((1 - par) * 64, (2 - par) * 64)
                    P.dma("sp", dsw, osb, dsw[pr, :], osb[opr, :])
                    P.op("act", lambda: nc.scalar.activation(out=rden[pr, :], in_=dsw[pr, :], func=AF.Ln),
                         rd=(dsw,), wr=(rden,))
                    P.op("act", lambda: nc.scalar.activation(out=rden[pr, :], in_=rden[pr, :], func=AF.Exp, scale=-1.0),
                         rd=(rden,), wr=(rden,))
                    P.op("pool", lambda: nc.gpsimd.tensor_tensor(out=onum[pr, :], in0=osb[pr, :], in1=rden[pr, :],
                                                                  op=ALU.mult), rd=(osb, rden), wr=(onum,))
                    P.op("pool", lambda: nc.gpsimd.tensor_tensor(out=oT[pr, ti * TT:(ti + 1) * TT], in0=onum[pr, :],
                                                                  in1=sgT[pr, ti * TT:(ti + 1) * TT], op=ALU.mult),
                         rd=(onum, sgT), wr=(oT,))
        P.dma("sp", oTd, oT, oTd_ap[:, hp, 0:T], oT[:])
    C.close()

    outproj_stage(P, T, xin, xout, xin_ap, xout_ap, oTd_ap, oTd, wout_ap, gpost_ap)


def outproj_stage(P, T, xin, xout, xin_ap, xout_ap, oTd_ap, oTd, wout_ap, gpost_ap):
    nc = P.nc
    TT = 512 if T >= 512 else T
    NT = T // TT
    NS = TT // 128
    xin_v = xin_ap.rearrange("(n p) d -> n p d", p=128)
    xout_v = xout_ap.rearrange("(n p) d -> n p d", p=128)
    C = Ctx(P)
    gpost = load_gain_bc(P, C, uname("gpost"), gpost_ap, 1.0)
    wo = C.sb(uname("wo"), (128, NCH, D), BF16, dma="sw")
    wo_v = wout_ap.rearrange("(c p) n -> p c n", p=128)
    for q in range(4):
        load_w(P, wo, wo[:, 2 * q:2 * q + 2, :], wo_v[:, 2 * q:2 * q + 2, :])
    ot = [C.sb(uname("ot"), (128, NCH, TT), BF16, dma=True) for _ in range(2)]
    xp = [C.sb(uname("xp"), (128, D), F32, dma=True) for _ in range(2)]
    sq_junk = C.sb(uname("sqj"), (128, D), BF16)
    tmp = C.sb(uname("tmp"), (128, D), F32)
    stats = [C.sb(uname("stat"), (128, 8), F32) for _ in range(4)]
    f_ps = [[C.ps(uname("fps")) for _ in range(2)] for _ in range(2)]
    it = 0
    for ti in range(NT):
        o_ = ot[ti % 2]
        P.dma("sp", o_, oTd, o_[:], oTd_ap[:, :, ti * TT:(ti + 1) * TT])
        for s_ in range(NS):
            xst = xp[it % 2]
            fp = f_ps[it % 2]
            P.dma("sp", xst, xin, xst[:], xin_v[ti * NS + s_])
            for hh in range(2):
                for c in range(NCH):
                    P.op("pe", lambda: nc.tensor.matmul(fp[hh][:, :], lhsT=o_[:, c, s_ * 128:(s_ + 1) * 128],
                                                        rhs=wo[:, c, hh * 512:(hh + 1) * 512],
                                                        start=(c == 0), stop=(c == NCH - 1)), rd=(o_, wo), wr=(fp[hh],))
            postnorm_tile(P, nc, fp, xst, gpost, xst, sq_junk, stats[it % 4], tmp)
            P.dma("sp", xout, xst, xout_v[ti * NS + s_], xst[:])
            it += 1
    C.close()


CW = 31
CK = 64
DEC_C = 0.6065306597126334


def rwkv_prep_stage(P, T, xin, xin_ap, gpre_ap, A, scr):
    nc = P.nc
    TT = 512 if T >= 512 else T
    NT = T // TT
    NS = TT // 128
    NQ = 4
    CR = 1792
    rw_ap, rw = scr["rw"]
    dc_ap, dcb = scr["dc"]
    yab_ap, yab = scr["oT"]
    xin_v = xin_ap.rearrange("(n p) d -> n p d", p=128)
    win_v = A["w_in"].rearrange("(c p) n -> p c n", p=128)
    dsrc = Buf(P, None, "dsrc")
    C = Ctx(P)
    idf, idb = make_consts(P, C)
    gpre = load_gain_bc(P, C, uname("gpre"), gpre_ap, 1.0)
    W1 = C.sb(uname("W1"), (128, NCH, CR), BF16)
    W2 = C.sb(uname("W2"), (128, NCH, CR), BF16)
    Wc = C.sb(uname("Wc"), (128, NCH, 1024), BF16, dma="sw")
    for q in range(4):
        load_w(P, Wc, Wc[:, 2 * q:2 * q + 2, :], win_v[:, 2 * q:2 * q + 2, CR:CR + 1024])
    C0 = Ctx(P)
    mu = C0.sb(uname("mu"), (128, CR), F32, dma=True)
    omu = C0.sb(uname("omu"), (128, CR), F32)
    P.dma("sp", mu, dsrc, mu[:], A["shift_mu"].partition_broadcast(128))
    P.op("dve", lambda: nc.vector.tensor_scalar(out=omu[:], in0=mu[:], scalar1=-1.0, scalar2=1.0, op0=ALU.mult,
                                                op1=ALU.add), rd=(mu,), wr=(omu,))
    stg = [C0.sb(uname("stg"), (128, CR), F32, dma=True) for _ in range(2)]
    for c in range(NCH):
        st = stg[c % 2]
        P.dma("sp", st, dsrc, st[:], win_v[:, c, 0:CR])
        P.op("dve", lambda: nc.vector.tensor_tensor(out=W2[:, c, :], in0=st[:], in1=mu[:], op=ALU.mult),
             rd=(st, mu), wr=(W2,))
        P.op("pool", lambda: nc.gpsimd.tensor_tensor(out=W1[:, c, :], in0=st[:], in1=omu[:], op=ALU.mult),
             rd=(st, omu), wr=(W1,))
    C0.close()
    lup = C.sb(uname("lup"), (128, 512), BF16, dma="sw")
    load_w(P, lup, lup[0:64, :], A["decay_up"])
    load_w(P, lup, lup[64:128, :], A["iclr_up"])
    gup = C.sb(uname("gup"), (128, 512), BF16, dma="sw")
    load_w(P, gup, gup[:], A["gate_up"])
    pnames = ("decay_base", "iclr_base", "k_k", "k_a", "r_k", "lnx_b", "conv_b", "ln_g", "ln_b")
    NPR = len(pnames) + CW
    prow = C.sb(uname("prow"), (NPR, 512), F32, dma=True)
    for i_, nm in enumerate(pnames):
        src = A[nm]
        if nm == "r_k":
            src = src.rearrange("h n -> (h n)")
        P.dma("sp", prow, dsrc, prow[i_:i_ + 1, :], src.rearrange("(o n) -> o n", o=1))
    P.dma("sp", prow, dsrc, prow[len(pnames):NPR, :], A["conv_w"])
    pc = {nm: C.sb(uname("pc_" + nm), (128, NQ), F32) for nm in pnames}
    cw = C.sb(uname("cw"), (128, NQ, CW), F32)
    ptp = C.ps(uname("ptp"))
    for q in range(NQ):
        P.op("pe", lambda: nc.tensor.transpose(out=ptp[:, q * 64:q * 64 + NPR], in_=prow[0:NPR, q * 128:(q + 1) * 128],
                                               identity=idf[0:NPR, 0:NPR]), rd=(prow, idf), wr=(ptp,))
    for q in range(NQ):
        for i_, nm in enumerate(pnames):
            P.op("dve", lambda: nc.vector.tensor_copy(out=pc[nm][:, q:q + 1], in_=ptp[:, q * 64 + i_:q * 64 + i_ + 1]),
                 rd=(ptp,), wr=(pc[nm],))
        P.op("act", lambda: nc.scalar.copy(out=cw[:, q, :], in_=ptp[:, q * 64 + len(pnames):q * 64 + NPR]),
             rd=(ptp,), wr=(cw,))
    omka = C.sb(uname("omka"), (128, NQ), F32)
    P.op("dve", lambda: nc.vector.tensor_scalar(out=omka[:], in0=pc["k_a"][:], scalar1=-1.0, scalar2=1.0,
                                                op0=ALU.mult, op1=ALU.add), rd=(pc["k_a"],), wr=(omka,))
    bones = C.sb(uname("bones"), (128, 128), F32)
    P.op("pool", lambda: nc.gpsimd.memset(bones[:], 0.0), wr=(bones,))
    P.op("pool", lambda: nc.gpsimd.memset(bones[0:64, 0:64], 1.0), rd=(bones,), wr=(bones,))
    P.op("pool", lambda: nc.gpsimd.memset(bones[64:128, 64:128], 1.0), rd=(bones,), wr=(bones,))
    ones = C.sb(uname("ones"), (128, 128), F32)
    P.op("pool", lambda: nc.gpsimd.memset(ones[:], 1.0), wr=(ones,))
    rmask = C.sb(uname("rmask"), (128, TT), F32)
    P.op("pool", lambda: nc.gpsimd.memset(rmask[:], 1.0), wr=(rmask,))
    P.op("pool", lambda: nc.gpsimd.memset(rmask[:].rearrange("p (c j) -> p c j", j=CK)[:, :, 0:1], 0.0),
         rd=(rmask,), wr=(rmask,))
    xs = [C.sb(uname("xs"), (128, D), F32, dma=True) for _ in range(2)]
    hTh = [C.sb(uname("hTh"), (128, NCH, TT + 1), BF16) for _ in range(2)]
    P.op("pool", lambda: nc.gpsimd.memset(hTh[0][:, :, 0:1], 0.0), wr=(hTh[0],))
    sq_junk = C.sb(uname("sqj"), (128, D), BF16)
    hb = [C.sb(uname("hb"), (128, D), BF16) for _ in range(2)]
    stats = [C.sb(uname("stat"), (128, 8), F32) for _ in range(4)]
    tdw = C.sb(uname("tdw"), (128, TT), BF16)
    sdg = C.sb(uname("sdg"), (128, TT), BF16)
    F = {}
    for nm in ("rf", "kf", "sgw", "av", "gf", "kk", "kk2", "rn", "kkn", "t1", "kn", "bb", "rk", "Lc", "eL", "enL",
               "Lx", "eLx", "bon"):
        F[nm] = C.sb(uname(nm), (128, TT), F32)
    pack = [C.sb(uname("pack"), (128, 7, TT), BF16) for _ in range(2)]
    dct = C.sb(uname("dct"), (128, NQ, T // CK), F32)
    glub = [C.sb(uname("glub"), (128, TT + CW - 1), F32) for _ in range(NQ)]
    for q in range(NQ):
        P.op("pool", lambda: nc.gpsimd.memset(glub[q][:, 0:CW - 1], 0.0), wr=(glub[q],))
    sgc = C.sb(uname("sgc"), (128, TT), F32)
    acc = [C.sb(uname("acc"), (128, TT), F32) for _ in range(NQ)]
    sqc = C.sb(uname("sqc"), (128, TT), F32)
    mean = C.sb(uname("mean"), (128, TT), F32)
    msq = C.sb(uname("msq"), (128, TT), F32)
    rstd = C.sb(uname("rstd"), (128, TT), F32)
    tcv = C.sb(uname("tcv"), (128, TT), F32)
    ybt = [C.sb(uname("ybt"), (128, TT), BF16) for _ in range(2)]
    tp_ps = C.ps(uname("tp"), (128, D), BF16)
    bk = [C.ps(uname("bk")) for _ in range(6)] + [ptp]

    def proj(pp, col0, shifted, ht):
        n = 2 * NCH if shifted else NCH
        i = 0
        for c in range(NCH):
            if shifted:
                P.op("pe", lambda: nc.tensor.matmul(pp[:, 0:TT], lhsT=W1[:, c, col0:col0 + 128], rhs=ht[:, c, 1:TT + 1],
                                                    start=(i == 0), stop=(i == n - 1)), rd=(W1, ht), wr=(pp,))
                i += 1
                P.op("pe", lambda: nc.tensor.matmul(pp[:, 0:TT], lhsT=W2[:, c, col0:col0 + 128], rhs=ht[:, c, 0:TT],
                                                    start=False, stop=(i == n - 1)), rd=(W2, ht), wr=(pp,))
                i += 1
            else:
                P.op("pe", lambda: nc.tensor.matmul(pp[:, 0:TT], lhsT=Wc[:, c, col0:col0 + 128], rhs=ht[:, c, 1:TT + 1],
                                                    start=(i == 0), stop=(i == n - 1)), rd=(Wc, ht), wr=(pp,))
                i += 1

    pk_i = 0
    for ti in range(NT):
        ht = hTh[ti % 2]
        tsl = slice(ti * TT, (ti + 1) * TT)
        for s_ in range(NS):
            xst = xs[s_ % 2]
            P.dma("sp", xst, xin, xst[:], xin_v[ti * NS + s_])
            prenorm_tile(P, nc, xst, gpre, ht, 1 + s_ * 128, idb, sq_junk, stats[s_ % 4], hb[s_ % 2], tp_ps)
        if ti + 1 < NT:
            P.op("pool", lambda: nc.gpsimd.tensor_copy(out=hTh[(ti + 1) % 2][:, :, 0:1], in_=ht[:, :, TT:TT + 1]),
                 rd=(ht,), wr=(hTh[(ti + 1) % 2],))
        proj(bk[6], 1536, True, ht)
        P.op("act", lambda: nc.scalar.activation(out=tdw[0:64, :], in_=bk[6][0:64, 0:TT], func=AF.Tanh),
             rd=(bk[6],), wr=(tdw,))
        P.op("act", lambda: nc.scalar.copy(out=tdw[64:128, :], in_=bk[6][64:128, 0:TT]), rd=(bk[6],), wr=(tdw,))
        proj(bk[5], 1664, True, ht)
        P.op("act", lambda: nc.scalar.activation(out=sdg[:], in_=bk[5][:, 0:TT], func=AF.Sigmoid),
             rd=(bk[5],), wr=(sdg,))
        for q in range(NQ):
            pk = pack[pk_i % 2]
            pk_i += 1
            qs = slice(q * 128, (q + 1) * 128)
            r_ps, k_ps, v_ps, zw_ps, za_ps, g_ps, s_ps = bk[0], bk[1], bk[2], bk[3], bk[4], bk[5], bk[6]
            proj(r_ps, q * 128, True, ht)
            proj(k_ps, 512 + q * 128, True, ht)
            proj(v_ps, 1024 + q * 128, True, ht)
            P.op("pe", lambda: nc.tensor.matmul(zw_ps[:, 0:TT], lhsT=lup[0:64, qs], rhs=tdw[0:64, :], start=True,
                                                stop=True), rd=(lup, tdw), wr=(zw_ps,))
            P.op("pe", lambda: nc.tensor.matmul(za_ps[:, 0:TT], lhsT=lup[64:128, qs], rhs=tdw[64:128, :], start=True,
                                                stop=True), rd=(lup, tdw), wr=(za_ps,))
            P.op("pe", lambda: nc.tensor.matmul(g_ps[:, 0:TT], lhsT=gup[:, qs], rhs=sdg[:], start=True, stop=True),
                 rd=(gup, sdg), wr=(g_ps,))
            col = lambda nm: pc[nm][:, q:q + 1]
            P.op("act", lambda: nc.scalar.copy(out=F["rf"][:], in_=r_ps[:, 0:TT]), rd=(r_ps,), wr=(F["rf"],))
            P.op("act", lambda: nc.scalar.copy(out=F["kf"][:], in_=k_ps[:, 0:TT]), rd=(k_ps,), wr=(F["kf"],))
            P.op("act", lambda: nc.scalar.copy(out=pk[:, 4, :], in_=v_ps[:, 0:TT]), rd=(v_ps,), wr=(pk,))
            P.op("act", lambda: nc.scalar.activation(out=F["sgw"][:], in_=zw_ps[:, 0:TT], func=AF.Sigmoid,
                                                     bias=col("decay_base")), rd=(zw_ps, pc["decay_base"]),
                 wr=(F["sgw"],))
            P.op("act", lambda: nc.scalar.activation(out=F["av"][:], in_=za_ps[:, 0:TT], func=AF.Sigmoid,
                                                     bias=col("iclr_base")), rd=(za_ps, pc["iclr_base"]),
                 wr=(F["av"],))
            P.op("act", lambda: nc.scalar.copy(out=F["gf"][:], in_=g_ps[:, 0:TT]), rd=(g_ps,), wr=(F["gf"],))
            P.op("dve", lambda: nc.vector.tensor_scalar(out=F["kk"][:], in0=F["kf"][:], scalar1=col("k_k"),
                                                        scalar2=None, op0=ALU.mult), rd=(F["kf"], pc["k_k"]),
                 wr=(F["kk"],))
            P.op("pool", lambda: nc.gpsimd.tensor_tensor(out=F["kk2"][:], in0=F["kk"][:], in1=F["kk"][:],
                                                          op=ALU.mult), rd=(F["kk"],), wr=(F["kk2"],))
            P.op("pe", lambda: nc.tensor.matmul(s_ps[:, 0:TT], lhsT=bones[:], rhs=F["kk2"][:], start=True, stop=True),
                 rd=(bones, F["kk2"]), wr=(s_ps,))
            P.op("act", lambda: nc.scalar.activation(out=F["rn"][:], in_=s_ps[:, 0:TT], func=AF.Ln, bias=1e-24,
                                                     scale=1.0), rd=(s_ps,), wr=(F["rn"],))
            P.op("act", lambda: nc.scalar.activation(out=F["rn"][:], in_=F["rn"][:], func=AF.Exp, scale=-0.5),
                 rd=(F["rn"],), wr=(F["rn"],))
            P.op("pool", lambda: nc.gpsimd.tensor_tensor(out=F["kkn"][:], in0=F["kk"][:], in1=F["rn"][:],
                                                          op=ALU.mult), rd=(F["kk"], F["rn"]), wr=(F["kkn"],))
            P.op("dve", lambda: nc.vector.tensor_scalar(out=F["t1"][:], in0=F["av"][:], scalar1=col("k_a"),
                                                        scalar2=omka[:, q:q + 1], op0=ALU.mult, op1=ALU.add),
                 rd=(F["av"], pc["k_a"], omka), wr=(F["t1"],))
            P.op("pool", lambda: nc.gpsimd.tensor_tensor(out=F["kn"][:], in0=F["kf"][:], in1=F["t1"][:],
                                                          op=ALU.mult), rd=(F["kf"], F["t1"]), wr=(F["kn"],))
            P.op("pool", lambda: nc.gpsimd.tensor_tensor(out=F["bb"][:], in0=F["kkn"][:], in1=F["av"][:],
                                                          op=ALU.mult), rd=(F["kkn"], F["av"]), wr=(F["bb"],))
            P.op("dve", lambda: nc.vector.scalar_tensor_tensor(out=F["rk"][:], in0=F["rf"][:], scalar=col("r_k"),
                                                               in1=F["kn"][:], op0=ALU.mult, op1=ALU.mult),
                 rd=(F["rf"], pc["r_k"], F["kn"]), wr=(F["rk"],))
            P.op("pe", lambda: nc.tensor.matmul(s_ps[:, 0:TT], lhsT=bones[:], rhs=F["rk"][:], start=True, stop=True),
                 rd=(bones, F["rk"]), wr=(s_ps,))
            P.op("dve", lambda: nc.vector.tensor_tensor(out=F["bon"][:], in0=s_ps[:, 0:TT], in1=pk[:, 4, :],
                                                        op=ALU.mult), rd=(s_ps, pk), wr=(F["bon"],))
            P.op("dve", lambda: nc.vector.scalar_tensor_tensor(out=pk[:, 6, :], in0=F["bon"][:], scalar=col("lnx_b"),
                                                               in1=F["gf"][:], op0=ALU.add, op1=ALU.mult),
                 rd=(F["bon"], pc["lnx_b"], F["gf"]), wr=(pk,))
            P.op("pool", lambda: nc.gpsimd.tensor_copy(out=pk[:, 5, :], in_=F["gf"][:]), rd=(F["gf"],), wr=(pk,))
            P.op("dve", lambda: nc.vector.tensor_tensor_scan(out=F["Lc"][:], data0=rmask[:], data1=F["sgw"][:],
                                                             initial=0.0, op0=ALU.mult, op1=ALU.add),
                 rd=(rmask, F["sgw"]), wr=(F["Lc"],))
            P.op("pool", lambda: nc.gpsimd.tensor_tensor(out=F["Lx"][:], in0=F["Lc"][:], in1=F["sgw"][:],
                                                          op=ALU.subtract), rd=(F["Lc"], F["sgw"]), wr=(F["Lx"],))
            P.op("act", lambda: nc.scalar.activation(out=F["eL"][:], in_=F["Lc"][:], func=AF.Exp, scale=-DEC_C),
                 rd=(F["Lc"],), wr=(F["eL"],))
            P.op("act", lambda: nc.scalar.activation(out=F["enL"][:], in_=F["Lc"][:], func=AF.Exp, scale=DEC_C),
                 rd=(F["Lc"],), wr=(F["enL"],))
            P.op("act", lambda: nc.scalar.activation(out=F["eLx"][:], in_=F["Lx"][:], func=AF.Exp, scale=-DEC_C),
                 rd=(F["Lx"],), wr=(F["eLx"],))
            nck = TT // CK
            P.op("pool", lambda: nc.gpsimd.tensor_copy(
                out=dct[:, q, ti * nck:(ti + 1) * nck],
                in_=F["eL"][:].rearrange("p (c j) -> p c j", j=CK)[:, :, CK - 1]), rd=(F["eL"],), wr=(dct,))
            P.op("pool", lambda: nc.gpsimd.tensor_tensor(out=pk[:, 0, :], in0=F["rf"][:], in1=F["eL"][:], op=ALU.mult),
                 rd=(F["rf"], F["eL"]), wr=(pk,))
            P.op("dve", lambda: nc.vector.scalar_tensor_tensor(out=pk[:, 1, :], in0=F["kkn"][:], scalar=-1.0,
                                                               in1=F["eLx"][:], op0=ALU.mult, op1=ALU.mult),
                 rd=(F["kkn"], F["eLx"]), wr=(pk,))
            P.op("pool", lambda: nc.gpsimd.tensor_tensor(out=pk[:, 2, :], in0=F["kn"][:], in1=F["enL"][:], op=ALU.mult),
                 rd=(F["kn"], F["enL"]), wr=(pk,))
            P.op("dve", lambda: nc.vector.tensor_tensor(out=pk[:, 3, :], in0=F["bb"][:], in1=F["enL"][:], op=ALU.mult),
                 rd=(F["bb"], F["enL"]), wr=(pk,))
            P.dma("sp", rw, pk, rw_ap[qs, :, tsl], pk[:])
        for q in range(NQ):
            val_ps, gt_ps = bk[0 + 2 * (q % 2)], bk[1 + 2 * (q % 2)]
            proj(val_ps, q * 128, False, ht)
            proj(gt_ps, 512 + q * 128, False, ht)
            gb = glub[q]
            P.op("act", lambda: nc.scalar.activation(out=sgc[:], in_=gt_ps[:, 0:TT], func=AF.Sigmoid),
                 rd=(gt_ps,), wr=(sgc,))
            P.op("dve", lambda: nc.vector.tensor_tensor(out=gb[:, CW - 1:CW - 1 + TT], in0=val_ps[:, 0:TT], in1=sgc[:],
                                                        op=ALU.mult), rd=(val_ps, sgc), wr=(gb,))
            ac = acc[q]
            P.op("dve", lambda: nc.vector.tensor_scalar(out=ac[:], in0=gb[:, 0:TT], scalar1=cw[:, q, 0:1],
                                                        scalar2=pc["conv_b"][:, q:q + 1], op0=ALU.mult, op1=ALU.add),
                 rd=(gb, cw, pc["conv_b"]), wr=(ac,))
            for j in range(1, CW):
                P.op("dve", lambda: nc.vector.scalar_tensor_tensor(out=ac[:], in0=gb[:, j:j + TT],
                                                                   scalar=cw[:, q, j:j + 1], in1=ac[:], op0=ALU.mult,
                                                                   op1=ALU.add), rd=(gb, cw, ac), wr=(ac,))
            P.op("pool", lambda: nc.gpsimd.tensor_copy(out=gb[:, 0:CW - 1], in_=gb[:, TT:TT + CW - 1]),
                 rd=(gb,), wr=(gb,))
        sum_ps, ssq_ps = bk[4], bk[5]
        for q in range(NQ):
            P.op("pe", lambda: nc.tensor.matmul(sum_ps[:, 0:TT], lhsT=ones[:], rhs=acc[q][:], start=(q == 0),
                                                stop=(q == NQ - 1)), rd=(ones, acc[q]), wr=(sum_ps,))
        for q in range(NQ):
            P.op("act", lambda: nc.scalar.activation(out=sqc[:], in_=acc[q][:], func=AF.Square), rd=(acc[q],),
                 wr=(sqc,))
            P.op("pe", lambda: nc.tensor.matmul(ssq_ps[:, 0:TT], lhsT=ones[:], rhs=sqc[:], start=(q == 0),
                                                stop=(q == NQ - 1)), rd=(ones, sqc), wr=(ssq_ps,))
        P.op("act", lambda: nc.scalar.activation(out=mean[:], in_=sum_ps[:, 0:TT], func=AF.Copy, scale=1.0 / 512),
             rd=(sum_ps,), wr=(mean,))
        P.op("pool", lambda: nc.gpsimd.tensor_tensor(out=msq[:], in0=mean[:], in1=mean[:], op=ALU.mult),
             rd=(mean,), wr=(msq,))
        P.op("dve", lambda: nc.vector.scalar_tensor_tensor(out=rstd[:], in0=ssq_ps[:, 0:TT], scalar=1.0 / 512,
                                                           in1=msq[:], op0=ALU.mult, op1=ALU.subtract),
             rd=(ssq_ps, msq), wr=(rstd,))
        P.op("act", lambda: nc.scalar.activation(out=rstd[:], in_=rstd[:], func=AF.Ln, bias=1e-5, scale=1.0),
             rd=(rstd,), wr=(rstd,))
        P.op("act", lambda: nc.scalar.activation(out=rstd[:], in_=rstd[:], func=AF.Exp, scale=-0.5),
             rd=(rstd,), wr=(rstd,))
        for q in range(NQ):
            yb_ = ybt[q % 2]
            P.op("pool", lambda: nc.gpsimd.tensor_tensor(out=tcv[:], in0=acc[q][:], in1=mean[:], op=ALU.subtract),
                 rd=(acc[q], mean), wr=(tcv,))
            P.op("pool", lambda: nc.gpsimd.tensor_tensor(out=tcv[:], in0=tcv[:], in1=rstd[:], op=ALU.mult),
                 rd=(tcv, rstd), wr=(tcv,))
            P.op("act", lambda: nc.scalar.activation(out=yb_[:], in_=tcv[:], func=AF.Silu,
                                                     bias=pc["ln_b"][:, q:q + 1], scale=pc["ln_g"][:, q:q + 1]),
                 rd=(tcv, pc["ln_b"], pc["ln_g"]), wr=(yb_,))
            P.dma("sp", yab, yb_, yab_ap[:, 4 + q, tsl], yb_[:])
    for q in range(NQ):
        P.dma("sp", dcb, dct, dc_ap[q * 128:(q + 1) * 128, :], dct[:, q, :])
    C.close()


def rwkv_scan_stage(P, T, A, scr):
    nc = P.nc
    NH = 8
    NHC = 4
    TT = 512 if T >= 512 else T
    NG = T // TT
    NCG = TT // CK
    NC = T // CK
    rw_ap, rw = scr["rw"]
    dc_ap, dcb = scr["dc"]
    yab_ap, yab = scr["oT"]
    dsrc = Buf(P, None, "dsrc")
    C = Ctx(P)
    idf, idb = make_consts(P, C)
    ones64 = C.sb(uname("ones64"), (128, 64), F32)
    P.op("pool", lambda: nc.gpsimd.memset(ones64[:], 1.0), wr=(ones64,))

    def mk_mask(name, base, cm, step):
        m = C.sb(uname(name), (64, NCG, CK), F32)
        P.op("pool", lambda: nc.gpsimd.memset(m[:], 1.0), wr=(m,))
        P.op("pool", lambda: nc.gpsimd.affine_select(out=m[:], in_=m[:], compare_op=ALU.is_ge, fill=0.0, base=base,
                                                      pattern=[[0, NCG], [step, CK]], channel_multiplier=cm),
             rd=(m,), wr=(m,))
        return m
    m_su = mk_mask("m_su", -1, -1, 1)
    m_sl = mk_mask("m_sl", -1, 1, -1)
    m_iu = mk_mask("m_iu", 0, -1, 1)
    I8 = C.sb(uname("I8"), (64, NCG, CK), F32)
    P.op("pool", lambda: nc.gpsimd.memset(I8[:], 0.0), wr=(I8,))
    P.op("pool", lambda: nc.gpsimd.affine_select(out=I8[:], in_=I8[:], compare_op=ALU.not_equal, fill=1.0, base=0,
                                                  pattern=[[0, NCG], [-1, CK]], channel_multiplier=1),
         rd=(I8,), wr=(I8,))
    lnrow = C.sb(uname("lnrow"), (1, 512), F32, dma=True)
    P.dma("sp", lnrow, dsrc, lnrow[:], A["lnx_g"].rearrange("(o n) -> o n", o=1))
    lnxg = C.sb(uname("lnxg"), (64, NH), F32)
    flat = lambda m: m[:].rearrange("p c i -> p (c i)")

    class Slot:
        pass
    slots = []
    for i in range(NHC):
        S = Slot()
        S.ops = [C.sb(uname("ops"), (128, 7, TT), BF16, dma=True) for _ in range(2)]
        for o_ in S.ops:
            P.op("pool", lambda: nc.gpsimd.memset(o_[64:128, :, :], 0.0), wr=(o_,))
        S.dC = C.sb(uname("dC"), (64, NC), F32, dma=True)
        for nm in ("Akt", "Arbt", "Arkt", "Tt", "Btok", "Ktok", "Vtok", "U", "Mb0", "Mb1", "Mt0", "Mt1"):
            setattr(S, nm, C.sb(uname(nm), (128, TT), BF16))
            P.op("pool", lambda: nc.gpsimd.memset(getattr(S, nm)[64:128, :], 0.0), wr=(getattr(S, nm),))
        S.Hall = C.sb(uname("Hall"), (128, NCG + 1, CK), BF16)
        P.op("pool", lambda: nc.gpsimd.memset(S.Hall[64:128, :, :], 0.0), wr=(S.Hall,))
        S.Hf = C.sb(uname("Hf"), (64, CK), F32)
        S.tmpH = C.sb(uname("tmpH"), (64, CK), F32)
        S.Wsb = C.sb(uname("Wsb"), (128, CK), BF16)
        P.op("pool", lambda: nc.gpsimd.memset(S.Wsb[64:128, :], 0.0), wr=(S.Wsb,))
        S.yT = C.sb(uname("yT"), (128, TT), F32)
        S.sqy = C.sb(uname("sqy"), (128, TT), F32)
        P.op("pool", lambda: nc.gpsimd.memset(S.yT[64:128, :], 0.0), wr=(S.yT,))
        P.op("pool", lambda: nc.gpsimd.memset(S.sqy[64:128, :], 0.0), wr=(S.sqy,))
        S.mean = C.sb(uname("mean"), (64, TT), F32)
        S.rstd = C.sb(uname("rstd"), (64, TT), F32)
        S.yo = C.sb(uname("yo"), (64, TT), BF16)
        S.bk = [C.ps(uname("bk")) for _ in range(2)]
        S.bi = 0
        slots.append(S)
    lps = slots[0].bk[0]
    for h_ in range(NH):
        P.op("pe", lambda: nc.tensor.transpose(out=lps[0:64, h_:h_ + 1], in_=lnrow[0:1, h_ * 64:(h_ + 1) * 64],
                                               identity=idf[0:1, 0:1]), rd=(lnrow, idf), wr=(lps,))
    P.op("dve", lambda: nc.vector.tensor_copy(out=lnxg[:], in_=lps[0:64, 0:NH]), rd=(lps,), wr=(lnxg,))

    def head_prog(S, h):
        def nb():
            S.bi += 1
            return S.bk[S.bi % 2]

        def macro(lt, lsl, rt, rsl, ps):
            for c in range(NCG):
                cs = slice(c * CK, (c + 1) * CK)
                P.op("pe", lambda: nc.tensor.matmul(ps[0:64, cs], lhsT=lsl(c), rhs=rsl(c), start=True, stop=True),
                     rd=(lt, rt), wr=(ps,))
        P.dma("sp", S.dC, dcb, S.dC[:], dc_ap[h * 64:(h + 1) * 64, :])
        P.op("pool", lambda: nc.gpsimd.memset(S.Hf[:], 0.0), wr=(S.Hf,))
        P.op("pool", lambda: nc.gpsimd.memset(S.Hall[0:64, 0, :], 0.0), wr=(S.Hall,))
        Mb, Mtb = [S.Mb0, S.Mb1], [S.Mt0, S.Mt1]
        for g in range(NG):
            g0 = g * TT
            ops = S.ops[g % 2]
            P.dma("sp", ops, rw, ops[0:64, :, :], rw_ap[h * 64:(h + 1) * 64, :, g0:g0 + TT])
            osl = lambda kind: (lambda c: ops[:, kind, c * CK:(c + 1) * CK])
            loc = lambda t_: (lambda c: t_[:, c * CK:(c + 1) * CK])
            Rs, As, Ks, Bs, Vs = osl(0), osl(1), osl(2), osl(3), osl(4)
            idl = lambda c: idb[:, 0:64]
            if g > 0:
                P.op("pool", lambda: nc.gpsimd.tensor_copy(out=S.Hall[0:64, 0, :], in_=S.Hall[0:64, NCG, :]), rd=(S.Hall,),
                     wr=(S.Hall,))
            yield
            ps = nb()
            macro(ops, Bs, ops, As, ps)
            P.op("dve", lambda: nc.vector.tensor_tensor(out=Mtb[0][0:64, :], in0=ps[0:64, 0:TT], in1=flat(m_su), op=ALU.mult),
                 rd=(ps, m_su), wr=(Mtb[0],))
            P.op("pool", lambda: nc.gpsimd.tensor_tensor(out=S.Tt[0:64, :], in0=Mtb[0][0:64, :], in1=flat(I8), op=ALU.add),
                 rd=(Mtb[0], I8), wr=(S.Tt,))
            yield
            ps = nb()
            macro(ops, As, ops, Bs, ps)
            P.op("dve", lambda: nc.vector.tensor_tensor(out=Mb[0][0:64, :], in0=ps[0:64, 0:TT], in1=flat(m_sl), op=ALU.mult),
                 rd=(ps, m_sl), wr=(Mb[0],))
            yield
            for (ls, rs_, dst, mk) in ((Ks, As, S.Akt, m_su), (Bs, Rs, S.Arbt, m_iu), (Ks, Rs, S.Arkt, m_iu)):
                ps = nb()
                macro(ops, ls, ops, rs_, ps)
                P.op("dve", lambda: nc.vector.tensor_tensor(out=dst[0:64, :], in0=ps[0:64, 0:TT], in1=flat(mk), op=ALU.mult),
                     rd=(ps, mk), wr=(dst,))
                yield
            for (src, dst) in ((Bs, S.Btok), (Ks, S.Ktok), (Vs, S.Vtok)):
                ps = nb()
                macro(ops, src, idb, idl, ps)
                P.op("act", lambda: nc.scalar.copy(out=dst[0:64, :], in_=ps[0:64, 0:TT]), rd=(ps,), wr=(dst,))
                yield
            cur = 0
            for p in range(1, 6):
                nxt = 1 - cur
                ps = nb()
                macro(Mtb[cur], loc(Mtb[cur]), Mb[cur], loc(Mb[cur]), ps)
                P.op("act", lambda: nc.scalar.copy(out=Mb[nxt][0:64, :], in_=ps[0:64, 0:TT]), rd=(ps,), wr=(Mb[nxt],))
                yield
                if p < 5:
                    ps2 = nb()
                    macro(Mb[cur], loc(Mb[cur]), Mtb[cur], loc(Mtb[cur]), ps2)
                    P.op("act", lambda: nc.scalar.copy(out=Mtb[nxt][0:64, :], in_=ps2[0:64, 0:TT]), rd=(ps2,),
                         wr=(Mtb[nxt],))
                    yield
                ps3 = nb()
                macro(Mb[nxt], loc(Mb[nxt]), S.Tt, loc(S.Tt), ps3)
                P.op("dve", lambda: nc.vector.tensor_tensor(out=S.Tt[0:64, :], in0=ps3[0:64, 0:TT], in1=S.Tt[0:64, :], op=ALU.add),
                     rd=(ps3, S.Tt), wr=(S.Tt,))
                yield
                cur = nxt
            for c in range(NCG):
                cg = g * NCG + c
                cs = slice(c * CK, (c + 1) * CK)
                w_ps, u_ps = S.bk[0], S.bk[1]
                P.op("pe", lambda: nc.tensor.matmul(w_ps[0:64, 0:CK], lhsT=As(c), rhs=S.Hall[:, c, :], start=True,
                                                    stop=False), rd=(ops, S.Hall), wr=(w_ps,))
                P.op("pe", lambda: nc.tensor.matmul(w_ps[0:64, 0:CK], lhsT=S.Akt[:, cs], rhs=S.Vtok[:, cs], start=False,
                                                    stop=True), rd=(S.Akt, S.Vtok), wr=(w_ps,))
                P.op("act", lambda: nc.scalar.copy(out=S.Wsb[0:64, :], in_=w_ps[0:64, 0:CK]), rd=(w_ps,), wr=(S.Wsb,))
                P.op("act", lambda: nc.scalar.activation(out=S.tmpH[:], in_=S.Hf[:], func=AF.Copy,
                                                         scale=S.dC[:, cg:cg + 1]), rd=(S.Hf, S.dC), wr=(S.tmpH,))
                yield
                P.op("pe", lambda: nc.tensor.matmul(u_ps[0:64, 0:CK], lhsT=S.Tt[:, cs], rhs=S.Wsb[:], start=True,
                                                    stop=True), rd=(S.Tt, S.Wsb), wr=(u_ps,))
                P.op("dve", lambda: nc.vector.tensor_copy(out=S.U[0:64, cs], in_=u_ps[0:64, 0:CK]), rd=(u_ps,), wr=(S.U,))
                yield
                h_ps = w_ps
                P.op("pe", lambda: nc.tensor.matmul(h_ps[0:64, 64:64 + CK], lhsT=S.Btok[:, cs], rhs=S.U[:, cs],
                                                    start=True, stop=False), rd=(S.Btok, S.U), wr=(h_ps,))
                P.op("pe", lambda: nc.tensor.matmul(h_ps[0:64, 64:64 + CK], lhsT=S.Ktok[:, cs], rhs=S.Vtok[:, cs],
                                                    start=False, stop=True), rd=(S.Ktok, S.Vtok), wr=(h_ps,))
                P.op("dve", lambda: nc.vector.scalar_tensor_tensor(out=S.Hf[:], in0=h_ps[0:64, 64:64 + CK],
                                                                   scalar=S.dC[:, cg:cg + 1], in1=S.tmpH[:],
                                                                   op0=ALU.mult, op1=ALU.add),
                     rd=(h_ps, S.dC, S.tmpH), wr=(S.Hf,))
                P.op("act", lambda: nc.scalar.copy(out=S.Hall[0:64, c + 1, :], in_=S.Hf[:]), rd=(S.Hf,), wr=(S.Hall,))
                yield
            y_ps = nb()
            for c in range(NCG):
                cs = slice(c * CK, (c + 1) * CK)
                P.op("pe", lambda: nc.tensor.matmul(y_ps[0:64, cs], lhsT=S.Hall[:, c, :], rhs=Rs(c), start=True,
                                                    stop=False), rd=(S.Hall, ops), wr=(y_ps,))
                P.op("pe", lambda: nc.tensor.matmul(y_ps[0:64, cs], lhsT=S.U[:, cs], rhs=S.Arbt[:, cs], start=False,
                                                    stop=False), rd=(S.U, S.Arbt), wr=(y_ps,))
                P.op("pe", lambda: nc.tensor.matmul(y_ps[0:64, cs], lhsT=S.Vtok[:, cs], rhs=S.Arkt[:, cs], start=False,
                                                    stop=True), rd=(S.Vtok, S.Arkt), wr=(y_ps,))
            P.op("act", lambda: nc.scalar.copy(out=S.yT[0:64, :], in_=y_ps[0:64, 0:TT]), rd=(y_ps,), wr=(S.yT,))
            P.op("act", lambda: nc.scalar.activation(out=S.sqy[0:64, :], in_=S.yT[0:64, :], func=AF.Square), rd=(S.yT,),
                 wr=(S.sqy,))
            yield
            s_ps, q_ps = nb(), nb()
            P.op("pe", lambda: nc.tensor.matmul(s_ps[0:64, 0:TT], lhsT=ones64[:], rhs=S.yT[:], start=True, stop=True),
                 rd=(ones64, S.yT), wr=(s_ps,))
            P.op("pe", lambda: nc.tensor.matmul(q_ps[0:64, 0:TT], lhsT=ones64[:], rhs=S.sqy[:], start=True, stop=True),
                 rd=(ones64, S.sqy), wr=(q_ps,))
            P.op("act", lambda: nc.scalar.activation(out=S.mean[:], in_=s_ps[0:64, 0:TT], func=AF.Copy,
                                                     scale=1.0 / 64), rd=(s_ps,), wr=(S.mean,))
            P.op("pool", lambda: nc.gpsimd.tensor_tensor(out=S.sqy[0:64, :], in0=S.mean[:], in1=S.mean[:], op=ALU.mult),
                 rd=(S.mean,), wr=(S.sqy,))
            P.op("dve", lambda: nc.vector.scalar_tensor_tensor(out=S.rstd[:], in0=q_ps[0:64, 0:TT], scalar=1.0 / 64,
                                                               in1=S.sqy[0:64, :], op0=ALU.mult, op1=ALU.subtract),
                 rd=(q_ps, S.sqy), wr=(S.rstd,))
            yield
            P.op("act", lambda: nc.scalar.activation(out=S.rstd[:], in_=S.rstd[:], func=AF.Ln, bias=64e-5, scale=1.0),
                 rd=(S.rstd,), wr=(S.rstd,))
            P.op("act", lambda: nc.scalar.activation(out=S.rstd[:], in_=S.rstd[:], func=AF.Exp, scale=-0.5),
                 rd=(S.rstd,), wr=(S.rstd,))
            P.op("pool", lambda: nc.gpsimd.tensor_tensor(out=S.yT[0:64, :], in0=S.yT[0:64, :], in1=S.mean[:], op=ALU.subtract),
                 rd=(S.yT, S.mean), wr=(S.yT,))
            yield
            P.op("dve", lambda: nc.vector.scalar_tensor_tensor(out=S.yT[0:64, :], in0=S.yT[0:64, :], scalar=lnxg[:, h:h + 1],
                                                               in1=S.rstd[:], op0=ALU.mult, op1=ALU.mult),
                 rd=(S.yT, lnxg, S.rstd), wr=(S.yT,))
            P.op("dve", lambda: nc.vector.tensor_tensor(out=S.yT[0:64, :], in0=S.yT[0:64, :], in1=ops[0:64, 5, :], op=ALU.mult),
                 rd=(S.yT, ops), wr=(S.yT,))
            P.op("pool", lambda: nc.gpsimd.tensor_tensor(out=S.yo[:], in0=S.yT[0:64, :], in1=ops[0:64, 6, :], op=ALU.add),
                 rd=(S.yT, ops), wr=(S.yo,))
            P.dma("sp", yab, S.yo, yab_ap[(h % 2) * 64:(h % 2) * 64 + 64, h // 2, g0:g0 + TT], S.yo[:])
            yield

    for h0 in range(0, NH, NHC):
        gens = [head_prog(slots[i], h0 + i) for i in range(NHC)]
        while gens:
            for gn in list(gens):
                try:
                    next(gn)
                except StopIteration:
                    gens.remove(gn)
    C.close()


def mixer_ab_phase(P, T, xin, xout, xin_ap, xout_ap, A, gpre_ap, gpost_ap, scr):
    rwkv_prep_stage(P, T, xin, xin_ap, gpre_ap, A, scr)
    rwkv_scan_stage(P, T, A, scr)
    outproj_stage(P, T, xin, xout, xin_ap, xout_ap, scr["oT"][0], scr["oT"][1], A["w_out"], gpost_ap)


def build(T, phases):
    nc = bass.Bass("TRN2", target_bir_lowering=False)
    t = {}
    t["x"] = nc.dram_tensor("x", [T, D], F32, kind="ExternalInput").ap()
    t["norm_gains"] = nc.dram_tensor("norm_gains", [4, 6, D], F32, kind="ExternalInput").ap()
    t["ffn_w_gate"] = nc.dram_tensor("ffn_w_gate", [4, 2, D, DFF], F32, kind="ExternalInput").ap()
    t["ffn_w_up"] = nc.dram_tensor("ffn_w_up", [4, 2, D, DFF], F32, kind="ExternalInput").ap()
    t["ffn_w_down"] = nc.dram_tensor("ffn_w_down", [4, 2, DFF, D], F32, kind="ExternalInput").ap()
    t["c_w_in"] = nc.dram_tensor("c_w_in", [2, D, 4 * D + 16], F32, kind="ExternalInput").ap()
    t["c_forget_bias"] = nc.dram_tensor("c_forget_bias", [2, 16], F32, kind="ExternalInput").ap()
    t["c_q_norm_g"] = nc.dram_tensor("c_q_norm_g", [2, 64], F32, kind="ExternalInput").ap()
    t["c_k_norm_g"] = nc.dram_tensor("c_k_norm_g", [2, 64], F32, kind="ExternalInput").ap()
    t["c_w_out"] = nc.dram_tensor("c_w_out", [2, D, D], F32, kind="ExternalInput").ap()
    for nm, shp in (("a_w_in", [2, D, 2816]), ("a_shift_mu", [2, 1792]), ("a_decay_up", [2, 64, 512]),
                    ("a_decay_base", [2, 512]), ("a_iclr_up", [2, 64, 512]), ("a_iclr_base", [2, 512]),
                    ("a_gate_up", [2, 128, 512]), ("a_k_k", [2, 512]), ("a_k_a", [2, 512]), ("a_r_k", [2, 8, 64]),
                    ("a_lnx_g", [2, 512]), ("a_lnx_b", [2, 512]), ("b_conv_w", [2, 31, 512]), ("b_conv_b", [2, 512]),
                    ("b_ln_g", [2, 512]), ("b_ln_b", [2, 512]), ("ab_w_out", [2, D, D])):
        t[nm] = nc.dram_tensor(nm, shp, F32, kind="ExternalInput").ap()
    y = nc.dram_tensor("y", [T, D], F32, kind="ExternalOutput").ap()
    rwd = nc.dram_tensor("rwd", [512, 7, T], BF16).ap()
    dcd = nc.dram_tensor("dcd", [512, max(T // 64, 1)], F32).ap()
    hTd = nc.dram_tensor("hTd", [128, NCH, T], BF16).ap()
    oTd = nc.dram_tensor("oTd", [128, NCH, T], BF16).ap()
    cumd = nc.dram_tensor("cumd", [16, T], F32).ap()
    xa = nc.dram_tensor("xa", [T, D], F32).ap()
    xb = nc.dram_tensor("xb", [T, D], F32).ap()
    P = Prog(nc)
    with nc.allow_low_precision("bf16 matmul operands, fp32 accumulation"):
        P.clear_all()
        nc.all_engine_barrier()
        bx = Buf(P, None, "x_in")
        cur_ap, cur = t["x"], bx
        pp = [(xa, Buf(P, None, "xa", dma=True)), (xb, Buf(P, None, "xb", dma=True))]
        yb = Buf(P, None, "y", dma=True)
        scr = {"hT": (hTd, Buf(P, None, "hTd", dma=True)), "oT": (oTd, Buf(P, None, "oTd", dma=True)),
               "cum": (cumd, Buf(P, None, "cumd", dma=True)), "rw": (rwd, Buf(P, None, "rwd", dma=True)),
               "dc": (dcd, Buf(P, None, "dcd", dma=True))}
        for i, ph in enumerate(phases):
            last = (i == len(phases) - 1)
            dst_ap, dst = (y, yb) if last else pp[i % 2]
            if ph[0] == "ffn":
                l, s = ph[1], ph[2]
                ffn_phase(P, T, cur, dst, cur_ap, dst_ap, t["ffn_w_gate"][l, s], t["ffn_w_up"][l, s],
                          t["ffn_w_down"][l, s], t["norm_gains"][l, 3 * s if s == 0 else 4],
                          t["norm_gains"][l, 1 if s == 0 else 5])
            elif ph[0] == "ab":
                l = ph[1]
                i2 = l // 2
                A = {"w_in": t["a_w_in"][i2], "shift_mu": t["a_shift_mu"][i2], "decay_up": t["a_decay_up"][i2],
                     "decay_base": t["a_decay_base"][i2], "iclr_up": t["a_iclr_up"][i2],
                     "iclr_base": t["a_iclr_base"][i2], "gate_up": t["a_gate_up"][i2], "k_k": t["a_k_k"][i2],
                     "k_a": t["a_k_a"][i2], "r_k": t["a_r_k"][i2], "lnx_g": t["a_lnx_g"][i2],
                     "lnx_b": t["a_lnx_b"][i2], "conv_w": t["b_conv_w"][i2], "conv_b": t["b_conv_b"][i2],
                     "ln_g": t["b_ln_g"][i2], "ln_b": t["b_ln_b"][i2], "w_out": t["ab_w_out"][i2]}
                mixer_ab_phase(P, T, cur, dst, cur_ap, dst_ap, A, t["norm_gains"][l, 2], t["norm_gains"][l, 3], scr)
            elif ph[0] in ("ab1", "ab2", "ab3"):
                i2 = 0
                A = {"w_in": t["a_w_in"][i2], "shift_mu": t["a_shift_mu"][i2], "decay_up": t["a_decay_up"][i2],
                     "decay_base": t["a_decay_base"][i2], "iclr_up": t["a_iclr_up"][i2],
                     "iclr_base": t["a_iclr_base"][i2], "gate_up": t["a_gate_up"][i2], "k_k": t["a_k_k"][i2],
                     "k_a": t["a_k_a"][i2], "r_k": t["a_r_k"][i2], "lnx_g": t["a_lnx_g"][i2],
                     "lnx_b": t["a_lnx_b"][i2], "conv_w": t["b_conv_w"][i2], "conv_b": t["b_conv_b"][i2],
                     "ln_g": t["b_ln_g"][i2], "ln_b": t["b_ln_b"][i2], "w_out": t["ab_w_out"][i2]}
                if ph[0] == "ab1":
                    rwkv_prep_stage(P, T, cur, cur_ap, t["norm_gains"][0, 2], A, scr)
                elif ph[0] == "ab2":
                    rwkv_scan_stage(P, T, A, scr)
                else:
                    outproj_stage(P, T, cur, dst, cur_ap, dst_ap, scr["oT"][0], scr["oT"][1], A["w_out"],
                                  t["norm_gains"][0, 3])
            elif ph[0] == "fox":
                l = ph[1]
                i2 = l // 2
                fox_phase(P, T, cur, dst, cur_ap, dst_ap, t["c_w_in"][i2], t["c_forget_bias"][i2],
                          t["c_q_norm_g"][i2], t["c_k_norm_g"][i2], t["c_w_out"][i2], t["norm_gains"][l, 2],
                          t["norm_gains"][l, 3], scr)
            cur_ap, cur = dst_ap, dst
        P.wait_all_dma("sp")
        nc.all_engine_barrier()
        P.clear_all()
    P.stack.close()
    print("instructions:", P.n_ins, "waits:", P.n_wait, "dsems:", P.ndsem)
    return nc


SEQ = 4096
N_CORES = 8
IN_NAMES = ("norm_gains", "ffn_w_gate", "ffn_w_up", "ffn_w_down", "a_w_in", "a_shift_mu", "a_decay_up",
            "a_decay_base", "a_iclr_up", "a_iclr_base", "a_gate_up", "a_k_k", "a_k_a", "a_r_k", "a_lnx_g", "a_lnx_b",
            "b_conv_w", "b_conv_b", "b_ln_g", "b_ln_b", "ab_w_out", "c_w_in", "c_forget_bias", "c_q_norm_g",
            "c_k_norm_g", "c_w_out")


def all_phases(depth=4):
    ph = []
    for l in range(depth):
        ph.append(("ffn", l, 0))
        ph.append(("ab", l) if l % 2 == 0 else ("fox", l))
        ph.append(("ffn", l, 1))
    return ph


_NC_CACHE = {}


def kernel(**inputs):
    x = np.ascontiguousarray(np.asarray(inputs["x"], dtype=np.float32))
    B, T, _ = x.shape
    key = (T,)
    if key not in _NC_CACHE:
        _NC_CACHE[key] = build(T, all_phases())
    nc = _NC_CACHE[key]
    shared = {k: np.ascontiguousarray(np.asarray(inputs[k], dtype=np.float32)) for k in IN_NAMES}
    in_maps = []
    for b in range(B):
        m = dict(shared)
        m["x"] = x[b]
        in_maps.append(m)
    res = run_bass_kernel_spmd(nc, in_maps, core_ids=list(range(B)))
    return np.stack([np.asarray(r["y"], dtype=np.float32) for r in res.results], axis=0)
```

```python
import contextlib
import numpy as np
import concourse.bass as bass
import concourse.mybir as mybir
from concourse.bass_utils import run_bass_kernel_spmd

F32 = mybir.dt.float32
BF16 = mybir.dt.bfloat16
AF = mybir.ActivationFunctionType
ALU = mybir.AluOpType
AX = mybir.AxisListType

D = 1024
DFF = 2816
NCH = D // 128
NFF = DFF // 128
EPS = 1e-6


class Buf:
    def __init__(self, prog, t, name, dma=False):
        self.t = t
        self.name = name
        self.lw = None
        self.rd = {}
        self.dsem = None
        self.psum = False
        self.dkind = None
        if dma:
            self.dkind = "sw" if dma == "sw" else "hw"
            self.dsem = prog.get_dsem(self.dkind)

    def __getitem__(self, k):
        return self.t[k]


class Prog:
    ENG = ("pe", "act", "dve", "pool", "sp")

    def __init__(self, nc):
        self.nc = nc
        self.e = {"pe": nc.tensor, "act": nc.scalar, "dve": nc.vector, "pool": nc.gpsimd, "sp": nc.sync}
        self.stack = contextlib.ExitStack()
        self.sems = {}
        self.val = {}
        self.seen = {e: {} for e in self.ENG}
        for e in self.ENG:
            self.sems[e] = self.stack.enter_context(nc.semaphore("c_" + e))
            self.val[e] = 0
        self.dpool = {"hw": [], "sw": []}
        self.ndsem = 0
        self.live_dsems = set()
        self.n_ins = 0
        self.n_wait = 0

    def get_dsem(self, kind):
        if self.dpool[kind]:
            k = self.dpool[kind].pop()
        else:
            k = "d%s%d" % (kind, self.ndsem)
            self.ndsem += 1
            self.sems[k] = self.stack.enter_context(self.nc.semaphore(k))
            self.val[k] = 0
        self.live_dsems.add(k)
        return k

    def release(self, bufs):
        for b in bufs:
            if b.dsem is not None:
                self.live_dsems.discard(b.dsem)
                self.dpool[b.dkind].append(b.dsem)
                b.dsem = None

    def clear_all(self):
        for k, h in self.sems.items():
            self.nc.gpsimd.sem_clear(h)

    def _need(self, e, waits, tok):
        if tok is None:
            return
        k, v = tok[0], tok[1]
        if self.seen[e].get(k, 0) >= v:
            return
        if waits.get(k, 0) < v:
            waits[k] = v

    def _deps(self, e, rd, wr):
        waits = {}
        for b in rd:
            self._need(e, waits, b.lw)
            if b.psum:
                for k, (v, re_) in b.rd.items():
                    if re_ != e:
                        self._need(e, waits, (k, v))
        for b in wr:
            if b.lw is not None and not (e == "pe" and b.lw[2] == "pe"):
                self._need(e, waits, b.lw)
            for k, (v, re_) in b.rd.items():
                self._need(e, waits, (k, v))
        for k, v in waits.items():
            self.e[e].wait_ge(self.sems[k], v)
            self.seen[e][k] = v
            self.n_wait += 1

    def op(self, e, fn, rd=(), wr=()):
        self._deps(e, rd, wr)
        ins = fn()
        self.val[e] += 1
        ins.then_inc(self.sems[e], 1)
        self.n_ins += 1
        v = self.val[e]
        for b in rd:
            b.rd[e] = (v, e)
        for b in wr:
            b.lw = (e, v, e)
            b.rd = {}
        return ins

    def dma(self, e, dst, src, out_ap, in_ap, **kw):
        assert dst.dsem is not None, dst.name
        assert (dst.dkind == "sw") == (e == "pool"), (dst.name, e)
        self._deps(e, (src,), (dst,))
        ins = self.e[e].dma_start(out=out_ap, in_=in_ap, **kw)
        k = dst.dsem
        self.val[k] += 16
        ins.then_inc(self.sems[k], 16)
        self.n_ins += 1
        v = self.val[k]
        src.rd[k] = (v, "dma")
        dst.lw = (k, v, "dma")
        dst.rd = {}
        return ins

    def barrier(self):
        for k in list(self.live_dsems):
            v = self.val[k]
            if v > 0 and self.seen["sp"].get(k, 0) < v:
                self.e["sp"].wait_ge(self.sems[k], v)
                self.seen["sp"][k] = v
        self.nc.all_engine_barrier()
        for e in self.ENG:
            for k in self.sems:
                self.seen[e][k] = self.val[k]

    def wait_all_dma(self, e):
        for k in list(self.live_dsems):
            v = self.val[k]
            if v > 0 and self.seen[e].get(k, 0) < v:
                self.e[e].wait_ge(self.sems[k], v)
                self.seen[e][k] = v


class Ctx:
    def __init__(self, P):
        self.P = P
        self.nc = P.nc
        self.stack = contextlib.ExitStack()
        self.bufs = []

    def sb(self, name, shape, dt, dma=False):
        t = self.stack.enter_context(self.nc.sbuf_tensor(name, list(shape), dt))
        b = Buf(self.P, t, name, dma=dma)
        self.bufs.append(b)
        return b

    def ps(self, name, shape=(128, 512), dt=F32):
        t = self.stack.enter_context(self.nc.psum_tensor(name, list(shape), dt))
        b = Buf(self.P, t, name)
        b.psum = True
        self.bufs.append(b)
        return b

    def close(self):
        self.P.barrier()
        self.P.release(self.bufs)
        self.stack.close()


_uid = [0]


def uname(s):
    _uid[0] += 1
    return "%s_%d" % (s, _uid[0])


def make_consts(P, C):
    nc = P.nc
    idf = C.sb(uname("ident_f"), (128, 128), F32)
    idb = C.sb(uname("ident_b"), (128, 128), BF16)
    P.op("pool", lambda: nc.gpsimd.memset(idf[:], 0.0), wr=(idf,))
    P.op("pool", lambda: nc.gpsimd.affine_select(out=idf[:], in_=idf[:], compare_op=ALU.not_equal, fill=1.0,
                                                  base=0, pattern=[[-1, 128]], channel_multiplier=1),
         rd=(idf,), wr=(idf,))
    P.op("pool", lambda: nc.gpsimd.tensor_copy(out=idb[:], in_=idf[:]), rd=(idf,), wr=(idb,))
    return idf, idb


def load_gain_bc(P, C, name, src_ap, scale):
    nc = P.nc
    g = C.sb(name, (128, D), F32, dma=True)
    dsrc = Buf(P, None, name + "_src")
    P.dma("sp", g, dsrc, g[:], src_ap.partition_broadcast(128))
    if scale != 1.0:
        P.op("pool", lambda: nc.gpsimd.tensor_scalar(out=g[:], in0=g[:], scalar1=float(scale), scalar2=None,
                                                      op0=ALU.mult), rd=(g,), wr=(g,))
    return g


def prenorm_tile(P, nc, xs, gpre, hT, col0, idb, sq_junk, stat, hb, tp_ps):
    P.op("act", lambda: nc.scalar.activation(out=sq_junk[:], in_=xs[:], func=AF.Square, accum_out=stat[:, 0:1]),
         rd=(xs,), wr=(sq_junk, stat))
    P.op("act", lambda: nc.scalar.activation(out=stat[:, 6:7], in_=stat[:, 0:1], func=AF.Sqrt, bias=float(EPS),
                                             scale=1.0 / D), rd=(stat,), wr=(stat,))
    P.op("dve", lambda: nc.vector.reciprocal(out=stat[:, 1:2], in_=stat[:, 6:7]), rd=(stat,), wr=(stat,))
    P.op("dve", lambda: nc.vector.scalar_tensor_tensor(out=hb[:], in0=xs[:], scalar=stat[:, 1:2], in1=gpre[:],
                                                       op0=ALU.mult, op1=ALU.mult), rd=(xs, stat, gpre), wr=(hb,))
    for c in range(NCH):
        P.op("pe", lambda c=c: nc.tensor.transpose(out=tp_ps[:, c * 128:(c + 1) * 128],
                                                   in_=hb[:, c * 128:(c + 1) * 128], identity=idb[:]),
             rd=(hb, idb), wr=(tp_ps,))
    P.op("act", lambda: nc.scalar.copy(out=hT[:, :, col0:col0 + 128],
                                       in_=tp_ps[:].rearrange("p (c t) -> p c t", c=NCH)),
         rd=(tp_ps,), wr=(hT,))


def postnorm_tile(P, nc, f_ps, xs, gpost, xo, sq_junk, stat, tmp):
    P.op("act", lambda: nc.scalar.activation(out=sq_junk[:, 0:512], in_=f_ps[0][:], func=AF.Square,
                                             accum_out=stat[:, 2:3]), rd=(f_ps[0],), wr=(sq_junk, stat))
    P.op("act", lambda: nc.scalar.activation(out=sq_junk[:, 512:1024], in_=f_ps[1][:], func=AF.Square,
                                             accum_out=stat[:, 3:4]), rd=(f_ps[1],), wr=(sq_junk, stat))
    P.op("dve", lambda: nc.vector.tensor_tensor(out=stat[:, 4:5], in0=stat[:, 2:3], in1=stat[:, 3:4], op=ALU.add),
         rd=(stat,), wr=(stat,))
    P.op("act", lambda: nc.scalar.activation(out=stat[:, 7:8], in_=stat[:, 4:5], func=AF.Sqrt, bias=float(EPS),
                                             scale=1.0 / D), rd=(stat,), wr=(stat,))
    P.op("dve", lambda: nc.vector.reciprocal(out=stat[:, 5:6], in_=stat[:, 7:8]), rd=(stat,), wr=(stat,))
    for h in range(2):
        P.op("dve", lambda h=h: nc.vector.scalar_tensor_tensor(out=tmp[:, h * 512:(h + 1) * 512], in0=f_ps[h][:],
                                                               scalar=stat[:, 5:6],
                                                               in1=gpost[:, h * 512:(h + 1) * 512],
                                                               op0=ALU.mult, op1=ALU.mult),
             rd=(f_ps[h], stat, gpost), wr=(tmp,))
    P.op("pool", lambda: nc.gpsimd.tensor_tensor(out=xo[:], in0=tmp[:], in1=xs[:], op=ALU.add),
         rd=(tmp, xs), wr=(xo,))


def load_w(P, wb, out_ap, src_ap):
    P.dma("pool", wb, Buf(P, None, "wsrc"), out_ap, src_ap)


def ffn_phase(P, T, xin, xout, xin_ap, xout_ap, wg_ap, wu_ap, wd_ap, gpre_ap, gpost_ap):
    nc = P.nc
    C = Ctx(P)
    idf, idb = make_consts(P, C)
    gpre = load_gain_bc(P, C, uname("gpre"), gpre_ap, 1.0)
    gpost = load_gain_bc(P, C, uname("gpost"), gpost_ap, 0.5)
    NB = 11
    CB = DFF // NB
    JB = NFF // NB
    wg = [C.sb(uname("wg"), (128, NCH, CB), BF16, dma="sw") for _ in range(NB)]
    wu = [C.sb(uname("wu"), (128, NCH, CB), BF16, dma="sw") for _ in range(NB)]
    wd = [C.sb(uname("wd"), (128, 2, D), BF16, dma="sw") for _ in range(NFF // 2)]
    wg_v = wg_ap.rearrange("(c p) n -> p c n", p=128)
    wu_v = wu_ap.rearrange("(c p) n -> p c n", p=128)
    wd_v = wd_ap.rearrange("(j p) n -> p j n", p=128)
    for b in range(NB):
        for (wl, wv) in ((wg, wg_v), (wu, wu_v)):
            load_w(P, wl[b], wl[b][:, :, :], wv[:, :, b * CB:(b + 1) * CB])
        if b >= 5:
            for j2 in (2 * (b - 5), 2 * (b - 5) + 1):
                if j2 < NFF // 2:
                    load_w(P, wd[j2], wd[j2][:, :, :], wd_v[:, 2 * j2:2 * j2 + 2, :])

    TT = 512 if T >= 512 else T
    NS = TT // 128
    xs = [C.sb(uname("xs"), (128, D), F32, dma=True) for _ in range(2)]
    xp = [C.sb(uname("xp"), (128, D), F32, dma=True) for _ in range(2)]
    hT = [C.sb(uname("hT"), (128, NCH, TT), BF16) for _ in range(2)]
    actT = C.sb(uname("actT"), (128, NFF, TT), BF16)
    sq_junk = C.sb(uname("sqj"), (128, D), BF16)
    hb = [C.sb(uname("hb"), (128, D), BF16) for _ in range(2)]
    tmp = C.sb(uname("tmp"), (128, D), F32)
    sg = [C.sb(uname("sg"), (128, TT), BF16) for _ in range(2)]
    stats = [C.sb(uname("stat"), (128, 8), F32) for _ in range(4)]
    tp_ps = C.ps(uname("tp"), (128, D), BF16)
    g_ps = [C.ps(uname("gps")) for _ in range(2)]
    u_ps = [C.ps(uname("ups")) for _ in range(2)]
    f3 = [C.ps(uname("fps")) for _ in range(3)]
    xin_v = xin_ap.rearrange("(n p) d -> n p d", p=128)
    xout_v = xout_ap.rearrange("(n p) d -> n p d", p=128)
    nt = T // TT

    def pre(ti, s):
        xst = xs[s % 2]
        P.dma("sp", xst, xin, xst[:], xin_v[ti * NS + s])
        prenorm_tile(P, nc, xst, gpre, hT[ti % 2], s * 128, idb, sq_junk, stats[s % 4], hb[s % 2], tp_ps)

    for s in range(NS):
        pre(0, s)
    it = 0
    for ti in range(nt):
        hTt = hT[ti % 2]
        for j in range(NFF):
            gp, up = g_ps[j % 2], u_ps[j % 2]
            bb, a0 = j // JB, (j % JB) * 128
            for (wl, pp) in ((wg, gp), (wu, up)):
                for c in range(NCH):
                    P.op("pe", lambda: nc.tensor.matmul(pp[:, 0:TT], lhsT=wl[bb][:, c, a0:a0 + 128],
                                                        rhs=hTt[:, c, :], start=(c == 0), stop=(c == NCH - 1)),
                         rd=(wl[bb], hTt), wr=(pp,))
            sgj = sg[j % 2]
            P.op("act", lambda: nc.scalar.activation(out=sgj[:], in_=gp[:, 0:TT], func=AF.Silu),
                 rd=(gp,), wr=(sgj,))
            P.op("dve", lambda: nc.vector.tensor_tensor(out=actT[:, j, :], in0=up[:, 0:TT], in1=sgj[:],
                                                        op=ALU.mult), rd=(up, sgj), wr=(actT,))
            if ti + 1 < nt and j in (4, 8, 12, 16) and (j // 4 - 1) < NS:
                pre(ti + 1, j // 4 - 1)
        for s in range(NS):
            xst = xp[it % 2]
            f_ps = [f3[(2 * it) % 3], f3[(2 * it + 1) % 3]]
            P.dma("sp", xst, xin, xst[:], xin_v[ti * NS + s])
            for h in range(2):
                for j in range(NFF):
                    P.op("pe", lambda: nc.tensor.matmul(
                        f_ps[h][:, :], lhsT=actT[:, j, s * 128:(s + 1) * 128],
                        rhs=wd[j // 2][:, j % 2, h * 512:(h + 1) * 512],
                        start=(j == 0), stop=(j == NFF - 1)), rd=(actT, wd[j // 2]), wr=(f_ps[h],))
            it += 1
            postnorm_tile(P, nc, f_ps, xst, gpost, xst, sq_junk, stats[s % 4], tmp)
            P.dma("sp", xout, xst, xout_v[ti * NS + s], xst[:])
    C.close()


def fox_phase(P, T, xin, xout, xin_ap, xout_ap, win_ap, fb_ap, qg_ap, kg_ap, wout_ap, gpre_ap, gpost_ap, scr):
    nc = P.nc
    H = 16
    NT = T // 512 if T >= 512 else 1
    TT = 512 if T >= 512 else T
    NSUB = T // 128
    hTd_ap, hTd = scr["hT"]
    oTd_ap, oTd = scr["oT"]
    win_v = win_ap.rearrange("(c p) n -> p c n", p=128)
    xin_v = xin_ap.rearrange("(n p) d -> n p d", p=128)
    xout_v = xout_ap.rearrange("(n p) d -> n p d", p=128)
    dsrc = Buf(P, None, "dsrc")

    C = Ctx(P)
    idf, idb = make_consts(P, C)
    gpre = load_gain_bc(P, C, uname("gpre"), gpre_ap, 1.0)
    wf = C.sb(uname("wf"), (128, NCH, H), BF16, dma="sw")
    load_w(P, wf, wf[:], win_v[:, :, 4 * D:4 * D + H])
    nfb = C.sb(uname("nfb"), (H, 1), F32, dma=True)
    P.dma("sp", nfb, dsrc, nfb[:], fb_ap.rearrange("(h o) -> h o", o=1))
    P.op("dve", lambda: nc.vector.tensor_scalar(out=nfb[:], in0=nfb[:], scalar1=-1.0, scalar2=None, op0=ALU.mult),
         rd=(nfb,), wr=(nfb,))
    cum = C.sb(uname("cum"), (H, T), F32)
    ones16 = C.sb(uname("ones16"), (H, TT), F32)
    P.op("pool", lambda: nc.gpsimd.memset(ones16[:], 1.0), wr=(ones16,))
    lfe = [C.sb(uname("lfe"), (H, TT), F32) for _ in range(2)]
    xs = [C.sb(uname("xs"), (128, D), F32, dma=True) for _ in range(2)]
    hTt = [C.sb(uname("hTt"), (128, NCH, TT), BF16) for _ in range(2)]
    sq_junk = C.sb(uname("sqj"), (128, D), BF16)
    hb = [C.sb(uname("hb"), (128, D), BF16) for _ in range(2)]
    stats = [C.sb(uname("stat"), (128, 8), F32) for _ in range(4)]
    tp_ps = C.ps(uname("tp"), (128, D), BF16)
    f_ps = [C.ps(uname("fl")) for _ in range(2)]
    NS = TT // 128
    for ti in range(NT):
        ht = hTt[ti % 2]
        for s_ in range(NS):
            xst = xs[s_ % 2]
            P.dma("sp", xst, xin, xst[:], xin_v[ti * NS + s_])
            prenorm_tile(P, nc, xst, gpre, ht, s_ * 128, idb, sq_junk, stats[s_ % 4], hb[s_ % 2], tp_ps)
        P.dma("sp", hTd, ht, hTd_ap[:, :, ti * TT:(ti + 1) * TT], ht[:])
        fp = f_ps[ti % 2]
        for c in range(NCH):
            P.op("pe", lambda: nc.tensor.matmul(fp[0:H, 0:TT], lhsT=wf[:, c, :], rhs=ht[:, c, :],
                                                start=(c == 0), stop=(c == NCH - 1)), rd=(wf, ht), wr=(fp,))
        le = lfe[ti % 2]
        P.op("act", lambda: nc.scalar.activation(out=le[:], in_=fp[0:H, 0:TT], func=AF.Exp, bias=nfb[:, 0:1],
                                                 scale=-1.0), rd=(fp, nfb), wr=(le,))
        P.op("act", lambda: nc.scalar.activation(out=le[:], in_=le[:], func=AF.Ln, bias=1.0, scale=1.0),
             rd=(le,), wr=(le,))
        init = 0.0 if ti == 0 else cum[:, ti * TT - 1:ti * TT]
        P.op("dve", lambda: nc.vector.tensor_tensor_scan(out=cum[:, ti * TT:(ti + 1) * TT], data0=ones16[:],
                                                         data1=le[:], initial=init, op0=ALU.mult,
                                                         op1=ALU.subtract), rd=(ones16, le, cum), wr=(cum,))
    cumd_ap, cumd = scr["cum"]
    P.dma("sp", cumd, cum, cumd_ap[:, 0:T], cum[:])
    C.close()

    C = Ctx(P)
    idf, idb = make_consts(P, C)
    cum = C.sb(uname("cum"), (H, T), F32, dma=True)
    P.dma("sp", cum, cumd, cum[:], cumd_ap[:, 0:T])
    sel = C.sb(uname("sel"), (H, H, 128), F32)
    P.op("pool", lambda: nc.gpsimd.memset(sel[:], 0.0), wr=(sel,))
    P.op("pool", lambda: nc.gpsimd.affine_select(out=sel[:], in_=sel[:], compare_op=ALU.not_equal, fill=1.0, base=0,
                                                  pattern=[[-1, H], [0, 128]], channel_multiplier=1),
         rd=(sel,), wr=(sel,))
    swp = C.sb(uname("swp"), (128, 128), F32)
    P.op("pool", lambda: nc.gpsimd.memset(swp[:], 0.0), wr=(swp,))
    P.op("pool", lambda: nc.gpsimd.affine_select(out=swp[:, 0:64], in_=swp[:, 0:64], compare_op=ALU.not_equal,
                                                  fill=1.0, base=-64, pattern=[[-1, 64]], channel_multiplier=1),
         rd=(swp,), wr=(swp,))
    P.op("pool", lambda: nc.gpsimd.affine_select(out=swp[:, 64:128], in_=swp[:, 64:128], compare_op=ALU.not_equal,
                                                  fill=1.0, base=0, pattern=[[-1, 64]], channel_multiplier=1),
         rd=(swp,), wr=(swp,))
    bones = C.sb(uname("bones"), (128, 128), F32)
    P.op("pool", lambda: nc.gpsimd.memset(bones[:], 0.0), wr=(bones,))
    P.op("pool", lambda: nc.gpsimd.memset(bones[0:64, 0:64], 1.0), wr=(bones,))
    P.op("pool", lambda: nc.gpsimd.memset(bones[64:128, 64:128], 1.0), wr=(bones,))
    tri = C.sb(uname("tri"), (128, 128), F32)
    P.op("pool", lambda: nc.gpsimd.memset(tri[:], 0.0), wr=(tri,))
    P.op("pool", lambda: nc.gpsimd.affine_select(out=tri[:], in_=tri[:], compare_op=ALU.is_ge, fill=-30000.0, base=0,
                                                  pattern=[[1, 128]], channel_multiplier=-1), rd=(tri,), wr=(tri,))
    ncumT = C.sb(uname("ncumT"), (128, NSUB, H), F32)
    ct_ps = C.ps(uname("ctps"))
    for g0 in range(0, NSUB, 32):
        gn = min(32, NSUB - g0)
        for b in range(gn):
            P.op("pe", lambda: nc.tensor.transpose(out=ct_ps[:, b * H:(b + 1) * H],
                                                   in_=cum[:, (g0 + b) * 128:(g0 + b + 1) * 128],
                                                   identity=idf[0:H, 0:H]), rd=(cum, idf), wr=(ct_ps,))
        P.op("dve", lambda: nc.vector.tensor_scalar(out=ncumT[:, g0:g0 + gn, :],
                                                    in0=ct_ps[:, 0:gn * H].rearrange("p (b h) -> p b h", h=H),
                                                    scalar1=-1.0, scalar2=None, op0=ALU.mult),
             rd=(ct_ps,), wr=(ncumT,))
    gq2 = C.sb(uname("gq2"), (128, 1), F32, dma=True)
    gk2 = C.sb(uname("gk2"), (128, 1), F32, dma=True)
    for hh in range(2):
        P.dma("sp", gq2, dsrc, gq2[hh * 64:(hh + 1) * 64, :], qg_ap.rearrange("(d o) -> d o", o=1))
        P.dma("sp", gk2, dsrc, gk2[hh * 64:(hh + 1) * 64, :], kg_ap.rearrange("(d o) -> d o", o=1))
    P.op("dve", lambda: nc.vector.tensor_scalar(out=gq2[:], in0=gq2[:], scalar1=0.125, scalar2=None, op0=ALU.mult),
         rd=(gq2,), wr=(gq2,))
    wq = [C.sb(uname("wq"), (128, NCH, 4, 128), BF16, dma="sw") for _ in range(2)]
    hTt = [C.sb(uname("hTt"), (128, NCH, TT), BF16, dma=True) for _ in range(2)]
    qT = C.sb(uname("qT"), (128, T), BF16)
    kT = "kT"
    kTz = [C.sb(uname("kTz"), (128, T), BF16) for _ in range(2)]
    for hh in range(2):
        P.op("pool", lambda: nc.gpsimd.memset(kTz[hh][:], 0.0), wr=(kTz[hh],))
    sgT = C.sb(uname("sgT"), (128, T), BF16)
    oT = C.sb(uname("oT"), (128, T), BF16)
    Va = [C.sb(uname("Va"), (128, NSUB, 128), BF16) for _ in range(2)]
    P.op("pool", lambda: nc.gpsimd.memset(Va[0][:, :, 64:128], 1.0), wr=(Va[0],))
    P.op("pool", lambda: nc.gpsimd.memset(Va[1][:, :, 0:64], 1.0), wr=(Va[1],))
    cbc = C.sb(uname("cbc"), (128, T), F32)
    sqf = [C.sb(uname("sqf"), (128, TT), F32) for _ in range(4)]
    rsf = [C.sb(uname("rsf"), (128, TT), F32) for _ in range(2)]
    tmpb = [C.sb(uname("tmpb"), (128, TT), F32) for _ in range(4)]
    osb = C.sb(uname("osb"), (128, TT), F32)
    rden = C.sb(uname("rden"), (128, TT), F32)
    onum = C.sb(uname("onum"), (128, TT), F32)
    st2 = [C.ps(uname("st2"), (128, 2 * TT), F32) for _ in range(2)]

    class HalfView:
        def __init__(self, t, off):
            self.t, self.off = t, off

        def __getitem__(self, k):
            p_, c_ = k
            return self.t[p_, slice(self.off + (c_.start or 0), self.off + c_.stop)]
    sth = []
    for k_ in range(2):
        for hf_ in range(2):
            b_ = Buf(P, HalfView(st2[k_].t, hf_ * TT), uname("sth"))
            b_.psum = True
            C.bufs.append(b_)
            sth.append(b_)
    acc = [C.ps(uname("acc")) for _ in range(3)] + [ct_ps]
    bank = [st2[0], st2[1]] + acc
    pT2 = [C.sb(uname("pT2"), (128, 2 * TT), BF16) for _ in range(4)]
    ecr = [C.sb(uname("ecr"), (128, TT), F32) for _ in range(2)]
    dsw = C.sb(uname("dsw"), (128, TT), F32, dma=True)
    cbcm = C.sb(uname("cbcm"), (128, T), F32)
    tri4 = C.sb(uname("tri4"), (128, TT), F32)
    for r_ in range(TT // 128):
        P.op("pool", lambda: nc.gpsimd.tensor_copy(out=tri4[:, r_ * 128:(r_ + 1) * 128], in_=tri[:]), rd=(tri,),
             wr=(tri4,))
    ncb0 = C.sb(uname("ncb0"), (128, NT), F32)
    biasall = C.sb(uname("biasall"), (128, NT, NSUB), F32)
    def load_pair_w(hp_):
        w_ = wq[hp_ % 2]
        for qi in range(4):
            load_w(P, w_, w_[:, :, qi, :], win_v[:, :, qi * D + hp_ * 128: qi * D + (hp_ + 1) * 128])

    load_pair_w(0)
    for hp in range(H // 2):
        w = wq[hp % 2]
        if hp + 1 < H // 2:
            load_pair_w(hp + 1)
        def psets(ti):
            pset = ti % 2
            return sth[2 * pset], sth[2 * pset + 1], acc[2 * pset], acc[2 * pset + 1]

        def p_mm(ti):
            ht = hTt[ti % 2]
            P.dma("sp", ht, hTd, ht[:], hTd_ap[:, :, ti * TT:(ti + 1) * TT])
            qb_, kb_, g_ps, v_ps = psets(ti)
            for (qi, pp, pb_) in ((0, qb_[:, 0:TT], qb_), (1, kb_[:, 0:TT], kb_), (3, g_ps[:, 0:TT], g_ps)):
                for c in range(NCH):
                    P.op("pe", lambda: nc.tensor.matmul(pp, lhsT=w[:, c, qi, :], rhs=ht[:, c, :],
                                                        start=(c == 0), stop=(c == NCH - 1)), rd=(w, ht), wr=(pb_,))
            for s_ in range(NS):
                for c in range(NCH):
                    P.op("pe", lambda: nc.tensor.matmul(v_ps[:, s_ * 128:(s_ + 1) * 128],
                                                        lhsT=ht[:, c, s_ * 128:(s_ + 1) * 128], rhs=w[:, c, 2, :],
                                                        start=(c == 0), stop=(c == NCH - 1)), rd=(w, ht), wr=(v_ps,))

        def p_part1(ti):
            tsl = slice(ti * TT, (ti + 1) * TT)
            qb_, kb_, g_ps, v_ps = psets(ti)
            vv = v_ps[:, 0:TT].rearrange("p (s e) -> p s e", e=128)
            P.op("act", lambda: nc.scalar.copy(out=Va[0][:, ti * NS:(ti + 1) * NS, 0:64], in_=vv[:, :, 0:64]),
                 rd=(v_ps,), wr=(Va[0],))
            P.op("dve", lambda: nc.vector.tensor_copy(out=Va[1][:, ti * NS:(ti + 1) * NS, 64:128],
                                                      in_=vv[:, :, 64:128]), rd=(v_ps,), wr=(Va[1],))
            P.op("act", lambda: nc.scalar.activation(out=sgT[:, tsl], in_=g_ps[:, 0:TT], func=AF.Sigmoid),
                 rd=(g_ps,), wr=(sgT,))
            for n_, qk_b in enumerate((qb_, kb_)):
                sq = sqf[2 * (ti % 2) + n_]
                P.op("act", lambda: nc.scalar.activation(out=sq[:], in_=qk_b[:, 0:TT], func=AF.Square),
                     rd=(qk_b,), wr=(sq,))

        def p_norm(ti):
            tsl = slice(ti * TT, (ti + 1) * TT)
            qb_, kb_, g_ps, v_ps = psets(ti)
            ss_ps = v_ps
            for n_, (gg, dst, qk_b) in enumerate(((gq2, qT, qb_), (gk2, kT, kb_))):
                sq, rs = sqf[2 * (ti % 2) + n_], rsf[n_]
                pp = qk_b[:, 0:TT]
                P.op("pe", lambda: nc.tensor.matmul(ss_ps[:, 0:TT], lhsT=bones[:], rhs=sq[:], start=True, stop=True),
                     rd=(bones, sq), wr=(ss_ps,))
                P.op("act", lambda: nc.scalar.activation(out=rs[:], in_=ss_ps[:, 0:TT], func=AF.Ln,
                                                         bias=float(EPS), scale=1.0 / 64), rd=(ss_ps,), wr=(rs,))
                P.op("act", lambda: nc.scalar.activation(out=rs[:], in_=rs[:], func=AF.Exp, scale=-0.5),
                     rd=(rs,), wr=(rs,))
                if dst is kT:
                    for hh in range(2):
                        hs_ = slice(hh * 64, (hh + 1) * 64)
                        P.op("dve", lambda: nc.vector.scalar_tensor_tensor(out=kTz[hh][hs_, tsl],
                                                                           in0=qk_b[hs_, 0:TT],
                                                                           scalar=gg[hs_, 0:1], in1=rs[hs_, :],
                                                                           op0=ALU.mult, op1=ALU.mult),
                             rd=(qk_b, gg, rs), wr=(kTz[hh],))
                else:
                    P.op("dve", lambda: nc.vector.scalar_tensor_tensor(out=dst[:, tsl], in0=pp,
                                                                       scalar=gg[:, 0:1], in1=rs[:], op0=ALU.mult,
                                                                       op1=ALU.mult), rd=(qk_b, gg, rs), wr=(dst,))

        p_mm(0)
        p_part1(0)
        for ti in range(1, NT):
            p_mm(ti)
            p_part1(ti)
            p_norm(ti - 1)
        p_norm(NT - 1)
        for par in range(2):
            h = 2 * hp + par
            pr = slice(par * 64, (par + 1) * 64)
            for ti in range(NT):
                cp = acc[ti % 4]
                P.op("pe", lambda: nc.tensor.matmul(cp[:, 0:TT], lhsT=sel[:, h, :], rhs=cum[:, ti * TT:(ti + 1) * TT],
                                                    start=True, stop=True), rd=(sel, cum), wr=(cp,))
                P.op("act", lambda: nc.scalar.copy(out=cbc[:, ti * TT:(ti + 1) * TT], in_=cp[:, 0:TT]),
                     rd=(cp,), wr=(cbc,))
                P.op("dve", lambda: nc.vector.tensor_tensor(out=cbcm[:, ti * TT:(ti + 1) * TT],
                                                            in0=cbc[:, ti * TT:(ti + 1) * TT], in1=tri4[:],
                                                            op=ALU.add), rd=(cbc, tri4), wr=(cbcm,))
            pairs = [list(range(t0_, min(t0_ + 2, NT))) for t0_ in range(0, NT, 2)]
            c0v = cbc[:].rearrange("p (n t) -> p n t", t=TT * 2 if NT > 1 else TT)[:, :, 0]
            P.op("dve", lambda: nc.vector.tensor_scalar(out=ncb0[:, 0:len(pairs)], in0=c0v, scalar1=-1.0, scalar2=None,
                                                        op0=ALU.mult), rd=(cbc,), wr=(ncb0,))
            for m, tl in enumerate(pairs):
                if m == 0:
                    continue
                nb_ = tl[0] * NS
                P.op("dve", lambda: nc.vector.tensor_scalar(out=biasall[:, m, 0:nb_], in0=ncumT[:, 0:nb_, h],
                                                            scalar1=cbc[:, tl[0] * TT:tl[0] * TT + 1], scalar2=None,
                                                            op0=ALU.add), rd=(ncumT, cbc), wr=(biasall,))
            for m, tl in enumerate(pairs):
                ntl = len(tl)
                noff = tl[0] * NS
                a_off = [acc[0], acc[1]]
                a_dg = [acc[2], acc[3]]
                offs = [("off", kb) for kb in range(noff)]
                dvs = []
                for xi, ti in enumerate(tl):
                    for kb in range(noff, (ti + 1) * NS):
                        r = kb - ti * NS
                        dvs.append(("dv", xi, kb, (128 * r if r > 0 else 0), r >= 0))
                items = offs + dvs
                dv_first, dv_last, off_first, off_last = {}, {}, None, None
                for n_, it_ in enumerate(items):
                    if it_[0] == "off":
                        off_first = n_ if off_first is None else off_first
                        off_last = n_
                    else:
                        dv_first.setdefault(it_[1], n_)
                        dv_last[it_[1]] = n_
                LAG = 3
                for xi, ti in enumerate(tl):
                    if noff > 0:
                        P.op("act", lambda: nc.scalar.activation(out=ecr[xi][:], in_=cbc[:, ti * TT:(ti + 1) * TT],
                                                                 func=AF.Exp, bias=ncb0[:, m:m + 1], scale=1.0),
                             rd=(cbc, ncb0), wr=(ecr[xi],))

                def emit_s(n):
                    it_ = items[n]
                    pb = pT2[n % 4]
                    if it_[0] == "off":
                        kb = it_[1]
                        hb_ = [sth[2 * (n % 2) + xi] for xi in range(ntl)]
                        for xi, ti in enumerate(tl):
                            P.op("pe", lambda: nc.tensor.matmul(hb_[xi][:, 0:TT],
                                                                lhsT=kTz[par][:, kb * 128:(kb + 1) * 128],
                                                                rhs=qT[:, ti * TT:(ti + 1) * TT], start=True, stop=True),
                                 rd=(kTz[par], qT), wr=(hb_[xi],))
                        P.op("act", lambda: nc.scalar.activation(out=pb[:, 0:ntl * TT], in_=st2[n % 2].t[:, 0:ntl * TT],
                                                                 func=AF.Exp, bias=biasall[:, m, kb:kb + 1], scale=1.0),
                             rd=tuple(hb_) + (biasall,), wr=(pb,))
                        return
                    _, xi, kb, c0, tri_ = it_
                    ti = tl[xi]
                    dvn = n - len(offs)
                    sp_ = sth[dvn % 4]
                    P.op("pe", lambda: nc.tensor.matmul(sp_[:, c0:TT], lhsT=kTz[par][:, kb * 128:(kb + 1) * 128],
                                                        rhs=qT[:, ti * TT + c0:(ti + 1) * TT], start=True, stop=True),
                         rd=(kTz[par], qT), wr=(sp_,))
                    tb = tmpb[dvn % 4]
                    if tri_:
                        P.op("dve", lambda: nc.vector.scalar_tensor_tensor(
                            out=tb[:, c0:c0 + 128], in0=sp_[:, c0:c0 + 128], scalar=ncumT[:, kb, h:h + 1],
                            in1=cbcm[:, ti * TT + c0:ti * TT + c0 + 128], op0=ALU.add, op1=ALU.add),
                            rd=(sp_, ncumT, cbcm), wr=(tb,))
                        c1 = c0 + 128
                    else:
                        c1 = c0
                    if c1 < TT:
                        P.op("dve", lambda: nc.vector.scalar_tensor_tensor(
                            out=tb[:, c1:TT], in0=sp_[:, c1:TT], scalar=ncumT[:, kb, h:h + 1],
                            in1=cbc[:, ti * TT + c1:(ti + 1) * TT], op0=ALU.add, op1=ALU.add),
                            rd=(sp_, ncumT, cbc), wr=(tb,))
                    P.op("act", lambda: nc.scalar.activation(out=pb[:, c0:TT], in_=tb[:, c0:TT], func=AF.Exp),
                         rd=(tb,), wr=(pb,))

                def emit_pv(n):
                    it_ = items[n]
                    pb = pT2[n % 4]
                    if it_[0] == "off":
                        kb = it_[1]
                        for xi, ti in enumerate(tl):
                            P.op("pe", lambda: nc.tensor.matmul(a_off[xi][:, 0:TT], lhsT=Va[par][:, kb, :],
                                                                rhs=pb[:, xi * TT:(xi + 1) * TT],
                                                                start=(n == off_first), stop=(n == off_last)),
                                 rd=(Va[par], pb), wr=(a_off[xi],))
                        return
                    _, xi, kb, c0, tri_ = it_
                    P.op("pe", lambda: nc.tensor.matmul(a_dg[xi][:, c0:TT], lhsT=Va[par][:, kb, :], rhs=pb[:, c0:TT],
                                                        start=(n == dv_first[xi]), stop=(n == dv_last[xi])),
                         rd=(Va[par], pb), wr=(a_dg[xi],))

                for n in range(len(items) + LAG):
                    if n < len(items):
                        emit_s(n)
                    if n - LAG >= 0:
                        emit_pv(n - LAG)
                for xi, ti in enumerate(tl):
                    if noff > 0:
                        P.op("dve", lambda: nc.vector.tensor_tensor(out=osb[:], in0=a_off[xi][:, 0:TT], in1=ecr[xi][:],
                                                                    op=ALU.mult), rd=(a_off[xi], ecr[xi]), wr=(osb,))
                        P.op("dve", lambda: nc.vector.tensor_tensor(out=osb[:], in0=a_dg[xi][:, 0:TT], in1=osb[:],
                                                                    op=ALU.add), rd=(a_dg[xi], osb), wr=(osb,))
                    else:
                        P.op("act", lambda: nc.scalar.copy(out=osb[:], in_=a_dg[xi][:, 0:TT]), rd=(a_dg[xi],), wr=(osb,))
                    opr = slice((1 - par) * 64, (2 - par) * 64)
                    P.dma("sp", dsw, osb, dsw[pr, :], osb[opr, :])
                    P.op("act", lambda: nc.scalar.activation(out=rden[pr, :], in_=dsw[pr, :], func=AF.Ln),
                         rd=(dsw,), wr=(rden,))
                    P.op("act", lambda: nc.scalar.activation(out=rden[pr, :], in_=rden[pr, :], func=AF.Exp, scale=-1.0),
                         rd=(rden,), wr=(rden,))
                    P.op("dve", lambda: nc.vector.tensor_tensor(out=onum[pr, :], in0=osb[pr, :], in1=rden[pr, :],
                                                                op=ALU.mult), rd=(osb, rden), wr=(onum,))
                    P.op("dve", lambda: nc.vector.tensor_tensor(out=oT[pr, ti * TT:(ti + 1) * TT], in0=onum[pr, :],
                                                                in1=sgT[pr, ti * TT:(ti + 1) * TT], op=ALU.mult),
                         rd=(onum, sgT), wr=(oT,))
        P.dma("sp", oTd, oT, oTd_ap[:, hp, 0:T], oT[:])
    C.close()

    outproj_stage(P, T, xin, xout, xin_ap, xout_ap, oTd_ap, oTd, wout_ap, gpost_ap)


def outproj_stage(P, T, xin, xout, xin_ap, xout_ap, oTd_ap, oTd, wout_ap, gpost_ap):
    nc = P.nc
    TT = 512 if T >= 512 else T
    NT = T // TT
    NS = TT // 128
    xin_v = xin_ap.rearrange("(n p) d -> n p d", p=128)
    xout_v = xout_ap.rearrange("(n p) d -> n p d", p=128)
    C = Ctx(P)
    gpost = load_gain_bc(P, C, uname("gpost"), gpost_ap, 1.0)
    wo = C.sb(uname("wo"), (128, NCH, D), BF16, dma="sw")
    wo_v = wout_ap.rearrange("(c p) n -> p c n", p=128)
    for q in range(4):
        load_w(P, wo, wo[:, 2 * q:2 * q + 2, :], wo_v[:, 2 * q:2 * q + 2, :])
    ot = [C.sb(uname("ot"), (128, NCH, TT), BF16, dma=True) for _ in range(2)]
    xp = [C.sb(uname("xp"), (128, D), F32, dma=True) for _ in range(2)]
    sq_junk = C.sb(uname("sqj"), (128, D), BF16)
    tmp = C.sb(uname("tmp"), (128, D), F32)
    stats = [C.sb(uname("stat"), (128, 8), F32) for _ in range(4)]
    f_ps = [[C.ps(uname("fps")) for _ in range(2)] for _ in range(2)]
    it = 0
    for ti in range(NT):
        o_ = ot[ti % 2]
        P.dma("sp", o_, oTd, o_[:], oTd_ap[:, :, ti * TT:(ti + 1) * TT])
        for s_ in range(NS):
            xst = xp[it % 2]
            fp = f_ps[it % 2]
            P.dma("sp", xst, xin, xst[:], xin_v[ti * NS + s_])
            for hh in range(2):
                for c in range(NCH):
                    P.op("pe", lambda: nc.tensor.matmul(fp[hh][:, :], lhsT=o_[:, c, s_ * 128:(s_ + 1) * 128],
                                                        rhs=wo[:, c, hh * 512:(hh + 1) * 512],
                                                        start=(c == 0), stop=(c == NCH - 1)), rd=(o_, wo), wr=(fp[hh],))
            postnorm_tile(P, nc, fp, xst, gpost, xst, sq_junk, stats[it % 4], tmp)
            P.dma("sp", xout, xst, xout_v[ti * NS + s_], xst[:])
            it += 1
    C.close()


CW = 31
CK = 64
DEC_C = 0.6065306597126334


def rwkv_prep_stage(P, T, xin, xin_ap, gpre_ap, A, scr):
    nc = P.nc
    TT = 512 if T >= 512 else T
    NT = T // TT
    NS = TT // 128
    NQ = 4
    CR = 1792
    rw_ap, rw = scr["rw"]
    dc_ap, dcb = scr["dc"]
    yab_ap, yab = scr["oT"]
    xin_v = xin_ap.rearrange("(n p) d -> n p d", p=128)
    win_v = A["w_in"].rearrange("(c p) n -> p c n", p=128)
    dsrc = Buf(P, None, "dsrc")
    C = Ctx(P)
    idf, idb = make_consts(P, C)
    gpre = load_gain_bc(P, C, uname("gpre"), gpre_ap, 1.0)
    W1 = C.sb(uname("W1"), (128, NCH, CR), BF16)
    W2 = C.sb(uname("W2"), (128, NCH, CR), BF16)
    Wc = C.sb(uname("Wc"), (128, NCH, 1024), BF16, dma="sw")
    for q in range(4):
        load_w(P, Wc, Wc[:, 2 * q:2 * q + 2, :], win_v[:, 2 * q:2 * q + 2, CR:CR + 1024])
    C0 = Ctx(P)
    mu = C0.sb(uname("mu"), (128, CR), F32, dma=True)
    omu = C0.sb(uname("omu"), (128, CR), F32)
    P.dma("sp", mu, dsrc, mu[:], A["shift_mu"].partition_broadcast(128))
    P.op("dve", lambda: nc.vector.tensor_scalar(out=omu[:], in0=mu[:], scalar1=-1.0, scalar2=1.0, op0=ALU.mult,
                                                op1=ALU.add), rd=(mu,), wr=(omu,))
    stg = [C0.sb(uname("stg"), (128, CR), F32, dma=True) for _ in range(2)]
    for c in range(NCH):
        st = stg[c % 2]
        P.dma("sp", st, dsrc, st[:], win_v[:, c, 0:CR])
        P.op("dve", lambda: nc.vector.tensor_tensor(out=W2[:, c, :], in0=st[:], in1=mu[:], op=ALU.mult),
             rd=(st, mu), wr=(W2,))
        P.op("pool", lambda: nc.gpsimd.tensor_tensor(out=W1[:, c, :], in0=st[:], in1=omu[:], op=ALU.mult),
             rd=(st, omu), wr=(W1,))
    C0.close()
    lup = C.sb(uname("lup"), (128, 512), BF16, dma="sw")
    load_w(P, lup, lup[0:64, :], A["decay_up"])
    load_w(P, lup, lup[64:128, :], A["iclr_up"])
    gup = C.sb(uname("gup"), (128, 512), BF16, dma="sw")
    load_w(P, gup, gup[:], A["gate_up"])
    pnames = ("decay_base", "iclr_base", "k_k", "k_a", "r_k", "lnx_b", "conv_b", "ln_g", "ln_b")
    NPR = len(pnames) + CW
    prow = C.sb(uname("prow"), (NPR, 512), F32, dma=True)
    for i_, nm in enumerate(pnames):
        src = A[nm]
        if nm == "r_k":
            src = src.rearrange("h n -> (h n)")
        P.dma("sp", prow, dsrc, prow[i_:i_ + 1, :], src.rearrange("(o n) -> o n", o=1))
    P.dma("sp", prow, dsrc, prow[len(pnames):NPR, :], A["conv_w"])
    pc = {nm: C.sb(uname("pc_" + nm), (128, NQ), F32) for nm in pnames}
    cw = C.sb(uname("cw"), (128, NQ, CW), F32)
    ptp = C.ps(uname("ptp"))
    for q in range(NQ):
        P.op("pe", lambda: nc.tensor.transpose(out=ptp[:, q * 64:q * 64 + NPR], in_=prow[0:NPR, q * 128:(q + 1) * 128],
                                               identity=idf[0:NPR, 0:NPR]), rd=(prow, idf), wr=(ptp,))
    for q in range(NQ):
        for i_, nm in enumerate(pnames):
            P.op("dve", lambda: nc.vector.tensor_copy(out=pc[nm][:, q:q + 1], in_=ptp[:, q * 64 + i_:q * 64 + i_ + 1]),
                 rd=(ptp,), wr=(pc[nm],))
        P.op("act", lambda: nc.scalar.copy(out=cw[:, q, :], in_=ptp[:, q * 64 + len(pnames):q * 64 + NPR]),
             rd=(ptp,), wr=(cw,))
    omka = C.sb(uname("omka"), (128, NQ), F32)
    P.op("dve", lambda: nc.vector.tensor_scalar(out=omka[:], in0=pc["k_a"][:], scalar1=-1.0, scalar2=1.0,
                                                op0=ALU.mult, op1=ALU.add), rd=(pc["k_a"],), wr=(omka,))
    bones = C.sb(uname("bones"), (128, 128), F32)
    P.op("pool", lambda: nc.gpsimd.memset(bones[:], 0.0), wr=(bones,))
    P.op("pool", lambda: nc.gpsimd.memset(bones[0:64, 0:64], 1.0), rd=(bones,), wr=(bones,))
    P.op("pool", lambda: nc.gpsimd.memset(bones[64:128, 64:128], 1.0), rd=(bones,), wr=(bones,))
    ones = C.sb(uname("ones"), (128, 128), F32)
    P.op("pool", lambda: nc.gpsimd.memset(ones[:], 1.0), wr=(ones,))
    rmask = C.sb(uname("rmask"), (128, TT), F32)
    P.op("pool", lambda: nc.gpsimd.memset(rmask[:], 1.0), wr=(rmask,))
    P.op("pool", lambda: nc.gpsimd.memset(rmask[:].rearrange("p (c j) -> p c j", j=CK)[:, :, 0:1], 0.0),
         rd=(rmask,), wr=(rmask,))
    xs = [C.sb(uname("xs"), (128, D), F32, dma=True) for _ in range(2)]
    hTh = [C.sb(uname("hTh"), (128, NCH, TT + 1), BF16) for _ in range(2)]
    P.op("pool", lambda: nc.gpsimd.memset(hTh[0][:, :, 0:1], 0.0), wr=(hTh[0],))
    sq_junk = C.sb(uname("sqj"), (128, D), BF16)
    hb = [C.sb(uname("hb"), (128, D), BF16) for _ in range(2)]
    stats = [C.sb(uname("stat"), (128, 8), F32) for _ in range(4)]
    tdw = C.sb(uname("tdw"), (128, TT), BF16)
    sdg = C.sb(uname("sdg"), (128, TT), BF16)
    F = {}
    for nm in ("rf", "kf", "sgw", "av", "gf", "kk", "kk2", "rn", "kkn", "t1", "kn", "bb", "rk", "Lc", "eL", "enL",
               "Lx", "eLx", "bon"):
        F[nm] = C.sb(uname(nm), (128, TT), F32)
    pack = [C.sb(uname("pack"), (128, 7, TT), BF16) for _ in range(2)]
    dct = C.sb(uname("dct"), (128, NQ, T // CK), F32)
    glub = [C.sb(uname("glub"), (128, TT + CW - 1), F32) for _ in range(NQ)]
    for q in range(NQ):
        P.op("pool", lambda: nc.gpsimd.memset(glub[q][:, 0:CW - 1], 0.0), wr=(glub[q],))
    sgc = C.sb(uname("sgc"), (128, TT), F32)
    acc = [C.sb(uname("acc"), (128, TT), F32) for _ in range(NQ)]
    sqc = C.sb(uname("sqc"), (128, TT), F32)
    mean = C.sb(uname("mean"), (128, TT), F32)
    msq = C.sb(uname("msq"), (128, TT), F32)
    rstd = C.sb(uname("rstd"), (128, TT), F32)
    tcv = C.sb(uname("tcv"), (128, TT), F32)
    ybt = [C.sb(uname("ybt"), (128, TT), BF16) for _ in range(2)]
    tp_ps = C.ps(uname("tp"), (128, D), BF16)
    bk = [C.ps(uname("bk")) for _ in range(6)] + [ptp]

    def proj(pp, col0, shifted, ht):
        n = 2 * NCH if shifted else NCH
        i = 0
        for c in range(NCH):
            if shifted:
                P.op("pe", lambda: nc.tensor.matmul(pp[:, 0:TT], lhsT=W1[:, c, col0:col0 + 128], rhs=ht[:, c, 1:TT + 1],
                                                    start=(i == 0), stop=(i == n - 1)), rd=(W1, ht), wr=(pp,))
                i += 1
                P.op("pe", lambda: nc.tensor.matmul(pp[:, 0:TT], lhsT=W2[:, c, col0:col0 + 128], rhs=ht[:, c, 0:TT],
                                                    start=False, stop=(i == n - 1)), rd=(W2, ht), wr=(pp,))
                i += 1
            else:
                P.op("pe", lambda: nc.tensor.matmul(pp[:, 0:TT], lhsT=Wc[:, c, col0:col0 + 128], rhs=ht[:, c, 1:TT + 1],
                                                    start=(i == 0), stop=(i == n - 1)), rd=(Wc, ht), wr=(pp,))
                i += 1

    pk_i = 0
    for ti in range(NT):
        ht = hTh[ti % 2]
        tsl = slice(ti * TT, (ti + 1) * TT)
        for s_ in range(NS):
            xst = xs[s_ % 2]
            P.dma("sp", xst, xin, xst[:], xin_v[ti * NS + s_])
            prenorm_tile(P, nc, xst, gpre, ht, 1 + s_ * 128, idb, sq_junk, stats[s_ % 4], hb[s_ % 2], tp_ps)
        if ti + 1 < NT:
            P.op("pool", lambda: nc.gpsimd.tensor_copy(out=hTh[(ti + 1) % 2][:, :, 0:1], in_=ht[:, :, TT:TT + 1]),
                 rd=(ht,), wr=(hTh[(ti + 1) % 2],))
        proj(bk[6], 1536, True, ht)
        P.op("act", lambda: nc.scalar.activation(out=tdw[0:64, :], in_=bk[6][0:64, 0:TT], func=AF.Tanh),
             rd=(bk[6],), wr=(tdw,))
        P.op("act", lambda: nc.scalar.copy(out=tdw[64:128, :], in_=bk[6][64:128, 0:TT]), rd=(bk[6],), wr=(tdw,))
        proj(bk[5], 1664, True, ht)
        P.op("act", lambda: nc.scalar.activation(out=sdg[:], in_=bk[5][:, 0:TT], func=AF.Sigmoid),
             rd=(bk[5],), wr=(sdg,))
        for q in range(NQ):
            pk = pack[pk_i % 2]
            pk_i += 1
            qs = slice(q * 128, (q + 1) * 128)
            r_ps, k_ps, v_ps, zw_ps, za_ps, g_ps, s_ps = bk[0], bk[1], bk[2], bk[3], bk[4], bk[5], bk[6]
            proj(r_ps, q * 128, True, ht)
            proj(k_ps, 512 + q * 128, True, ht)
            proj(v_ps, 1024 + q * 128, True, ht)
            P.op("pe", lambda: nc.tensor.matmul(zw_ps[:, 0:TT], lhsT=lup[0:64, qs], rhs=tdw[0:64, :], start=True,
                                                stop=True), rd=(lup, tdw), wr=(zw_ps,))
            P.op("pe", lambda: nc.tensor.matmul(za_ps[:, 0:TT], lhsT=lup[64:128, qs], rhs=tdw[64:128, :], start=True,
                                                stop=True), rd=(lup, tdw), wr=(za_ps,))
            P.op("pe", lambda: nc.tensor.matmul(g_ps[:, 0:TT], lhsT=gup[:, qs], rhs=sdg[:], start=True, stop=True),
                 rd=(gup, sdg), wr=(g_ps,))
            col = lambda nm: pc[nm][:, q:q + 1]
            P.op("act", lambda: nc.scalar.copy(out=F["rf"][:], in_=r_ps[:, 0:TT]), rd=(r_ps,), wr=(F["rf"],))
            P.op("act", lambda: nc.scalar.copy(out=F["kf"][:], in_=k_ps[:, 0:TT]), rd=(k_ps,), wr=(F["kf"],))
            P.op("act", lambda: nc.scalar.copy(out=pk[:, 4, :], in_=v_ps[:, 0:TT]), rd=(v_ps,), wr=(pk,))
            P.op("act", lambda: nc.scalar.activation(out=F["sgw"][:], in_=zw_ps[:, 0:TT], func=AF.Sigmoid,
                                                     bias=col("decay_base")), rd=(zw_ps, pc["decay_base"]),
                 wr=(F["sgw"],))
            P.op("act", lambda: nc.scalar.activation(out=F["av"][:], in_=za_ps[:, 0:TT], func=AF.Sigmoid,
                                                     bias=col("iclr_base")), rd=(za_ps, pc["iclr_base"]),
                 wr=(F["av"],))
            P.op("act", lambda: nc.scalar.copy(out=F["gf"][:], in_=g_ps[:, 0:TT]), rd=(g_ps,), wr=(F["gf"],))
            P.op("dve", lambda: nc.vector.tensor_scalar(out=F["kk"][:], in0=F["kf"][:], scalar1=col("k_k"),
                                                        scalar2=None, op0=ALU.mult), rd=(F["kf"], pc["k_k"]),
                 wr=(F["kk"],))
            P.op("pool", lambda: nc.gpsimd.tensor_tensor(out=F["kk2"][:], in0=F["kk"][:], in1=F["kk"][:],
                                                          op=ALU.mult), rd=(F["kk"],), wr=(F["kk2"],))
            P.op("pe", lambda: nc.tensor.matmul(s_ps[:, 0:TT], lhsT=bones[:], rhs=F["kk2"][:], start=True, stop=True),
                 rd=(bones, F["kk2"]), wr=(s_ps,))
            P.op("act", lambda: nc.scalar.activation(out=F["rn"][:], in_=s_ps[:, 0:TT], func=AF.Ln, bias=1e-24,
                                                     scale=1.0), rd=(s_ps,), wr=(F["rn"],))
            P.op("act", lambda: nc.scalar.activation(out=F["rn"][:], in_=F["rn"][:], func=AF.Exp, scale=-0.5),
                 rd=(F["rn"],), wr=(F["rn"],))
            P.op("pool", lambda: nc.gpsimd.tensor_tensor(out=F["kkn"][:], in0=F["kk"][:], in1=F["rn"][:],
                                                          op=ALU.mult), rd=(F["kk"], F["rn"]), wr=(F["kkn"],))
            P.op("dve", lambda: nc.vector.tensor_scalar(out=F["t1"][:], in0=F["av"][:], scalar1=col("k_a"),
                                                        scalar2=omka[:, q:q + 1], op0=ALU.mult, op1=ALU.add),
                 rd=(F["av"], pc["k_a"], omka), wr=(F["t1"],))
            P.op("pool", lambda: nc.gpsimd.tensor_tensor(out=F["kn"][:], in0=F["kf"][:], in1=F["t1"][:],
                                                          op=ALU.mult), rd=(F["kf"], F["t1"]), wr=(F["kn"],))
            P.op("pool", lambda: nc.gpsimd.tensor_tensor(out=F["bb"][:], in0=F["kkn"][:], in1=F["av"][:],
                                                          op=ALU.mult), rd=(F["kkn"], F["av"]), wr=(F["bb"],))
            P.op("dve", lambda: nc.vector.scalar_tensor_tensor(out=F["rk"][:], in0=F["rf"][:], scalar=col("r_k"),
                                                               in1=F["kn"][:], op0=ALU.mult, op1=ALU.mult),
                 rd=(F["rf"], pc["r_k"], F["kn"]), wr=(F["rk"],))
            P.op("pe", lambda: nc.tensor.matmul(s_ps[:, 0:TT], lhsT=bones[:], rhs=F["rk"][:], start=True, stop=True),
                 rd=(bones, F["rk"]), wr=(s_ps,))
            P.op("dve", lambda: nc.vector.tensor_tensor(out=F["bon"][:], in0=s_ps[:, 0:TT], in1=pk[:, 4, :],
                                                        op=ALU.mult), rd=(s_ps, pk), wr=(F["bon"],))
            P.op("dve", lambda: nc.vector.scalar_tensor_tensor(out=pk[:, 6, :], in0=F["bon"][:], scalar=col("lnx_b"),
                                                               in1=F["gf"][:], op0=ALU.add, op1=ALU.mult),
                 rd=(F["bon"], pc["lnx_b"], F["gf"]), wr=(pk,))
            P.op("pool", lambda: nc.gpsimd.tensor_copy(out=pk[:, 5, :], in_=F["gf"][:]), rd=(F["gf"],), wr=(pk,))
            P.op("dve", lambda: nc.vector.tensor_tensor_scan(out=F["Lc"][:], data0=rmask[:], data1=F["sgw"][:],
                                                             initial=0.0, op0=ALU.mult, op1=ALU.add),
                 rd=(rmask, F["sgw"]), wr=(F["Lc"],))
            P.op("pool", lambda: nc.gpsimd.tensor_tensor(out=F["Lx"][:], in0=F["Lc"][:], in1=F["sgw"][:],
                                                          op=ALU.subtract), rd=(F["Lc"], F["sgw"]), wr=(F["Lx"],))
            P.op("act", lambda: nc.scalar.activation(out=F["eL"][:], in_=F["Lc"][:], func=AF.Exp, scale=-DEC_C),
                 rd=(F["Lc"],), wr=(F["eL"],))
            P.op("act", lambda: nc.scalar.activation(out=F["enL"][:], in_=F["Lc"][:], func=AF.Exp, scale=DEC_C),
                 rd=(F["Lc"],), wr=(F["enL"],))
            P.op("act", lambda: nc.scalar.activation(out=F["eLx"][:], in_=F["Lx"][:], func=AF.Exp, scale=-DEC_C),
                 rd=(F["Lx"],), wr=(F["eLx"],))
            nck = TT // CK
            P.op("pool", lambda: nc.gpsimd.tensor_copy(
                out=dct[:, q, ti * nck:(ti + 1) * nck],
                in_=F["eL"][:].rearrange("p (c j) -> p c j", j=CK)[:, :, CK - 1]), rd=(F["eL"],), wr=(dct,))
            P.op("pool", lambda: nc.gpsimd.tensor_tensor(out=pk[:, 0, :], in0=F["rf"][:], in1=F["eL"][:], op=ALU.mult),
                 rd=(F["rf"], F["eL"]), wr=(pk,))
            P.op("dve", lambda: nc.vector.scalar_tensor_tensor(out=pk[:, 1, :], in0=F["kkn"][:], scalar=-1.0,
                                                               in1=F["eLx"][:], op0=ALU.mult, op1=ALU.mult),
                 rd=(F["kkn"], F["eLx"]), wr=(pk,))
            P.op("pool", lambda: nc.gpsimd.tensor_tensor(out=pk[:, 2, :], in0=F["kn"][:], in1=F["enL"][:], op=ALU.mult),
                 rd=(F["kn"], F["enL"]), wr=(pk,))
            P.op("dve", lambda: nc.vector.tensor_tensor(out=pk[:, 3, :], in0=F["bb"][:], in1=F["enL"][:], op=ALU.mult),
                 rd=(F["bb"], F["enL"]), wr=(pk,))
            P.dma("sp", rw, pk, rw_ap[qs, :, tsl], pk[:])
        for q in range(NQ):
            val_ps, gt_ps = bk[0 + 2 * (q % 2)], bk[1 + 2 * (q % 2)]
            proj(val_ps, q * 128, False, ht)
            proj(gt_ps, 512 + q * 128, False, ht)
            gb = glub[q]
            P.op("act", lambda: nc.scalar.activation(out=sgc[:], in_=gt_ps[:, 0:TT], func=AF.Sigmoid),
                 rd=(gt_ps,), wr=(sgc,))
            P.op("dve", lambda: nc.vector.tensor_tensor(out=gb[:, CW - 1:CW - 1 + TT], in0=val_ps[:, 0:TT], in1=sgc[:],
                                                        op=ALU.mult), rd=(val_ps, sgc), wr=(gb,))
            ac = acc[q]
            P.op("dve", lambda: nc.vector.tensor_scalar(out=ac[:], in0=gb[:, 0:TT], scalar1=cw[:, q, 0:1],
                                                        scalar2=pc["conv_b"][:, q:q + 1], op0=ALU.mult, op1=ALU.add),
                 rd=(gb, cw, pc["conv_b"]), wr=(ac,))
            for j in range(1, CW):
                P.op("dve", lambda: nc.vector.scalar_tensor_tensor(out=ac[:], in0=gb[:, j:j + TT],
                                                                   scalar=cw[:, q, j:j + 1], in1=ac[:], op0=ALU.mult,
                                                                   op1=ALU.add), rd=(gb, cw, ac), wr=(ac,))
            P.op("pool", lambda: nc.gpsimd.tensor_copy(out=gb[:, 0:CW - 1], in_=gb[:, TT:TT + CW - 1]),
                 rd=(gb,), wr=(gb,))
        sum_ps, ssq_ps = bk[4], bk[5]
        for q in range(NQ):
            P.op("pe", lambda: nc.tensor.matmul(sum_ps[:, 0:TT], lhsT=ones[:], rhs=acc[q][:], start=(q == 0),
                                                stop=(q == NQ - 1)), rd=(ones, acc[q]), wr=(sum_ps,))
        for q in range(NQ):
            P.op("act", lambda: nc.scalar.activation(out=sqc[:], in_=acc[q][:], func=AF.Square), rd=(acc[q],),
                 wr=(sqc,))
            P.op("pe", lambda: nc.tensor.matmul(ssq_ps[:, 0:TT], lhsT=ones[:], rhs=sqc[:], start=(q == 0),
                                                stop=(q == NQ - 1)), rd=(ones, sqc), wr=(ssq_ps,))
        P.op("act", lambda: nc.scalar.activation(out=mean[:], in_=sum_ps[:, 0:TT], func=AF.Copy, scale=1.0 / 512),
             rd=(sum_ps,), wr=(mean,))
        P.op("pool", lambda: nc.gpsimd.tensor_tensor(out=msq[:], in0=mean[:], in1=mean[:], op=ALU.mult),
             rd=(mean,), wr=(msq,))
        P.op("dve", lambda: nc.vector.scalar_tensor_tensor(out=rstd[:], in0=ssq_ps[:, 0:TT], scalar=1.0 / 512,
                                                           in1=msq[:], op0=ALU.mult, op1=ALU.subtract),
             rd=(ssq_ps, msq), wr=(rstd,))
        P.op("act", lambda: nc.scalar.activation(out=rstd[:], in_=rstd[:], func=AF.Ln, bias=1e-5, scale=1.0),
             rd=(rstd,), wr=(rstd,))
        P.op("act", lambda: nc.scalar.activation(out=rstd[:], in_=rstd[:], func=AF.Exp, scale=-0.5),
             rd=(rstd,), wr=(rstd,))
        for q in range(NQ):
            yb_ = ybt[q % 2]
            P.op("pool", lambda: nc.gpsimd.tensor_tensor(out=tcv[:], in0=acc[q][:], in1=mean[:], op=ALU.subtract),
                 rd=(acc[q], mean), wr=(tcv,))
            P.op("pool", lambda: nc.gpsimd.tensor_tensor(out=tcv[:], in0=tcv[:], in1=rstd[:], op=ALU.mult),
                 rd=(tcv, rstd), wr=(tcv,))
            P.op("act", lambda: nc.scalar.activation(out=yb_[:], in_=tcv[:], func=AF.Silu,
                                                     bias=pc["ln_b"][:, q:q + 1], scale=pc["ln_g"][:, q:q + 1]),
                 rd=(tcv, pc["ln_b"], pc["ln_g"]), wr=(yb_,))
            P.dma("sp", yab, yb_, yab_ap[:, 4 + q, tsl], yb_[:])
    for q in range(NQ):
        P.dma("sp", dcb, dct, dc_ap[q * 128:(q + 1) * 128, :], dct[:, q, :])
    C.close()


def rwkv_scan_stage(P, T, A, scr):
    nc = P.nc
    NH = 8
    NHC = 4
    TT = 512 if T >= 512 else T
    NG = T // TT
    NCG = TT // CK
    NC = T // CK
    rw_ap, rw = scr["rw"]
    dc_ap, dcb = scr["dc"]
    yab_ap, yab = scr["oT"]
    dsrc = Buf(P, None, "dsrc")
    C = Ctx(P)
    idf, idb = make_consts(P, C)
    ones64 = C.sb(uname("ones64"), (128, 64), F32)
    P.op("pool", lambda: nc.gpsimd.memset(ones64[:], 1.0), wr=(ones64,))

    def mk_mask(name, base, cm, step):
        m = C.sb(uname(name), (64, NCG, CK), F32)
        P.op("pool", lambda: nc.gpsimd.memset(m[:], 1.0), wr=(m,))
        P.op("pool", lambda: nc.gpsimd.affine_select(out=m[:], in_=m[:], compare_op=ALU.is_ge, fill=0.0, base=base,
                                                      pattern=[[0, NCG], [step, CK]], channel_multiplier=cm),
             rd=(m,), wr=(m,))
        return m
    m_su = mk_mask("m_su", -1, -1, 1)
    m_sl = mk_mask("m_sl", -1, 1, -1)
    m_iu = mk_mask("m_iu", 0, -1, 1)
    I8 = C.sb(uname("I8"), (64, NCG, CK), F32)
    P.op("pool", lambda: nc.gpsimd.memset(I8[:], 0.0), wr=(I8,))
    P.op("pool", lambda: nc.gpsimd.affine_select(out=I8[:], in_=I8[:], compare_op=ALU.not_equal, fill=1.0, base=0,
                                                  pattern=[[0, NCG], [-1, CK]], channel_multiplier=1),
         rd=(I8,), wr=(I8,))
    lnrow = C.sb(uname("lnrow"), (1, 512), F32, dma=True)
    P.dma("sp", lnrow, dsrc, lnrow[:], A["lnx_g"].rearrange("(o n) -> o n", o=1))
    lnxg = C.sb(uname("lnxg"), (64, NH), F32)
    flat = lambda m: m[:].rearrange("p c i -> p (c i)")

    class Slot:
        pass
    slots = []
    for i in range(NHC):
        S = Slot()
        S.ops = [C.sb(uname("ops"), (128, 7, TT), BF16, dma=True) for _ in range(2)]
        for o_ in S.ops:
            P.op("pool", lambda: nc.gpsimd.memset(o_[64:128, :, :], 0.0), wr=(o_,))
        S.dC = C.sb(uname("dC"), (64, NC), F32, dma=True)
        for nm in ("Akt", "Arbt", "Arkt", "Tt", "Btok", "Ktok", "Vtok", "U", "Mb0", "Mb1", "Mt0", "Mt1"):
            setattr(S, nm, C.sb(uname(nm), (128, TT), BF16))
            P.op("pool", lambda: nc.gpsimd.memset(getattr(S, nm)[64:128, :], 0.0), wr=(getattr(S, nm),))
        S.Hall = C.sb(uname("Hall"), (128, NCG + 1, CK), BF16)
        P.op("pool", lambda: nc.gpsimd.memset(S.Hall[64:128, :, :], 0.0), wr=(S.Hall,))
        S.Hf = C.sb(uname("Hf"), (64, CK), F32)
        S.tmpH = C.sb(uname("tmpH"), (64, CK), F32)
        S.Wsb = C.sb(uname("Wsb"), (128, CK), BF16)
        P.op("pool", lambda: nc.gpsimd.memset(S.Wsb[64:128, :], 0.0), wr=(S.Wsb,))
        S.yT = C.sb(uname("yT"), (128, TT), F32)
        S.sqy = C.sb(uname("sqy"), (128, TT), F32)
        P.op("pool", lambda: nc.gpsimd.memset(S.yT[64:128, :], 0.0), wr=(S.yT,))
        P.op("pool", lambda: nc.gpsimd.memset(S.sqy[64:128, :], 0.0), wr=(S.sqy,))
        S.mean = C.sb(uname("mean"), (64, TT), F32)
        S.rstd = C.sb(uname("rstd"), (64, TT), F32)
        S.yo = C.sb(uname("yo"), (64, TT), BF16)
        S.bk = [C.ps(uname("bk")) for _ in range(2)]
        S.bi = 0
        slots.append(S)
    lps = slots[0].bk[0]
    for h_ in range(NH):
        P.op("pe", lambda: nc.tensor.transpose(out=lps[0:64, h_:h_ + 1], in_=lnrow[0:1, h_ * 64:(h_ + 1) * 64],
                                               identity=idf[0:1, 0:1]), rd=(lnrow, idf), wr=(lps,))
    P.op("dve", lambda: nc.vector.tensor_copy(out=lnxg[:], in_=lps[0:64, 0:NH]), rd=(lps,), wr=(lnxg,))

    def head_prog(S, h):
        def nb():
            S.bi += 1
            return S.bk[S.bi % 2]

        def macro(lt, lsl, rt, rsl, ps):
            for c in range(NCG):
                cs = slice(c * CK, (c + 1) * CK)
                P.op("pe", lambda: nc.tensor.matmul(ps[0:64, cs], lhsT=lsl(c), rhs=rsl(c), start=True, stop=True),
                     rd=(lt, rt), wr=(ps,))
        P.dma("sp", S.dC, dcb, S.dC[:], dc_ap[h * 64:(h + 1) * 64, :])
        P.op("pool", lambda: nc.gpsimd.memset(S.Hf[:], 0.0), wr=(S.Hf,))
        P.op("pool", lambda: nc.gpsimd.memset(S.Hall[0:64, 0, :], 0.0), wr=(S.Hall,))
        Mb, Mtb = [S.Mb0, S.Mb1], [S.Mt0, S.Mt1]
        for g in range(NG):
            g0 = g * TT
            ops = S.ops[g % 2]
            P.dma("sp", ops, rw, ops[0:64, :, :], rw_ap[h * 64:(h + 1) * 64, :, g0:g0 + TT])
            osl = lambda kind: (lambda c: ops[:, kind, c * CK:(c + 1) * CK])
            loc = lambda t_: (lambda c: t_[:, c * CK:(c + 1) * CK])
            Rs, As, Ks, Bs, Vs = osl(0), osl(1), osl(2), osl(3), osl(4)
            idl = lambda c: idb[:, 0:64]
            if g > 0:
                P.op("pool", lambda: nc.gpsimd.tensor_copy(out=S.Hall[0:64, 0, :], in_=S.Hall[0:64, NCG, :]), rd=(S.Hall,),
                     wr=(S.Hall,))
            yield
            ps = nb()
            macro(ops, Bs, ops, As, ps)
            P.op("dve", lambda: nc.vector.tensor_tensor(out=Mtb[0][0:64, :], in0=ps[0:64, 0:TT], in1=flat(m_su), op=ALU.mult),
                 rd=(ps, m_su), wr=(Mtb[0],))
            P.op("pool", lambda: nc.gpsimd.tensor_tensor(out=S.Tt[0:64, :], in0=Mtb[0][0:64, :], in1=flat(I8), op=ALU.add),
                 rd=(Mtb[0], I8), wr=(S.Tt,))
            yield
            ps = nb()
            macro(ops, As, ops, Bs, ps)
            P.op("dve", lambda: nc.vector.tensor_tensor(out=Mb[0][0:64, :], in0=ps[0:64, 0:TT], in1=flat(m_sl), op=ALU.mult),
                 rd=(ps, m_sl), wr=(Mb[0],))
            yield
            for (ls, rs_, dst, mk) in ((Ks, As, S.Akt, m_su), (Bs, Rs, S.Arbt, m_iu), (Ks, Rs, S.Arkt, m_iu)):
                ps = nb()
                macro(ops, ls, ops, rs_, ps)
                P.op("dve", lambda: nc.vector.tensor_tensor(out=dst[0:64, :], in0=ps[0:64, 0:TT], in1=flat(mk), op=ALU.mult),
                     rd=(ps, mk), wr=(dst,))
                yield
            for (src, dst) in ((Bs, S.Btok), (Ks, S.Ktok), (Vs, S.Vtok)):
                ps = nb()
                macro(ops, src, idb, idl, ps)
                P.op("act", lambda: nc.scalar.copy(out=dst[0:64, :], in_=ps[0:64, 0:TT]), rd=(ps,), wr=(dst,))
                yield
            cur = 0
            for p in range(1, 6):
                nxt = 1 - cur
                ps = nb()
                macro(Mtb[cur], loc(Mtb[cur]), Mb[cur], loc(Mb[cur]), ps)
                P.op("act", lambda: nc.scalar.copy(out=Mb[nxt][0:64, :], in_=ps[0:64, 0:TT]), rd=(ps,), wr=(Mb[nxt],))
                yield
                if p < 5:
                    ps2 = nb()
                    macro(Mb[cur], loc(Mb[cur]), Mtb[cur], loc(Mtb[cur]), ps2)
                    P.op("act", lambda: nc.scalar.copy(out=Mtb[nxt][0:64, :], in_=ps2[0:64, 0:TT]), rd=(ps2,),
                         wr=(Mtb[nxt],))
                    yield
                ps3 = nb()
                macro(Mb[nxt], loc(Mb[nxt]), S.Tt, loc(S.Tt), ps3)
                P.op("dve", lambda: nc.vector.tensor_tensor(out=S.Tt[0:64, :], in0=ps3[0:64, 0:TT], in1=S.Tt[0:64, :], op=ALU.add),
                     rd=(ps3, S.Tt), wr=(S.Tt,))
                yield
                cur = nxt
            for c in range(NCG):
                cg = g * NCG + c
                cs = slice(c * CK, (c + 1) * CK)
                w_ps, u_ps = S.bk[0], S.bk[1]
                P.op("pe", lambda: nc.tensor.matmul(w_ps[0:64, 0:CK], lhsT=As(c), rhs=S.Hall[:, c, :], start=True,
                                                    stop=False), rd=(ops, S.Hall), wr=(w_ps,))
                P.op("pe", lambda: nc.tensor.matmul(w_ps[0:64, 0:CK], lhsT=S.Akt[:, cs], rhs=S.Vtok[:, cs], start=False,
                                                    stop=True), rd=(S.Akt, S.Vtok), wr=(w_ps,))
                P.op("act", lambda: nc.scalar.copy(out=S.Wsb[0:64, :], in_=w_ps[0:64, 0:CK]), rd=(w_ps,), wr=(S.Wsb,))
                P.op("act", lambda: nc.scalar.activation(out=S.tmpH[:], in_=S.Hf[:], func=AF.Copy,
                                                         scale=S.dC[:, cg:cg + 1]), rd=(S.Hf, S.dC), wr=(S.tmpH,))
                yield
                P.op("pe", lambda: nc.tensor.matmul(u_ps[0:64, 0:CK], lhsT=S.Tt[:, cs], rhs=S.Wsb[:], start=True,
                                                    stop=True), rd=(S.Tt, S.Wsb), wr=(u_ps,))
                P.op("dve", lambda: nc.vector.tensor_copy(out=S.U[0:64, cs], in_=u_ps[0:64, 0:CK]), rd=(u_ps,), wr=(S.U,))
                yield
                h_ps = w_ps
                P.op("pe", lambda: nc.tensor.matmul(h_ps[0:64, 64:64 + CK], lhsT=S.Btok[:, cs], rhs=S.U[:, cs],
                                                    start=True, stop=False), rd=(S.Btok, S.U), wr=(h_ps,))
                P.op("pe", lambda: nc.tensor.matmul(h_ps[0:64, 64:64 + CK], lhsT=S.Ktok[:, cs], rhs=S.Vtok[:, cs],
                                                    start=False, stop=True), rd=(S.Ktok, S.Vtok), wr=(h_ps,))
                P.op("dve", lambda: nc.vector.scalar_tensor_tensor(out=S.Hf[:], in0=h_ps[0:64, 64:64 + CK],
                                                                   scalar=S.dC[:, cg:cg + 1], in1=S.tmpH[:],
                                                                   op0=ALU.mult, op1=ALU.add),
                     rd=(h_ps, S.dC, S.tmpH), wr=(S.Hf,))
                P.op("act", lambda: nc.scalar.copy(out=S.Hall[0:64, c + 1, :], in_=S.Hf[:]), rd=(S.Hf,), wr=(S.Hall,))
                yield
            y_ps = nb()
            for c in range(NCG):
                cs = slice(c * CK, (c + 1) * CK)
                P.op("pe", lambda: nc.tensor.matmul(y_ps[0:64, cs], lhsT=S.Hall[:, c, :], rhs=Rs(c), start=True,
                                                    stop=False), rd=(S.Hall, ops), wr=(y_ps,))
                P.op("pe", lambda: nc.tensor.matmul(y_ps[0:64, cs], lhsT=S.U[:, cs], rhs=S.Arbt[:, cs], start=False,
                                                    stop=False), rd=(S.U, S.Arbt), wr=(y_ps,))
                P.op("pe", lambda: nc.tensor.matmul(y_ps[0:64, cs], lhsT=S.Vtok[:, cs], rhs=S.Arkt[:, cs], start=False,
                                                    stop=True), rd=(S.Vtok, S.Arkt), wr=(y_ps,))
            P.op("act", lambda: nc.scalar.copy(out=S.yT[0:64, :], in_=y_ps[0:64, 0:TT]), rd=(y_ps,), wr=(S.yT,))
            P.op("act", lambda: nc.scalar.activation(out=S.sqy[0:64, :], in_=S.yT[0:64, :], func=AF.Square), rd=(S.yT,),
                 wr=(S.sqy,))
            yield
            s_ps, q_ps = nb(), nb()
            P.op("pe", lambda: nc.tensor.matmul(s_ps[0:64, 0:TT], lhsT=ones64[:], rhs=S.yT[:], start=True, stop=True),
                 rd=(ones64, S.yT), wr=(s_ps,))
            P.op("pe", lambda: nc.tensor.matmul(q_ps[0:64, 0:TT], lhsT=ones64[:], rhs=S.sqy[:], start=True, stop=True),
                 rd=(ones64, S.sqy), wr=(q_ps,))
            P.op("act", lambda: nc.scalar.activation(out=S.mean[:], in_=s_ps[0:64, 0:TT], func=AF.Copy,
                                                     scale=1.0 / 64), rd=(s_ps,), wr=(S.mean,))
            P.op("pool", lambda: nc.gpsimd.tensor_tensor(out=S.sqy[0:64, :], in0=S.mean[:], in1=S.mean[:], op=ALU.mult),
                 rd=(S.mean,), wr=(S.sqy,))
            P.op("dve", lambda: nc.vector.scalar_tensor_tensor(out=S.rstd[:], in0=q_ps[0:64, 0:TT], scalar=1.0 / 64,
                                                               in1=S.sqy[0:64, :], op0=ALU.mult, op1=ALU.subtract),
                 rd=(q_ps, S.sqy), wr=(S.rstd,))
            yield
            P.op("act", lambda: nc.scalar.activation(out=S.rstd[:], in_=S.rstd[:], func=AF.Ln, bias=64e-5, scale=1.0),
                 rd=(S.rstd,), wr=(S.rstd,))
            P.op("act", lambda: nc.scalar.activation(out=S.rstd[:], in_=S.rstd[:], func=AF.Exp, scale=-0.5),
                 rd=(S.rstd,), wr=(S.rstd,))
            P.op("pool", lambda: nc.gpsimd.tensor_tensor(out=S.yT[0:64, :], in0=S.yT[0:64, :], in1=S.mean[:], op=ALU.subtract),
                 rd=(S.yT, S.mean), wr=(S.yT,))
            yield
            P.op("dve", lambda: nc.vector.scalar_tensor_tensor(out=S.yT[0:64, :], in0=S.yT[0:64, :], scalar=lnxg[:, h:h + 1],
                                                               in1=S.rstd[:], op0=ALU.mult, op1=ALU.mult),
                 rd=(S.yT, lnxg, S.rstd), wr=(S.yT,))
            P.op("dve", lambda: nc.vector.tensor_tensor(out=S.yT[0:64, :], in0=S.yT[0:64, :], in1=ops[0:64, 5, :], op=ALU.mult),
                 rd=(S.yT, ops), wr=(S.yT,))
            P.op("pool", lambda: nc.gpsimd.tensor_tensor(out=S.yo[:], in0=S.yT[0:64, :], in1=ops[0:64, 6, :], op=ALU.add),
                 rd=(S.yT, ops), wr=(S.yo,))
            P.dma("sp", yab, S.yo, yab_ap[(h % 2) * 64:(h % 2) * 64 + 64, h // 2, g0:g0 + TT], S.yo[:])
            yield

    for h0 in range(0, NH, NHC):
        gens = [head_prog(slots[i], h0 + i) for i in range(NHC)]
        while gens:
            for gn in list(gens):
                try:
                    next(gn)
                except StopIteration:
                    gens.remove(gn)
    C.close()


def mixer_ab_phase(P, T, xin, xout, xin_ap, xout_ap, A, gpre_ap, gpost_ap, scr):
    rwkv_prep_stage(P, T, xin, xin_ap, gpre_ap, A, scr)
    rwkv_scan_stage(P, T, A, scr)
    outproj_stage(P, T, xin, xout, xin_ap, xout_ap, scr["oT"][0], scr["oT"][1], A["w_out"], gpost_ap)


def build(T, phases):
    nc = bass.Bass("TRN2", target_bir_lowering=False)
    t = {}
    t["x"] = nc.dram_tensor("x", [T, D], F32, kind="ExternalInput").ap()
    t["norm_gains"] = nc.dram_tensor("norm_gains", [4, 6, D], F32, kind="ExternalInput").ap()
    t["ffn_w_gate"] = nc.dram_tensor("ffn_w_gate", [4, 2, D, DFF], F32, kind="ExternalInput").ap()
    t["ffn_w_up"] = nc.dram_tensor("ffn_w_up", [4, 2, D, DFF], F32, kind="ExternalInput").ap()
    t["ffn_w_down"] = nc.dram_tensor("ffn_w_down", [4, 2, DFF, D], F32, kind="ExternalInput").ap()
    t["c_w_in"] = nc.dram_tensor("c_w_in", [2, D, 4 * D + 16], F32, kind="ExternalInput").ap()
    t["c_forget_bias"] = nc.dram_tensor("c_forget_bias", [2, 16], F32, kind="ExternalInput").ap()
    t["c_q_norm_g"] = nc.dram_tensor("c_q_norm_g", [2, 64], F32, kind="ExternalInput").ap()
    t["c_k_norm_g"] = nc.dram_tensor("c_k_norm_g", [2, 64], F32, kind="ExternalInput").ap()
    t["c_w_out"] = nc.dram_tensor("c_w_out", [2, D, D], F32, kind="ExternalInput").ap()
    for nm, shp in (("a_w_in", [2, D, 2816]), ("a_shift_mu", [2, 1792]), ("a_decay_up", [2, 64, 512]),
                    ("a_decay_base", [2, 512]), ("a_iclr_up", [2, 64, 512]), ("a_iclr_base", [2, 512]),
                    ("a_gate_up", [2, 128, 512]), ("a_k_k", [2, 512]), ("a_k_a", [2, 512]), ("a_r_k", [2, 8, 64]),
                    ("a_lnx_g", [2, 512]), ("a_lnx_b", [2, 512]), ("b_conv_w", [2, 31, 512]), ("b_conv_b", [2, 512]),
                    ("b_ln_g", [2, 512]), ("b_ln_b", [2, 512]), ("ab_w_out", [2, D, D])):
        t[nm] = nc.dram_tensor(nm, shp, F32, kind="ExternalInput").ap()
    y = nc.dram_tensor("y", [T, D], F32, kind="ExternalOutput").ap()
    rwd = nc.dram_tensor("rwd", [512, 7, T], BF16).ap()
    dcd = nc.dram_tensor("dcd", [512, max(T // 64, 1)], F32).ap()
    hTd = nc.dram_tensor("hTd", [128, NCH, T], BF16).ap()
    oTd = nc.dram_tensor("oTd", [128, NCH, T], BF16).ap()
    cumd = nc.dram_tensor("cumd", [16, T], F32).ap()
    xa = nc.dram_tensor("xa", [T, D], F32).ap()
    xb = nc.dram_tensor("xb", [T, D], F32).ap()
    P = Prog(nc)
    with nc.allow_low_precision("bf16 matmul operands, fp32 accumulation"):
        P.clear_all()
        nc.all_engine_barrier()
        bx = Buf(P, None, "x_in")
        cur_ap, cur = t["x"], bx
        pp = [(xa, Buf(P, None, "xa", dma=True)), (xb, Buf(P, None, "xb", dma=True))]
        yb = Buf(P, None, "y", dma=True)
        scr = {"hT": (hTd, Buf(P, None, "hTd", dma=True)), "oT": (oTd, Buf(P, None, "oTd", dma=True)),
               "cum": (cumd, Buf(P, None, "cumd", dma=True)), "rw": (rwd, Buf(P, None, "rwd", dma=True)),
               "dc": (dcd, Buf(P, None, "dcd", dma=True))}
        for i, ph in enumerate(phases):
            last = (i == len(phases) - 1)
            dst_ap, dst = (y, yb) if last else pp[i % 2]
            if ph[0] == "ffn":
                l, s = ph[1], ph[2]
                ffn_phase(P, T, cur, dst, cur_ap, dst_ap, t["ffn_w_gate"][l, s], t["ffn_w_up"][l, s],
                          t["ffn_w_down"][l, s], t["norm_gains"][l, 3 * s if s == 0 else 4],
                          t["norm_gains"][l, 1 if s == 0 else 5])
            elif ph[0] == "ab":
                l = ph[1]
                i2 = l // 2
                A = {"w_in": t["a_w_in"][i2], "shift_mu": t["a_shift_mu"][i2], "decay_up": t["a_decay_up"][i2],
                     "decay_base": t["a_decay_base"][i2], "iclr_up": t["a_iclr_up"][i2],
                     "iclr_base": t["a_iclr_base"][i2], "gate_up": t["a_gate_up"][i2], "k_k": t["a_k_k"][i2],
                     "k_a": t["a_k_a"][i2], "r_k": t["a_r_k"][i2], "lnx_g": t["a_lnx_g"][i2],
                     "lnx_b": t["a_lnx_b"][i2], "conv_w": t["b_conv_w"][i2], "conv_b": t["b_conv_b"][i2],
                     "ln_g": t["b_ln_g"][i2], "ln_b": t["b_ln_b"][i2], "w_out": t["ab_w_out"][i2]}
                mixer_ab_phase(P, T, cur, dst, cur_ap, dst_ap, A, t["norm_gains"][l, 2], t["norm_gains"][l, 3], scr)
            elif ph[0] in ("ab1", "ab2", "ab3"):
                i2 = 0
                A = {"w_in": t["a_w_in"][i2], "shift_mu": t["a_shift_mu"][i2], "decay_up": t["a_decay_up"][i2],
                     "decay_base": t["a_decay_base"][i2], "iclr_up": t["a_iclr_up"][i2],
                     "iclr_base": t["a_iclr_base"][i2], "gate_up": t["a_gate_up"][i2], "k_k": t["a_k_k"][i2],
                     "k_a": t["a_k_a"][i2], "r_k": t["a_r_k"][i2], "lnx_g": t["a_lnx_g"][i2],
                     "lnx_b": t["a_lnx_b"][i2], "conv_w": t["b_conv_w"][i2], "conv_b": t["b_conv_b"][i2],
                     "ln_g": t["b_ln_g"][i2], "ln_b": t["b_ln_b"][i2], "w_out": t["ab_w_out"][i2]}
                if ph[0] == "ab1":
                    rwkv_prep_stage(P, T, cur, cur_ap, t["norm_gains"][0, 2], A, scr)
                elif ph[0] == "ab2":
                    rwkv_scan_stage(P, T, A, scr)
                else:
                    outproj_stage(P, T, cur, dst, cur_ap, dst_ap, scr["oT"][0], scr["oT"][1], A["w_out"],
                                  t["norm_gains"][0, 3])
            elif ph[0] == "fox":
                l = ph[1]
                i2 = l // 2
                fox_phase(P, T, cur, dst, cur_ap, dst_ap, t["c_w_in"][i2], t["c_forget_bias"][i2],
                          t["c_q_norm_g"][i2], t["c_k_norm_g"][i2], t["c_w_out"][i2], t["norm_gains"][l, 2],
                          t["norm_gains"][l, 3], scr)
            cur_ap, cur = dst_ap, dst
        P.wait_all_dma("sp")
        nc.all_engine_barrier()
        P.clear_all()
    P.stack.close()
    print("instructions:", P.n_ins, "waits:", P.n_wait, "dsems:", P.ndsem)
    return nc


SEQ = 4096
N_CORES = 8
IN_NAMES = ("norm_gains", "ffn_w_gate", "ffn_w_up", "ffn_w_down", "a_w_in", "a_shift_mu", "a_decay_up",
            "a_decay_base", "a_iclr_up", "a_iclr_base", "a_gate_up", "a_k_k", "a_k_a", "a_r_k", "a_lnx_g", "a_lnx_b",
            "b_conv_w", "b_conv_b", "b_ln_g", "b_ln_b", "ab_w_out", "c_w_in", "c_forget_bias", "c_q_norm_g",
            "c_k_norm_g", "c_w_out")


def all_phases(depth=4):
    ph = []
    for l in range(depth):
        ph.append(("ffn", l, 0))
        ph.append(("ab", l) if l % 2 == 0 else ("fox", l))
        ph.append(("ffn", l, 1))
    return ph


_NC_CACHE = {}


def kernel(**inputs):
    x = np.ascontiguousarray(np.asarray(inputs["x"], dtype=np.float32))
    B, T, _ = x.shape
    key = (T,)
    if key not in _NC_CACHE:
        _NC_CACHE[key] = build(T, all_phases())
    nc = _NC_CACHE[key]
    shared = {k: np.ascontiguousarray(np.asarray(inputs[k], dtype=np.float32)) for k in IN_NAMES}
    in_maps = []
    for b in range(B):
        m = dict(shared)
        m["x"] = x[b]
        in_maps.append(m)
    res = run_bass_kernel_spmd(nc, in_maps, core_ids=list(range(B)))
    return np.stack([np.asarray(r["y"], dtype=np.float32) for r in res.results], axis=0)
```
